# Optimizing a Trainium2 kernel written in Bass

```python
import math
import jax, jax.numpy as jnp
from jax import lax
import numpy as np

D_MODEL = 1024
BATCH = 4
SEQ = 4096
DEPTH = 2
DEC_BATCH = 32
DEC_SEQ = 4
PAST_LEN = 8192
PAGE_SIZE = 128

N_MIXERS = 2
N_HGRN = (DEPTH + 1) // 2
N_DSA = DEPTH // 2
MEM_WIDTH = D_MODEL // 4
TOK_WIDTH = D_MODEL - MEM_WIDTH
MEM_HEADS = 4
MEM_HD = MEM_WIDTH // MEM_HEADS
N_MEM = 256
HGRN_DK = 128
HGRN_HEADS = TOK_WIDTH // HGRN_DK
HGRN_CHUNK = 32
HEAD_DIM = 128
N_Q_HEADS = TOK_WIDTH // HEAD_DIM
N_KV_HEADS = 2
IDX_HEADS = 8
IDX_DIM = 64
TOPK_MAX = 256
Q_BLOCK = 128
ROPE_THETA = 10000.0
D_FF = 2816
CONV_W = 3
EPS = 1e-6
HGRN_COLS = 4 * TOK_WIDTH + MEM_WIDTH
DSA_SIZES = (N_Q_HEADS * HEAD_DIM, N_KV_HEADS * HEAD_DIM, N_KV_HEADS * HEAD_DIM,
             IDX_HEADS * IDX_DIM, IDX_DIM, IDX_HEADS)
DSA_COLS = sum(DSA_SIZES) + MEM_WIDTH

kernel_name = 'hgrn2_dsa_memory_hybrid_step'


def rms_norm(x, gain):
    xf = x.astype(jnp.float32)
    y = xf * lax.rsqrt(jnp.mean(xf * xf, axis=-1, keepdims=True) + EPS)
    return (y * gain.astype(jnp.float32)).astype(x.dtype)


def rope(x, pos):
    half = x.shape[-1] // 2
    inv_freq = ROPE_THETA ** (-jnp.arange(half, dtype=jnp.float32) / half)
    ang = pos.astype(jnp.float32)[:, None] * inv_freq[None, :]
    cos = jnp.cos(ang)[:, None, :]
    sin = jnp.sin(ang)[:, None, :]
    xf = x.astype(jnp.float32)
    x1, x2 = xf[..., :half], xf[..., half:]
    return jnp.concatenate([x1 * cos - x2 * sin, x2 * cos + x1 * sin], axis=-1).astype(x.dtype)


def hgrn2_recurrence(q, k, v, log_f, s0):
    B, T, H, DK = q.shape
    DV = v.shape[-1]
    c = math.gcd(T, HGRN_CHUNK)
    n = T // c

    def chunks(a):
        return a.astype(jnp.float32).reshape(B, n, c, H, a.shape[-1]).transpose(1, 0, 3, 2, 4)

    causal = jnp.tril(jnp.ones((c, c), dtype=bool))

    def step(S, inp):
        qc, kc, vc, gc = inp
        b = jnp.cumsum(gc, axis=2)
        b_last = b[:, :, -1:, :]
        q_dec = qc * jnp.exp(b)
        k_dec = kc * jnp.exp(-b)
        scores = jnp.where(causal, jnp.einsum('bhtk,bhsk->bhts', q_dec, k_dec), 0.0)
        o = jnp.einsum('bhtk,bhkv->bhtv', q_dec, S) + jnp.einsum('bhts,bhsv->bhtv', scores, vc)
        S_new = (jnp.exp(b_last[:, :, 0, :])[..., None] * S
                 + jnp.einsum('bhsk,bhsv->bhkv', kc * jnp.exp(b_last - b), vc))
        return S_new, o

    S, o = lax.scan(step, s0.astype(jnp.float32), tuple(chunks(a) for a in (q, k, v, log_f)))
    return o.transpose(1, 0, 3, 2, 4).reshape(B, T, H, DV), S


def hgrn2_mixer(z, s0, lower_bound, out_norm):
    B, T, _ = z.shape
    heads = lambda a: a.reshape(B, T, HGRN_HEADS, HGRN_DK)
    q, f_pre, i, g = jnp.split(z, 4, axis=-1)
    lb = lower_bound.reshape(HGRN_HEADS, HGRN_DK)
    f = lb + (1.0 - lb) * jax.nn.sigmoid(heads(f_pre).astype(jnp.float32))
    o, s = hgrn2_recurrence(heads(jax.nn.silu(q)), 1.0 - f, heads(i), jnp.log(f), s0)
    o = rms_norm(o, out_norm) * jax.nn.silu(heads(g).astype(jnp.float32))
    return o.reshape(B, T, TOK_WIDTH).astype(z.dtype), s


def dsa_project(z, pos, q_norm, k_norm, ik_norm):
    B, T, _ = z.shape
    q, k, v, iq, ik, iw = jnp.split(z, np.cumsum(DSA_SIZES)[:-1].tolist(), axis=-1)
    q = rope(rms_norm(q.reshape(B, T, N_Q_HEADS, HEAD_DIM), q_norm), pos)
    k = rope(rms_norm(k.reshape(B, T, N_KV_HEADS, HEAD_DIM), k_norm), pos)
    v = v.reshape(B, T, N_KV_HEADS, HEAD_DIM)
    iq = rope(iq.reshape(B, T, IDX_HEADS, IDX_DIM), pos)
    ik = rope(rms_norm(ik, ik_norm)[:, :, None, :], pos)[:, :, 0, :]
    return q, k, v, iq, ik, iw


def indexer_select(iq, iw, ik, q_pos, n_sel):
    dots = jnp.einsum('bthd,bsd->bths', iq.astype(jnp.float32), ik.astype(jnp.float32)) * (IDX_DIM ** -0.5)
    score = jnp.einsum('bth,bths->bts', iw.astype(jnp.float32) * (IDX_HEADS ** -0.5), jax.nn.relu(dots))
    allowed = jnp.arange(ik.shape[1])[None, :] <= q_pos[:, None]
    score = jnp.where(allowed[None], score, -jnp.inf)
    vals, idx = lax.top_k(score, n_sel)
    return idx, jnp.isfinite(vals)


def gathered_attention(q, kg, vg, valid):
    B, T, H, D = q.shape
    qg = q.reshape(B, T, N_KV_HEADS, H // N_KV_HEADS, D)
    s = jnp.einsum('btgrd,btkgd->btgrk', qg, kg).astype(jnp.float32) * (D ** -0.5)
    s = jnp.where(valid[:, :, None, None, :], s, -jnp.inf)
    p = jax.nn.softmax(s, axis=-1).astype(vg.dtype)
    return jnp.einsum('btgrk,btkgd->btgrd', p, vg).reshape(B, T, H, D)


def dsa_prompt(q, k, v, iq, ik, iw):
    B, T = q.shape[:2]
    n_sel = min(TOPK_MAX, T // 4)
    bidx = jnp.arange(B)[:, None, None]

    def block(start):
        qb = lax.dynamic_slice_in_dim(q, start, Q_BLOCK, axis=1)
        iqb = lax.dynamic_slice_in_dim(iq, start, Q_BLOCK, axis=1)
        iwb = lax.dynamic_slice_in_dim(iw, start, Q_BLOCK, axis=1)
        idx, valid = indexer_select(iqb, iwb, ik, start + jnp.arange(Q_BLOCK), n_sel)
        return gathered_attention(qb, k[bidx, idx], v[bidx, idx], valid)

    o = lax.map(block, jnp.arange(0, T, Q_BLOCK))
    return o.transpose(1, 0, 2, 3, 4).reshape(B, T, N_Q_HEADS * HEAD_DIM)


def dsa_sample(q, k_new, v_new, iq, ik_new, iw, pool_k, pool_v, pool_ik, page_table):
    B, T = q.shape[:2]
    n_sel = min(TOPK_MAX, (PAST_LEN + T) // 4)
    ik_past = pool_ik[page_table].reshape(B, PAST_LEN, IDX_DIM)
    ik_all = jnp.concatenate([ik_past, ik_new.astype(ik_past.dtype)], axis=1)
    idx, valid = indexer_select(iq, iw, ik_all, PAST_LEN + jnp.arange(T), n_sel)
    bidx = jnp.arange(B)[:, None, None]
    is_past = (idx < PAST_LEN)[..., None, None]
    pi = jnp.minimum(idx, PAST_LEN - 1)
    phys = page_table[bidx, pi // PAGE_SIZE]
    row = pi % PAGE_SIZE
    ni = jnp.clip(idx - PAST_LEN, 0, T - 1)
    kg = jnp.where(is_past, pool_k[phys, row], k_new[bidx, ni])
    vg = jnp.where(is_past, pool_v[phys, row], v_new[bidx, ni])
    return gathered_attention(q, kg, vg, valid).reshape(B, T, N_Q_HEADS * HEAD_DIM)


def memory_keys_values(mem, w_kv, k_norm):
    B, M, _ = mem.shape
    k, v = jnp.split(mem @ w_kv, 2, axis=-1)
    return (rms_norm(k.reshape(B, M, MEM_HEADS, MEM_HD), k_norm),
            v.reshape(B, M, MEM_HEADS, MEM_HD))


def memory_attention(zq, q_norm, mk, mv):
    B, T, _ = zq.shape
    q = rms_norm(zq.reshape(B, T, MEM_HEADS, MEM_HD), q_norm)
    s = jnp.einsum('bthd,bmhd->bhtm', q, mk).astype(jnp.float32) * (MEM_HD ** -0.5)
    p = jax.nn.softmax(s, axis=-1).astype(mv.dtype)
    return jnp.einsum('bhtm,bmhd->bthd', p, mv).reshape(B, T, MEM_WIDTH)


def conv_ffn(h, conv_state, w_gate, w_up, conv_w, conv_b, w_down):
    T = h.shape[1]
    u = h @ w_gate
    ext = jnp.concatenate([conv_state.astype(u.dtype), u], axis=1)
    c = conv_b + sum(ext[:, j:j + T] * conv_w[j] for j in range(CONV_W))
    y = (jax.nn.silu(c) * (h @ w_up)) @ w_down
    return y, ext[:, T:]


def setup_inputs(seed: int = 0) -> dict:
    key = jax.random.key(seed)
    ks = list(jax.random.split(key, 40))
    nrm = lambda shape, scale=1.0: jax.random.normal(ks.pop(), shape, jnp.float32) * scale
    gain = lambda shape: 1.0 + 0.1 * jax.random.normal(ks.pop(), shape, jnp.float32)
    n_pages = PAST_LEN // PAGE_SIZE
    n_used = DEC_BATCH * n_pages
    n_pool = n_used + max(1, n_used // 4)
    page_table = jax.random.permutation(ks.pop(), n_pool)[:n_used].reshape(DEC_BATCH, n_pages).astype(jnp.int32)
    d = {}
    d['x_prompt'] = nrm((BATCH, SEQ, D_MODEL))
    d['x_sample'] = nrm((DEC_BATCH, DEC_SEQ, D_MODEL))
    d['cache_k'] = nrm((N_DSA, n_pool, PAGE_SIZE, N_KV_HEADS, HEAD_DIM))
    d['cache_v'] = nrm((N_DSA, n_pool, PAGE_SIZE, N_KV_HEADS, HEAD_DIM))
    d['cache_idx_k'] = nrm((N_DSA, n_pool, PAGE_SIZE, IDX_DIM))
    d['cache_mem_k'] = nrm((DEPTH, DEC_BATCH, N_MEM, MEM_HEADS, MEM_HD))
    d['cache_mem_v'] = nrm((DEPTH, DEC_BATCH, N_MEM, MEM_HEADS, MEM_HD))
    d['state_hgrn'] = nrm((N_HGRN, DEC_BATCH, HGRN_HEADS, HGRN_DK, HGRN_DK), 0.5)
    d['state_ffn_conv'] = nrm((DEPTH, DEC_BATCH, CONV_W - 1, D_FF))
    d['page_table'] = page_table
    d['mem_prompt'] = nrm((BATCH, N_MEM, D_MODEL))
    d['norm_mix'] = gain((DEPTH, D_MODEL))
    d['norm_ffn'] = gain((DEPTH, D_MODEL))
    d['w_in_hgrn'] = nrm((N_HGRN, D_MODEL, HGRN_COLS), D_MODEL ** -0.5)
    d['hgrn_lb_logits'] = nrm((N_HGRN + 1, TOK_WIDTH), 0.5)
    d['hgrn_out_norm'] = gain((N_HGRN, HGRN_DK))
    d['w_in_dsa'] = nrm((N_DSA, D_MODEL, DSA_COLS), D_MODEL ** -0.5)
    d['dsa_q_norm'] = gain((N_DSA, HEAD_DIM))
    d['dsa_k_norm'] = gain((N_DSA, HEAD_DIM))
    d['idx_k_norm'] = gain((N_DSA, IDX_DIM))
    d['w_mem_kv'] = nrm((DEPTH, D_MODEL, 2 * MEM_WIDTH), D_MODEL ** -0.5)
    d['mem_q_norm'] = gain((DEPTH, MEM_HD))
    d['mem_k_norm'] = gain((DEPTH, MEM_HD))
    d['w_out'] = nrm((DEPTH, D_MODEL, D_MODEL), D_MODEL ** -0.5)
    d['w_ffn_gate'] = nrm((DEPTH, D_MODEL, D_FF), D_MODEL ** -0.5)
    d['w_ffn_up'] = nrm((DEPTH, D_MODEL, D_FF), D_MODEL ** -0.5)
    d['ffn_conv_w'] = nrm((DEPTH, CONV_W, D_FF), CONV_W ** -0.5)
    d['ffn_conv_b'] = nrm((DEPTH, D_FF), 0.02)
    d['w_ffn_down'] = nrm((DEPTH, D_FF, D_MODEL), D_FF ** -0.5)
    return d


def reference(x_prompt, x_sample, cache_k, cache_v, cache_idx_k, cache_mem_k, cache_mem_v,
              state_hgrn, state_ffn_conv, page_table, mem_prompt,
              norm_mix, norm_ffn, w_in_hgrn, hgrn_lb_logits, hgrn_out_norm,
              w_in_dsa, dsa_q_norm, dsa_k_norm, idx_k_norm,
              w_mem_kv, mem_q_norm, mem_k_norm, w_out,
              w_ffn_gate, w_ffn_up, ffn_conv_w, ffn_conv_b, w_ffn_down):
    pos_p = jnp.arange(SEQ)
    pos_s = PAST_LEN + jnp.arange(DEC_SEQ)
    lower_bounds = jnp.cumsum(jax.nn.softmax(hgrn_lb_logits.astype(jnp.float32), axis=0), axis=0)
    xp, xs = x_prompt, x_sample
    bp = xp.shape[0]
    hgrn_p, hgrn_s = [], []
    kp_l, vp_l, ikp_l, ks_l, vs_l, iks_l = [], [], [], [], [], []
    mk_l, mv_l, cvp_l, cvs_l = [], [], [], []
    for layer in range(DEPTH):
        j = layer // N_MIXERS
        hp = rms_norm(xp, norm_mix[layer])
        hs = rms_norm(xs, norm_mix[layer])
        if layer % N_MIXERS == 0:
            zp = hp @ w_in_hgrn[j]
            zs = hs @ w_in_hgrn[j]
            s0 = jnp.zeros((bp, HGRN_HEADS, HGRN_DK, HGRN_DK), jnp.float32)
            tp, sp = hgrn2_mixer(zp[..., :-MEM_WIDTH], s0, lower_bounds[j], hgrn_out_norm[j])
            ts, ss = hgrn2_mixer(zs[..., :-MEM_WIDTH], state_hgrn[j], lower_bounds[j], hgrn_out_norm[j])
            hgrn_p.append(sp.astype(state_hgrn.dtype))
            hgrn_s.append(ss.astype(state_hgrn.dtype))
        else:
            zp = hp @ w_in_dsa[j]
            zs = hs @ w_in_dsa[j]
            qp, kp, vp, iqp, ikp, iwp = dsa_project(zp[..., :-MEM_WIDTH], pos_p, dsa_q_norm[j], dsa_k_norm[j], idx_k_norm[j])
            tp = dsa_prompt(qp, kp, vp, iqp, ikp, iwp)
            qs, kn, vn, iqs, ikn, iws = dsa_project(zs[..., :-MEM_WIDTH], pos_s, dsa_q_norm[j], dsa_k_norm[j], idx_k_norm[j])
            ts = dsa_sample(qs, kn, vn, iqs, ikn, iws, cache_k[j], cache_v[j], cache_idx_k[j], page_table)
            kp_l.append(kp); vp_l.append(vp); ikp_l.append(ikp)
            ks_l.append(kn); vs_l.append(vn); iks_l.append(ikn)
        mkp, mvp = memory_keys_values(mem_prompt, w_mem_kv[layer], mem_k_norm[layer])
        mk_l.append(mkp); mv_l.append(mvp)
        cp = memory_attention(zp[..., -MEM_WIDTH:], mem_q_norm[layer], mkp, mvp)
        cs = memory_attention(zs[..., -MEM_WIDTH:], mem_q_norm[layer], cache_mem_k[layer], cache_mem_v[layer])
        xp = xp + jnp.concatenate([tp, cp], axis=-1) @ w_out[layer]
        xs = xs + jnp.concatenate([ts, cs], axis=-1) @ w_out[layer]
        fp, cvp = conv_ffn(rms_norm(xp, norm_ffn[layer]), jnp.zeros((bp, CONV_W - 1, D_FF), xp.dtype),
                           w_ffn_gate[layer], w_ffn_up[layer], ffn_conv_w[layer], ffn_conv_b[layer], w_ffn_down[layer])
        fs, cvs = conv_ffn(rms_norm(xs, norm_ffn[layer]), state_ffn_conv[layer],
                           w_ffn_gate[layer], w_ffn_up[layer], ffn_conv_w[layer], ffn_conv_b[layer], w_ffn_down[layer])
        xp = xp + fp
        xs = xs + fs
        cvp_l.append(cvp); cvs_l.append(cvs)
    return (xp, xs,
            jnp.stack(hgrn_p), jnp.stack(hgrn_s),
            jnp.stack(kp_l), jnp.stack(vp_l), jnp.stack(ikp_l),
            jnp.stack(ks_l), jnp.stack(vs_l), jnp.stack(iks_l),
            jnp.stack(mk_l), jnp.stack(mv_l),
            jnp.stack(cvp_l), jnp.stack(cvs_l))
```

```python
import numpy as np
from contextlib import ExitStack
import concourse.bass as bass
import concourse.mybir as mybir
from concourse.bass_utils import run_bass_kernel_spmd

F32 = mybir.dt.float32
BF16 = mybir.dt.bfloat16
I32 = mybir.dt.int32
AF = mybir.ActivationFunctionType
ALU = mybir.AluOpType
AX = mybir.AxisListType

D = 1024
DFF = 2816
NFC = 22
HG_COLS = 3328
DSA_COLS = 2120
EPS = 1e-6
NEG = -1.0e30
DEBUG = False
STAGE = 99


class _Stop(Exception):
    pass


_PROG = [None]


def _stage(n):
    if STAGE <= n and not _PROG[0].stopped:
        _PROG[0].barrier()
        _PROG[0].stopped = True


class Buf:
    __slots__ = ("w", "r", "grp")

    def __init__(self):
        self.w = None
        self.r = {}
        self.grp = None


class DGroup:
    def __init__(self, sem):
        self.sem = sem
        self.cnt = 0


class Eng:
    def __init__(self, h, sem):
        self.h = h
        self.sem = sem
        self.cnt = 0
        self.known = {}


class Prog:
    def __init__(self, nc, es):
        self.nc = nc
        self.es = es
        self.nsem = 0
        self.PE = Eng(nc.tensor, self.newsem())
        self.ACT = Eng(nc.scalar, self.newsem())
        self.DVE = Eng(nc.vector, self.newsem())
        self.POOL = Eng(nc.gpsimd, self.newsem())
        self.SP = Eng(nc.sync, self.newsem())
        self.engs = [self.PE, self.ACT, self.DVE, self.POOL, self.SP]
        self.groups = []
        self.out_deps = []
        self.stopped = False
        self.free_groups = {}
        _PROG[0] = self

    def newsem(self):
        self.nsem += 1
        return self.es.enter_context(self.nc.semaphore("s%d" % self.nsem))

    def newgroup(self, kind="sp"):
        fl = self.free_groups.setdefault(kind, [])
        if fl:
            g = fl.pop()
            g.recycled = True
            return g
        g = DGroup(self.newsem())
        g.recycled = False
        g.kind = kind
        self.groups.append(g)
        return g

    def release(self, buf):
        if buf.grp is not None:
            self.free_groups[buf.grp.kind].append(buf.grp)
            buf.grp = None

    def _wait(self, eng, dep):
        sem, val = dep
        if isinstance(val, DGroup):
            val = val.cnt
        if sem is eng.sem and eng is self.PE:
            return
        k = id(sem)
        if eng.known.get(k, 0) >= val:
            return
        eng.h.wait_ge(sem, val)
        eng.known[k] = val

    def _deps(self, eng, r, w, skip_grp=None):
        if self.stopped:
            return
        for b in r:
            if b.w is not None:
                self._wait(eng, b.w)
        for b in w:
            if b.w is not None:
                if not (skip_grp is not None and b.w[0] is skip_grp.sem):
                    self._wait(eng, b.w)
            for sem, val in list(b.r.values()):
                self._wait(eng, (sem, val))

    def op(self, eng, fn, r=(), w=()):
        if self.stopped:
            return None
        self._deps(eng, r, w)
        ins = fn(eng.h)
        eng.cnt += 1
        ins.then_inc(eng.sem, 1)
        me = (eng.sem, eng.cnt)
        for b in w:
            b.w = me
            b.r = {}
        for b in r:
            if b not in w:
                b.r[id(eng.sem)] = me
        return ins

    def dma(self, q, out, in_, buf, load, grp=None, is_out=False, other=()):
        if self.stopped:
            return None
        if grp is None:
            if buf.grp is None:
                buf.grp = self.newgroup("pool" if q is self.POOL else "sp")
            grp = buf.grp
            assert (grp.kind == "pool") == (q is self.POOL), "mixed DMA queue kinds on one buffer"
        if getattr(grp, "recycled", False):
            self._wait(q, (grp.sem, grp.cnt))
            grp.recycled = False
        if load:
            self._deps(q, other, [buf], skip_grp=grp)
        else:
            self._deps(q, [buf] + list(other), [])
        ins = q.h.dma_start(out=out, in_=in_)
        grp.cnt += 16
        ins.then_inc(grp.sem, 16)
        if load:
            buf.w = (grp.sem, grp)
            buf.r = {}
        else:
            buf.r[id(grp.sem)] = (grp.sem, grp.cnt)
            if is_out:
                self.out_deps.append(grp)
        return ins

    def idma(self, out, in_, idx_ap, buf, other=()):
        q = self.POOL
        if self.stopped:
            return None
        if buf.grp is None:
            buf.grp = self.newgroup("pool")
        grp = buf.grp
        assert grp.kind == "pool"
        if getattr(grp, "recycled", False):
            self._wait(q, (grp.sem, grp.cnt))
            grp.recycled = False
        self._deps(q, other, [buf], skip_grp=grp)
        ins = q.h.indirect_dma_start(out=out, out_offset=None, in_=in_,
                                     in_offset=bass.IndirectOffsetOnAxis(ap=idx_ap, axis=0))
        grp.cnt += 16
        ins.then_inc(grp.sem, 16)
        buf.w = (grp.sem, grp)
        buf.r = {}
        return ins

    def barrier(self, force=False):
        if self.stopped and not force:
            return
        for e in self.engs:
            if e is self.SP and not force:
                continue
            for o in self.engs:
                if o is not e and o.cnt > 0:
                    self._wait(e, (o.sem, o.cnt))
            for g in self.groups:
                if g.cnt > 0:
                    self._wait(e, (g.sem, g.cnt))

    def finish(self):
        self.barrier(force=True)


def build(NT, NPAGE, NPOOL, NSEL, NBIS, G=4, NSEL_S=256):
    T = NT * 128
    NG = NT // G
    PAST = NPAGE * 128
    nc = bass.Bass("TRN2", target_bir_lowering=False)

    def din(name, shape, dt=F32):
        return nc.dram_tensor(name, list(shape), dt, kind="ExternalInput").ap()

    def dout(name, shape, dt=F32):
        return nc.dram_tensor(name, list(shape), dt, kind="ExternalOutput").ap()

    x_d = din("x", [T, D])
    xs_d = din("xs", [128, D])
    w_in0 = din("w_in0", [D, HG_COLS])
    w_in1 = din("w_in1", [D, DSA_COLS])
    w_kv = din("w_kv", [2, D, 512])
    w_o = din("w_o", [2, D, D])
    w_g = din("w_g", [2, D, DFF])
    w_u = din("w_u", [2, D, DFF])
    w_d = din("w_d", [2, DFF, D])
    bc_nmix = din("bc_nmix", [2, 128, D])
    bc_nffn = din("bc_nffn", [2, 128, D])
    lbl_d = din("lbl", [128, 2, 6])
    onorm_d = din("onorm", [128, 1])
    bc_qn = din("bc_qn", [128, 128])
    bc_kn = din("bc_kn", [128, 128])
    bc_ikn = din("bc_ikn", [128, 64])
    mqn_d = din("mqn", [128, 2])
    bc_mkn = din("bc_mkn", [128, 2, 256])
    cw_d = din("cw", [128, 2, NFC, 3])
    cb_d = din("cb", [128, 2, NFC])
    mem_d = din("mem", [256, D])
    mkT_s = din("mkT_s", [2, 4, 128, 2, 256])
    mv_s = din("mv_s", [2, 4, 256, 256])
    st_h = din("st_h", [4, 6, 128, 128])
    cst_d = din("cst", [2, 128, NFC, 128])
    pt_d = din("pt", [1, 4 * NPAGE], I32)
    pool_ik = din("pool_ik", [NPOOL * 64, 128])
    pool_k = din("pool_k", [NPOOL * 128, 256])
    pool_v = din("pool_v", [NPOOL * 128, 256])
    pidx_d = din("pidx", [128, 1])
    tab_d = din("tab", [(NT + 1) * 128, 1280])
    ident_d = din("ident", [128, 128])
    mhg_d = din("mhg", [128, 384])
    resetm_d = din("resetm", [128, 512])
    validm_d = din("validm", [128, 128])
    cmdiag_d = din("cmdiag", [128, 128])
    cms_d = din("cms", [128, 128])
    bones_d = din("bones", [128, 128])
    m96_d = din("m96", [128, 128])

    y_d = dout("y", [T, D])
    ys_d = dout("ys", [128, D])
    hstp_d = dout("hstp", [6, 128, 128])
    hsts_d = dout("hsts", [4, 6, 128, 128])
    ko_d = dout("ko", [T, 256])
    vo_d = dout("vo", [T, 256])
    iko_d = dout("iko", [T, 64])
    kso_d = dout("kso", [128, 256])
    vso_d = dout("vso", [128, 256])
    ikso_d = dout("ikso", [128, 64])
    mko_d = dout("mko", [2, 256, 256])
    mvo_d = dout("mvo", [2, 256, 256])
    cvp_d = dout("cvp", [2, 128, DFF])
    cvs_d = dout("cvs", [2, 128, DFF])

    if DEBUG:
        dbg_mix = dout("dbg_mix", [128, 8, G * 128])
        dbg_x1 = dout("dbg_x1", [128, G, D])
        dbg_x2 = dout("dbg_x2", [128, G, D])
    es = ExitStack()
    with es:
        P = Prog(nc, es)
        PE, ACT, DVE, POOL, SP = P.PE, P.ACT, P.DVE, P.POOL, P.SP
        cgrp = P.newgroup("pool")

        uid = [0]

        def sb(stack, name, shape, dt=F32):
            uid[0] += 1
            t = stack.enter_context(nc.sbuf_tensor("%s_%d" % (name, uid[0]), list(shape), dt))
            b = Buf()
            if stack is not es:
                stack.callback(P.release, b)
            return t, b

        PS = []
        for i in range(8):
            t = es.enter_context(nc.psum_tensor("ps%d" % i, [128, 512], F32))
            PS.append((t, Buf()))
        psrr = [0]

        def pb():
            i = psrr[0]
            psrr[0] = (i + 1) % 8
            return PS[i]

        def act(out, in_, func, r, w, scale=1.0, bias=0.0, accum=None):
            kw = {}
            if accum is not None:
                kw["accum_out"] = accum
            return P.op(ACT, lambda h: h.activation(out=out, in_=in_, func=func, bias=bias, scale=scale, **kw), r, w)

        def tt(out, in0, in1, op, r, w, eng=None):
            return P.op(eng or DVE, lambda h: h.tensor_tensor(out=out, in0=in0, in1=in1, op=op), r, w)

        def tsc(out, in0, s1, s2, op0, op1, r, w, accum=None, eng=None):
            kw = {}
            if accum is not None:
                kw["accum_out"] = accum
            return P.op(eng or DVE, lambda h: h.tensor_scalar(out=out, in0=in0, scalar1=s1, scalar2=s2, op0=op0, op1=op1, **kw), r, w)

        def stt(out, in0, scalar, in1, op0, op1, r, w):
            return P.op(DVE, lambda h: h.scalar_tensor_tensor(out=out, in0=in0, scalar=scalar, in1=in1, op0=op0, op1=op1), r, w)

        def mm(out, lhsT, rhs, start, stop, r, w):
            return P.op(PE, lambda h: h.matmul(out, lhsT=lhsT, rhs=rhs, start=start, stop=stop), r, w)

        def cp(out, in_, r, w, eng=None):
            e = eng or DVE
            if e is ACT:
                return P.op(ACT, lambda h: h.activation(out=out, in_=in_, func=AF.Copy), r, w)
            return P.op(e, lambda h: h.tensor_copy(out=out, in_=in_), r, w)

        def const(name, src, shape, dt=F32, q=None):
            t, b = sb(es, name, shape, F32)
            P.dma(q or POOL, t[:], src, b, True, grp=cgrp)
            if dt is F32:
                return t, b
            t2, b2 = sb(es, name + "_bf", shape, dt)
            deferred.append(lambda: cp(t2[:], t[:], [b], [b2]))
            return t2, b2
        deferred = []

        identF, identF_b = const("identF", ident_d[:, :], [128, 128])
        mhg, mhg_b = const("mhg", mhg_d[:, 0:128], [128, 128])
        resetm, resetm_b = const("resetm", resetm_d[:, :], [128, 512])
        validm, validm_b = const("validm", validm_d[:, :], [128, 128])
        cmdiag, cmdiag_b = const("cmdiag", cmdiag_d[:, :], [128, 128])
        cms, cms_b = const("cms", cms_d[:, :], [128, 128])
        bones, bones_b = const("bones", bones_d[:, :], [128, 128], BF16)
        m96, m96_b = const("m96", m96_d[:, :], [128, 128])
        pidx, pidx_b = const("pidx", pidx_d[:, :], [128, 1])
        lbl, lbl_b = const("lblc", lbl_d[:, :, :], [128, 2, 6])
        onorm, onorm_b = const("onormc", onorm_d[:, :], [128, 1])
        gq, gq_b = const("gq", bc_qn[:, :], [128, 128])
        gk, gk_b = const("gk", bc_kn[:, :], [128, 128])
        gik, gik_b = const("gik", bc_ikn[:, :], [128, 64])
        mqn, mqn_b = const("mqnc", mqn_d[:, :], [128, 2])
        gmk, gmk_b = const("gmk", bc_mkn[:, :, :], [128, 2, 256])
        cw, cw_b = const("cwc", cw_d[:, :, :, :], [128, 2, NFC, 3])
        cbv, cbv_b = const("cbc", cb_d[:, :, :], [128, 2, NFC])
        for fn_ in deferred:
            fn_()
        identB, identB_b = sb(es, "identB", [128, 128], BF16)
        cp(identB[:], identF[:], [identF_b], [identB_b])
        onesB, onesB_b = sb(es, "onesB", [128, 128], BF16)
        P.op(DVE, lambda h: h.memset(onesB[:], 1.0), [], [onesB_b])
        GN, GN_b = sb(es, "GN", [128, D])
        gmix, gmix_b = bc_nmix, None
        gffn, gffn_b = bc_nffn, None
        ident3, ident3_b = sb(es, "ident3", [128, 3, 128], BF16)
        for j in range(3):
            cp(ident3[:, j, :], identF[:], [identF_b], [ident3_b])
        oml, oml_b = sb(es, "oml", [128, 6])
        tt(oml[:], lbl[:, 1, :], lbl[:, 0, :], ALU.subtract, [lbl_b], [oml_b])
        act(oml[:], oml[:], AF.Sigmoid, [oml_b], [oml_b])
        epsc, epsc_b = sb(es, "epsc", [128, 1])
        P.op(DVE, lambda h: h.memset(epsc[:], EPS), [], [epsc_b])

        NSLOT = 3
        ring = [sb(es, "wr%d" % i, [128, 8, 512], BF16) for i in range(NSLOT)]
        rr = [0]

        wconv = {}

        def convert(name, wap, K_, N_):
            wb_ = nc.dram_tensor(name + "_bf", [K_, N_], BF16, kind="Internal").ap()
            cb_ = Buf()
            cb_.grp = P.newgroup("pool")
            for r0 in range(0, K_, 128):
                if P.stopped:
                    break
                ins = POOL.h.dma_start(out=wb_[r0:r0 + 128, :], in_=wap[r0:r0 + 128, :], max_dma_last_dim=4096)
                cb_.grp.cnt += 16
                ins.then_inc(cb_.grp.sem, 16)
            cb_.w = (cb_.grp.sem, cb_.grp)
            wconv[name] = (wb_, cb_)

        def wblock(wname, r0, kc, c0, ncols):
            wap, cb_ = wconv[wname]
            t, b = ring[rr[0]]
            rr[0] = (rr[0] + 1) % NSLOT
            src = wap[r0:r0 + kc * 128, c0:c0 + ncols].rearrange("(k p) n -> p k n", p=128)
            P.dma(SP, t[:, 0:kc, 0:ncols], src, b, True, other=[cb_])
            return t, b

        convert("w_in0", w_in0, D, HG_COLS)
        for l in range(2):
            convert("w_kv%d" % l, w_kv[l], D, 512)
        convert("w_o0", w_o[0], D, D)
        convert("w_g0", w_g[0], D, DFF)
        convert("w_u0", w_u[0], D, DFF)
        convert("w_d0", w_d[0], DFF, D)
        convert("w_in1", w_in1, D, DSA_COLS)
        convert("w_o1", w_o[1], D, D)
        convert("w_g1", w_g[1], D, DFF)
        convert("w_u1", w_u[1], D, DFF)
        convert("w_d1", w_d[1], DFF, D)

        X, X_b = sb(es, "X", [128, G, D])
        HT, HT_b = sb(es, "HT", [128, 8, G * 128], BF16)
        MIX, MIX_b = sb(es, "MIX", [128, 8, G * 128], BF16)
        HN = [sb(es, "HN%d" % i, [128, D]) for i in range(1)]
        SS, SS_b = sb(es, "SS", [128, 16])
        Sf, Sf_b = [], []
        for h in range(6):
            t, b = sb(es, "Sf%d" % h, [128, 128])
            Sf.append(t); Sf_b.append(b)
        Sb = [[sb(es, "Sb%d_%d" % (h, i), [128, 128], BF16) for i in range(2)] for h in range(6)]
        UH, UH_b = sb(es, "UH", [128, NFC, 2])
        MEMT, MEMT_b = sb(es, "MEMT", [128, 8, 256], BF16)
        hn_i = [0]

        def rmsnorm_T(gt, ntile, gain, gain_b, lyr):
            P.dma(POOL, GN[:], gain[lyr], GN_b, True)
            gain_b = GN_b
            for t in range(ntile):
                hn, hn_b = HN[0]
                act(hn[:], X[:, t, :], AF.Square, [X_b], [hn_b, SS_b], accum=SS[:, 0:1])
                act(SS[:, 1:2], SS[:, 0:1], AF.Ln, [SS_b], [SS_b], scale=1.0 / D, bias=epsc[:, 0:1])
                act(SS[:, 2:3], SS[:, 1:2], AF.Exp, [SS_b], [SS_b], scale=-0.5)
                stt(hn[:], X[:, t, :], SS[:, 2:3], GN[:], ALU.mult, ALU.mult, [X_b, SS_b, gain_b], [hn_b])
                for half in range(2):
                    pt_, pb_ = pb()
                    for j in range(4):
                        kc = half * 4 + j
                        P.op(PE, lambda h: h.transpose(pt_[:, j * 128:(j + 1) * 128], hn[:, kc * 128:(kc + 1) * 128], identF[:]),
                             [hn_b, identF_b], [pb_])
                    act(HT[:, half * 4:half * 4 + 4, t * 128:(t + 1) * 128],
                        pt_[:].rearrange("p (k n) -> p k n", k=4), AF.Copy, [pb_], [HT_b])

        def proj_fm(wap, c0, nch, ntok, consume):
            ci = 0
            while ci < nch:
                nb = min(4, nch - ci)
                wt, wb = wblock(wap, 0, 8, c0 + ci * 128, nb * 128)
                for j in range(nb):
                    pt_, pb_ = pb()
                    for kc in range(8):
                        mm(pt_[:, 0:ntok], wt[:, kc, j * 128:(j + 1) * 128], HT[:, kc, 0:ntok], kc == 0, kc == 7, [wb, HT_b], [pb_])
                    consume(ci + j, pt_, pb_)
                ci += nb

        def proj_tm(wap, c0, ncols, ntile, consume, src=None, src_b=None, kchunks=8, r0=0):
            src = HT if src is None else src
            src_b = HT_b if src_b is None else src_b
            cb_ = 0
            c = 0
            while c < ncols:
                n = min(512, ncols - c)
                wt, wb = wblock(wap, r0, kchunks, c0 + c, n)
                for t in range(ntile):
                    pt_, pb_ = pb()
                    for kc in range(kchunks):
                        mm(pt_[:, 0:n], src[:, kc, t * 128:(t + 1) * 128], wt[:, kc, 0:n], kc == 0, kc == kchunks - 1, [wb, src_b], [pb_])
                    consume(cb_, c, n, t, pt_, pb_)
                c += n
                cb_ += 1

        def mem_attention(st, ntok, slots, l):
            MQ, MQ_b = st["MQ"]
            sq, sq_b = st["mq_sq"]
            rs, rs_b = st["mq_rs"]
            mqb, mqb_b = st["mq_bf"]
            for c in range(2):
                act(sq[:, 0:ntok], MQ[:, c, 0:ntok], AF.Square, [MQ_b], [sq_b])
                pt_, pb_ = pb()
                mm(pt_[:, 0:ntok], bones[:], sq[:, 0:ntok], True, True, [bones_b, sq_b], [pb_])
                act(rs[:, 0:ntok], pt_[:, 0:ntok], AF.Ln, [pb_], [rs_b], scale=1.0 / 64, bias=epsc[:, 0:1])
                act(rs[:, 0:ntok], rs[:, 0:ntok], AF.Exp, [rs_b], [rs_b], scale=-0.5)
                stt(mqb[:, c, 0:ntok], MQ[:, c, 0:ntok], mqn[:, l:l + 1], rs[:, 0:ntok], ALU.mult, ALU.mult, [MQ_b, mqn_b, rs_b], [mqb_b])
            mexps = st["mexp"]
            mei = [0]
            rd, rd_b = st["mrd"]
            for (c0, ncol, MKT, MKT_b, MV, MV_b) in slots:
                for c in range(2):
                    pn, pn_b = pb()
                    pd, pd_b = pb()
                    for j in range(2):
                        hh = 2 * c + j
                        lo, hi = 64 * j, 64 * j + 64
                        for mt in range(2):
                            ps_, ps_b = pb()
                            pe_t, pe_b = mexps[mei[0] % len(mexps)]
                            mei[0] += 1
                            mm(ps_[:, 0:ncol], MKT[lo:hi, c, mt * 128:(mt + 1) * 128], mqb[lo:hi, c, c0:c0 + ncol], True, True,
                               [MKT_b, mqb_b], [ps_b])
                            act(pe_t[:, 0:ncol], ps_[:, 0:ncol], AF.Exp, [ps_b], [pe_b], scale=0.125)
                            mm(pn[lo:hi, 0:ncol], MV[:, mt, hh * 64:(hh + 1) * 64], pe_t[:, 0:ncol], mt == 0, mt == 1, [MV_b, pe_b], [pn_b])
                            mm(pd[lo:hi, 0:ncol], onesB[:, 0:64], pe_t[:, 0:ncol], mt == 0, mt == 1, [onesB_b, pe_b], [pd_b])
                    act(rd[:, 0:ncol], pd[:, 0:ncol], AF.Ln, [pd_b], [rd_b])
                    act(rd[:, 0:ncol], rd[:, 0:ncol], AF.Exp, [rd_b], [rd_b], scale=-1.0)
                    tt(MIX[:, 6 + c, c0:c0 + ncol], pn[:, 0:ncol], rd[:, 0:ncol], ALU.mult, [pn_b, rd_b], [MIX_b])

        def out_proj(l, ntile):
            def consume(cb_, c, n, t, pt_, pb_):
                tt(X[:, t, c:c + n], X[:, t, c:c + n], pt_[:, 0:n], ALU.add, [X_b, pb_], [X_b])
            proj_tm("w_o%d" % l, 0, D, ntile, consume, src=MIX, src_b=MIX_b)

        def ffn(l, ntile, sample, cv_out):
            ntok = ntile * 128
            with ExitStack() as fs:
                AT, AT_b = sb(fs, "AT", [128, NFC, ntok], BF16)
                Uc = [sb(fs, "Uc%d" % i, [128, 2 + ntok]) for i in range(2)]
                Cc = [sb(fs, "Cc%d" % i, [128, ntok]) for i in range(2)]
                Sg = [sb(fs, "Sg%d" % i, [128, ntok]) for i in range(4)]
                UTOK, UTOK_b = sb(fs, "UTOK", [128, DFF])
                if sample:
                    CST, CST_b = sb(fs, "CST", [128, NFC, 128])
                    P.dma(POOL, CST[:], cst_d[l], CST_b, True)
                rmsnorm_T(None, ntile, gffn, gffn_b, l)
                gps = {}

                def cons_g(ci, pt_, pb_):
                    u, u_b = Uc[ci % 2]
                    cp(u[:, 0:2], UH[:, ci, :], [UH_b], [u_b])
                    if sample:
                        tt(u[:, 2:2 + ntok], pt_[:, 0:ntok], validm[:, 0:ntok], ALU.mult, [pb_, validm_b], [u_b])
                        tt(u[:, 2:2 + ntok], u[:, 2:2 + ntok], CST[:, ci, :], ALU.add, [u_b, CST_b], [u_b])
                    else:
                        act(u[:, 2:2 + ntok], pt_[:, 0:ntok], AF.Copy, [pb_], [u_b])
                    cp(UH[:, ci, :], u[:, ntok:ntok + 2], [u_b], [UH_b])
                    c_, c_b = Cc[ci % 2]
                    if sample:
                        tsc(c_[:], u[:, 2:2 + ntok], cw[:, l, ci, 2:3], cbv[:, l, ci:ci + 1], ALU.mult, ALU.add, [u_b, cw_b, cbv_b], [c_b])
                    else:
                        act(c_[:], pt_[:, 0:ntok], AF.Identity, [pb_, cw_b, cbv_b], [c_b], scale=cw[:, l, ci, 2:3], bias=cbv[:, l, ci:ci + 1])
                    stt(c_[:], u[:, 1:1 + ntok], cw[:, l, ci, 1:2], c_[:], ALU.mult, ALU.add, [u_b, cw_b], [c_b])
                    stt(c_[:], u[:, 0:ntok], cw[:, l, ci, 0:1], c_[:], ALU.mult, ALU.add, [u_b, cw_b], [c_b])
                    s_, s_b = Sg[ci % 4]
                    act(s_[:], c_[:], AF.Silu, [c_b], [s_b])
                    if cv_out is not None:
                        pq, pq_b = pb()
                        P.op(PE, lambda h: h.transpose(pq[:, 0:128], u[:, 2 + ntok - 128:2 + ntok], identF[:]), [u_b, identF_b], [pq_b])
                        act(UTOK[:, ci * 128:(ci + 1) * 128], pq[:, 0:128], AF.Copy, [pq_b], [UTOK_b])
                    gps[ci] = (s_, s_b)

                ci = 0
                while ci < NFC:
                    nb = min(4, NFC - ci)
                    wt, wb = wblock("w_g%d" % l, 0, 8, ci * 128, nb * 128)
                    for j in range(nb):
                        pt_, pb_ = pb()
                        for kc in range(8):
                            mm(pt_[:, 0:ntok], wt[:, kc, j * 128:(j + 1) * 128], HT[:, kc, 0:ntok], kc == 0, kc == 7, [wb, HT_b], [pb_])
                        cons_g(ci + j, pt_, pb_)
                    wt, wb = wblock("w_u%d" % l, 0, 8, ci * 128, nb * 128)
                    for j in range(nb):
                        pt_, pb_ = pb()
                        for kc in range(8):
                            mm(pt_[:, 0:ntok], wt[:, kc, j * 128:(j + 1) * 128], HT[:, kc, 0:ntok], kc == 0, kc == 7, [wb, HT_b], [pb_])
                        s_, s_b = gps[ci + j]
                        tt(AT[:, ci + j, :], s_[:], pt_[:, 0:ntok], ALU.mult, [s_b, pb_], [AT_b])
                    ci += nb
                if cv_out is not None:
                    P.dma(POOL, cv_out, UTOK[:], UTOK_b, False, is_out=True)
                for cbk in range(2):
                    accs = [pb() for _ in range(ntile)]
                    for kh, (k0, kn) in enumerate(((0, 8), (8, 8), (16, 6))):
                        wt, wb = wblock("w_d%d" % l, k0 * 128, kn, cbk * 512, 512)
                        for t in range(ntile):
                            pt_, pb_ = accs[t]
                            for k in range(kn):
                                fc = k0 + k
                                mm(pt_[:, :], AT[:, fc, t * 128:(t + 1) * 128], wt[:, k, :], fc == 0, fc == NFC - 1, [wb, AT_b], [pb_])
                    for t in range(ntile):
                        pt_, pb_ = accs[t]
                        tt(X[:, t, cbk * 512:(cbk + 1) * 512], X[:, t, cbk * 512:(cbk + 1) * 512], pt_[:, :], ALU.add, [X_b, pb_], [X_b])
            P.barrier()

        def layer0(ntile, sample, mslots, first, last):
            ntok = ntile * 128
            nch = ntok // 32
            with ExitStack() as ls:
                st = {}
                st["MQ"] = sb(ls, "MQ", [128, 2, ntok])
                st["mq_sq"] = sb(ls, "mq_sq", [128, ntok], BF16)
                st["mq_rs"] = sb(ls, "mq_rs", [128, ntok])
                st["mq_bf"] = sb(ls, "mq_bf", [128, 2, ntok], BF16)
                st["mexp"] = [sb(ls, "mexp%d" % i, [128, ntok], BF16) for i in range(2)]
                st["mrd"] = sb(ls, "mrd", [128, ntok])
                QS, QS_b = sb(ls, "QS", [128, 6, ntok], BF16)
                GS, GS_b = sb(ls, "GS", [128, 6, ntok], BF16)
                VBh, VBh_b = sb(ls, "VBh", [128, ntile, 768], BF16)
                QD, QD_b = sb(ls, "QD", [128, 6, ntok], BF16)
                KD, KD_b = sb(ls, "KD", [128, 6, ntok], BF16)
                EBL, EBL_b = sb(ls, "EBL", [128, 6, nch])
                tmp = [sb(ls, "l0t%d" % i, [128, ntok]) for i in range(4)]
                K2, K2_b = sb(ls, "K2", [128, ntok])
                K2m, K2m_b = sb(ls, "K2m", [128, 128])
                K2T, K2T_b = sb(ls, "K2T", [128, ntile, 768], BF16)
                K2T3, K2T3_b = sb(ls, "K2T3", [128, ntile, 768], BF16)
                SMs = [sb(ls, "SM%d" % i, [128, 128], BF16) for i in range(3)]
                Ohs = [sb(ls, "Oh%d" % i, [128, ntok]) for i in range(3)]
                Osq, Osq_b = sb(ls, "Osq", [128, ntok], BF16)
                Ors, Ors_b = sb(ls, "Ors", [128, ntok])
                MQ, MQ_b = st["MQ"]

                rmsnorm_T(None, ntile, gmix, gmix_b, 0)
                _stage(1.20)

                def cons_q(ci, pt_, pb_):
                    act(QS[:, ci, :], pt_[:, 0:ntok], AF.Silu, [pb_], [QS_b])
                proj_fm("w_in0", 0, 6, ntok, cons_q)
                _stage(1.21)

                def cons_f(h, pt_, pb_):
                    a, a_b = tmp[0]
                    kk, kk_b = tmp[1]
                    lf, lf_b = tmp[2]
                    bb, bb_b = tmp[0]
                    eb, eb_b = tmp[3]
                    en, en_b = tmp[2]
                    act(a[:], pt_[:, 0:ntok], AF.Sigmoid, [pb_], [a_b], scale=-1.0)
                    tsc(kk[:], a[:], oml[:, h:h + 1], None, ALU.mult, ALU.bypass, [a_b, oml_b], [kk_b])
                    _stage(1.211)
                    act(lf[:], kk[:], AF.Ln, [kk_b], [lf_b], scale=-1.0, bias=1.0)
                    if sample:
                        tt(lf[:], lf[:], validm[:, 0:ntok], ALU.mult, [lf_b, validm_b], [lf_b])
                    _stage(1.212)
                    P.op(DVE, lambda hd: hd.tensor_tensor_scan(out=bb[:], data0=resetm[:, 0:ntok], data1=lf[:], initial=0.0,
                                                               op0=ALU.mult, op1=ALU.add), [resetm_b, lf_b], [bb_b])
                    _stage(1.213)
                    act(eb[:], bb[:], AF.Exp, [bb_b], [eb_b])
                    act(en[:], bb[:], AF.Exp, [bb_b], [en_b], scale=-1.0)
                    tt(QD[:, h, :], QS[:, h, :], eb[:], ALU.mult, [QS_b, eb_b], [QD_b])
                    tt(KD[:, h, :], kk[:], en[:], ALU.mult, [kk_b, en_b], [KD_b])
                    _stage(1.214)
                    cp(EBL[:, h, :], eb[:].rearrange("p (c j) -> p c j", j=32)[:, :, 31], [eb_b], [EBL_b])
                    _stage(1.215)
                    tt(K2[:].rearrange("p (c j) -> p c j", j=32), KD[:, h, :].rearrange("p (c j) -> p c j", j=32),
                       EBL[:, h, :].unsqueeze(2).to_broadcast([128, nch, 32]), ALU.mult, [KD_b, EBL_b], [K2_b])
                    _stage(1.216)
                    for t in range(ntile):
                        pq, pq_b = pb()
                        P.op(PE, lambda hd: hd.transpose(pq[:, 0:128], K2[:, t * 128:(t + 1) * 128], identF[:]), [K2_b, identF_b], [pq_b])
                        act(K2T[:, t, h * 128:(h + 1) * 128], pq[:, 0:128], AF.Copy, [pq_b], [K2T_b])
                        _stage(1.217)
                        tt(K2m[:], K2[:, t * 128:(t + 1) * 128], m96[:], ALU.mult, [K2_b, m96_b], [K2m_b])
                        pq2, pq2_b = pb()
                        P.op(PE, lambda hd: hd.transpose(pq2[:, 0:128], K2m[:], identF[:]), [K2m_b, identF_b], [pq2_b])
                        act(K2T3[:, t, h * 128:(h + 1) * 128], pq2[:, 0:128], AF.Copy, [pq2_b], [K2T3_b])
                    _stage(1.2171 + 0.0001 * h)
                proj_fm("w_in0", 768, 6, ntok, cons_f)
                _stage(1.22)

                def cons_g(ci, pt_, pb_):
                    act(GS[:, ci, :], pt_[:, 0:ntok], AF.Silu, [pb_], [GS_b])
                proj_fm("w_in0", 2304, 6, ntok, cons_g)

                def cons_m(ci, pt_, pb_):
                    act(MQ[:, ci, :], pt_[:, 0:ntok], AF.Copy, [pb_], [MQ_b])
                proj_fm("w_in0", 3072, 2, ntok, cons_m)

                def cons_i(cb_, c, n, t, pt_, pb_):
                    act(VBh[:, t, c:c + n], pt_[:, 0:n], AF.Copy, [pb_], [VBh_b])
                proj_tm("w_in0", 1536, 768, ntile, cons_i)
                _stage(1.23)

                for hh in range(2):
                    heads = [3 * hh + i for i in range(3)]
                    par = {h: 0 for h in heads}
                    if first and not sample:
                        for h in heads:
                            P.op(DVE, lambda hd: hd.memset(Sf[h][:], 0.0), [], [Sf_b[h]])
                            P.op(DVE, lambda hd: hd.memset(Sb[h][0][0][:], 0.0), [], [Sb[h][0][1]])
                    k_ps = 0
                    k_pu = 0
                    for t in range(ntile):
                        cols = slice(t * 128, (t + 1) * 128)
                        pos = {}
                        for i, h in enumerate(heads):
                            ps_, ps_b = PS[3 + (k_ps % 2)]
                            k_ps += 1
                            sm, sm_b = SMs[i]
                            mm(ps_[:, 0:128], KD[:, h, cols], QD[:, h, cols], True, True, [KD_b, QD_b], [ps_b])
                            tt(sm[:], ps_[:, 0:128], mhg[:], ALU.mult, [ps_b, mhg_b], [sm_b])
                            po, po_b = PS[i]
                            pos[h] = (po, po_b)
                            mm(po[:, 0:128], VBh[:, t, h * 128:(h + 1) * 128], sm[:], True, False, [VBh_b, sm_b], [po_b])
                        for c in range(4):
                            ccols = slice(t * 128 + c * 32, t * 128 + c * 32 + 32)
                            for i, h in enumerate(heads):
                                po, po_b = pos[h]
                                if sample:
                                    P.dma(POOL, Sf[h][:], st_h[c, h], Sf_b[h], True)
                                    par[h] = 0
                                    cp(Sb[h][0][0][:], Sf[h][:], [Sf_b[h]], [Sb[h][0][1]], eng=ACT)
                                sbt, sbb = Sb[h][par[h]]
                                mm(po[:, c * 32:(c + 1) * 32], sbt[:], QD[:, h, ccols], False, c == 3, [sbb, QD_b], [po_b])
                                pu, pu_b = PS[5 + (k_pu % 3)]
                                k_pu += 1
                                if c < 3:
                                    mm(pu[:, 0:128], K2T[c * 32:(c + 1) * 32, t, h * 128:(h + 1) * 128],
                                       VBh[c * 32:(c + 1) * 32, t, h * 128:(h + 1) * 128], True, True, [K2T_b, VBh_b], [pu_b])
                                else:
                                    mm(pu[:, 0:128], K2T3[64:128, t, h * 128:(h + 1) * 128],
                                       VBh[64:128, t, h * 128:(h + 1) * 128], True, True, [K2T3_b, VBh_b], [pu_b])
                                stt(Sf[h][:], Sf[h][:], EBL[:, h, t * 4 + c:t * 4 + c + 1], pu[:, 0:128], ALU.mult, ALU.add,
                                    [Sf_b[h], EBL_b, pu_b], [Sf_b[h]])
                                par[h] ^= 1
                                cp(Sb[h][par[h]][0][:], Sf[h][:], [Sf_b[h]], [Sb[h][par[h]][1]], eng=ACT)
                                if sample:
                                    P.dma(POOL, hsts_d[c, h], Sf[h][:], Sf_b[h], False, is_out=True)
                        for i, h in enumerate(heads):
                            po, po_b = pos[h]
                            act(Ohs[i][0][:, cols], po[:, 0:128], AF.Copy, [po_b], [Ohs[i][1]])
                    for i, h in enumerate(heads):
                        Oh, Oh_b = Ohs[i]
                        if par[h] == 1:
                            cp(Sb[h][0][0][:], Sf[h][:], [Sf_b[h]], [Sb[h][0][1]], eng=ACT)
                        if last and not sample:
                            P.dma(POOL, hstp_d[h], Sf[h][:], Sf_b[h], False, is_out=True)
                        act(Osq[:], Oh[:], AF.Square, [Oh_b], [Osq_b])
                        pn_, pn_b = pb()
                        mm(pn_[:, 0:ntok], onesB[:], Osq[:], True, True, [onesB_b, Osq_b], [pn_b])
                        act(Ors[:], pn_[:, 0:ntok], AF.Ln, [pn_b], [Ors_b], scale=1.0 / 128, bias=epsc[:, 0:1])
                        act(Ors[:], Ors[:], AF.Exp, [Ors_b], [Ors_b], scale=-0.5)
                        tt(Oh[:], Oh[:], Ors[:], ALU.mult, [Oh_b, Ors_b], [Oh_b])
                        stt(MIX[:, h, 0:ntok], Oh[:], onorm[:, 0:1], GS[:, h, :], ALU.mult, ALU.mult, [Oh_b, onorm_b, GS_b], [MIX_b])

                _stage(1.24)
                mem_attention(st, ntok, mslots, 0)
            P.barrier()

        def memkv_prompt(l, MKT, MKT_b, MV, MV_b):
            with ExitStack() as ms:
                KVt, KVt_b = sb(ms, "KVt", [128, 512])
                KN, KN_b = sb(ms, "KN", [128, 256])
                sq, sq_b = sb(ms, "kvsq", [128, 64])
                ss4, ss4_b = sb(ms, "ss4", [128, 8])

                def consume(cb_, c, n, t, pt_, pb_):
                    act(KVt[:], pt_[:, :], AF.Copy, [pb_], [KVt_b])
                    for hh in range(4):
                        act(sq[:], KVt[:, hh * 64:(hh + 1) * 64], AF.Square, [KVt_b], [sq_b, ss4_b], accum=ss4[:, hh:hh + 1])
                    act(ss4[:, 4:8], ss4[:, 0:4], AF.Ln, [ss4_b], [ss4_b], scale=1.0 / 64, bias=epsc[:, 0:1])
                    act(ss4[:, 4:8], ss4[:, 4:8], AF.Exp, [ss4_b], [ss4_b], scale=-0.5)
                    for hh in range(4):
                        stt(KN[:, hh * 64:(hh + 1) * 64], KVt[:, hh * 64:(hh + 1) * 64], ss4[:, 4 + hh:5 + hh], gmk[:, l, hh * 64:(hh + 1) * 64],
                            ALU.mult, ALU.mult, [KVt_b, ss4_b, gmk_b], [KN_b])
                    P.dma(POOL, mko_d[l, t * 128:(t + 1) * 128, :], KN[:], KN_b, False, is_out=True)
                    P.dma(POOL, mvo_d[l, t * 128:(t + 1) * 128, :], KVt[:, 256:512], KVt_b, False, is_out=True)
                    cp(MV[:, t, :], KVt[:, 256:512], [KVt_b], [MV_b])
                    for c2 in range(2):
                        pq, pq_b = pb()
                        P.op(PE, lambda h: h.transpose(pq[:, 0:128], KN[:, c2 * 128:(c2 + 1) * 128], identF[:]), [KN_b, identF_b], [pq_b])
                        act(MKT[:, c2, t * 128:(t + 1) * 128], pq[:, 0:128], AF.Copy, [pq_b], [MKT_b])
                proj_tm("w_kv%d" % l, 0, 512, 2, consume, src=MEMT, src_b=MEMT_b)
            P.barrier()

        def dsa_inproj(ntile, tile0, st, sample, KT, KT_b, VB, VB_b, IKT, IKT_b):
            ntok = ntile * 128
            Z, Z_b = st["Z"]
            QN, QN_b = st["QN"]
            QR, QR_b = st["QR"]
            TB, TB_b = st["TB"]
            T1, T1_b = st["T1"]
            T2, T2_b = st["T2"]
            QT, QT_b = st["QT"]
            IQT, IQT_b = st["IQT"]
            IW, IW_b = st["IW"]
            MQ, MQ_b = st["MQ"]
            ss, ss_b = st["ss16"]
            sq, sq_b = st["sqj"]
            rmsnorm_T(None, ntile, gmix, gmix_b, 1)

            def cons_m(ci, pt_, pb_):
                act(MQ[:, ci, 0:ntok], pt_[:, 0:ntok], AF.Copy, [pb_], [MQ_b])
            proj_fm("w_in1", 1864, 2, ntok, cons_m)

            blocks = [(0, 512), (512, 512), (1024, 512), (1536, 328)]
            wts = []
            Zs = st["Zs"]

            def consume(cb_, c, n, t, pt_, pb_):
                z, z_b = Zs[t]
                act(z[:, c:c + n], pt_[:, 0:n], AF.Copy, [pb_], [z_b])
            proj_tm("w_in1", 0, 1864, ntile, consume)

            for t in range(ntile):
                z, z_b = Zs[t]
                gt = tile0 + t
                trow = (NT * 128) if sample else gt * 128
                P.dma(POOL, TB[:], tab_d[trow:trow + 128, :], TB_b, True)
                for j in range(8):
                    act(sq[:, 0:128], z[:, j * 128:(j + 1) * 128], AF.Square, [z_b], [sq_b, ss_b], accum=ss[:, j:j + 1])
                act(sq[:, 0:64], z[:, 1792:1856], AF.Square, [z_b], [sq_b, ss_b], accum=ss[:, 8:9])
                act(ss[:, 0:8], ss[:, 0:8], AF.Ln, [ss_b], [ss_b], scale=1.0 / 128, bias=epsc[:, 0:1])
                act(ss[:, 8:9], ss[:, 8:9], AF.Ln, [ss_b], [ss_b], scale=1.0 / 64, bias=epsc[:, 0:1])
                act(ss[:, 0:9], ss[:, 0:9], AF.Exp, [ss_b], [ss_b], scale=-0.5)
                for j in range(6):
                    stt(QN[:, j * 128:(j + 1) * 128], z[:, j * 128:(j + 1) * 128], ss[:, j:j + 1], gq[:], ALU.mult, ALU.mult, [z_b, ss_b, gq_b], [QN_b])
                for j in range(2):
                    stt(QN[:, 768 + j * 128:768 + (j + 1) * 128], z[:, 768 + j * 128:768 + (j + 1) * 128], ss[:, 6 + j:7 + j], gk[:],
                        ALU.mult, ALU.mult, [z_b, ss_b, gk_b], [QN_b])
                cp(QN[:, 1024:1536], z[:, 1280:1792], [z_b], [QN_b], eng=POOL)
                stt(QN[:, 1536:1600], z[:, 1792:1856], ss[:, 8:9], gik[:], ALU.mult, ALU.mult, [z_b, ss_b, gik_b], [QN_b])
                cp(IW[:, t, :], z[:, 1856:1864], [z_b], [IW_b], eng=POOL)

                def rope(c0, nh, half, cc, sc):
                    w_ = nh * half
                    xv = QN[:, c0:c0 + 2 * w_].rearrange("p (h two d) -> p h two d", h=nh, two=2)
                    ov = QR[:, c0:c0 + 2 * w_].rearrange("p (h two d) -> p h two d", h=nh, two=2)
                    cosv = TB[:, cc:cc + w_].rearrange("p (h d) -> p h d", h=nh)
                    sinv = TB[:, sc:sc + w_].rearrange("p (h d) -> p h d", h=nh)
                    t1 = T1[:, 0:w_].rearrange("p (h d) -> p h d", h=nh)
                    t2 = T2[:, 0:w_].rearrange("p (h d) -> p h d", h=nh)
                    x1, x2 = xv[:, :, 0, :], xv[:, :, 1, :]
                    tt(t1, x1, cosv, ALU.mult, [QN_b, TB_b], [T1_b])
                    tt(t2, x2, sinv, ALU.mult, [QN_b, TB_b], [T2_b], eng=POOL)
                    tt(ov[:, :, 0, :], t1, t2, ALU.subtract, [T1_b, T2_b], [QR_b])
                    tt(t1, x2, cosv, ALU.mult, [QN_b, TB_b], [T1_b])
                    tt(t2, x1, sinv, ALU.mult, [QN_b, TB_b], [T2_b], eng=POOL)
                    tt(ov[:, :, 1, :], t1, t2, ALU.add, [T1_b, T2_b], [QR_b])
                rope(0, 6, 64, 0, 384)
                rope(768, 2, 64, 0, 384)
                rope(1024, 8, 32, 768, 1024)
                rope(1536, 1, 32, 768, 1024)
                if sample:
                    P.dma(POOL, kso_d[:, :], QR[:, 768:1024], QR_b, False, is_out=True)
                    P.dma(POOL, vso_d[:, :], z[:, 1024:1280], z_b, False, is_out=True)
                    P.dma(POOL, ikso_d[:, :], QR[:, 1536:1600], QR_b, False, is_out=True)
                else:
                    P.dma(POOL, ko_d[gt * 128:(gt + 1) * 128, :], QR[:, 768:1024], QR_b, False, is_out=True)
                    P.dma(POOL, vo_d[gt * 128:(gt + 1) * 128, :], z[:, 1024:1280], z_b, False, is_out=True)
                    P.dma(POOL, iko_d[gt * 128:(gt + 1) * 128, :], QR[:, 1536:1600], QR_b, False, is_out=True)
                kcol = 0 if sample else gt * 128
                cp(VB[:, 0 if sample else gt, :], z[:, 1024:1280], [z_b], [VB_b[0 if sample else gt]], eng=ACT)
                for half_ in range(2):
                    pt_, pb_ = pb()
                    nh = 4 if half_ == 0 else 2
                    for j in range(nh):
                        hq = half_ * 4 + j
                        P.op(PE, lambda h: h.transpose(pt_[:, j * 128:(j + 1) * 128], QR[:, hq * 128:(hq + 1) * 128], identF[:]), [QR_b, identF_b], [pb_])
                    act(QT[:, half_ * 4:half_ * 4 + nh, t * 128:(t + 1) * 128], pt_[:, 0:nh * 128].rearrange("p (k n) -> p k n", k=nh),
                        AF.Copy, [pb_], [QT_b])
                pt_, pb_ = pb()
                for j in range(2):
                    P.op(PE, lambda h: h.transpose(pt_[:, j * 128:(j + 1) * 128], QR[:, 768 + j * 128:768 + (j + 1) * 128], identF[:]), [QR_b, identF_b], [pb_])
                P.op(PE, lambda h: h.transpose(pt_[0:64, 256:384], QR[:, 1536:1600], identF[:]), [QR_b, identF_b], [pb_])
                act(KT[:, :, kcol:kcol + 128], pt_[:, 0:256].rearrange("p (k n) -> p k n", k=2), AF.Copy, [pb_], [KT_b[0 if sample else gt]])
                act(IKT[0:64, kcol:kcol + 128], pt_[0:64, 256:384], AF.Copy, [pb_], [IKT_b[0 if sample else gt]])
                for half_ in range(2):
                    pt_, pb_ = pb()
                    for j in range(4):
                        hq = half_ * 4 + j
                        P.op(PE, lambda h: h.transpose(pt_[0:64, j * 128:(j + 1) * 128], QR[:, 1024 + hq * 64:1024 + (hq + 1) * 64], identF[:]),
                             [QR_b, identF_b], [pb_])
                    act(IQT[0:64, half_ * 4:half_ * 4 + 4, t * 128:(t + 1) * 128], pt_[0:64, :].rearrange("p (k n) -> p k n", k=4),
                        AF.Copy, [pb_], [IQT_b])

        def index_scores(st, t, keysets, SC, SC_b, row_lo=0, row_hi=128):
            IQT, IQT_b = st["IQT"]
            IW, IW_b = st["IW"]
            R = st["R"]
            DG = st["DG"]
            for h in range(8):
                tsc(DG[h][0][:], identF[:], IW[:, t, h:h + 1], None, ALU.mult, ALU.bypass, [identF_b, IW_b], [DG[h][1]])
            ri = 0
            bi = 0
            di = 0
            for (rhsf, kb_, ncols, sc0) in keysets:
                k0 = 0
                while k0 < ncols:
                    n = min(512, ncols - k0)
                    pa, pa_b = PS[6 + (bi % 2)]
                    bi += 1
                    prev = None
                    for h in range(8):
                        pt_, pb_ = PS[di % 6]
                        di += 1
                        mm(pt_[:, 0:n], IQT[0:64, h, t * 128:(t + 1) * 128], rhsf(k0, n), True, True, [IQT_b, kb_], [pb_])
                        r_, r_b = R[ri % 4]
                        ri += 1
                        act(r_[:, 0:n], pt_[:, 0:n], AF.Relu, [pb_], [r_b])
                        if prev is not None:
                            ph, pr, pr_b = prev
                            mm(pa[:, 0:n], DG[ph][0][:], pr[:, 0:n], ph == 0, False, [DG[ph][1], pr_b], [pa_b])
                        prev = (h, r_, r_b)
                    ph, pr, pr_b = prev
                    mm(pa[:, 0:n], DG[ph][0][:], pr[:, 0:n], False, True, [DG[ph][1], pr_b], [pa_b])
                    cp(SC[row_lo:row_hi, sc0 + k0:sc0 + k0 + n], pa[row_lo:row_hi, 0:n], [pa_b], [SC_b])
                    k0 += n

        def select_mask(st, L, SC, SC_b, MK, MK_b, cm, cm_b, cmcol, nsel):
            bs, bs_b = st["bs"]
            P.op(DVE, lambda h: h.tensor_reduce(out=bs[:, 0:1], in_=SC[:, 0:L], axis=AX.X, op=ALU.min), [SC_b], [bs_b])
            tt(SC[:, cmcol:cmcol + 128], SC[:, cmcol:cmcol + 128], cm[:], ALU.add, [SC_b, cm_b], [SC_b])
            P.op(DVE, lambda h: h.tensor_reduce(out=bs[:, 1:2], in_=SC[:, 0:L], axis=AX.X, op=ALU.max), [SC_b], [bs_b])
            tt(bs[:, 1:2], bs[:, 1:2], bs[:, 0:1], ALU.subtract, [bs_b], [bs_b])
            for i in range(NBIS):
                stt(bs[:, 2:3], bs[:, 1:2], float(2.0 ** -(i + 1)), bs[:, 0:1], ALU.mult, ALU.add, [bs_b], [bs_b])
                tsc(MK[:, 0:L], SC[:, 0:L], bs[:, 2:3], None, ALU.is_ge, ALU.add, [SC_b, bs_b], [MK_b, bs_b], accum=bs[:, 3:4])
                tsc(bs[:, 4:5], bs[:, 3:4], float(nsel) - 0.5, None, ALU.is_ge, ALU.bypass, [bs_b], [bs_b])
                tt(bs[:, 5:6], bs[:, 2:3], bs[:, 0:1], ALU.subtract, [bs_b], [bs_b])
                stt(bs[:, 0:1], bs[:, 5:6], bs[:, 4:5], bs[:, 0:1], ALU.mult, ALU.add, [bs_b], [bs_b])
            tsc(MK[:, 0:L], SC[:, 0:L], bs[:, 0:1], None, ALU.is_ge, ALU.bypass, [SC_b, bs_b], [MK_b])

        def attention(st, qcols, nq, sel, sel_b, keytiles, MK, MK_b, mixcols, page_loader=None):
            QT, QT_b = st["QT"]
            E = st["E"]
            PP = st["PP"]
            RD, RD_b = st["RD"]
            n3 = 3 * nq
            accO = [PS[0], PS[1]]
            accD = [PS[2], PS[3]]
            nk = len(keytiles)
            ei = 0
            pend = []

            def flush(pl):
                for (g, vf_g, vb_, p_, p_b, kidx) in pl:
                    mm(accO[g][0][:, 0:n3], vf_g, p_[:, 0:n3], kidx == 0, kidx == nk - 1, [vb_, p_b], [accO[g][1]])
                    mm(accD[g][0][:, 0:n3], onesB[:], p_[:, 0:n3], kidx == 0, kidx == nk - 1, [onesB_b, p_b], [accD[g][1]])
            for ki, ent in enumerate(keytiles):
                if ent[0] == "page":
                    mc0 = ent[2]
                    ktf, kb_, vf, vb_ = page_loader(ki, ent[1])
                else:
                    ktf, kb_, vf, vb_, mc0 = ent
                pm, pm_b = PS[6 + (ki % 2)]
                mm(pm[:, 0:n3], MK[:, mc0:mc0 + 128], sel, True, True, [MK_b, sel_b], [pm_b])
                cur = []
                for g in range(2):
                    ps_, ps_b = PS[4 + g]
                    mm(ps_[:, 0:n3].rearrange("p (h n) -> p h n", h=3), ktf(g), QT[:, 3 * g:3 * g + 3, qcols], True, True, [kb_, QT_b], [ps_b])
                    e_, e_b = E[ei % 4]
                    p_, p_b = PP[ei % 4]
                    ei += 1
                    act(e_[:, 0:n3], ps_[:, 0:n3], AF.Exp, [ps_b], [e_b], scale=float(128 ** -0.5))
                    tt(p_[:, 0:n3], e_[:, 0:n3], pm[:, 0:n3], ALU.mult, [e_b, pm_b], [p_b])
                    cur.append((g, vf(g), vb_, p_, p_b, ki))
                flush(pend)
                pend = cur
            flush(pend)
            for g in range(2):
                act(RD[:, 0:n3], accD[g][0][:, 0:n3], AF.Ln, [accD[g][1]], [RD_b])
                act(RD[:, 0:n3], RD[:, 0:n3], AF.Exp, [RD_b], [RD_b], scale=-1.0)
                tt(MIX[:, 3 * g:3 * g + 3, mixcols], accO[g][0][:, 0:n3].rearrange("p (h n) -> p h n", h=3),
                   RD[:, 0:n3].rearrange("p (h n) -> p h n", h=3), ALU.mult, [accO[g][1], RD_b], [MIX_b])

        def l1_common(ls, ntile):
            ntok = ntile * 128
            st = {}
            st["MQ"] = sb(ls, "MQ1", [128, 2, ntok])
            st["mq_sq"] = sb(ls, "mq_sq1", [128, ntok], BF16)
            st["mq_rs"] = sb(ls, "mq_rs1", [128, ntok])
            st["mq_bf"] = sb(ls, "mq_bf1", [128, 2, ntok], BF16)
            st["mexp"] = [sb(ls, "mexp1%d" % i, [128, ntok], BF16) for i in range(3)]
            st["mrd"] = sb(ls, "mrd1", [128, ntok])
            st["QT"] = sb(ls, "QT", [128, 6, ntok], BF16)
            st["IQT"] = sb(ls, "IQT", [64, 8, ntok], BF16)
            st["IW"] = sb(ls, "IW", [128, ntile, 8])
            return st

        def l1_inproj(ls, ntile, st):
            st["Zs"] = [sb(ls, "Z%d" % i, [128, 1864]) for i in range(ntile)]
            st["Z"] = st["Zs"][0]
            st["QN"] = sb(ls, "QN", [128, 1600])
            st["QR"] = sb(ls, "QR", [128, 1600])
            st["TB"] = sb(ls, "TB", [128, 1280])
            st["T1"] = sb(ls, "T1", [128, 384])
            st["T2"] = sb(ls, "T2", [128, 384])
            st["ss16"] = sb(ls, "ss16", [128, 16])
            st["sqj"] = st["T1"]

        def l1_attn(ls, st, sccols):
            st["R"] = [sb(ls, "R%d" % i, [128, 512], BF16) for i in range(4)]
            st["DG"] = [sb(ls, "DG%d" % i, [128, 128], BF16) for i in range(8)]
            st["bs"] = sb(ls, "bs", [128, 8])
            st["E"] = [sb(ls, "E%d" % i, [128, 384], BF16) for i in range(4)]
            st["PP"] = [sb(ls, "PP%d" % i, [128, 384], BF16) for i in range(4)]
            st["RD"] = sb(ls, "RD", [128, 384])
            st["SC"] = sb(ls, "SC", [128, sccols])
            st["MK"] = sb(ls, "MK", [128, sccols], BF16)

        try:
            with ExitStack() as ms0:
                memf = [sb(ms0, "memf%d" % i, [128, D]) for i in range(2)]
                for mt in range(2):
                    P.dma(POOL, memf[mt][0][:], mem_d[mt * 128:(mt + 1) * 128, :], memf[mt][1], True)
                    for half in range(2):
                        pt_, pb_ = pb()
                        for j in range(4):
                            kc = half * 4 + j
                            P.op(PE, lambda h: h.transpose(pt_[:, j * 128:(j + 1) * 128], memf[mt][0][:, kc * 128:(kc + 1) * 128], identF[:]),
                                 [memf[mt][1], identF_b], [pb_])
                        act(MEMT[:, half * 4:half * 4 + 4, mt * 128:(mt + 1) * 128], pt_[:].rearrange("p (k n) -> p k n", k=4), AF.Copy, [pb_], [MEMT_b])
            P.barrier()
            MKTp = [sb(es, "MKTp%d" % l, [128, 2, 256], BF16) for l in range(2)]
            MVp = [sb(es, "MVp%d" % l, [128, 2, 256], BF16) for l in range(2)]
            for l in range(2):
                memkv_prompt(l, MKTp[l][0], MKTp[l][1], MVp[l][0], MVp[l][1])
            _stage(1)

            def sample_group():
                P.dma(POOL, X[:, 0, :], xs_d[:, :], X_b, True)
                P.op(DVE, lambda h: h.memset(UH[:], 0.0), [], [UH_b])
                for l in range(2):
                    with ExitStack() as ss_:
                        MKs = [sb(ss_, "MKs%d" % b, [128, 2, 256], BF16) for b in range(4)]
                        MVs = [sb(ss_, "MVs%d" % b, [128, 2, 256], BF16) for b in range(4)]
                        slots = []
                        for b in range(4):
                            P.dma(POOL, MKs[b][0][:], mkT_s[l, b], MKs[b][1], True)
                            P.dma(POOL, MVs[b][0][:], mv_s[l, b].rearrange("(t p) d -> p t d", p=128), MVs[b][1], True)
                            slots.append((32 * b, 32, MKs[b][0], MKs[b][1], MVs[b][0], MVs[b][1]))
                        if l == 0:
                            layer0(1, True, slots, True, True)
                            _stage(2)
                        else:
                            with ExitStack() as ls:
                                st = l1_common(ls, 1)
                                KTs, KTs_b = sb(ls, "KTs", [128, 2, 128], BF16)
                                VBs, VBs_b = sb(ls, "VBs", [128, 1, 256], BF16)
                                IKTs, IKTs_b = sb(ls, "IKTs", [64, 128], BF16)
                                with ExitStack() as zs:
                                    l1_inproj(zs, 1, st)
                                    dsa_inproj(1, 0, st, True, KTs, [KTs_b], VBs, [VBs_b], IKTs, [IKTs_b])
                                    _stage(5)
                                P.barrier()
                                with ExitStack() as as_:
                                    l1_attn(as_, st, PAST + 128)
                                    SC, SC_b = st["SC"]
                                    MK, MK_b = st["MK"]
                                    IKP, IKP_b = sb(as_, "IKP", [64, PAST], BF16)
                                    PTs, PTs_b = sb(as_, "PTs", [128, 4 * NPAGE], I32)
                                    IDX = sb(as_, "IDX", [128, 4, 4 * NPAGE], I32)
                                    sel3 = sb(as_, "sel3", [128, 4, 3, 32], BF16)
                                    for b in range(4):
                                        for j in range(3):
                                            cp(sel3[0][:, b, j, :], identF[:, 32 * b:32 * b + 32], [identF_b], [sel3[1]])
                                    KPr = [sb(as_, "KPr%d" % i, [128, 2, 128], BF16) for i in range(4)]
                                    VPr = [sb(as_, "VPr%d" % i, [128, 256], BF16) for i in range(4)]
                                    P.dma(POOL, PTs[:], pt_d.partition_broadcast(128), PTs_b, True)
                                    tsc(IDX[0][:, 0, :], PTs[:], 64.0, pidx[:, 0:1], ALU.mult, ALU.add, [PTs_b, pidx_b], [IDX[1]])
                                    tsc(IDX[0][:, 1, :], PTs[:], 256.0, pidx[:, 0:1], ALU.mult, ALU.add, [PTs_b, pidx_b], [IDX[1]])
                                    tsc(IDX[0][:, 2, :], IDX[0][:, 1, :], 128.0, None, ALU.add, ALU.bypass, [IDX[1]], [IDX[1]])
                                    tsc(IDX[0][:, 3, :], PTs[:], 128.0, pidx[:, 0:1], ALU.mult, ALU.add, [PTs_b, pidx_b], [IDX[1]])
                                    for b in (3, 2, 1, 0):
                                        rows = (64, 128) if b == 3 else (32 * b, 32 * b + 32)
                                        for pg in range(NPAGE):
                                            pi = b * NPAGE + pg
                                            P.idma(IKP[0:64, pg * 128:(pg + 1) * 128], pool_ik[:, :], IDX[0][0:64, 0, pi:pi + 1], IKP_b, [IDX[1]])
                                        keysets = [(lambda k0, n: IKP[0:64, k0:k0 + n], IKP_b, PAST, 0),
                                                   (lambda k0, n: IKTs[0:64, k0:k0 + n], IKTs_b, 128, PAST)]
                                        index_scores(st, 0, keysets, SC, SC_b, rows[0], rows[1])
                                    select_mask(st, PAST + 128, SC, SC_b, MK, MK_b, cms, cms_b, PAST, NSEL_S)
                                    _stage(6)
                                    for b in range(4):
                                        kts = []
                                        for pg in range(NPAGE):
                                            kts.append(("page", b * NPAGE + pg, pg * 128))
                                        kts.append(((lambda g: KTs[:, g, :]), KTs_b, (lambda g: VBs[:, 0, g * 128:(g + 1) * 128]), VBs_b, PAST))

                                        def page_loader(i, pi, KPr=KPr, VPr=VPr):
                                            kp, kp_b = KPr[i % 4]
                                            vp, vp_b = VPr[i % 4]
                                            P.idma(kp[:].rearrange("p g r -> p (g r)"), pool_k[:, :], IDX[0][:, 3, pi:pi + 1], kp_b, [IDX[1]])
                                            P.idma(vp[:], pool_v[:, :], IDX[0][:, 3, pi:pi + 1], vp_b, [IDX[1]])
                                            return ((lambda g: kp[:, g, :]), kp_b, (lambda g: vp[:, g * 128:(g + 1) * 128]), vp_b)
                                        attention(st, slice(32 * b, 32 * b + 32), 32, sel3[0][:, b, :, :].rearrange("p j n -> p (j n)"), sel3[1],
                                                  kts, MK, MK_b, slice(32 * b, 32 * b + 32), page_loader)
                                    mem_attention(st, 128, slots, 1)
                            P.barrier()
                    out_proj(l, 1)
                    _stage(3 + 4 * l)
                    ffn(l, 1, True, cvs_d[l])
                    _stage(4 + 4 * l)
                P.dma(POOL, ys_d[:, :], X[:, 0, :], X_b, False, is_out=True)

            sample_group()
            P.barrier()
            _stage(9)

            KT, _ = sb(es, "KT", [128, 2, T], BF16)
            VB, _ = sb(es, "VB", [128, NT, 256], BF16)
            IKT, _ = sb(es, "IKT", [64, T], BF16)
            KT_b = [Buf() for _ in range(NT)]
            VB_b = [Buf() for _ in range(NT)]
            IKT_b = [Buf() for _ in range(NT)]
            UHs = [sb(es, "UHs%d" % l, [128, NFC, 2]) for l in range(2)]
            for l in range(2):
                P.op(DVE, lambda h: h.memset(UHs[l][0][:], 0.0), [], [UHs[l][1]])

            for g in range(NG):
                first, last = (g == 0), (g == NG - 1)
                for t in range(G):
                    P.dma(POOL, X[:, t, :], x_d[(g * G + t) * 128:(g * G + t + 1) * 128, :], X_b, True)
                for l in range(2):
                    slots = [(0, G * 128, MKTp[l][0], MKTp[l][1], MVp[l][0], MVp[l][1])]
                    if l == 0:
                        layer0(G, False, slots, first, last)
                    else:
                        with ExitStack() as ls:
                            st = l1_common(ls, G)
                            with ExitStack() as zs:
                                l1_inproj(zs, G, st)
                                dsa_inproj(G, g * G, st, False, KT, KT_b, VB, VB_b, IKT, IKT_b)
                            P.barrier()
                            with ExitStack() as as_:
                                l1_attn(as_, st, T)
                                SC, SC_b = st["SC"]
                                MK, MK_b = st["MK"]
                                for t in range(G):
                                    gt = g * G + t
                                    L = (gt + 1) * 128
                                    merged = []
                                    kt = 0
                                    while kt <= gt:
                                        nkt = min(4, gt + 1 - kt)
                                        merged.append(((lambda k0, n, kt=kt: IKT[0:64, kt * 128 + k0:kt * 128 + k0 + n]), IKT_b[kt + nkt - 1], nkt * 128, kt * 128))
                                        for q in range(kt, kt + nkt - 1):
                                            P._deps(PE, [IKT_b[q]], [])
                                        kt += nkt
                                    index_scores(st, t, merged, SC, SC_b)
                                    select_mask(st, L, SC, SC_b, MK, MK_b, cmdiag, cmdiag_b, gt * 128, NSEL)
                                    kts = []
                                    for kt in range(gt + 1):
                                        kts.append(((lambda gg, kt=kt: KT[:, gg, kt * 128:(kt + 1) * 128]), KT_b[kt],
                                                    (lambda gg, kt=kt: VB[:, kt, gg * 128:(gg + 1) * 128]), VB_b[kt], kt * 128))
                                    attention(st, slice(t * 128, (t + 1) * 128), 128, ident3[:].rearrange("p j n -> p (j n)"), ident3_b,
                                              kts, MK, MK_b, slice(t * 128, (t + 1) * 128))
                                mem_attention(st, G * 128, slots, 1)
                        P.barrier()
                    if DEBUG and g == 0 and l == 0:
                        with ExitStack() as ds:
                            dm, dm_b = sb(ds, "dbgm", [128, 8, G * 128])
                            cp(dm[:], MIX[:], [MIX_b], [dm_b])
                            P.dma(POOL, dbg_mix[:, :, :], dm[:], dm_b, False, is_out=True)
                            P.barrier()
                    out_proj(l, G)
                    if DEBUG and g == 0 and l == 0:
                        P.dma(POOL, dbg_x1[:, :, :], X[:], X_b, False, is_out=True)
                    cp(UH[:], UHs[l][0][:], [UHs[l][1]], [UH_b])
                    ffn(l, G, False, cvp_d[l] if last else None)
                    cp(UHs[l][0][:], UH[:], [UH_b], [UHs[l][1]])
                    if DEBUG and g == 0 and l == 0:
                        P.dma(POOL, dbg_x2[:, :, :], X[:], X_b, False, is_out=True)
                for t in range(G):
                    P.dma(POOL, y_d[(g * G + t) * 128:(g * G + t + 1) * 128, :], X[:, t, :], X_b, False, is_out=True)
        except _Stop:
            P.barrier()
        P.finish()
    return nc


def _rope_tab(pos, half):
    inv = 10000.0 ** (-np.arange(half, dtype=np.float32) / half)
    ang = pos.astype(np.float32)[:, None] * inv[None, :]
    return np.cos(ang).astype(np.float32), np.sin(ang).astype(np.float32)


def _consts(NT, PASTLEN):
    c = {}
    c["ident"] = np.eye(128, dtype=np.float32)
    s = np.arange(128)[:, None]
    t = np.arange(128)[None, :]
    m = ((s // 32 == t // 32) & (s <= t)).astype(np.float32)
    c["mhg"] = np.tile(m, (1, 3))
    r = np.ones((128, 512), np.float32)
    r[:, ::32] = 0.0
    c["resetm"] = r
    v = np.zeros((128, 128), np.float32)
    for b in range(4):
        v[:, 32 * b + 2:32 * b + 6] = 1.0
    c["validm"] = v
    c["cmdiag"] = np.where(t <= s, 0.0, NEG).astype(np.float32)
    cm = np.full((128, 128), NEG, np.float32)
    for b in range(4):
        for tq in range(4):
            cm[32 * b + 2 + tq, 32 * b + 2:32 * b + 3 + tq] = 0.0
    c["cms"] = cm
    bo = np.zeros((128, 128), np.float32)
    bo[:64, :64] = 1.0
    bo[64:, 64:] = 1.0
    c["bones"] = bo
    m9 = np.zeros((128, 128), np.float32)
    m9[:, 96:] = 1.0
    c["m96"] = m9
    c["pidx"] = np.arange(128, dtype=np.float32).reshape(128, 1)
    T = NT * 128
    pos = np.zeros(T + 128, np.float32)
    pos[:T] = np.arange(T)
    for b in range(4):
        for tq in range(4):
            pos[T + 32 * b + 2 + tq] = PASTLEN + tq
    cq, sq = _rope_tab(pos, 64)
    ci, si = _rope_tab(pos, 32)
    c["tab"] = np.concatenate([np.tile(cq, (1, 6)), np.tile(sq, (1, 6)), np.tile(ci, (1, 8)), np.tile(si, (1, 8))], axis=1).astype(np.float32)
    return c


def _fm(v, nch):
    return np.ascontiguousarray(v.reshape(nch, 128).T)


def make_in_maps(inp, NT, NPAGE, NPOOL, ncores=8):
    f = lambda a: np.ascontiguousarray(np.asarray(a, dtype=np.float32))
    PASTLEN = NPAGE * 128
    cst = _consts(NT, PASTLEN)
    T = NT * 128
    shared = {}
    shared["w_in0"] = f(inp["w_in_hgrn"][0])
    shared["w_in1"] = f(inp["w_in_dsa"][0])
    shared["w_kv"] = f(inp["w_mem_kv"])
    shared["w_o"] = f(inp["w_out"])
    shared["w_g"] = f(inp["w_ffn_gate"])
    shared["w_u"] = f(inp["w_ffn_up"])
    shared["w_d"] = f(inp["w_ffn_down"])
    bc = lambda a: f(np.broadcast_to(np.asarray(a)[:, None, :], (a.shape[0], 128, a.shape[1])))
    shared["bc_nmix"] = bc(np.asarray(inp["norm_mix"]))
    shared["bc_nffn"] = bc(np.asarray(inp["norm_ffn"]))
    lb = np.asarray(inp["hgrn_lb_logits"], np.float32)
    shared["lbl"] = f(np.stack([_fm(lb[0], 6), _fm(lb[1], 6)], axis=1))
    shared["onorm"] = f(np.asarray(inp["hgrn_out_norm"])[0].reshape(128, 1))
    shared["bc_qn"] = f(np.broadcast_to(np.asarray(inp["dsa_q_norm"])[0][None, :], (128, 128)))
    shared["bc_kn"] = f(np.broadcast_to(np.asarray(inp["dsa_k_norm"])[0][None, :], (128, 128)))
    shared["bc_ikn"] = f(np.broadcast_to(np.asarray(inp["idx_k_norm"])[0][None, :], (128, 64)))
    mq = np.asarray(inp["mem_q_norm"], np.float32)
    shared["mqn"] = f(np.concatenate([mq, mq], axis=1).T)
    mk = np.asarray(inp["mem_k_norm"], np.float32)
    shared["bc_mkn"] = f(np.broadcast_to(np.tile(mk, (1, 4))[None, :, :], (128, 2, 256)))
    cwv = np.asarray(inp["ffn_conv_w"], np.float32)
    shared["cw"] = f(cwv.reshape(2, 3, NFC, 128).transpose(3, 0, 2, 1))
    cbv = np.asarray(inp["ffn_conv_b"], np.float32)
    shared["cb"] = f(cbv.reshape(2, NFC, 128).transpose(2, 0, 1))
    ck = np.asarray(inp["cache_k"], np.float32)[0]
    shared["pool_k"] = f(ck.transpose(0, 3, 2, 1)).reshape(NPOOL * 128, 256)
    shared["pool_v"] = f(np.asarray(inp["cache_v"], np.float32)[0].reshape(NPOOL * 128, 256))
    shared["pool_ik"] = f(np.asarray(inp["cache_idx_k"], np.float32)[0].transpose(0, 2, 1)).reshape(NPOOL * 64, 128)
    for k in ("ident", "mhg", "resetm", "validm", "cmdiag", "cms", "bones", "tab", "m96", "pidx"):
        shared[k] = cst[k]
    xs_all = np.asarray(inp["x_sample"], np.float32)
    xp = np.asarray(inp["x_prompt"], np.float32)
    memp = np.asarray(inp["mem_prompt"], np.float32)
    cmk = np.asarray(inp["cache_mem_k"], np.float32)
    cmv = np.asarray(inp["cache_mem_v"], np.float32)
    sth = np.asarray(inp["state_hgrn"], np.float32)[0]
    sfc = np.asarray(inp["state_ffn_conv"], np.float32)
    ptab = np.asarray(inp["page_table"]).astype(np.int32)
    nseq = xp.shape[0]
    maps = []
    for c in range(ncores):
        m = dict(shared)
        sq_ = c % nseq
        m["x"] = f(xp[sq_, :T])
        m["mem"] = f(memp[sq_])
        xs = np.zeros((128, D), np.float32)
        cstt = np.zeros((2, 128, NFC, 128), np.float32)
        for b in range(4):
            gb = 4 * c + b
            xs[32 * b + 2:32 * b + 6] = xs_all[gb]
            for l in range(2):
                stt_ = sfc[l, gb].reshape(2, NFC, 128)
                cstt[l, :, :, 32 * b:32 * b + 2] = stt_.transpose(2, 1, 0)
        m["xs"] = xs
        m["cst"] = cstt
        kk = cmk[:, 4 * c:4 * c + 4]
        kk = kk.reshape(2, 4, 256, 2, 2, 64)
        m["mkT_s"] = f(kk.transpose(0, 1, 4, 5, 3, 2).reshape(2, 4, 128, 2, 256))
        m["mv_s"] = f(cmv[:, 4 * c:4 * c + 4].reshape(2, 4, 256, 256))
        m["st_h"] = f(sth[4 * c:4 * c + 4])
        m["pt"] = np.ascontiguousarray(ptab[4 * c:4 * c + 4].reshape(1, -1))
        maps.append(m)
    return maps


_CACHE = {}


def run(inp, NT, NPAGE, NPOOL, NSEL, NBIS=10):
    key = (NT, NPAGE, NPOOL, NSEL, NBIS)
    if key not in _CACHE:
        _CACHE[key] = build(NT, NPAGE, NPOOL, NSEL, NBIS)
    nc = _CACHE[key]
    maps = make_in_maps(inp, NT, NPAGE, NPOOL)
    res = run_bass_kernel_spmd(nc, maps, core_ids=list(range(8)))
    return res.results


def assemble(R, NT, nseq=4):
    T = NT * 128
    y = np.stack([R[c]["y"] for c in range(nseq)])
    rows = np.array([32 * b + 2 + t for b in range(4) for t in range(4)])
    ys = np.concatenate([R[c]["ys"][rows].reshape(4, 4, D) for c in range(8)])
    hp = np.stack([R[c]["hstp"] for c in range(nseq)])[None]
    hs = np.concatenate([R[c]["hsts"] for c in range(8)])[None]
    kp = np.stack([R[c]["ko"].reshape(T, 2, 128) for c in range(nseq)])[None]
    vp = np.stack([R[c]["vo"].reshape(T, 2, 128) for c in range(nseq)])[None]
    ikp = np.stack([R[c]["iko"] for c in range(nseq)])[None]
    ks = np.concatenate([R[c]["kso"][rows].reshape(4, 4, 2, 128) for c in range(8)])[None]
    vs = np.concatenate([R[c]["vso"][rows].reshape(4, 4, 2, 128) for c in range(8)])[None]
    iks = np.concatenate([R[c]["ikso"][rows].reshape(4, 4, 64) for c in range(8)])[None]
    mk = np.stack([R[c]["mko"].reshape(2, 256, 4, 64) for c in range(nseq)], axis=1)
    mv = np.stack([R[c]["mvo"].reshape(2, 256, 4, 64) for c in range(nseq)], axis=1)
    cvp = np.stack([R[c]["cvp"][:, 126:128, :] for c in range(nseq)], axis=1)
    rows2 = np.array([32 * b + 4 + j for b in range(4) for j in range(2)])
    cvs = np.concatenate([R[c]["cvs"][:, rows2, :].reshape(2, 4, 2, DFF) for c in range(8)], axis=1)
    outs = (y, ys, hp, hs, kp, vp, ikp, ks, vs, iks, mk, mv, cvp, cvs)
    return tuple(np.ascontiguousarray(o.astype(np.float32)) for o in outs)


def kernel(**inputs):
    NT = 32
    NPAGE = 64
    NPOOL = int(np.asarray(inputs["cache_k"]).shape[1])
    R = run(inputs, NT, NPAGE, NPOOL, 256)
    return assemble(R, NT)
```

```python
import numpy as np
from contextlib import ExitStack
import concourse.bass as bass
import concourse.mybir as mybir
from concourse.bass_utils import run_bass_kernel_spmd

F32 = mybir.dt.float32
BF16 = mybir.dt.bfloat16
I32 = mybir.dt.int32
AF = mybir.ActivationFunctionType
ALU = mybir.AluOpType
AX = mybir.AxisListType

D = 1024
DFF = 2816
NFC = 22
HG_COLS = 3328
DSA_COLS = 2120
EPS = 1e-6
NEG = -1.0e30
DEBUG = False
STAGE = 99


class _Stop(Exception):
    pass


_PROG = [None]


def _stage(n):
    if STAGE <= n and not _PROG[0].stopped:
        _PROG[0].barrier()
        _PROG[0].stopped = True


class Buf:
    __slots__ = ("w", "r", "grp")

    def __init__(self):
        self.w = None
        self.r = {}
        self.grp = None


class DGroup:
    def __init__(self, sem):
        self.sem = sem
        self.cnt = 0


class Eng:
    def __init__(self, h, sem):
        self.h = h
        self.sem = sem
        self.cnt = 0
        self.known = {}


class Prog:
    def __init__(self, nc, es):
        self.nc = nc
        self.es = es
        self.nsem = 0
        self.PE = Eng(nc.tensor, self.newsem())
        self.ACT = Eng(nc.scalar, self.newsem())
        self.DVE = Eng(nc.vector, self.newsem())
        self.POOL = Eng(nc.gpsimd, self.newsem())
        self.SP = Eng(nc.sync, self.newsem())
        self.engs = [self.PE, self.ACT, self.DVE, self.POOL, self.SP]
        self.groups = []
        self.out_deps = []
        self.stopped = False
        self.free_groups = {}
        _PROG[0] = self

    def newsem(self):
        self.nsem += 1
        return self.es.enter_context(self.nc.semaphore("s%d" % self.nsem))

    def newgroup(self, kind="sp"):
        fl = self.free_groups.setdefault(kind, [])
        if fl:
            g = fl.pop()
            g.recycled = True
            return g
        g = DGroup(self.newsem())
        g.recycled = False
        g.kind = kind
        self.groups.append(g)
        return g

    def release(self, buf):
        if buf.grp is not None:
            self.free_groups[buf.grp.kind].append(buf.grp)
            buf.grp = None

    def _wait(self, eng, dep):
        sem, val = dep
        if isinstance(val, DGroup):
            val = val.cnt
        if sem is eng.sem and eng is self.PE:
            return
        k = id(sem)
        if eng.known.get(k, 0) >= val:
            return
        eng.h.wait_ge(sem, val)
        eng.known[k] = val

    def _deps(self, eng, r, w, skip_grp=None):
        if self.stopped:
            return
        for b in r:
            if b.w is not None:
                self._wait(eng, b.w)
        for b in w:
            if b.w is not None:
                if not (skip_grp is not None and b.w[0] is skip_grp.sem):
                    self._wait(eng, b.w)
            for sem, val in list(b.r.values()):
                self._wait(eng, (sem, val))

    def op(self, eng, fn, r=(), w=()):
        if self.stopped:
            return None
        self._deps(eng, r, w)
        ins = fn(eng.h)
        eng.cnt += 1
        ins.then_inc(eng.sem, 1)
        me = (eng.sem, eng.cnt)
        for b in w:
            b.w = me
            b.r = {}
        for b in r:
            if b not in w:
                b.r[id(eng.sem)] = me
        return ins

    def dma(self, q, out, in_, buf, load, grp=None, is_out=False, other=()):
        if self.stopped:
            return None
        if grp is None:
            if buf.grp is None:
                buf.grp = self.newgroup("pool" if q is self.POOL else "sp")
            grp = buf.grp
            assert (grp.kind == "pool") == (q is self.POOL), "mixed DMA queue kinds on one buffer"
        if getattr(grp, "recycled", False):
            self._wait(q, (grp.sem, grp.cnt))
            grp.recycled = False
        if load:
            self._deps(q, other, [buf], skip_grp=grp)
        else:
            self._deps(q, [buf] + list(other), [])
        ins = q.h.dma_start(out=out, in_=in_)
        grp.cnt += 16
        ins.then_inc(grp.sem, 16)
        if load:
            buf.w = (grp.sem, grp)
            buf.r = {}
        else:
            buf.r[id(grp.sem)] = (grp.sem, grp.cnt)
            if is_out:
                self.out_deps.append(grp)
        return ins

    def idma(self, out, in_, idx_ap, buf, other=()):
        q = self.POOL
        if self.stopped:
            return None
        if buf.grp is None:
            buf.grp = self.newgroup("pool")
        grp = buf.grp
        assert grp.kind == "pool"
        if getattr(grp, "recycled", False):
            self._wait(q, (grp.sem, grp.cnt))
            grp.recycled = False
        self._deps(q, other, [buf], skip_grp=grp)
        ins = q.h.indirect_dma_start(out=out, out_offset=None, in_=in_,
                                     in_offset=bass.IndirectOffsetOnAxis(ap=idx_ap, axis=0))
        grp.cnt += 16
        ins.then_inc(grp.sem, 16)
        buf.w = (grp.sem, grp)
        buf.r = {}
        return ins

    def barrier(self, force=False):
        if self.stopped and not force:
            return
        for e in self.engs:
            if e is self.SP and not force:
                continue
            for o in self.engs:
                if o is not e and o.cnt > 0:
                    self._wait(e, (o.sem, o.cnt))
            for g in self.groups:
                if g.cnt > 0:
                    self._wait(e, (g.sem, g.cnt))

    def finish(self):
        self.barrier(force=True)


def build(NT, NPAGE, NPOOL, NSEL, NBIS, G=4, NSEL_S=256):
    T = NT * 128
    NG = NT // G
    PAST = NPAGE * 128
    nc = bass.Bass("TRN2", target_bir_lowering=False)

    def din(name, shape, dt=F32):
        return nc.dram_tensor(name, list(shape), dt, kind="ExternalInput").ap()

    def dout(name, shape, dt=F32):
        return nc.dram_tensor(name, list(shape), dt, kind="ExternalOutput").ap()

    x_d = din("x", [T, D])
    xs_d = din("xs", [128, D])
    w_in0 = din("w_in0", [D, HG_COLS])
    w_in1 = din("w_in1", [D, DSA_COLS])
    w_kv = din("w_kv", [2, D, 512])
    w_o = din("w_o", [2, D, D])
    w_g = din("w_g", [2, D, DFF])
    w_u = din("w_u", [2, D, DFF])
    w_d = din("w_d", [2, DFF, D])
    bc_nmix = din("bc_nmix", [2, 128, D])
    bc_nffn = din("bc_nffn", [2, 128, D])
    lbl_d = din("lbl", [128, 2, 6])
    onorm_d = din("onorm", [128, 1])
    bc_qn = din("bc_qn", [128, 128])
    bc_kn = din("bc_kn", [128, 128])
    bc_ikn = din("bc_ikn", [128, 64])
    mqn_d = din("mqn", [128, 2])
    bc_mkn = din("bc_mkn", [128, 2, 256])
    cw_d = din("cw", [128, 2, NFC, 3])
    cb_d = din("cb", [128, 2, NFC])
    mem_d = din("mem", [256, D])
    mkT_s = din("mkT_s", [2, 4, 128, 2, 256])
    mv_s = din("mv_s", [2, 4, 256, 256])
    st_h = din("st_h", [4, 6, 128, 128])
    cst_d = din("cst", [2, 128, NFC, 128])
    pt_d = din("pt", [1, 4 * NPAGE], I32)
    pool_ik = din("pool_ik", [NPOOL * 64, 128])
    pool_k = din("pool_k", [NPOOL * 128, 256])
    pool_v = din("pool_v", [NPOOL * 128, 256])
    pidx_d = din("pidx", [128, 1])
    tab_d = din("tab", [(NT + 1) * 128, 1280])
    ident_d = din("ident", [128, 128])
    mhg_d = din("mhg", [128, 384])
    resetm_d = din("resetm", [128, 512])
    validm_d = din("validm", [128, 128])
    cmdiag_d = din("cmdiag", [128, 128])
    cms_d = din("cms", [128, 128])
    bones_d = din("bones", [128, 128])
    m96_d = din("m96", [128, 128])

    y_d = dout("y", [T, D])
    ys_d = dout("ys", [128, D])
    hstp_d = dout("hstp", [6, 128, 128])
    hsts_d = dout("hsts", [4, 6, 128, 128])
    ko_d = dout("ko", [T, 256])
    vo_d = dout("vo", [T, 256])
    iko_d = dout("iko", [T, 64])
    kso_d = dout("kso", [128, 256])
    vso_d = dout("vso", [128, 256])
    ikso_d = dout("ikso", [128, 64])
    mko_d = dout("mko", [2, 256, 256])
    mvo_d = dout("mvo", [2, 256, 256])
    cvp_d = dout("cvp", [2, 128, DFF])
    cvs_d = dout("cvs", [2, 128, DFF])

    if DEBUG:
        dbg_mix = dout("dbg_mix", [128, 8, G * 128])
        dbg_x1 = dout("dbg_x1", [128, G, D])
        dbg_x2 = dout("dbg_x2", [128, G, D])
    es = ExitStack()
    with es:
        P = Prog(nc, es)
        PE, ACT, DVE, POOL, SP = P.PE, P.ACT, P.DVE, P.POOL, P.SP
        cgrp = P.newgroup("pool")

        uid = [0]

        def sb(stack, name, shape, dt=F32):
            uid[0] += 1
            t = stack.enter_context(nc.sbuf_tensor("%s_%d" % (name, uid[0]), list(shape), dt))
            b = Buf()
            if stack is not es:
                stack.callback(P.release, b)
            return t, b

        PS = []
        for i in range(8):
            t = es.enter_context(nc.psum_tensor("ps%d" % i, [128, 512], F32))
            PS.append((t, Buf()))
        psrr = [0]

        def pb():
            i = psrr[0]
            psrr[0] = (i + 1) % 8
            return PS[i]

        def act(out, in_, func, r, w, scale=1.0, bias=0.0, accum=None):
            kw = {}
            if accum is not None:
                kw["accum_out"] = accum
            return P.op(ACT, lambda h: h.activation(out=out, in_=in_, func=func, bias=bias, scale=scale, **kw), r, w)

        def tt(out, in0, in1, op, r, w, eng=None):
            return P.op(eng or DVE, lambda h: h.tensor_tensor(out=out, in0=in0, in1=in1, op=op), r, w)

        def tsc(out, in0, s1, s2, op0, op1, r, w, accum=None, eng=None):
            kw = {}
            if accum is not None:
                kw["accum_out"] = accum
            return P.op(eng or DVE, lambda h: h.tensor_scalar(out=out, in0=in0, scalar1=s1, scalar2=s2, op0=op0, op1=op1, **kw), r, w)

        def stt(out, in0, scalar, in1, op0, op1, r, w):
            return P.op(DVE, lambda h: h.scalar_tensor_tensor(out=out, in0=in0, scalar=scalar, in1=in1, op0=op0, op1=op1), r, w)

        def mm(out, lhsT, rhs, start, stop, r, w):
            return P.op(PE, lambda h: h.matmul(out, lhsT=lhsT, rhs=rhs, start=start, stop=stop), r, w)

        def cp(out, in_, r, w, eng=None):
            e = eng or DVE
            if e is ACT:
                return P.op(ACT, lambda h: h.activation(out=out, in_=in_, func=AF.Copy), r, w)
            return P.op(e, lambda h: h.tensor_copy(out=out, in_=in_), r, w)

        def const(name, src, shape, dt=F32, q=None):
            t, b = sb(es, name, shape, F32)
            P.dma(q or POOL, t[:], src, b, True, grp=cgrp)
            if dt is F32:
                return t, b
            t2, b2 = sb(es, name + "_bf", shape, dt)
            deferred.append(lambda: cp(t2[:], t[:], [b], [b2]))
            return t2, b2
        deferred = []

        identF, identF_b = const("identF", ident_d[:, :], [128, 128])
        mhg, mhg_b = const("mhg", mhg_d[:, 0:128], [128, 128])
        resetm, resetm_b = const("resetm", resetm_d[:, :], [128, 512])
        validm, validm_b = const("validm", validm_d[:, :], [128, 128])
        cmdiag, cmdiag_b = const("cmdiag", cmdiag_d[:, :], [128, 128])
        cms, cms_b = const("cms", cms_d[:, :], [128, 128])
        bones, bones_b = const("bones", bones_d[:, :], [128, 128], BF16)
        m96, m96_b = const("m96", m96_d[:, :], [128, 128])
        pidx, pidx_b = const("pidx", pidx_d[:, :], [128, 1])
        lbl, lbl_b = const("lblc", lbl_d[:, :, :], [128, 2, 6])
        onorm, onorm_b = const("onormc", onorm_d[:, :], [128, 1])
        gq, gq_b = const("gq", bc_qn[:, :], [128, 128])
        gk, gk_b = const("gk", bc_kn[:, :], [128, 128])
        gik, gik_b = const("gik", bc_ikn[:, :], [128, 64])
        mqn, mqn_b = const("mqnc", mqn_d[:, :], [128, 2])
        gmk, gmk_b = const("gmk", bc_mkn[:, :, :], [128, 2, 256])
        cw, cw_b = const("cwc", cw_d[:, :, :, :], [128, 2, NFC, 3])
        cbv, cbv_b = const("cbc", cb_d[:, :, :], [128, 2, NFC])
        for fn_ in deferred:
            fn_()
        identB, identB_b = sb(es, "identB", [128, 128], BF16)
        cp(identB[:], identF[:], [identF_b], [identB_b])
        onesB, onesB_b = sb(es, "onesB", [128, 128], BF16)
        P.op(DVE, lambda h: h.memset(onesB[:], 1.0), [], [onesB_b])
        GN, GN_b = sb(es, "GN", [128, D])
        gmix, gmix_b = bc_nmix, None
        gffn, gffn_b = bc_nffn, None
        ident3, ident3_b = sb(es, "ident3", [128, 3, 128], BF16)
        for j in range(3):
            cp(ident3[:, j, :], identF[:], [identF_b], [ident3_b])
        oml, oml_b = sb(es, "oml", [128, 6])
        tt(oml[:], lbl[:, 1, :], lbl[:, 0, :], ALU.subtract, [lbl_b], [oml_b])
        act(oml[:], oml[:], AF.Sigmoid, [oml_b], [oml_b])
        epsc, epsc_b = sb(es, "epsc", [128, 1])
        P.op(DVE, lambda h: h.memset(epsc[:], EPS), [], [epsc_b])

        NSLOT = 3
        ring = [sb(es, "wr%d" % i, [128, 8, 512], BF16) for i in range(NSLOT)]
        rr = [0]

        wconv = {}

        def convert(name, wap, K_, N_):
            wb_ = nc.dram_tensor(name + "_bf", [K_, N_], BF16, kind="Internal").ap()
            cb_ = Buf()
            cb_.grp = P.newgroup("pool")
            for r0 in range(0, K_, 128):
                if P.stopped:
                    break
                ins = POOL.h.dma_start(out=wb_[r0:r0 + 128, :], in_=wap[r0:r0 + 128, :], max_dma_last_dim=4096)
                cb_.grp.cnt += 16
                ins.then_inc(cb_.grp.sem, 16)
            cb_.w = (cb_.grp.sem, cb_.grp)
            wconv[name] = (wb_, cb_)

        def wblock(wname, r0, kc, c0, ncols):
            wap, cb_ = wconv[wname]
            t, b = ring[rr[0]]
            rr[0] = (rr[0] + 1) % NSLOT
            src = wap[r0:r0 + kc * 128, c0:c0 + ncols].rearrange("(k p) n -> p k n", p=128)
            P.dma(SP, t[:, 0:kc, 0:ncols], src, b, True, other=[cb_])
            return t, b

        convert("w_in0", w_in0, D, HG_COLS)
        for l in range(2):
            convert("w_kv%d" % l, w_kv[l], D, 512)
        convert("w_o0", w_o[0], D, D)
        convert("w_g0", w_g[0], D, DFF)
        convert("w_u0", w_u[0], D, DFF)
        convert("w_d0", w_d[0], DFF, D)
        convert("w_in1", w_in1, D, DSA_COLS)
        convert("w_o1", w_o[1], D, D)
        convert("w_g1", w_g[1], D, DFF)
        convert("w_u1", w_u[1], D, DFF)
        convert("w_d1", w_d[1], DFF, D)

        X, X_b = sb(es, "X", [128, G, D])
        HT, HT_b = sb(es, "HT", [128, 8, G * 128], BF16)
        MIX, MIX_b = sb(es, "MIX", [128, 8, G * 128], BF16)
        HN = [sb(es, "HN%d" % i, [128, D]) for i in range(1)]
        SS, SS_b = sb(es, "SS", [128, 16])
        Sf, Sf_b = [], []
        for h in range(6):
            t, b = sb(es, "Sf%d" % h, [128, 128])
            Sf.append(t); Sf_b.append(b)
        Sb = [[sb(es, "Sb%d_%d" % (h, i), [128, 128], BF16) for i in range(2)] for h in range(6)]
        UH, UH_b = sb(es, "UH", [128, NFC, 2])
        MEMT, MEMT_b = sb(es, "MEMT", [128, 8, 256], BF16)
        hn_i = [0]

        def rmsnorm_T(gt, ntile, gain, gain_b, lyr):
            P.dma(POOL, GN[:], gain[lyr], GN_b, True)
            gain_b = GN_b
            for t in range(ntile):
                hn, hn_b = HN[0]
                act(hn[:], X[:, t, :], AF.Square, [X_b], [hn_b, SS_b], accum=SS[:, 0:1])
                act(SS[:, 1:2], SS[:, 0:1], AF.Ln, [SS_b], [SS_b], scale=1.0 / D, bias=epsc[:, 0:1])
                act(SS[:, 2:3], SS[:, 1:2], AF.Exp, [SS_b], [SS_b], scale=-0.5)
                stt(hn[:], X[:, t, :], SS[:, 2:3], GN[:], ALU.mult, ALU.mult, [X_b, SS_b, gain_b], [hn_b])
                for half in range(2):
                    pt_, pb_ = pb()
                    for j in range(4):
                        kc = half * 4 + j
                        P.op(PE, lambda h: h.transpose(pt_[:, j * 128:(j + 1) * 128], hn[:, kc * 128:(kc + 1) * 128], identF[:]),
                             [hn_b, identF_b], [pb_])
                    act(HT[:, half * 4:half * 4 + 4, t * 128:(t + 1) * 128],
                        pt_[:].rearrange("p (k n) -> p k n", k=4), AF.Copy, [pb_], [HT_b])

        def proj_fm(wap, c0, nch, ntok, consume):
            ci = 0
            while ci < nch:
                nb = min(4, nch - ci)
                wt, wb = wblock(wap, 0, 8, c0 + ci * 128, nb * 128)
                for j in range(nb):
                    pt_, pb_ = pb()
                    for kc in range(8):
                        mm(pt_[:, 0:ntok], wt[:, kc, j * 128:(j + 1) * 128], HT[:, kc, 0:ntok], kc == 0, kc == 7, [wb, HT_b], [pb_])
                    consume(ci + j, pt_, pb_)
                ci += nb

        def proj_tm(wap, c0, ncols, ntile, consume, src=None, src_b=None, kchunks=8, r0=0):
            src = HT if src is None else src
            src_b = HT_b if src_b is None else src_b
            cb_ = 0
            c = 0
            while c < ncols:
                n = min(512, ncols - c)
                wt, wb = wblock(wap, r0, kchunks, c0 + c, n)
                for t in range(ntile):
                    pt_, pb_ = pb()
                    for kc in range(kchunks):
                        mm(pt_[:, 0:n], src[:, kc, t * 128:(t + 1) * 128], wt[:, kc, 0:n], kc == 0, kc == kchunks - 1, [wb, src_b], [pb_])
                    consume(cb_, c, n, t, pt_, pb_)
                c += n
                cb_ += 1

        def mem_attention(st, ntok, slots, l):
            MQ, MQ_b = st["MQ"]
            sq, sq_b = st["mq_sq"]
            rs, rs_b = st["mq_rs"]
            mqb, mqb_b = st["mq_bf"]
            for c in range(2):
                act(sq[:, 0:ntok], MQ[:, c, 0:ntok], AF.Square, [MQ_b], [sq_b])
                pt_, pb_ = pb()
                mm(pt_[:, 0:ntok], bones[:], sq[:, 0:ntok], True, True, [bones_b, sq_b], [pb_])
                act(rs[:, 0:ntok], pt_[:, 0:ntok], AF.Ln, [pb_], [rs_b], scale=1.0 / 64, bias=epsc[:, 0:1])
                act(rs[:, 0:ntok], rs[:, 0:ntok], AF.Exp, [rs_b], [rs_b], scale=-0.5)
                stt(mqb[:, c, 0:ntok], MQ[:, c, 0:ntok], mqn[:, l:l + 1], rs[:, 0:ntok], ALU.mult, ALU.mult, [MQ_b, mqn_b, rs_b], [mqb_b])
            mexps = st["mexp"]
            mei = [0]
            rd, rd_b = st["mrd"]
            for (c0, ncol, MKT, MKT_b, MV, MV_b) in slots:
                for c in range(2):
                    pn, pn_b = pb()
                    pd, pd_b = pb()
                    for j in range(2):
                        hh = 2 * c + j
                        lo, hi = 64 * j, 64 * j + 64
                        for mt in range(2):
                            ps_, ps_b = pb()
                            pe_t, pe_b = mexps[mei[0] % len(mexps)]
                            mei[0] += 1
                            mm(ps_[:, 0:ncol], MKT[lo:hi, c, mt * 128:(mt + 1) * 128], mqb[lo:hi, c, c0:c0 + ncol], True, True,
                               [MKT_b, mqb_b], [ps_b])
                            act(pe_t[:, 0:ncol], ps_[:, 0:ncol], AF.Exp, [ps_b], [pe_b], scale=0.125)
                            mm(pn[lo:hi, 0:ncol], MV[:, mt, hh * 64:(hh + 1) * 64], pe_t[:, 0:ncol], mt == 0, mt == 1, [MV_b, pe_b], [pn_b])
                            mm(pd[lo:hi, 0:ncol], onesB[:, 0:64], pe_t[:, 0:ncol], mt == 0, mt == 1, [onesB_b, pe_b], [pd_b])
                    act(rd[:, 0:ncol], pd[:, 0:ncol], AF.Ln, [pd_b], [rd_b])
                    act(rd[:, 0:ncol], rd[:, 0:ncol], AF.Exp, [rd_b], [rd_b], scale=-1.0)
                    tt(MIX[:, 6 + c, c0:c0 + ncol], pn[:, 0:ncol], rd[:, 0:ncol], ALU.mult, [pn_b, rd_b], [MIX_b])

        def out_proj(l, ntile):
            def consume(cb_, c, n, t, pt_, pb_):
                tt(X[:, t, c:c + n], X[:, t, c:c + n], pt_[:, 0:n], ALU.add, [X_b, pb_], [X_b])
            proj_tm("w_o%d" % l, 0, D, ntile, consume, src=MIX, src_b=MIX_b)

        def ffn(l, ntile, sample, cv_out):
            ntok = ntile * 128
            with ExitStack() as fs:
                AT, AT_b = sb(fs, "AT", [128, NFC, ntok], BF16)
                Uc = [sb(fs, "Uc%d" % i, [128, 2 + ntok]) for i in range(2)]
                Cc = [sb(fs, "Cc%d" % i, [128, ntok]) for i in range(2)]
                Sg = [sb(fs, "Sg%d" % i, [128, ntok]) for i in range(4)]
                UTOK, UTOK_b = sb(fs, "UTOK", [128, DFF])
                if sample:
                    CST, CST_b = sb(fs, "CST", [128, NFC, 128])
                    P.dma(POOL, CST[:], cst_d[l], CST_b, True)
                rmsnorm_T(None, ntile, gffn, gffn_b, l)
                gps = {}

                def cons_g(ci, pt_, pb_):
                    u, u_b = Uc[ci % 2]
                    cp(u[:, 0:2], UH[:, ci, :], [UH_b], [u_b])
                    if sample:
                        tt(u[:, 2:2 + ntok], pt_[:, 0:ntok], validm[:, 0:ntok], ALU.mult, [pb_, validm_b], [u_b])
                        tt(u[:, 2:2 + ntok], u[:, 2:2 + ntok], CST[:, ci, :], ALU.add, [u_b, CST_b], [u_b])
                    else:
                        act(u[:, 2:2 + ntok], pt_[:, 0:ntok], AF.Copy, [pb_], [u_b])
                    cp(UH[:, ci, :], u[:, ntok:ntok + 2], [u_b], [UH_b])
                    c_, c_b = Cc[ci % 2]
                    if sample:
                        tsc(c_[:], u[:, 2:2 + ntok], cw[:, l, ci, 2:3], cbv[:, l, ci:ci + 1], ALU.mult, ALU.add, [u_b, cw_b, cbv_b], [c_b])
                    else:
                        act(c_[:], pt_[:, 0:ntok], AF.Identity, [pb_, cw_b, cbv_b], [c_b], scale=cw[:, l, ci, 2:3], bias=cbv[:, l, ci:ci + 1])
                    stt(c_[:], u[:, 1:1 + ntok], cw[:, l, ci, 1:2], c_[:], ALU.mult, ALU.add, [u_b, cw_b], [c_b])
                    stt(c_[:], u[:, 0:ntok], cw[:, l, ci, 0:1], c_[:], ALU.mult, ALU.add, [u_b, cw_b], [c_b])
                    s_, s_b = Sg[ci % 4]
                    act(s_[:], c_[:], AF.Silu, [c_b], [s_b])
                    if cv_out is not None:
                        pq, pq_b = pb()
                        P.op(PE, lambda h: h.transpose(pq[:, 0:128], u[:, 2 + ntok - 128:2 + ntok], identF[:]), [u_b, identF_b], [pq_b])
                        act(UTOK[:, ci * 128:(ci + 1) * 128], pq[:, 0:128], AF.Copy, [pq_b], [UTOK_b])
                    gps[ci] = (s_, s_b)

                ci = 0
                while ci < NFC:
                    nb = min(4, NFC - ci)
                    wt, wb = wblock("w_g%d" % l, 0, 8, ci * 128, nb * 128)
                    for j in range(nb):
                        pt_, pb_ = pb()
                        for kc in range(8):
                            mm(pt_[:, 0:ntok], wt[:, kc, j * 128:(j + 1) * 128], HT[:, kc, 0:ntok], kc == 0, kc == 7, [wb, HT_b], [pb_])
                        cons_g(ci + j, pt_, pb_)
                    wt, wb = wblock("w_u%d" % l, 0, 8, ci * 128, nb * 128)
                    for j in range(nb):
                        pt_, pb_ = pb()
                        for kc in range(8):
                            mm(pt_[:, 0:ntok], wt[:, kc, j * 128:(j + 1) * 128], HT[:, kc, 0:ntok], kc == 0, kc == 7, [wb, HT_b], [pb_])
                        s_, s_b = gps[ci + j]
                        tt(AT[:, ci + j, :], s_[:], pt_[:, 0:ntok], ALU.mult, [s_b, pb_], [AT_b])
                    ci += nb
                if cv_out is not None:
                    P.dma(POOL, cv_out, UTOK[:], UTOK_b, False, is_out=True)
                for cbk in range(2):
                    accs = [pb() for _ in range(ntile)]
                    for kh, (k0, kn) in enumerate(((0, 8), (8, 8), (16, 6))):
                        wt, wb = wblock("w_d%d" % l, k0 * 128, kn, cbk * 512, 512)
                        for t in range(ntile):
                            pt_, pb_ = accs[t]
                            for k in range(kn):
                                fc = k0 + k
                                mm(pt_[:, :], AT[:, fc, t * 128:(t + 1) * 128], wt[:, k, :], fc == 0, fc == NFC - 1, [wb, AT_b], [pb_])
                    for t in range(ntile):
                        pt_, pb_ = accs[t]
                        tt(X[:, t, cbk * 512:(cbk + 1) * 512], X[:, t, cbk * 512:(cbk + 1) * 512], pt_[:, :], ALU.add, [X_b, pb_], [X_b])
            P.barrier()

        def layer0(ntile, sample, mslots, first, last):
            ntok = ntile * 128
            nch = ntok // 32
            with ExitStack() as ls:
                st = {}
                st["MQ"] = sb(ls, "MQ", [128, 2, ntok])
                st["mq_sq"] = sb(ls, "mq_sq", [128, ntok], BF16)
                st["mq_rs"] = sb(ls, "mq_rs", [128, ntok])
                st["mq_bf"] = sb(ls, "mq_bf", [128, 2, ntok], BF16)
                st["mexp"] = [sb(ls, "mexp%d" % i, [128, ntok], BF16) for i in range(2)]
                st["mrd"] = sb(ls, "mrd", [128, ntok])
                QS, QS_b = sb(ls, "QS", [128, 6, ntok], BF16)
                GS, GS_b = sb(ls, "GS", [128, 6, ntok], BF16)
                VBh, VBh_b = sb(ls, "VBh", [128, ntile, 768], BF16)
                QD, QD_b = sb(ls, "QD", [128, 6, ntok], BF16)
                KD, KD_b = sb(ls, "KD", [128, 6, ntok], BF16)
                EBL, EBL_b = sb(ls, "EBL", [128, 6, nch])
                tmp = [sb(ls, "l0t%d" % i, [128, ntok]) for i in range(4)]
                K2, K2_b = sb(ls, "K2", [128, ntok])
                K2m, K2m_b = sb(ls, "K2m", [128, 128])
                K2T, K2T_b = sb(ls, "K2T", [128, ntile, 768], BF16)
                K2T3, K2T3_b = sb(ls, "K2T3", [128, ntile, 768], BF16)
                SMs = [sb(ls, "SM%d" % i, [128, 128], BF16) for i in range(3)]
                Ohs = [sb(ls, "Oh%d" % i, [128, ntok]) for i in range(3)]
                Osq, Osq_b = sb(ls, "Osq", [128, ntok], BF16)
                Ors, Ors_b = sb(ls, "Ors", [128, ntok])
                MQ, MQ_b = st["MQ"]

                rmsnorm_T(None, ntile, gmix, gmix_b, 0)
                _stage(1.20)

                def cons_q(ci, pt_, pb_):
                    act(QS[:, ci, :], pt_[:, 0:ntok], AF.Silu, [pb_], [QS_b])
                proj_fm("w_in0", 0, 6, ntok, cons_q)
                _stage(1.21)

                def cons_f(h, pt_, pb_):
                    a, a_b = tmp[0]
                    kk, kk_b = tmp[1]
                    lf, lf_b = tmp[2]
                    bb, bb_b = tmp[0]
                    eb, eb_b = tmp[3]
                    en, en_b = tmp[2]
                    act(a[:], pt_[:, 0:ntok], AF.Sigmoid, [pb_], [a_b], scale=-1.0)
                    tsc(kk[:], a[:], oml[:, h:h + 1], None, ALU.mult, ALU.bypass, [a_b, oml_b], [kk_b])
                    _stage(1.211)
                    act(lf[:], kk[:], AF.Ln, [kk_b], [lf_b], scale=-1.0, bias=1.0)
                    if sample:
                        tt(lf[:], lf[:], validm[:, 0:ntok], ALU.mult, [lf_b, validm_b], [lf_b])
                    _stage(1.212)
                    P.op(DVE, lambda hd: hd.tensor_tensor_scan(out=bb[:], data0=resetm[:, 0:ntok], data1=lf[:], initial=0.0,
                                                               op0=ALU.mult, op1=ALU.add), [resetm_b, lf_b], [bb_b])
                    _stage(1.213)
                    act(eb[:], bb[:], AF.Exp, [bb_b], [eb_b])
                    act(en[:], bb[:], AF.Exp, [bb_b], [en_b], scale=-1.0)
                    tt(QD[:, h, :], QS[:, h, :], eb[:], ALU.mult, [QS_b, eb_b], [QD_b])
                    tt(KD[:, h, :], kk[:], en[:], ALU.mult, [kk_b, en_b], [KD_b])
                    _stage(1.214)
                    cp(EBL[:, h, :], eb[:].rearrange("p (c j) -> p c j", j=32)[:, :, 31], [eb_b], [EBL_b])
                    _stage(1.215)
                    tt(K2[:].rearrange("p (c j) -> p c j", j=32), KD[:, h, :].rearrange("p (c j) -> p c j", j=32),
                       EBL[:, h, :].unsqueeze(2).to_broadcast([128, nch, 32]), ALU.mult, [KD_b, EBL_b], [K2_b])
                    _stage(1.216)
                    for t in range(ntile):
                        pq, pq_b = pb()
                        P.op(PE, lambda hd: hd.transpose(pq[:, 0:128], K2[:, t * 128:(t + 1) * 128], identF[:]), [K2_b, identF_b], [pq_b])
                        act(K2T[:, t, h * 128:(h + 1) * 128], pq[:, 0:128], AF.Copy, [pq_b], [K2T_b])
                        _stage(1.217)
                        tt(K2m[:], K2[:, t * 128:(t + 1) * 128], m96[:], ALU.mult, [K2_b, m96_b], [K2m_b])
                        pq2, pq2_b = pb()
                        P.op(PE, lambda hd: hd.transpose(pq2[:, 0:128], K2m[:], identF[:]), [K2m_b, identF_b], [pq2_b])
                        act(K2T3[:, t, h * 128:(h + 1) * 128], pq2[:, 0:128], AF.Copy, [pq2_b], [K2T3_b])
                    _stage(1.2171 + 0.0001 * h)
                proj_fm("w_in0", 768, 6, ntok, cons_f)
                _stage(1.22)

                def cons_g(ci, pt_, pb_):
                    act(GS[:, ci, :], pt_[:, 0:ntok], AF.Silu, [pb_], [GS_b])
                proj_fm("w_in0", 2304, 6, ntok, cons_g)

                def cons_m(ci, pt_, pb_):
                    act(MQ[:, ci, :], pt_[:, 0:ntok], AF.Copy, [pb_], [MQ_b])
                proj_fm("w_in0", 3072, 2, ntok, cons_m)

                def cons_i(cb_, c, n, t, pt_, pb_):
                    act(VBh[:, t, c:c + n], pt_[:, 0:n], AF.Copy, [pb_], [VBh_b])
                proj_tm("w_in0", 1536, 768, ntile, cons_i)
                _stage(1.23)

                for hh in range(2):
                    heads = [3 * hh + i for i in range(3)]
                    par = {h: 0 for h in heads}
                    if first and not sample:
                        for h in heads:
                            P.op(DVE, lambda hd: hd.memset(Sf[h][:], 0.0), [], [Sf_b[h]])
                            P.op(DVE, lambda hd: hd.memset(Sb[h][0][0][:], 0.0), [], [Sb[h][0][1]])
                    k_ps = 0
                    k_pu = 0
                    for t in range(ntile):
                        cols = slice(t * 128, (t + 1) * 128)
                        pos = {}
                        for i, h in enumerate(heads):
                            ps_, ps_b = PS[3 + (k_ps % 2)]
                            k_ps += 1
                            sm, sm_b = SMs[i]
                            mm(ps_[:, 0:128], KD[:, h, cols], QD[:, h, cols], True, True, [KD_b, QD_b], [ps_b])
                            tt(sm[:], ps_[:, 0:128], mhg[:], ALU.mult, [ps_b, mhg_b], [sm_b])
                            po, po_b = PS[i]
                            pos[h] = (po, po_b)
                            mm(po[:, 0:128], VBh[:, t, h * 128:(h + 1) * 128], sm[:], True, False, [VBh_b, sm_b], [po_b])
                        for c in range(4):
                            ccols = slice(t * 128 + c * 32, t * 128 + c * 32 + 32)
                            for i, h in enumerate(heads):
                                po, po_b = pos[h]
                                if sample:
                                    P.dma(POOL, Sf[h][:], st_h[c, h], Sf_b[h], True)
                                    par[h] = 0
                                    cp(Sb[h][0][0][:], Sf[h][:], [Sf_b[h]], [Sb[h][0][1]], eng=ACT)
                                sbt, sbb = Sb[h][par[h]]
                                mm(po[:, c * 32:(c + 1) * 32], sbt[:], QD[:, h, ccols], False, c == 3, [sbb, QD_b], [po_b])
                                pu, pu_b = PS[5 + (k_pu % 3)]
                                k_pu += 1
                                if c < 3:
                                    mm(pu[:, 0:128], K2T[c * 32:(c + 1) * 32, t, h * 128:(h + 1) * 128],
                                       VBh[c * 32:(c + 1) * 32, t, h * 128:(h + 1) * 128], True, True, [K2T_b, VBh_b], [pu_b])
                                else:
                                    mm(pu[:, 0:128], K2T3[64:128, t, h * 128:(h + 1) * 128],
                                       VBh[64:128, t, h * 128:(h + 1) * 128], True, True, [K2T3_b, VBh_b], [pu_b])
                                stt(Sf[h][:], Sf[h][:], EBL[:, h, t * 4 + c:t * 4 + c + 1], pu[:, 0:128], ALU.mult, ALU.add,
                                    [Sf_b[h], EBL_b, pu_b], [Sf_b[h]])
                                par[h] ^= 1
                                cp(Sb[h][par[h]][0][:], Sf[h][:], [Sf_b[h]], [Sb[h][par[h]][1]], eng=ACT)
                                if sample:
                                    P.dma(POOL, hsts_d[c, h], Sf[h][:], Sf_b[h], False, is_out=True)
                        for i, h in enumerate(heads):
                            po, po_b = pos[h]
                            act(Ohs[i][0][:, cols], po[:, 0:128], AF.Copy, [po_b], [Ohs[i][1]])
                    for i, h in enumerate(heads):
                        Oh, Oh_b = Ohs[i]
                        if par[h] == 1:
                            cp(Sb[h][0][0][:], Sf[h][:], [Sf_b[h]], [Sb[h][0][1]], eng=ACT)
                        if last and not sample:
                            P.dma(POOL, hstp_d[h], Sf[h][:], Sf_b[h], False, is_out=True)
                        act(Osq[:], Oh[:], AF.Square, [Oh_b], [Osq_b])
                        pn_, pn_b = pb()
                        mm(pn_[:, 0:ntok], onesB[:], Osq[:], True, True, [onesB_b, Osq_b], [pn_b])
                        act(Ors[:], pn_[:, 0:ntok], AF.Ln, [pn_b], [Ors_b], scale=1.0 / 128, bias=epsc[:, 0:1])
                        act(Ors[:], Ors[:], AF.Exp, [Ors_b], [Ors_b], scale=-0.5)
                        tt(Oh[:], Oh[:], Ors[:], ALU.mult, [Oh_b, Ors_b], [Oh_b])
                        stt(MIX[:, h, 0:ntok], Oh[:], onorm[:, 0:1], GS[:, h, :], ALU.mult, ALU.mult, [Oh_b, onorm_b, GS_b], [MIX_b])

                _stage(1.24)
                mem_attention(st, ntok, mslots, 0)
            P.barrier()

        def memkv_prompt(l, MKT, MKT_b, MV, MV_b):
            with ExitStack() as ms:
                KVt, KVt_b = sb(ms, "KVt", [128, 512])
                KN, KN_b = sb(ms, "KN", [128, 256])
                sq, sq_b = sb(ms, "kvsq", [128, 64])
                ss4, ss4_b = sb(ms, "ss4", [128, 8])

                def consume(cb_, c, n, t, pt_, pb_):
                    act(KVt[:], pt_[:, :], AF.Copy, [pb_], [KVt_b])
                    for hh in range(4):
                        act(sq[:], KVt[:, hh * 64:(hh + 1) * 64], AF.Square, [KVt_b], [sq_b, ss4_b], accum=ss4[:, hh:hh + 1])
                    act(ss4[:, 4:8], ss4[:, 0:4], AF.Ln, [ss4_b], [ss4_b], scale=1.0 / 64, bias=epsc[:, 0:1])
                    act(ss4[:, 4:8], ss4[:, 4:8], AF.Exp, [ss4_b], [ss4_b], scale=-0.5)
                    for hh in range(4):
                        stt(KN[:, hh * 64:(hh + 1) * 64], KVt[:, hh * 64:(hh + 1) * 64], ss4[:, 4 + hh:5 + hh], gmk[:, l, hh * 64:(hh + 1) * 64],
                            ALU.mult, ALU.mult, [KVt_b, ss4_b, gmk_b], [KN_b])
                    P.dma(POOL, mko_d[l, t * 128:(t + 1) * 128, :], KN[:], KN_b, False, is_out=True)
                    P.dma(POOL, mvo_d[l, t * 128:(t + 1) * 128, :], KVt[:, 256:512], KVt_b, False, is_out=True)
                    cp(MV[:, t, :], KVt[:, 256:512], [KVt_b], [MV_b])
                    for c2 in range(2):
                        pq, pq_b = pb()
                        P.op(PE, lambda h: h.transpose(pq[:, 0:128], KN[:, c2 * 128:(c2 + 1) * 128], identF[:]), [KN_b, identF_b], [pq_b])
                        act(MKT[:, c2, t * 128:(t + 1) * 128], pq[:, 0:128], AF.Copy, [pq_b], [MKT_b])
                proj_tm("w_kv%d" % l, 0, 512, 2, consume, src=MEMT, src_b=MEMT_b)
            P.barrier()

        def dsa_inproj(ntile, tile0, st, sample, KT, KT_b, VB, VB_b, IKT, IKT_b):
            ntok = ntile * 128
            Z, Z_b = st["Z"]
            QN, QN_b = st["QN"]
            QR, QR_b = st["QR"]
            TB, TB_b = st["TB"]
            T1, T1_b = st["T1"]
            T2, T2_b = st["T2"]
            QT, QT_b = st["QT"]
            IQT, IQT_b = st["IQT"]
            IW, IW_b = st["IW"]
            MQ, MQ_b = st["MQ"]
            ss, ss_b = st["ss16"]
            sq, sq_b = st["sqj"]
            rmsnorm_T(None, ntile, gmix, gmix_b, 1)

            def cons_m(ci, pt_, pb_):
                act(MQ[:, ci, 0:ntok], pt_[:, 0:ntok], AF.Copy, [pb_], [MQ_b])
            proj_fm("w_in1", 1864, 2, ntok, cons_m)

            blocks = [(0, 512), (512, 512), (1024, 512), (1536, 328)]
            wts = []
            Zs = st["Zs"]

            def consume(cb_, c, n, t, pt_, pb_):
                z, z_b = Zs[t]
                act(z[:, c:c + n], pt_[:, 0:n], AF.Copy, [pb_], [z_b])
            proj_tm("w_in1", 0, 1864, ntile, consume)

            for t in range(ntile):
                z, z_b = Zs[t]
                gt = tile0 + t
                trow = (NT * 128) if sample else gt * 128
                P.dma(POOL, TB[:], tab_d[trow:trow + 128, :], TB_b, True)
                for j in range(8):
                    act(sq[:, 0:128], z[:, j * 128:(j + 1) * 128], AF.Square, [z_b], [sq_b, ss_b], accum=ss[:, j:j + 1])
                act(sq[:, 0:64], z[:, 1792:1856], AF.Square, [z_b], [sq_b, ss_b], accum=ss[:, 8:9])
                act(ss[:, 0:8], ss[:, 0:8], AF.Ln, [ss_b], [ss_b], scale=1.0 / 128, bias=epsc[:, 0:1])
                act(ss[:, 8:9], ss[:, 8:9], AF.Ln, [ss_b], [ss_b], scale=1.0 / 64, bias=epsc[:, 0:1])
                act(ss[:, 0:9], ss[:, 0:9], AF.Exp, [ss_b], [ss_b], scale=-0.5)
                for j in range(6):
                    stt(QN[:, j * 128:(j + 1) * 128], z[:, j * 128:(j + 1) * 128], ss[:, j:j + 1], gq[:], ALU.mult, ALU.mult, [z_b, ss_b, gq_b], [QN_b])
                for j in range(2):
                    stt(QN[:, 768 + j * 128:768 + (j + 1) * 128], z[:, 768 + j * 128:768 + (j + 1) * 128], ss[:, 6 + j:7 + j], gk[:],
                        ALU.mult, ALU.mult, [z_b, ss_b, gk_b], [QN_b])
                cp(QN[:, 1024:1536], z[:, 1280:1792], [z_b], [QN_b], eng=POOL)
                stt(QN[:, 1536:1600], z[:, 1792:1856], ss[:, 8:9], gik[:], ALU.mult, ALU.mult, [z_b, ss_b, gik_b], [QN_b])
                cp(IW[:, t, :], z[:, 1856:1864], [z_b], [IW_b], eng=POOL)

                def rope(c0, nh, half, cc, sc):
                    w_ = nh * half
                    xv = QN[:, c0:c0 + 2 * w_].rearrange("p (h two d) -> p h two d", h=nh, two=2)
                    ov = QR[:, c0:c0 + 2 * w_].rearrange("p (h two d) -> p h two d", h=nh, two=2)
                    cosv = TB[:, cc:cc + w_].rearrange("p (h d) -> p h d", h=nh)
                    sinv = TB[:, sc:sc + w_].rearrange("p (h d) -> p h d", h=nh)
                    t1 = T1[:, 0:w_].rearrange("p (h d) -> p h d", h=nh)
                    t2 = T2[:, 0:w_].rearrange("p (h d) -> p h d", h=nh)
                    x1, x2 = xv[:, :, 0, :], xv[:, :, 1, :]
                    tt(t1, x1, cosv, ALU.mult, [QN_b, TB_b], [T1_b])
                    tt(t2, x2, sinv, ALU.mult, [QN_b, TB_b], [T2_b], eng=POOL)
                    tt(ov[:, :, 0, :], t1, t2, ALU.subtract, [T1_b, T2_b], [QR_b])
                    tt(t1, x2, cosv, ALU.mult, [QN_b, TB_b], [T1_b])
                    tt(t2, x1, sinv, ALU.mult, [QN_b, TB_b], [T2_b], eng=POOL)
                    tt(ov[:, :, 1, :], t1, t2, ALU.add, [T1_b, T2_b], [QR_b])
                rope(0, 6, 64, 0, 384)
                rope(768, 2, 64, 0, 384)
                rope(1024, 8, 32, 768, 1024)
                rope(1536, 1, 32, 768, 1024)
                if sample:
                    P.dma(POOL, kso_d[:, :], QR[:, 768:1024], QR_b, False, is_out=True)
                    P.dma(POOL, vso_d[:, :], z[:, 1024:1280], z_b, False, is_out=True)
                    P.dma(POOL, ikso_d[:, :], QR[:, 1536:1600], QR_b, False, is_out=True)
                else:
                    P.dma(POOL, ko_d[gt * 128:(gt + 1) * 128, :], QR[:, 768:1024], QR_b, False, is_out=True)
                    P.dma(POOL, vo_d[gt * 128:(gt + 1) * 128, :], z[:, 1024:1280], z_b, False, is_out=True)
                    P.dma(POOL, iko_d[gt * 128:(gt + 1) * 128, :], QR[:, 1536:1600], QR_b, False, is_out=True)
                kcol = 0 if sample else gt * 128
                cp(VB[:, 0 if sample else gt, :], z[:, 1024:1280], [z_b], [VB_b[0 if sample else gt]], eng=ACT)
                for half_ in range(2):
                    pt_, pb_ = pb()
                    nh = 4 if half_ == 0 else 2
                    for j in range(nh):
                        hq = half_ * 4 + j
                        P.op(PE, lambda h: h.transpose(pt_[:, j * 128:(j + 1) * 128], QR[:, hq * 128:(hq + 1) * 128], identF[:]), [QR_b, identF_b], [pb_])
                    act(QT[:, half_ * 4:half_ * 4 + nh, t * 128:(t + 1) * 128], pt_[:, 0:nh * 128].rearrange("p (k n) -> p k n", k=nh),
                        AF.Copy, [pb_], [QT_b])
                pt_, pb_ = pb()
                for j in range(2):
                    P.op(PE, lambda h: h.transpose(pt_[:, j * 128:(j + 1) * 128], QR[:, 768 + j * 128:768 + (j + 1) * 128], identF[:]), [QR_b, identF_b], [pb_])
                P.op(PE, lambda h: h.transpose(pt_[0:64, 256:384], QR[:, 1536:1600], identF[:]), [QR_b, identF_b], [pb_])
                act(KT[:, :, kcol:kcol + 128], pt_[:, 0:256].rearrange("p (k n) -> p k n", k=2), AF.Copy, [pb_], [KT_b[0 if sample else gt]])
                act(IKT[0:64, kcol:kcol + 128], pt_[0:64, 256:384], AF.Copy, [pb_], [IKT_b[0 if sample else gt]])
                for half_ in range(2):
                    pt_, pb_ = pb()
                    for j in range(4):
                        hq = half_ * 4 + j
                        P.op(PE, lambda h: h.transpose(pt_[0:64, j * 128:(j + 1) * 128], QR[:, 1024 + hq * 64:1024 + (hq + 1) * 64], identF[:]),
                             [QR_b, identF_b], [pb_])
                    act(IQT[0:64, half_ * 4:half_ * 4 + 4, t * 128:(t + 1) * 128], pt_[0:64, :].rearrange("p (k n) -> p k n", k=4),
                        AF.Copy, [pb_], [IQT_b])

        def index_scores(st, t, keysets, SC, SC_b, row_lo=0, row_hi=128):
            IQT, IQT_b = st["IQT"]
            IW, IW_b = st["IW"]
            R = st["R"]
            ri = 0
            for (rhsf, kb_, ncols, sc0) in keysets:
                k0 = 0
                while k0 < ncols:
                    n = min(512, ncols - k0)
                    for h in range(8):
                        pt_, pb_ = pb()
                        mm(pt_[:, 0:n], IQT[0:64, h, t * 128:(t + 1) * 128], rhsf(k0, n), True, True, [IQT_b, kb_], [pb_])
                        r_, r_b = R[ri % 4]
                        ri += 1
                        act(r_[row_lo:row_hi, 0:n], pt_[row_lo:row_hi, 0:n], AF.Relu, [pb_], [r_b])
                        dst = SC[row_lo:row_hi, sc0 + k0:sc0 + k0 + n]
                        if h == 0:
                            tsc(dst, r_[row_lo:row_hi, 0:n], IW[row_lo:row_hi, t, 0:1], None, ALU.mult, ALU.bypass, [r_b, IW_b], [SC_b])
                        else:
                            stt(dst, r_[row_lo:row_hi, 0:n], IW[row_lo:row_hi, t, h:h + 1], dst, ALU.mult, ALU.add, [r_b, IW_b], [SC_b])
                    k0 += n

        def select_mask(st, L, SC, SC_b, MK, MK_b, cm, cm_b, cmcol, nsel):
            bs, bs_b = st["bs"]
            P.op(DVE, lambda h: h.tensor_reduce(out=bs[:, 0:1], in_=SC[:, 0:L], axis=AX.X, op=ALU.min), [SC_b], [bs_b])
            tt(SC[:, cmcol:cmcol + 128], SC[:, cmcol:cmcol + 128], cm[:], ALU.add, [SC_b, cm_b], [SC_b])
            P.op(DVE, lambda h: h.tensor_reduce(out=bs[:, 1:2], in_=SC[:, 0:L], axis=AX.X, op=ALU.max), [SC_b], [bs_b])
            tt(bs[:, 1:2], bs[:, 1:2], bs[:, 0:1], ALU.subtract, [bs_b], [bs_b])
            for i in range(NBIS):
                stt(bs[:, 2:3], bs[:, 1:2], float(2.0 ** -(i + 1)), bs[:, 0:1], ALU.mult, ALU.add, [bs_b], [bs_b])
                tsc(MK[:, 0:L], SC[:, 0:L], bs[:, 2:3], None, ALU.is_ge, ALU.add, [SC_b, bs_b], [MK_b, bs_b], accum=bs[:, 3:4])
                tsc(bs[:, 4:5], bs[:, 3:4], float(nsel) - 0.5, float(2.0 ** -(i + 1)), ALU.is_ge, ALU.mult, [bs_b], [bs_b])
                stt(bs[:, 0:1], bs[:, 1:2], bs[:, 4:5], bs[:, 0:1], ALU.mult, ALU.add, [bs_b], [bs_b])
            tsc(MK[:, 0:L], SC[:, 0:L], bs[:, 0:1], None, ALU.is_ge, ALU.bypass, [SC_b, bs_b], [MK_b])

        def attention(st, qcols, nq, sel, sel_b, keytiles, MK, MK_b, mixcols, page_loader=None):
            QT, QT_b = st["QT"]
            E = st["E"]
            PP = st["PP"]
            RD, RD_b = st["RD"]
            n3 = 3 * nq
            accO = [PS[0], PS[1]]
            accD = [PS[2], PS[3]]
            nk = len(keytiles)
            ei = 0
            pend = []

            def flush(pl):
                for (g, vf_g, vb_, p_, p_b, kidx) in pl:
                    mm(accO[g][0][:, 0:n3], vf_g, p_[:, 0:n3], kidx == 0, kidx == nk - 1, [vb_, p_b], [accO[g][1]])
                    mm(accD[g][0][:, 0:n3], onesB[:], p_[:, 0:n3], kidx == 0, kidx == nk - 1, [onesB_b, p_b], [accD[g][1]])
            for ki, ent in enumerate(keytiles):
                if ent[0] == "page":
                    mc0 = ent[2]
                    ktf, kb_, vf, vb_ = page_loader(ki, ent[1])
                else:
                    ktf, kb_, vf, vb_, mc0 = ent
                pm, pm_b = PS[6 + (ki % 2)]
                mm(pm[:, 0:n3], MK[:, mc0:mc0 + 128], sel, True, True, [MK_b, sel_b], [pm_b])
                cur = []
                for g in range(2):
                    ps_, ps_b = PS[4 + g]
                    mm(ps_[:, 0:n3].rearrange("p (h n) -> p h n", h=3), ktf(g), QT[:, 3 * g:3 * g + 3, qcols], True, True, [kb_, QT_b], [ps_b])
                    e_, e_b = E[ei % 4]
                    p_, p_b = PP[ei % 4]
                    ei += 1
                    act(e_[:, 0:n3], ps_[:, 0:n3], AF.Exp, [ps_b], [e_b], scale=float(128 ** -0.5))
                    tt(p_[:, 0:n3], e_[:, 0:n3], pm[:, 0:n3], ALU.mult, [e_b, pm_b], [p_b])
                    cur.append((g, vf(g), vb_, p_, p_b, ki))
                flush(pend)
                pend = cur
            flush(pend)
            for g in range(2):
                act(RD[:, 0:n3], accD[g][0][:, 0:n3], AF.Ln, [accD[g][1]], [RD_b])
                act(RD[:, 0:n3], RD[:, 0:n3], AF.Exp, [RD_b], [RD_b], scale=-1.0)
                tt(MIX[:, 3 * g:3 * g + 3, mixcols], accO[g][0][:, 0:n3].rearrange("p (h n) -> p h n", h=3),
                   RD[:, 0:n3].rearrange("p (h n) -> p h n", h=3), ALU.mult, [accO[g][1], RD_b], [MIX_b])

        def l1_common(ls, ntile):
            ntok = ntile * 128
            st = {}
            st["MQ"] = sb(ls, "MQ1", [128, 2, ntok])
            st["mq_sq"] = sb(ls, "mq_sq1", [128, ntok], BF16)
            st["mq_rs"] = sb(ls, "mq_rs1", [128, ntok])
            st["mq_bf"] = sb(ls, "mq_bf1", [128, 2, ntok], BF16)
            st["mexp"] = [sb(ls, "mexp1%d" % i, [128, ntok], BF16) for i in range(3)]
            st["mrd"] = sb(ls, "mrd1", [128, ntok])
            st["QT"] = sb(ls, "QT", [128, 6, ntok], BF16)
            st["IQT"] = sb(ls, "IQT", [64, 8, ntok], BF16)
            st["IW"] = sb(ls, "IW", [128, ntile, 8])
            return st

        def l1_inproj(ls, ntile, st):
            st["Zs"] = [sb(ls, "Z%d" % i, [128, 1864]) for i in range(ntile)]
            st["Z"] = st["Zs"][0]
            st["QN"] = sb(ls, "QN", [128, 1600])
            st["QR"] = sb(ls, "QR", [128, 1600])
            st["TB"] = sb(ls, "TB", [128, 1280])
            st["T1"] = sb(ls, "T1", [128, 384])
            st["T2"] = sb(ls, "T2", [128, 384])
            st["ss16"] = sb(ls, "ss16", [128, 16])
            st["sqj"] = st["T1"]

        def l1_attn(ls, st, sccols):
            st["R"] = [sb(ls, "R%d" % i, [128, 512], BF16) for i in range(4)]
            st["bs"] = sb(ls, "bs", [128, 8])
            st["E"] = [sb(ls, "E%d" % i, [128, 384], BF16) for i in range(4)]
            st["PP"] = [sb(ls, "PP%d" % i, [128, 384], BF16) for i in range(4)]
            st["RD"] = sb(ls, "RD", [128, 384])
            st["SC"] = sb(ls, "SC", [128, sccols])
            st["MK"] = sb(ls, "MK", [128, sccols], BF16)

        try:
            with ExitStack() as ms0:
                memf = [sb(ms0, "memf%d" % i, [128, D]) for i in range(2)]
                for mt in range(2):
                    P.dma(POOL, memf[mt][0][:], mem_d[mt * 128:(mt + 1) * 128, :], memf[mt][1], True)
                    for half in range(2):
                        pt_, pb_ = pb()
                        for j in range(4):
                            kc = half * 4 + j
                            P.op(PE, lambda h: h.transpose(pt_[:, j * 128:(j + 1) * 128], memf[mt][0][:, kc * 128:(kc + 1) * 128], identF[:]),
                                 [memf[mt][1], identF_b], [pb_])
                        act(MEMT[:, half * 4:half * 4 + 4, mt * 128:(mt + 1) * 128], pt_[:].rearrange("p (k n) -> p k n", k=4), AF.Copy, [pb_], [MEMT_b])
            P.barrier()
            MKTp = [sb(es, "MKTp%d" % l, [128, 2, 256], BF16) for l in range(2)]
            MVp = [sb(es, "MVp%d" % l, [128, 2, 256], BF16) for l in range(2)]
            for l in range(2):
                memkv_prompt(l, MKTp[l][0], MKTp[l][1], MVp[l][0], MVp[l][1])
            _stage(1)

            def sample_group():
                P.dma(POOL, X[:, 0, :], xs_d[:, :], X_b, True)
                P.op(DVE, lambda h: h.memset(UH[:], 0.0), [], [UH_b])
                for l in range(2):
                    with ExitStack() as ss_:
                        MKs = [sb(ss_, "MKs%d" % b, [128, 2, 256], BF16) for b in range(4)]
                        MVs = [sb(ss_, "MVs%d" % b, [128, 2, 256], BF16) for b in range(4)]
                        slots = []
                        for b in range(4):
                            P.dma(POOL, MKs[b][0][:], mkT_s[l, b], MKs[b][1], True)
                            P.dma(POOL, MVs[b][0][:], mv_s[l, b].rearrange("(t p) d -> p t d", p=128), MVs[b][1], True)
                            slots.append((32 * b, 32, MKs[b][0], MKs[b][1], MVs[b][0], MVs[b][1]))
                        if l == 0:
                            layer0(1, True, slots, True, True)
                            _stage(2)
                        else:
                            with ExitStack() as ls:
                                st = l1_common(ls, 1)
                                KTs, KTs_b = sb(ls, "KTs", [128, 2, 128], BF16)
                                VBs, VBs_b = sb(ls, "VBs", [128, 1, 256], BF16)
                                IKTs, IKTs_b = sb(ls, "IKTs", [64, 128], BF16)
                                with ExitStack() as zs:
                                    l1_inproj(zs, 1, st)
                                    dsa_inproj(1, 0, st, True, KTs, [KTs_b], VBs, [VBs_b], IKTs, [IKTs_b])
                                    _stage(5)
                                P.barrier()
                                with ExitStack() as as_:
                                    l1_attn(as_, st, PAST + 128)
                                    SC, SC_b = st["SC"]
                                    MK, MK_b = st["MK"]
                                    IKP, IKP_b = sb(as_, "IKP", [64, PAST], BF16)
                                    PTs, PTs_b = sb(as_, "PTs", [128, 4 * NPAGE], I32)
                                    IDX = sb(as_, "IDX", [128, 4, 4 * NPAGE], I32)
                                    sel3 = sb(as_, "sel3", [128, 4, 3, 32], BF16)
                                    for b in range(4):
                                        for j in range(3):
                                            cp(sel3[0][:, b, j, :], identF[:, 32 * b:32 * b + 32], [identF_b], [sel3[1]])
                                    KPr = [sb(as_, "KPr%d" % i, [128, 2, 128], BF16) for i in range(4)]
                                    VPr = [sb(as_, "VPr%d" % i, [128, 256], BF16) for i in range(4)]
                                    P.dma(POOL, PTs[:], pt_d.partition_broadcast(128), PTs_b, True)
                                    tsc(IDX[0][:, 0, :], PTs[:], 64.0, pidx[:, 0:1], ALU.mult, ALU.add, [PTs_b, pidx_b], [IDX[1]])
                                    tsc(IDX[0][:, 1, :], PTs[:], 256.0, pidx[:, 0:1], ALU.mult, ALU.add, [PTs_b, pidx_b], [IDX[1]])
                                    tsc(IDX[0][:, 2, :], IDX[0][:, 1, :], 128.0, None, ALU.add, ALU.bypass, [IDX[1]], [IDX[1]])
                                    tsc(IDX[0][:, 3, :], PTs[:], 128.0, pidx[:, 0:1], ALU.mult, ALU.add, [PTs_b, pidx_b], [IDX[1]])
                                    for b in (3, 2, 1, 0):
                                        rows = (64, 128) if b == 3 else (32 * b, 32 * b + 32)
                                        for pg in range(NPAGE):
                                            pi = b * NPAGE + pg
                                            P.idma(IKP[0:64, pg * 128:(pg + 1) * 128], pool_ik[:, :], IDX[0][0:64, 0, pi:pi + 1], IKP_b, [IDX[1]])
                                        keysets = [(lambda k0, n: IKP[0:64, k0:k0 + n], IKP_b, PAST, 0),
                                                   (lambda k0, n: IKTs[0:64, k0:k0 + n], IKTs_b, 128, PAST)]
                                        index_scores(st, 0, keysets, SC, SC_b, rows[0], rows[1])
                                    select_mask(st, PAST + 128, SC, SC_b, MK, MK_b, cms, cms_b, PAST, NSEL_S)
                                    _stage(6)
                                    for b in range(4):
                                        kts = []
                                        for pg in range(NPAGE):
                                            kts.append(("page", b * NPAGE + pg, pg * 128))
                                        kts.append(((lambda g: KTs[:, g, :]), KTs_b, (lambda g: VBs[:, 0, g * 128:(g + 1) * 128]), VBs_b, PAST))

                                        def page_loader(i, pi, KPr=KPr, VPr=VPr):
                                            kp, kp_b = KPr[i % 4]
                                            vp, vp_b = VPr[i % 4]
                                            P.idma(kp[:].rearrange("p g r -> p (g r)"), pool_k[:, :], IDX[0][:, 3, pi:pi + 1], kp_b, [IDX[1]])
                                            P.idma(vp[:], pool_v[:, :], IDX[0][:, 3, pi:pi + 1], vp_b, [IDX[1]])
                                            return ((lambda g: kp[:, g, :]), kp_b, (lambda g: vp[:, g * 128:(g + 1) * 128]), vp_b)
                                        attention(st, slice(32 * b, 32 * b + 32), 32, sel3[0][:, b, :, :].rearrange("p j n -> p (j n)"), sel3[1],
                                                  kts, MK, MK_b, slice(32 * b, 32 * b + 32), page_loader)
                                    mem_attention(st, 128, slots, 1)
                            P.barrier()
                    out_proj(l, 1)
                    _stage(3 + 4 * l)
                    ffn(l, 1, True, cvs_d[l])
                    _stage(4 + 4 * l)
                P.dma(POOL, ys_d[:, :], X[:, 0, :], X_b, False, is_out=True)

            sample_group()
            P.barrier()
            _stage(9)

            KT, _ = sb(es, "KT", [128, 2, T], BF16)
            VB, _ = sb(es, "VB", [128, NT, 256], BF16)
            IKT, _ = sb(es, "IKT", [64, T], BF16)
            KT_b = [Buf() for _ in range(NT)]
            VB_b = [Buf() for _ in range(NT)]
            IKT_b = [Buf() for _ in range(NT)]
            UHs = [sb(es, "UHs%d" % l, [128, NFC, 2]) for l in range(2)]
            for l in range(2):
                P.op(DVE, lambda h: h.memset(UHs[l][0][:], 0.0), [], [UHs[l][1]])

            for g in range(NG):
                first, last = (g == 0), (g == NG - 1)
                for t in range(G):
                    P.dma(POOL, X[:, t, :], x_d[(g * G + t) * 128:(g * G + t + 1) * 128, :], X_b, True)
                for l in range(2):
                    slots = [(0, G * 128, MKTp[l][0], MKTp[l][1], MVp[l][0], MVp[l][1])]
                    if l == 0:
                        layer0(G, False, slots, first, last)
                    else:
                        with ExitStack() as ls:
                            st = l1_common(ls, G)
                            with ExitStack() as zs:
                                l1_inproj(zs, G, st)
                                dsa_inproj(G, g * G, st, False, KT, KT_b, VB, VB_b, IKT, IKT_b)
                            P.barrier()
                            with ExitStack() as as_:
                                l1_attn(as_, st, T)
                                SC, SC_b = st["SC"]
                                MK, MK_b = st["MK"]
                                for t in range(G):
                                    gt = g * G + t
                                    L = (gt + 1) * 128
                                    merged = []
                                    kt = 0
                                    while kt <= gt:
                                        nkt = min(4, gt + 1 - kt)
                                        merged.append(((lambda k0, n, kt=kt: IKT[0:64, kt * 128 + k0:kt * 128 + k0 + n]), IKT_b[kt + nkt - 1], nkt * 128, kt * 128))
                                        for q in range(kt, kt + nkt - 1):
                                            P._deps(PE, [IKT_b[q]], [])
                                        kt += nkt
                                    index_scores(st, t, merged, SC, SC_b)
                                    select_mask(st, L, SC, SC_b, MK, MK_b, cmdiag, cmdiag_b, gt * 128, NSEL)
                                    kts = []
                                    for kt in range(gt + 1):
                                        kts.append(((lambda gg, kt=kt: KT[:, gg, kt * 128:(kt + 1) * 128]), KT_b[kt],
                                                    (lambda gg, kt=kt: VB[:, kt, gg * 128:(gg + 1) * 128]), VB_b[kt], kt * 128))
                                    attention(st, slice(t * 128, (t + 1) * 128), 128, ident3[:].rearrange("p j n -> p (j n)"), ident3_b,
                                              kts, MK, MK_b, slice(t * 128, (t + 1) * 128))
                                mem_attention(st, G * 128, slots, 1)
                        P.barrier()
                    if DEBUG and g == 0 and l == 0:
                        with ExitStack() as ds:
                            dm, dm_b = sb(ds, "dbgm", [128, 8, G * 128])
                            cp(dm[:], MIX[:], [MIX_b], [dm_b])
                            P.dma(POOL, dbg_mix[:, :, :], dm[:], dm_b, False, is_out=True)
                            P.barrier()
                    out_proj(l, G)
                    if DEBUG and g == 0 and l == 0:
                        P.dma(POOL, dbg_x1[:, :, :], X[:], X_b, False, is_out=True)
                    cp(UH[:], UHs[l][0][:], [UHs[l][1]], [UH_b])
                    ffn(l, G, False, cvp_d[l] if last else None)
                    cp(UHs[l][0][:], UH[:], [UH_b], [UHs[l][1]])
                    if DEBUG and g == 0 and l == 0:
                        P.dma(POOL, dbg_x2[:, :, :], X[:], X_b, False, is_out=True)
                for t in range(G):
                    P.dma(POOL, y_d[(g * G + t) * 128:(g * G + t + 1) * 128, :], X[:, t, :], X_b, False, is_out=True)
        except _Stop:
            P.barrier()
        P.finish()
    return nc


def _rope_tab(pos, half):
    inv = 10000.0 ** (-np.arange(half, dtype=np.float32) / half)
    ang = pos.astype(np.float32)[:, None] * inv[None, :]
    return np.cos(ang).astype(np.float32), np.sin(ang).astype(np.float32)


def _consts(NT, PASTLEN):
    c = {}
    c["ident"] = np.eye(128, dtype=np.float32)
    s = np.arange(128)[:, None]
    t = np.arange(128)[None, :]
    m = ((s // 32 == t // 32) & (s <= t)).astype(np.float32)
    c["mhg"] = np.tile(m, (1, 3))
    r = np.ones((128, 512), np.float32)
    r[:, ::32] = 0.0
    c["resetm"] = r
    v = np.zeros((128, 128), np.float32)
    for b in range(4):
        v[:, 32 * b + 2:32 * b + 6] = 1.0
    c["validm"] = v
    c["cmdiag"] = np.where(t <= s, 0.0, NEG).astype(np.float32)
    cm = np.full((128, 128), NEG, np.float32)
    for b in range(4):
        for tq in range(4):
            cm[32 * b + 2 + tq, 32 * b + 2:32 * b + 3 + tq] = 0.0
    c["cms"] = cm
    bo = np.zeros((128, 128), np.float32)
    bo[:64, :64] = 1.0
    bo[64:, 64:] = 1.0
    c["bones"] = bo
    m9 = np.zeros((128, 128), np.float32)
    m9[:, 96:] = 1.0
    c["m96"] = m9
    c["pidx"] = np.arange(128, dtype=np.float32).reshape(128, 1)
    T = NT * 128
    pos = np.zeros(T + 128, np.float32)
    pos[:T] = np.arange(T)
    for b in range(4):
        for tq in range(4):
            pos[T + 32 * b + 2 + tq] = PASTLEN + tq
    cq, sq = _rope_tab(pos, 64)
    ci, si = _rope_tab(pos, 32)
    c["tab"] = np.concatenate([np.tile(cq, (1, 6)), np.tile(sq, (1, 6)), np.tile(ci, (1, 8)), np.tile(si, (1, 8))], axis=1).astype(np.float32)
    return c


def _fm(v, nch):
    return np.ascontiguousarray(v.reshape(nch, 128).T)


def make_in_maps(inp, NT, NPAGE, NPOOL, ncores=8):
    f = lambda a: np.ascontiguousarray(np.asarray(a, dtype=np.float32))
    PASTLEN = NPAGE * 128
    cst = _consts(NT, PASTLEN)
    T = NT * 128
    shared = {}
    shared["w_in0"] = f(inp["w_in_hgrn"][0])
    shared["w_in1"] = f(inp["w_in_dsa"][0])
    shared["w_kv"] = f(inp["w_mem_kv"])
    shared["w_o"] = f(inp["w_out"])
    shared["w_g"] = f(inp["w_ffn_gate"])
    shared["w_u"] = f(inp["w_ffn_up"])
    shared["w_d"] = f(inp["w_ffn_down"])
    bc = lambda a: f(np.broadcast_to(np.asarray(a)[:, None, :], (a.shape[0], 128, a.shape[1])))
    shared["bc_nmix"] = bc(np.asarray(inp["norm_mix"]))
    shared["bc_nffn"] = bc(np.asarray(inp["norm_ffn"]))
    lb = np.asarray(inp["hgrn_lb_logits"], np.float32)
    shared["lbl"] = f(np.stack([_fm(lb[0], 6), _fm(lb[1], 6)], axis=1))
    shared["onorm"] = f(np.asarray(inp["hgrn_out_norm"])[0].reshape(128, 1))
    shared["bc_qn"] = f(np.broadcast_to(np.asarray(inp["dsa_q_norm"])[0][None, :], (128, 128)))
    shared["bc_kn"] = f(np.broadcast_to(np.asarray(inp["dsa_k_norm"])[0][None, :], (128, 128)))
    shared["bc_ikn"] = f(np.broadcast_to(np.asarray(inp["idx_k_norm"])[0][None, :], (128, 64)))
    mq = np.asarray(inp["mem_q_norm"], np.float32)
    shared["mqn"] = f(np.concatenate([mq, mq], axis=1).T)
    mk = np.asarray(inp["mem_k_norm"], np.float32)
    shared["bc_mkn"] = f(np.broadcast_to(np.tile(mk, (1, 4))[None, :, :], (128, 2, 256)))
    cwv = np.asarray(inp["ffn_conv_w"], np.float32)
    shared["cw"] = f(cwv.reshape(2, 3, NFC, 128).transpose(3, 0, 2, 1))
    cbv = np.asarray(inp["ffn_conv_b"], np.float32)
    shared["cb"] = f(cbv.reshape(2, NFC, 128).transpose(2, 0, 1))
    ck = np.asarray(inp["cache_k"], np.float32)[0]
    shared["pool_k"] = f(ck.transpose(0, 3, 2, 1)).reshape(NPOOL * 128, 256)
    shared["pool_v"] = f(np.asarray(inp["cache_v"], np.float32)[0].reshape(NPOOL * 128, 256))
    shared["pool_ik"] = f(np.asarray(inp["cache_idx_k"], np.float32)[0].transpose(0, 2, 1)).reshape(NPOOL * 64, 128)
    for k in ("ident", "mhg", "resetm", "validm", "cmdiag", "cms", "bones", "tab", "m96", "pidx"):
        shared[k] = cst[k]
    xs_all = np.asarray(inp["x_sample"], np.float32)
    xp = np.asarray(inp["x_prompt"], np.float32)
    memp = np.asarray(inp["mem_prompt"], np.float32)
    cmk = np.asarray(inp["cache_mem_k"], np.float32)
    cmv = np.asarray(inp["cache_mem_v"], np.float32)
    sth = np.asarray(inp["state_hgrn"], np.float32)[0]
    sfc = np.asarray(inp["state_ffn_conv"], np.float32)
    ptab = np.asarray(inp["page_table"]).astype(np.int32)
    nseq = xp.shape[0]
    maps = []
    for c in range(ncores):
        m = dict(shared)
        sq_ = c % nseq
        m["x"] = f(xp[sq_, :T])
        m["mem"] = f(memp[sq_])
        xs = np.zeros((128, D), np.float32)
        cstt = np.zeros((2, 128, NFC, 128), np.float32)
        for b in range(4):
            gb = 4 * c + b
            xs[32 * b + 2:32 * b + 6] = xs_all[gb]
            for l in range(2):
                stt_ = sfc[l, gb].reshape(2, NFC, 128)
                cstt[l, :, :, 32 * b:32 * b + 2] = stt_.transpose(2, 1, 0)
        m["xs"] = xs
        m["cst"] = cstt
        kk = cmk[:, 4 * c:4 * c + 4]
        kk = kk.reshape(2, 4, 256, 2, 2, 64)
        m["mkT_s"] = f(kk.transpose(0, 1, 4, 5, 3, 2).reshape(2, 4, 128, 2, 256))
        m["mv_s"] = f(cmv[:, 4 * c:4 * c + 4].reshape(2, 4, 256, 256))
        m["st_h"] = f(sth[4 * c:4 * c + 4])
        m["pt"] = np.ascontiguousarray(ptab[4 * c:4 * c + 4].reshape(1, -1))
        maps.append(m)
    return maps


_CACHE = {}


def run(inp, NT, NPAGE, NPOOL, NSEL, NBIS=10):
    key = (NT, NPAGE, NPOOL, NSEL, NBIS)
    if key not in _CACHE:
        _CACHE[key] = build(NT, NPAGE, NPOOL, NSEL, NBIS)
    nc = _CACHE[key]
    maps = make_in_maps(inp, NT, NPAGE, NPOOL)
    res = run_bass_kernel_spmd(nc, maps, core_ids=list(range(8)))
    return res.results


def assemble(R, NT, nseq=4):
    T = NT * 128
    y = np.stack([R[c]["y"] for c in range(nseq)])
    rows = np.array([32 * b + 2 + t for b in range(4) for t in range(4)])
    ys = np.concatenate([R[c]["ys"][rows].reshape(4, 4, D) for c in range(8)])
    hp = np.stack([R[c]["hstp"] for c in range(nseq)])[None]
    hs = np.concatenate([R[c]["hsts"] for c in range(8)])[None]
    kp = np.stack([R[c]["ko"].reshape(T, 2, 128) for c in range(nseq)])[None]
    vp = np.stack([R[c]["vo"].reshape(T, 2, 128) for c in range(nseq)])[None]
    ikp = np.stack([R[c]["iko"] for c in range(nseq)])[None]
    ks = np.concatenate([R[c]["kso"][rows].reshape(4, 4, 2, 128) for c in range(8)])[None]
    vs = np.concatenate([R[c]["vso"][rows].reshape(4, 4, 2, 128) for c in range(8)])[None]
    iks = np.concatenate([R[c]["ikso"][rows].reshape(4, 4, 64) for c in range(8)])[None]
    mk = np.stack([R[c]["mko"].reshape(2, 256, 4, 64) for c in range(nseq)], axis=1)
    mv = np.stack([R[c]["mvo"].reshape(2, 256, 4, 64) for c in range(nseq)], axis=1)
    cvp = np.stack([R[c]["cvp"][:, 126:128, :] for c in range(nseq)], axis=1)
    rows2 = np.array([32 * b + 4 + j for b in range(4) for j in range(2)])
    cvs = np.concatenate([R[c]["cvs"][:, rows2, :].reshape(2, 4, 2, DFF) for c in range(8)], axis=1)
    outs = (y, ys, hp, hs, kp, vp, ikp, ks, vs, iks, mk, mv, cvp, cvs)
    return tuple(np.ascontiguousarray(o.astype(np.float32)) for o in outs)


def kernel(**inputs):
    NT = 32
    NPAGE = 64
    NPOOL = int(np.asarray(inputs["cache_k"]).shape[1])
    R = run(inputs, NT, NPAGE, NPOOL, 256)
    return assemble(R, NT)
```

```python
import numpy as np
from contextlib import ExitStack
import concourse.bass as bass
import concourse.mybir as mybir
from concourse.bass_utils import run_bass_kernel_spmd

F32 = mybir.dt.float32
BF16 = mybir.dt.bfloat16
I32 = mybir.dt.int32
AF = mybir.ActivationFunctionType
ALU = mybir.AluOpType
AX = mybir.AxisListType

D = 1024
DFF = 2816
NFC = 22
HG_COLS = 3328
DSA_COLS = 2120
EPS = 1e-6
NEG = -1.0e30
DEBUG = False
STAGE = 99


class _Stop(Exception):
    pass


_PROG = [None]


def _stage(n):
    if STAGE <= n and not _PROG[0].stopped:
        _PROG[0].barrier()
        _PROG[0].stopped = True


class Buf:
    __slots__ = ("w", "r", "grp")

    def __init__(self):
        self.w = None
        self.r = {}
        self.grp = None


class DGroup:
    def __init__(self, sem):
        self.sem = sem
        self.cnt = 0


class Eng:
    def __init__(self, h, sem):
        self.h = h
        self.sem = sem
        self.cnt = 0
        self.known = {}


class Prog:
    def __init__(self, nc, es):
        self.nc = nc
        self.es = es
        self.nsem = 0
        self.PE = Eng(nc.tensor, self.newsem())
        self.ACT = Eng(nc.scalar, self.newsem())
        self.DVE = Eng(nc.vector, self.newsem())
        self.POOL = Eng(nc.gpsimd, self.newsem())
        self.SP = Eng(nc.sync, self.newsem())
        self.engs = [self.PE, self.ACT, self.DVE, self.POOL, self.SP]
        self.groups = []
        self.out_deps = []
        self.stopped = False
        self.free_groups = {}
        _PROG[0] = self

    def newsem(self):
        self.nsem += 1
        return self.es.enter_context(self.nc.semaphore("s%d" % self.nsem))

    def newgroup(self, kind="sp"):
        fl = self.free_groups.setdefault(kind, [])
        if fl:
            g = fl.pop()
            g.recycled = True
            return g
        g = DGroup(self.newsem())
        g.recycled = False
        g.kind = kind
        self.groups.append(g)
        return g

    def release(self, buf):
        if buf.grp is not None:
            self.free_groups[buf.grp.kind].append(buf.grp)
            buf.grp = None

    def _wait(self, eng, dep):
        sem, val = dep
        if isinstance(val, DGroup):
            val = val.cnt
        if sem is eng.sem and eng is self.PE:
            return
        k = id(sem)
        if eng.known.get(k, 0) >= val:
            return
        eng.h.wait_ge(sem, val)
        eng.known[k] = val

    def _deps(self, eng, r, w, skip_grp=None):
        if self.stopped:
            return
        for b in r:
            if b.w is not None:
                self._wait(eng, b.w)
        for b in w:
            if b.w is not None:
                if not (skip_grp is not None and b.w[0] is skip_grp.sem):
                    self._wait(eng, b.w)
            for sem, val in list(b.r.values()):
                self._wait(eng, (sem, val))

    def op(self, eng, fn, r=(), w=()):
        if self.stopped:
            return None
        self._deps(eng, r, w)
        ins = fn(eng.h)
        eng.cnt += 1
        ins.then_inc(eng.sem, 1)
        me = (eng.sem, eng.cnt)
        for b in w:
            b.w = me
            b.r = {}
        for b in r:
            if b not in w:
                b.r[id(eng.sem)] = me
        return ins

    def dma(self, q, out, in_, buf, load, grp=None, is_out=False, other=()):
        if self.stopped:
            return None
        if grp is None:
            if buf.grp is None:
                buf.grp = self.newgroup("pool" if q is self.POOL else "sp")
            grp = buf.grp
            assert (grp.kind == "pool") == (q is self.POOL), "mixed DMA queue kinds on one buffer"
        if getattr(grp, "recycled", False):
            self._wait(q, (grp.sem, grp.cnt))
            grp.recycled = False
        if load:
            self._deps(q, other, [buf], skip_grp=grp)
        else:
            self._deps(q, [buf] + list(other), [])
        ins = q.h.dma_start(out=out, in_=in_)
        grp.cnt += 16
        ins.then_inc(grp.sem, 16)
        if load:
            buf.w = (grp.sem, grp)
            buf.r = {}
        else:
            buf.r[id(grp.sem)] = (grp.sem, grp.cnt)
            if is_out:
                self.out_deps.append(grp)
        return ins

    def idma(self, out, in_, idx_ap, buf, other=()):
        q = self.POOL
        if self.stopped:
            return None
        if buf.grp is None:
            buf.grp = self.newgroup("pool")
        grp = buf.grp
        assert grp.kind == "pool"
        if getattr(grp, "recycled", False):
            self._wait(q, (grp.sem, grp.cnt))
            grp.recycled = False
        self._deps(q, other, [buf], skip_grp=grp)
        ins = q.h.indirect_dma_start(out=out, out_offset=None, in_=in_,
                                     in_offset=bass.IndirectOffsetOnAxis(ap=idx_ap, axis=0))
        grp.cnt += 16
        ins.then_inc(grp.sem, 16)
        buf.w = (grp.sem, grp)
        buf.r = {}
        return ins

    def barrier(self, force=False):
        if self.stopped and not force:
            return
        for e in self.engs:
            if e is self.SP and not force:
                continue
            for o in self.engs:
                if o is not e and o.cnt > 0:
                    self._wait(e, (o.sem, o.cnt))
            for g in self.groups:
                if g.cnt > 0:
                    self._wait(e, (g.sem, g.cnt))

    def finish(self):
        self.barrier(force=True)


def build(NT, NPAGE, NPOOL, NSEL, NBIS, G=4, NSEL_S=256):
    T = NT * 128
    NG = NT // G
    PAST = NPAGE * 128
    nc = bass.Bass("TRN2", target_bir_lowering=False)

    def din(name, shape, dt=F32):
        return nc.dram_tensor(name, list(shape), dt, kind="ExternalInput").ap()

    def dout(name, shape, dt=F32):
        return nc.dram_tensor(name, list(shape), dt, kind="ExternalOutput").ap()

    x_d = din("x", [T, D])
    xs_d = din("xs", [128, D])
    w_in0 = din("w_in0", [D, HG_COLS])
    w_in1 = din("w_in1", [D, DSA_COLS])
    w_kv = din("w_kv", [2, D, 512])
    w_o = din("w_o", [2, D, D])
    w_g = din("w_g", [2, D, DFF])
    w_u = din("w_u", [2, D, DFF])
    w_d = din("w_d", [2, DFF, D])
    bc_nmix = din("bc_nmix", [2, 128, D])
    bc_nffn = din("bc_nffn", [2, 128, D])
    lbl_d = din("lbl", [128, 2, 6])
    onorm_d = din("onorm", [128, 1])
    bc_qn = din("bc_qn", [128, 128])
    bc_kn = din("bc_kn", [128, 128])
    bc_ikn = din("bc_ikn", [128, 64])
    mqn_d = din("mqn", [128, 2])
    bc_mkn = din("bc_mkn", [128, 2, 256])
    cw_d = din("cw", [128, 2, NFC, 3])
    cb_d = din("cb", [128, 2, NFC])
    mem_d = din("mem", [256, D])
    mkT_s = din("mkT_s", [2, 4, 128, 2, 256])
    mv_s = din("mv_s", [2, 4, 256, 256])
    st_h = din("st_h", [4, 6, 128, 128])
    cst_d = din("cst", [2, 128, NFC, 128])
    pt_d = din("pt", [1, 4 * NPAGE], I32)
    pool_ik = din("pool_ik", [NPOOL * 64, 128])
    pool_k = din("pool_k", [NPOOL * 128, 256])
    pool_v = din("pool_v", [NPOOL * 128, 256])
    pidx_d = din("pidx", [128, 1])
    tab_d = din("tab", [(NT + 1) * 128, 1280])
    ident_d = din("ident", [128, 128])
    mhg_d = din("mhg", [128, 384])
    resetm_d = din("resetm", [128, 512])
    validm_d = din("validm", [128, 128])
    cmdiag_d = din("cmdiag", [128, 128])
    cms_d = din("cms", [128, 128])
    bones_d = din("bones", [128, 128])
    m96_d = din("m96", [128, 128])

    y_d = dout("y", [T, D])
    ys_d = dout("ys", [128, D])
    hstp_d = dout("hstp", [6, 128, 128])
    hsts_d = dout("hsts", [4, 6, 128, 128])
    ko_d = dout("ko", [T, 256])
    vo_d = dout("vo", [T, 256])
    iko_d = dout("iko", [T, 64])
    kso_d = dout("kso", [128, 256])
    vso_d = dout("vso", [128, 256])
    ikso_d = dout("ikso", [128, 64])
    mko_d = dout("mko", [2, 256, 256])
    mvo_d = dout("mvo", [2, 256, 256])
    cvp_d = dout("cvp", [2, 128, DFF])
    cvs_d = dout("cvs", [2, 128, DFF])

    if DEBUG:
        dbg_mix = dout("dbg_mix", [128, 8, G * 128])
        dbg_x1 = dout("dbg_x1", [128, G, D])
        dbg_x2 = dout("dbg_x2", [128, G, D])
    es = ExitStack()
    with es:
        P = Prog(nc, es)
        PE, ACT, DVE, POOL, SP = P.PE, P.ACT, P.DVE, P.POOL, P.SP
        cgrp = P.newgroup("pool")

        uid = [0]

        def sb(stack, name, shape, dt=F32):
            uid[0] += 1
            t = stack.enter_context(nc.sbuf_tensor("%s_%d" % (name, uid[0]), list(shape), dt))
            b = Buf()
            if stack is not es:
                stack.callback(P.release, b)
            return t, b

        PS = []
        for i in range(8):
            t = es.enter_context(nc.psum_tensor("ps%d" % i, [128, 512], F32))
            PS.append((t, Buf()))
        psrr = [0]

        def pb():
            i = psrr[0]
            psrr[0] = (i + 1) % 8
            return PS[i]

        def act(out, in_, func, r, w, scale=1.0, bias=0.0, accum=None):
            kw = {}
            if accum is not None:
                kw["accum_out"] = accum
            return P.op(ACT, lambda h: h.activation(out=out, in_=in_, func=func, bias=bias, scale=scale, **kw), r, w)

        def tt(out, in0, in1, op, r, w, eng=None):
            return P.op(eng or DVE, lambda h: h.tensor_tensor(out=out, in0=in0, in1=in1, op=op), r, w)

        def tsc(out, in0, s1, s2, op0, op1, r, w, accum=None, eng=None):
            kw = {}
            if accum is not None:
                kw["accum_out"] = accum
            return P.op(eng or DVE, lambda h: h.tensor_scalar(out=out, in0=in0, scalar1=s1, scalar2=s2, op0=op0, op1=op1, **kw), r, w)

        def stt(out, in0, scalar, in1, op0, op1, r, w):
            return P.op(DVE, lambda h: h.scalar_tensor_tensor(out=out, in0=in0, scalar=scalar, in1=in1, op0=op0, op1=op1), r, w)

        def mm(out, lhsT, rhs, start, stop, r, w):
            return P.op(PE, lambda h: h.matmul(out, lhsT=lhsT, rhs=rhs, start=start, stop=stop), r, w)

        def cp(out, in_, r, w, eng=None):
            e = eng or DVE
            if e is ACT:
                return P.op(ACT, lambda h: h.activation(out=out, in_=in_, func=AF.Copy), r, w)
            return P.op(e, lambda h: h.tensor_copy(out=out, in_=in_), r, w)

        def const(name, src, shape, dt=F32, q=None):
            t, b = sb(es, name, shape, F32)
            P.dma(q or POOL, t[:], src, b, True, grp=cgrp)
            if dt is F32:
                return t, b
            t2, b2 = sb(es, name + "_bf", shape, dt)
            deferred.append(lambda: cp(t2[:], t[:], [b], [b2]))
            return t2, b2
        deferred = []

        identF, identF_b = const("identF", ident_d[:, :], [128, 128])
        mhg, mhg_b = const("mhg", mhg_d[:, 0:128], [128, 128])
        resetm, resetm_b = const("resetm", resetm_d[:, :], [128, 512])
        validm, validm_b = const("validm", validm_d[:, :], [128, 128])
        cmdiag, cmdiag_b = const("cmdiag", cmdiag_d[:, :], [128, 128])
        cms, cms_b = const("cms", cms_d[:, :], [128, 128])
        bones, bones_b = const("bones", bones_d[:, :], [128, 128], BF16)
        m96, m96_b = const("m96", m96_d[:, :], [128, 128])
        pidx, pidx_b = const("pidx", pidx_d[:, :], [128, 1])
        lbl, lbl_b = const("lblc", lbl_d[:, :, :], [128, 2, 6])
        onorm, onorm_b = const("onormc", onorm_d[:, :], [128, 1])
        gq, gq_b = const("gq", bc_qn[:, :], [128, 128])
        gk, gk_b = const("gk", bc_kn[:, :], [128, 128])
        gik, gik_b = const("gik", bc_ikn[:, :], [128, 64])
        mqn, mqn_b = const("mqnc", mqn_d[:, :], [128, 2])
        gmk, gmk_b = const("gmk", bc_mkn[:, :, :], [128, 2, 256])
        cw, cw_b = const("cwc", cw_d[:, :, :, :], [128, 2, NFC, 3])
        cbv, cbv_b = const("cbc", cb_d[:, :, :], [128, 2, NFC])
        for fn_ in deferred:
            fn_()
        identB, identB_b = sb(es, "identB", [128, 128], BF16)
        cp(identB[:], identF[:], [identF_b], [identB_b])
        onesB, onesB_b = sb(es, "onesB", [128, 128], BF16)
        P.op(DVE, lambda h: h.memset(onesB[:], 1.0), [], [onesB_b])
        GN, GN_b = sb(es, "GN", [128, D])
        gmix, gmix_b = bc_nmix, None
        gffn, gffn_b = bc_nffn, None
        ident3, ident3_b = sb(es, "ident3", [128, 3, 128], BF16)
        for j in range(3):
            cp(ident3[:, j, :], identF[:], [identF_b], [ident3_b])
        oml, oml_b = sb(es, "oml", [128, 6])
        tt(oml[:], lbl[:, 1, :], lbl[:, 0, :], ALU.subtract, [lbl_b], [oml_b])
        act(oml[:], oml[:], AF.Sigmoid, [oml_b], [oml_b])
        epsc, epsc_b = sb(es, "epsc", [128, 1])
        P.op(DVE, lambda h: h.memset(epsc[:], EPS), [], [epsc_b])

        NSLOT = 3
        ring = [sb(es, "wr%d" % i, [128, 8, 512], BF16) for i in range(NSLOT)]
        rr = [0]

        wconv = {}

        def convert(name, wap, K_, N_):
            wb_ = nc.dram_tensor(name + "_bf", [K_, N_], BF16, kind="Internal").ap()
            cb_ = Buf()
            cb_.grp = P.newgroup("pool")
            for r0 in range(0, K_, 128):
                if P.stopped:
                    break
                ins = POOL.h.dma_start(out=wb_[r0:r0 + 128, :], in_=wap[r0:r0 + 128, :], max_dma_last_dim=4096)
                cb_.grp.cnt += 16
                ins.then_inc(cb_.grp.sem, 16)
            cb_.w = (cb_.grp.sem, cb_.grp)
            wconv[name] = (wb_, cb_)

        def wblock(wname, r0, kc, c0, ncols):
            wap, cb_ = wconv[wname]
            t, b = ring[rr[0]]
            rr[0] = (rr[0] + 1) % NSLOT
            src = wap[r0:r0 + kc * 128, c0:c0 + ncols].rearrange("(k p) n -> p k n", p=128)
            P.dma(SP, t[:, 0:kc, 0:ncols], src, b, True, other=[cb_])
            return t, b

        convert("w_in0", w_in0, D, HG_COLS)
        for l in range(2):
            convert("w_kv%d" % l, w_kv[l], D, 512)
        convert("w_o0", w_o[0], D, D)
        convert("w_g0", w_g[0], D, DFF)
        convert("w_u0", w_u[0], D, DFF)
        convert("w_d0", w_d[0], DFF, D)
        convert("w_in1", w_in1, D, DSA_COLS)
        convert("w_o1", w_o[1], D, D)
        convert("w_g1", w_g[1], D, DFF)
        convert("w_u1", w_u[1], D, DFF)
        convert("w_d1", w_d[1], DFF, D)

        X, X_b = sb(es, "X", [128, G, D])
        HT, HT_b = sb(es, "HT", [128, 8, G * 128], BF16)
        MIX, MIX_b = sb(es, "MIX", [128, 8, G * 128], BF16)
        HN = [sb(es, "HN%d" % i, [128, D]) for i in range(1)]
        SS, SS_b = sb(es, "SS", [128, 16])
        Sf, Sf_b = [], []
        for h in range(6):
            t, b = sb(es, "Sf%d" % h, [128, 128])
            Sf.append(t); Sf_b.append(b)
        Sb = [[sb(es, "Sb%d_%d" % (h, i), [128, 128], BF16) for i in range(2)] for h in range(6)]
        UH, UH_b = sb(es, "UH", [128, NFC, 2])
        MEMT, MEMT_b = sb(es, "MEMT", [128, 8, 256], BF16)
        hn_i = [0]

        def rmsnorm_T(gt, ntile, gain, gain_b, lyr):
            P.dma(POOL, GN[:], gain[lyr], GN_b, True)
            gain_b = GN_b
            for t in range(ntile):
                hn, hn_b = HN[0]
                act(hn[:], X[:, t, :], AF.Square, [X_b], [hn_b, SS_b], accum=SS[:, 0:1])
                act(SS[:, 1:2], SS[:, 0:1], AF.Ln, [SS_b], [SS_b], scale=1.0 / D, bias=epsc[:, 0:1])
                act(SS[:, 2:3], SS[:, 1:2], AF.Exp, [SS_b], [SS_b], scale=-0.5)
                stt(hn[:], X[:, t, :], SS[:, 2:3], GN[:], ALU.mult, ALU.mult, [X_b, SS_b, gain_b], [hn_b])
                for half in range(2):
                    pt_, pb_ = pb()
                    for j in range(4):
                        kc = half * 4 + j
                        P.op(PE, lambda h: h.transpose(pt_[:, j * 128:(j + 1) * 128], hn[:, kc * 128:(kc + 1) * 128], identF[:]),
                             [hn_b, identF_b], [pb_])
                    act(HT[:, half * 4:half * 4 + 4, t * 128:(t + 1) * 128],
                        pt_[:].rearrange("p (k n) -> p k n", k=4), AF.Copy, [pb_], [HT_b])

        def proj_fm(wap, c0, nch, ntok, consume):
            ci = 0
            while ci < nch:
                nb = min(4, nch - ci)
                wt, wb = wblock(wap, 0, 8, c0 + ci * 128, nb * 128)
                for j in range(nb):
                    pt_, pb_ = pb()
                    for kc in range(8):
                        mm(pt_[:, 0:ntok], wt[:, kc, j * 128:(j + 1) * 128], HT[:, kc, 0:ntok], kc == 0, kc == 7, [wb, HT_b], [pb_])
                    consume(ci + j, pt_, pb_)
                ci += nb

        def proj_tm(wap, c0, ncols, ntile, consume, src=None, src_b=None, kchunks=8, r0=0):
            src = HT if src is None else src
            src_b = HT_b if src_b is None else src_b
            cb_ = 0
            c = 0
            while c < ncols:
                n = min(512, ncols - c)
                wt, wb = wblock(wap, r0, kchunks, c0 + c, n)
                for t in range(ntile):
                    pt_, pb_ = pb()
                    for kc in range(kchunks):
                        mm(pt_[:, 0:n], src[:, kc, t * 128:(t + 1) * 128], wt[:, kc, 0:n], kc == 0, kc == kchunks - 1, [wb, src_b], [pb_])
                    consume(cb_, c, n, t, pt_, pb_)
                c += n
                cb_ += 1

        def mem_attention(st, ntok, slots, l):
            MQ, MQ_b = st["MQ"]
            sq, sq_b = st["mq_sq"]
            rs, rs_b = st["mq_rs"]
            mqb, mqb_b = st["mq_bf"]
            for c in range(2):
                act(sq[:, 0:ntok], MQ[:, c, 0:ntok], AF.Square, [MQ_b], [sq_b])
                pt_, pb_ = pb()
                mm(pt_[:, 0:ntok], bones[:], sq[:, 0:ntok], True, True, [bones_b, sq_b], [pb_])
                act(rs[:, 0:ntok], pt_[:, 0:ntok], AF.Ln, [pb_], [rs_b], scale=1.0 / 64, bias=epsc[:, 0:1])
                act(rs[:, 0:ntok], rs[:, 0:ntok], AF.Exp, [rs_b], [rs_b], scale=-0.5)
                stt(mqb[:, c, 0:ntok], MQ[:, c, 0:ntok], mqn[:, l:l + 1], rs[:, 0:ntok], ALU.mult, ALU.mult, [MQ_b, mqn_b, rs_b], [mqb_b])
            mexps = st["mexp"]
            mei = [0]
            rd, rd_b = st["mrd"]
            for (c0, ncol, MKT, MKT_b, MV, MV_b) in slots:
                for c in range(2):
                    pn, pn_b = pb()
                    pd, pd_b = pb()
                    for j in range(2):
                        hh = 2 * c + j
                        lo, hi = 64 * j, 64 * j + 64
                        for mt in range(2):
                            ps_, ps_b = pb()
                            pe_t, pe_b = mexps[mei[0] % len(mexps)]
                            mei[0] += 1
                            mm(ps_[:, 0:ncol], MKT[lo:hi, c, mt * 128:(mt + 1) * 128], mqb[lo:hi, c, c0:c0 + ncol], True, True,
                               [MKT_b, mqb_b], [ps_b])
                            act(pe_t[:, 0:ncol], ps_[:, 0:ncol], AF.Exp, [ps_b], [pe_b], scale=0.125)
                            mm(pn[lo:hi, 0:ncol], MV[:, mt, hh * 64:(hh + 1) * 64], pe_t[:, 0:ncol], mt == 0, mt == 1, [MV_b, pe_b], [pn_b])
                            mm(pd[lo:hi, 0:ncol], onesB[:, 0:64], pe_t[:, 0:ncol], mt == 0, mt == 1, [onesB_b, pe_b], [pd_b])
                    act(rd[:, 0:ncol], pd[:, 0:ncol], AF.Ln, [pd_b], [rd_b])
                    act(rd[:, 0:ncol], rd[:, 0:ncol], AF.Exp, [rd_b], [rd_b], scale=-1.0)
                    tt(MIX[:, 6 + c, c0:c0 + ncol], pn[:, 0:ncol], rd[:, 0:ncol], ALU.mult, [pn_b, rd_b], [MIX_b])

        def out_proj(l, ntile):
            def consume(cb_, c, n, t, pt_, pb_):
                tt(X[:, t, c:c + n], X[:, t, c:c + n], pt_[:, 0:n], ALU.add, [X_b, pb_], [X_b])
            proj_tm("w_o%d" % l, 0, D, ntile, consume, src=MIX, src_b=MIX_b)

        def ffn(l, ntile, sample, cv_out):
            ntok = ntile * 128
            with ExitStack() as fs:
                AT, AT_b = sb(fs, "AT", [128, NFC, ntok], BF16)
                Uc = [sb(fs, "Uc%d" % i, [128, 2 + ntok]) for i in range(2)]
                Cc = [sb(fs, "Cc%d" % i, [128, ntok]) for i in range(2)]
                Sg = [sb(fs, "Sg%d" % i, [128, ntok]) for i in range(4)]
                UTOK, UTOK_b = sb(fs, "UTOK", [128, DFF])
                if sample:
                    CST, CST_b = sb(fs, "CST", [128, NFC, 128])
                    P.dma(POOL, CST[:], cst_d[l], CST_b, True)
                rmsnorm_T(None, ntile, gffn, gffn_b, l)
                gps = {}

                def cons_g(ci, pt_, pb_):
                    u, u_b = Uc[ci % 2]
                    cp(u[:, 0:2], UH[:, ci, :], [UH_b], [u_b])
                    if sample:
                        tt(u[:, 2:2 + ntok], pt_[:, 0:ntok], validm[:, 0:ntok], ALU.mult, [pb_, validm_b], [u_b])
                        tt(u[:, 2:2 + ntok], u[:, 2:2 + ntok], CST[:, ci, :], ALU.add, [u_b, CST_b], [u_b])
                    else:
                        act(u[:, 2:2 + ntok], pt_[:, 0:ntok], AF.Copy, [pb_], [u_b])
                    cp(UH[:, ci, :], u[:, ntok:ntok + 2], [u_b], [UH_b])
                    c_, c_b = Cc[ci % 2]
                    if sample:
                        tsc(c_[:], u[:, 2:2 + ntok], cw[:, l, ci, 2:3], cbv[:, l, ci:ci + 1], ALU.mult, ALU.add, [u_b, cw_b, cbv_b], [c_b])
                    else:
                        act(c_[:], pt_[:, 0:ntok], AF.Identity, [pb_, cw_b, cbv_b], [c_b], scale=cw[:, l, ci, 2:3], bias=cbv[:, l, ci:ci + 1])
                    stt(c_[:], u[:, 1:1 + ntok], cw[:, l, ci, 1:2], c_[:], ALU.mult, ALU.add, [u_b, cw_b], [c_b])
                    stt(c_[:], u[:, 0:ntok], cw[:, l, ci, 0:1], c_[:], ALU.mult, ALU.add, [u_b, cw_b], [c_b])
                    s_, s_b = Sg[ci % 4]
                    act(s_[:], c_[:], AF.Silu, [c_b], [s_b])
                    if cv_out is not None:
                        pq, pq_b = pb()
                        P.op(PE, lambda h: h.transpose(pq[:, 0:128], u[:, 2 + ntok - 128:2 + ntok], identF[:]), [u_b, identF_b], [pq_b])
                        act(UTOK[:, ci * 128:(ci + 1) * 128], pq[:, 0:128], AF.Copy, [pq_b], [UTOK_b])
                    gps[ci] = (s_, s_b)

                ci = 0
                while ci < NFC:
                    nb = min(4, NFC - ci)
                    wt, wb = wblock("w_g%d" % l, 0, 8, ci * 128, nb * 128)
                    for j in range(nb):
                        pt_, pb_ = pb()
                        for kc in range(8):
                            mm(pt_[:, 0:ntok], wt[:, kc, j * 128:(j + 1) * 128], HT[:, kc, 0:ntok], kc == 0, kc == 7, [wb, HT_b], [pb_])
                        cons_g(ci + j, pt_, pb_)
                    wt, wb = wblock("w_u%d" % l, 0, 8, ci * 128, nb * 128)
                    for j in range(nb):
                        pt_, pb_ = pb()
                        for kc in range(8):
                            mm(pt_[:, 0:ntok], wt[:, kc, j * 128:(j + 1) * 128], HT[:, kc, 0:ntok], kc == 0, kc == 7, [wb, HT_b], [pb_])
                        s_, s_b = gps[ci + j]
                        tt(AT[:, ci + j, :], s_[:], pt_[:, 0:ntok], ALU.mult, [s_b, pb_], [AT_b])
                    ci += nb
                if cv_out is not None:
                    P.dma(POOL, cv_out, UTOK[:], UTOK_b, False, is_out=True)
                for cbk in range(2):
                    accs = [pb() for _ in range(ntile)]
                    for kh, (k0, kn) in enumerate(((0, 8), (8, 8), (16, 6))):
                        wt, wb = wblock("w_d%d" % l, k0 * 128, kn, cbk * 512, 512)
                        for t in range(ntile):
                            pt_, pb_ = accs[t]
                            for k in range(kn):
                                fc = k0 + k
                                mm(pt_[:, :], AT[:, fc, t * 128:(t + 1) * 128], wt[:, k, :], fc == 0, fc == NFC - 1, [wb, AT_b], [pb_])
                    for t in range(ntile):
                        pt_, pb_ = accs[t]
                        tt(X[:, t, cbk * 512:(cbk + 1) * 512], X[:, t, cbk * 512:(cbk + 1) * 512], pt_[:, :], ALU.add, [X_b, pb_], [X_b])
            P.barrier()

        def layer0(ntile, sample, mslots, first, last):
            ntok = ntile * 128
            nch = ntok // 32
            with ExitStack() as ls:
                st = {}
                st["MQ"] = sb(ls, "MQ", [128, 2, ntok])
                st["mq_sq"] = sb(ls, "mq_sq", [128, ntok], BF16)
                st["mq_rs"] = sb(ls, "mq_rs", [128, ntok])
                st["mq_bf"] = sb(ls, "mq_bf", [128, 2, ntok], BF16)
                st["mexp"] = [sb(ls, "mexp%d" % i, [128, ntok], BF16) for i in range(2)]
                st["mrd"] = sb(ls, "mrd", [128, ntok])
                QS, QS_b = sb(ls, "QS", [128, 6, ntok], BF16)
                GS, GS_b = sb(ls, "GS", [128, 6, ntok], BF16)
                VBh, VBh_b = sb(ls, "VBh", [128, ntile, 768], BF16)
                QD, QD_b = sb(ls, "QD", [128, 6, ntok], BF16)
                KD, KD_b = sb(ls, "KD", [128, 6, ntok], BF16)
                EBL, EBL_b = sb(ls, "EBL", [128, 6, nch])
                tmp = [sb(ls, "l0t%d" % i, [128, ntok]) for i in range(4)]
                K2, K2_b = sb(ls, "K2", [128, ntok])
                K2m, K2m_b = sb(ls, "K2m", [128, 128])
                K2T, K2T_b = sb(ls, "K2T", [128, ntile, 768], BF16)
                K2T3, K2T3_b = sb(ls, "K2T3", [128, ntile, 768], BF16)
                SMs = [sb(ls, "SM%d" % i, [128, 128], BF16) for i in range(3)]
                Ohs = [sb(ls, "Oh%d" % i, [128, ntok]) for i in range(3)]
                Osq, Osq_b = sb(ls, "Osq", [128, ntok], BF16)
                Ors, Ors_b = sb(ls, "Ors", [128, ntok])
                MQ, MQ_b = st["MQ"]

                rmsnorm_T(None, ntile, gmix, gmix_b, 0)
                _stage(1.20)

                def cons_q(ci, pt_, pb_):
                    act(QS[:, ci, :], pt_[:, 0:ntok], AF.Silu, [pb_], [QS_b])
                proj_fm("w_in0", 0, 6, ntok, cons_q)
                _stage(1.21)

                def cons_f(h, pt_, pb_):
                    a, a_b = tmp[0]
                    kk, kk_b = tmp[1]
                    lf, lf_b = tmp[2]
                    bb, bb_b = tmp[0]
                    eb, eb_b = tmp[3]
                    en, en_b = tmp[2]
                    act(a[:], pt_[:, 0:ntok], AF.Sigmoid, [pb_], [a_b], scale=-1.0)
                    tsc(kk[:], a[:], oml[:, h:h + 1], None, ALU.mult, ALU.bypass, [a_b, oml_b], [kk_b])
                    _stage(1.211)
                    act(lf[:], kk[:], AF.Ln, [kk_b], [lf_b], scale=-1.0, bias=1.0)
                    if sample:
                        tt(lf[:], lf[:], validm[:, 0:ntok], ALU.mult, [lf_b, validm_b], [lf_b])
                    _stage(1.212)
                    P.op(DVE, lambda hd: hd.tensor_tensor_scan(out=bb[:], data0=resetm[:, 0:ntok], data1=lf[:], initial=0.0,
                                                               op0=ALU.mult, op1=ALU.add), [resetm_b, lf_b], [bb_b])
                    _stage(1.213)
                    act(eb[:], bb[:], AF.Exp, [bb_b], [eb_b])
                    act(en[:], bb[:], AF.Exp, [bb_b], [en_b], scale=-1.0)
                    tt(QD[:, h, :], QS[:, h, :], eb[:], ALU.mult, [QS_b, eb_b], [QD_b])
                    tt(KD[:, h, :], kk[:], en[:], ALU.mult, [kk_b, en_b], [KD_b])
                    _stage(1.214)
                    cp(EBL[:, h, :], eb[:].rearrange("p (c j) -> p c j", j=32)[:, :, 31], [eb_b], [EBL_b])
                    _stage(1.215)
                    tt(K2[:].rearrange("p (c j) -> p c j", j=32), KD[:, h, :].rearrange("p (c j) -> p c j", j=32),
                       EBL[:, h, :].unsqueeze(2).to_broadcast([128, nch, 32]), ALU.mult, [KD_b, EBL_b], [K2_b])
                    _stage(1.216)
                    for t in range(ntile):
                        pq, pq_b = pb()
                        P.op(PE, lambda hd: hd.transpose(pq[:, 0:128], K2[:, t * 128:(t + 1) * 128], identF[:]), [K2_b, identF_b], [pq_b])
                        act(K2T[:, t, h * 128:(h + 1) * 128], pq[:, 0:128], AF.Copy, [pq_b], [K2T_b])
                        _stage(1.217)
                        tt(K2m[:], K2[:, t * 128:(t + 1) * 128], m96[:], ALU.mult, [K2_b, m96_b], [K2m_b])
                        pq2, pq2_b = pb()
                        P.op(PE, lambda hd: hd.transpose(pq2[:, 0:128], K2m[:], identF[:]), [K2m_b, identF_b], [pq2_b])
                        act(K2T3[:, t, h * 128:(h + 1) * 128], pq2[:, 0:128], AF.Copy, [pq2_b], [K2T3_b])
                    _stage(1.2171 + 0.0001 * h)
                proj_fm("w_in0", 768, 6, ntok, cons_f)
                _stage(1.22)

                def cons_g(ci, pt_, pb_):
                    act(GS[:, ci, :], pt_[:, 0:ntok], AF.Silu, [pb_], [GS_b])
                proj_fm("w_in0", 2304, 6, ntok, cons_g)

                def cons_m(ci, pt_, pb_):
                    act(MQ[:, ci, :], pt_[:, 0:ntok], AF.Copy, [pb_], [MQ_b])
                proj_fm("w_in0", 3072, 2, ntok, cons_m)

                def cons_i(cb_, c, n, t, pt_, pb_):
                    act(VBh[:, t, c:c + n], pt_[:, 0:n], AF.Copy, [pb_], [VBh_b])
                proj_tm("w_in0", 1536, 768, ntile, cons_i)
                _stage(1.23)

                for hh in range(2):
                    heads = [3 * hh + i for i in range(3)]
                    par = {h: 0 for h in heads}
                    if first and not sample:
                        for h in heads:
                            P.op(DVE, lambda hd: hd.memset(Sf[h][:], 0.0), [], [Sf_b[h]])
                            P.op(DVE, lambda hd: hd.memset(Sb[h][0][0][:], 0.0), [], [Sb[h][0][1]])
                    k_ps = 0
                    k_pu = 0
                    for t in range(ntile):
                        cols = slice(t * 128, (t + 1) * 128)
                        pos = {}
                        for i, h in enumerate(heads):
                            ps_, ps_b = PS[3 + (k_ps % 2)]
                            k_ps += 1
                            sm, sm_b = SMs[i]
                            mm(ps_[:, 0:128], KD[:, h, cols], QD[:, h, cols], True, True, [KD_b, QD_b], [ps_b])
                            tt(sm[:], ps_[:, 0:128], mhg[:], ALU.mult, [ps_b, mhg_b], [sm_b])
                            po, po_b = PS[i]
                            pos[h] = (po, po_b)
                            mm(po[:, 0:128], VBh[:, t, h * 128:(h + 1) * 128], sm[:], True, False, [VBh_b, sm_b], [po_b])
                        for c in range(4):
                            ccols = slice(t * 128 + c * 32, t * 128 + c * 32 + 32)
                            for i, h in enumerate(heads):
                                po, po_b = pos[h]
                                if sample:
                                    P.dma(POOL, Sf[h][:], st_h[c, h], Sf_b[h], True)
                                    par[h] = 0
                                    cp(Sb[h][0][0][:], Sf[h][:], [Sf_b[h]], [Sb[h][0][1]], eng=ACT)
                                sbt, sbb = Sb[h][par[h]]
                                mm(po[:, c * 32:(c + 1) * 32], sbt[:], QD[:, h, ccols], False, c == 3, [sbb, QD_b], [po_b])
                                pu, pu_b = PS[5 + (k_pu % 3)]
                                k_pu += 1
                                if c < 3:
                                    mm(pu[:, 0:128], K2T[c * 32:(c + 1) * 32, t, h * 128:(h + 1) * 128],
                                       VBh[c * 32:(c + 1) * 32, t, h * 128:(h + 1) * 128], True, True, [K2T_b, VBh_b], [pu_b])
                                else:
                                    mm(pu[:, 0:128], K2T3[64:128, t, h * 128:(h + 1) * 128],
                                       VBh[64:128, t, h * 128:(h + 1) * 128], True, True, [K2T3_b, VBh_b], [pu_b])
                                stt(Sf[h][:], Sf[h][:], EBL[:, h, t * 4 + c:t * 4 + c + 1], pu[:, 0:128], ALU.mult, ALU.add,
                                    [Sf_b[h], EBL_b, pu_b], [Sf_b[h]])
                                par[h] ^= 1
                                cp(Sb[h][par[h]][0][:], Sf[h][:], [Sf_b[h]], [Sb[h][par[h]][1]], eng=ACT)
                                if sample:
                                    P.dma(POOL, hsts_d[c, h], Sf[h][:], Sf_b[h], False, is_out=True)
                        for i, h in enumerate(heads):
                            po, po_b = pos[h]
                            act(Ohs[i][0][:, cols], po[:, 0:128], AF.Copy, [po_b], [Ohs[i][1]])
                    for i, h in enumerate(heads):
                        Oh, Oh_b = Ohs[i]
                        if par[h] == 1:
                            cp(Sb[h][0][0][:], Sf[h][:], [Sf_b[h]], [Sb[h][0][1]], eng=ACT)
                        if last and not sample:
                            P.dma(POOL, hstp_d[h], Sf[h][:], Sf_b[h], False, is_out=True)
                        act(Osq[:], Oh[:], AF.Square, [Oh_b], [Osq_b])
                        pn_, pn_b = pb()
                        mm(pn_[:, 0:ntok], onesB[:], Osq[:], True, True, [onesB_b, Osq_b], [pn_b])
                        act(Ors[:], pn_[:, 0:ntok], AF.Ln, [pn_b], [Ors_b], scale=1.0 / 128, bias=epsc[:, 0:1])
                        act(Ors[:], Ors[:], AF.Exp, [Ors_b], [Ors_b], scale=-0.5)
                        tt(Oh[:], Oh[:], Ors[:], ALU.mult, [Oh_b, Ors_b], [Oh_b])
                        stt(MIX[:, h, 0:ntok], Oh[:], onorm[:, 0:1], GS[:, h, :], ALU.mult, ALU.mult, [Oh_b, onorm_b, GS_b], [MIX_b])

                _stage(1.24)
                mem_attention(st, ntok, mslots, 0)
            P.barrier()

        def memkv_prompt(l, MKT, MKT_b, MV, MV_b):
            with ExitStack() as ms:
                KVt, KVt_b = sb(ms, "KVt", [128, 512])
                KN, KN_b = sb(ms, "KN", [128, 256])
                sq, sq_b = sb(ms, "kvsq", [128, 64])
                ss4, ss4_b = sb(ms, "ss4", [128, 8])

                def consume(cb_, c, n, t, pt_, pb_):
                    act(KVt[:], pt_[:, :], AF.Copy, [pb_], [KVt_b])
                    for hh in range(4):
                        act(sq[:], KVt[:, hh * 64:(hh + 1) * 64], AF.Square, [KVt_b], [sq_b, ss4_b], accum=ss4[:, hh:hh + 1])
                    act(ss4[:, 4:8], ss4[:, 0:4], AF.Ln, [ss4_b], [ss4_b], scale=1.0 / 64, bias=epsc[:, 0:1])
                    act(ss4[:, 4:8], ss4[:, 4:8], AF.Exp, [ss4_b], [ss4_b], scale=-0.5)
                    for hh in range(4):
                        stt(KN[:, hh * 64:(hh + 1) * 64], KVt[:, hh * 64:(hh + 1) * 64], ss4[:, 4 + hh:5 + hh], gmk[:, l, hh * 64:(hh + 1) * 64],
                            ALU.mult, ALU.mult, [KVt_b, ss4_b, gmk_b], [KN_b])
                    P.dma(POOL, mko_d[l, t * 128:(t + 1) * 128, :], KN[:], KN_b, False, is_out=True)
                    P.dma(POOL, mvo_d[l, t * 128:(t + 1) * 128, :], KVt[:, 256:512], KVt_b, False, is_out=True)
                    cp(MV[:, t, :], KVt[:, 256:512], [KVt_b], [MV_b])
                    for c2 in range(2):
                        pq, pq_b = pb()
                        P.op(PE, lambda h: h.transpose(pq[:, 0:128], KN[:, c2 * 128:(c2 + 1) * 128], identF[:]), [KN_b, identF_b], [pq_b])
                        act(MKT[:, c2, t * 128:(t + 1) * 128], pq[:, 0:128], AF.Copy, [pq_b], [MKT_b])
                proj_tm("w_kv%d" % l, 0, 512, 2, consume, src=MEMT, src_b=MEMT_b)
            P.barrier()

        def dsa_inproj(ntile, tile0, st, sample, KT, KT_b, VB, VB_b, IKT, IKT_b):
            ntok = ntile * 128
            Z, Z_b = st["Z"]
            QN, QN_b = st["QN"]
            QR, QR_b = st["QR"]
            TB, TB_b = st["TB"]
            T1, T1_b = st["T1"]
            T2, T2_b = st["T2"]
            QT, QT_b = st["QT"]
            IQT, IQT_b = st["IQT"]
            IW, IW_b = st["IW"]
            MQ, MQ_b = st["MQ"]
            ss, ss_b = st["ss16"]
            sq, sq_b = st["sqj"]
            rmsnorm_T(None, ntile, gmix, gmix_b, 1)

            def cons_m(ci, pt_, pb_):
                act(MQ[:, ci, 0:ntok], pt_[:, 0:ntok], AF.Copy, [pb_], [MQ_b])
            proj_fm("w_in1", 1864, 2, ntok, cons_m)

            blocks = [(0, 512), (512, 512), (1024, 512), (1536, 328)]
            wts = []
            Zs = st["Zs"]

            def consume(cb_, c, n, t, pt_, pb_):
                z, z_b = Zs[t]
                act(z[:, c:c + n], pt_[:, 0:n], AF.Copy, [pb_], [z_b])
            proj_tm("w_in1", 0, 1864, ntile, consume)

            for t in range(ntile):
                z, z_b = Zs[t]
                gt = tile0 + t
                trow = (NT * 128) if sample else gt * 128
                P.dma(POOL, TB[:], tab_d[trow:trow + 128, :], TB_b, True)
                for j in range(8):
                    act(sq[:, 0:128], z[:, j * 128:(j + 1) * 128], AF.Square, [z_b], [sq_b, ss_b], accum=ss[:, j:j + 1])
                act(sq[:, 0:64], z[:, 1792:1856], AF.Square, [z_b], [sq_b, ss_b], accum=ss[:, 8:9])
                act(ss[:, 0:8], ss[:, 0:8], AF.Ln, [ss_b], [ss_b], scale=1.0 / 128, bias=epsc[:, 0:1])
                act(ss[:, 8:9], ss[:, 8:9], AF.Ln, [ss_b], [ss_b], scale=1.0 / 64, bias=epsc[:, 0:1])
                act(ss[:, 0:9], ss[:, 0:9], AF.Exp, [ss_b], [ss_b], scale=-0.5)
                for j in range(6):
                    stt(QN[:, j * 128:(j + 1) * 128], z[:, j * 128:(j + 1) * 128], ss[:, j:j + 1], gq[:], ALU.mult, ALU.mult, [z_b, ss_b, gq_b], [QN_b])
                for j in range(2):
                    stt(QN[:, 768 + j * 128:768 + (j + 1) * 128], z[:, 768 + j * 128:768 + (j + 1) * 128], ss[:, 6 + j:7 + j], gk[:],
                        ALU.mult, ALU.mult, [z_b, ss_b, gk_b], [QN_b])
                cp(QN[:, 1024:1536], z[:, 1280:1792], [z_b], [QN_b], eng=POOL)
                stt(QN[:, 1536:1600], z[:, 1792:1856], ss[:, 8:9], gik[:], ALU.mult, ALU.mult, [z_b, ss_b, gik_b], [QN_b])
                cp(IW[:, t, :], z[:, 1856:1864], [z_b], [IW_b], eng=POOL)

                def rope(c0, nh, half, cc, sc):
                    w_ = nh * half
                    xv = QN[:, c0:c0 + 2 * w_].rearrange("p (h two d) -> p h two d", h=nh, two=2)
                    ov = QR[:, c0:c0 + 2 * w_].rearrange("p (h two d) -> p h two d", h=nh, two=2)
                    cosv = TB[:, cc:cc + w_].rearrange("p (h d) -> p h d", h=nh)
                    sinv = TB[:, sc:sc + w_].rearrange("p (h d) -> p h d", h=nh)
                    t1 = T1[:, 0:w_].rearrange("p (h d) -> p h d", h=nh)
                    t2 = T2[:, 0:w_].rearrange("p (h d) -> p h d", h=nh)
                    x1, x2 = xv[:, :, 0, :], xv[:, :, 1, :]
                    tt(t1, x1, cosv, ALU.mult, [QN_b, TB_b], [T1_b])
                    tt(t2, x2, sinv, ALU.mult, [QN_b, TB_b], [T2_b], eng=POOL)
                    tt(ov[:, :, 0, :], t1, t2, ALU.subtract, [T1_b, T2_b], [QR_b])
                    tt(t1, x2, cosv, ALU.mult, [QN_b, TB_b], [T1_b])
                    tt(t2, x1, sinv, ALU.mult, [QN_b, TB_b], [T2_b], eng=POOL)
                    tt(ov[:, :, 1, :], t1, t2, ALU.add, [T1_b, T2_b], [QR_b])
                rope(0, 6, 64, 0, 384)
                rope(768, 2, 64, 0, 384)
                rope(1024, 8, 32, 768, 1024)
                rope(1536, 1, 32, 768, 1024)
                if sample:
                    P.dma(POOL, kso_d[:, :], QR[:, 768:1024], QR_b, False, is_out=True)
                    P.dma(POOL, vso_d[:, :], z[:, 1024:1280], z_b, False, is_out=True)
                    P.dma(POOL, ikso_d[:, :], QR[:, 1536:1600], QR_b, False, is_out=True)
                else:
                    P.dma(POOL, ko_d[gt * 128:(gt + 1) * 128, :], QR[:, 768:1024], QR_b, False, is_out=True)
                    P.dma(POOL, vo_d[gt * 128:(gt + 1) * 128, :], z[:, 1024:1280], z_b, False, is_out=True)
                    P.dma(POOL, iko_d[gt * 128:(gt + 1) * 128, :], QR[:, 1536:1600], QR_b, False, is_out=True)
                kcol = 0 if sample else gt * 128
                cp(VB[:, 0 if sample else gt, :], z[:, 1024:1280], [z_b], [VB_b[0 if sample else gt]], eng=ACT)
                for half_ in range(2):
                    pt_, pb_ = pb()
                    nh = 4 if half_ == 0 else 2
                    for j in range(nh):
                        hq = half_ * 4 + j
                        P.op(PE, lambda h: h.transpose(pt_[:, j * 128:(j + 1) * 128], QR[:, hq * 128:(hq + 1) * 128], identF[:]), [QR_b, identF_b], [pb_])
                    act(QT[:, half_ * 4:half_ * 4 + nh, t * 128:(t + 1) * 128], pt_[:, 0:nh * 128].rearrange("p (k n) -> p k n", k=nh),
                        AF.Copy, [pb_], [QT_b])
                pt_, pb_ = pb()
                for j in range(2):
                    P.op(PE, lambda h: h.transpose(pt_[:, j * 128:(j + 1) * 128], QR[:, 768 + j * 128:768 + (j + 1) * 128], identF[:]), [QR_b, identF_b], [pb_])
                P.op(PE, lambda h: h.transpose(pt_[0:64, 256:384], QR[:, 1536:1600], identF[:]), [QR_b, identF_b], [pb_])
                act(KT[:, :, kcol:kcol + 128], pt_[:, 0:256].rearrange("p (k n) -> p k n", k=2), AF.Copy, [pb_], [KT_b[0 if sample else gt]])
                act(IKT[0:64, kcol:kcol + 128], pt_[0:64, 256:384], AF.Copy, [pb_], [IKT_b[0 if sample else gt]])
                for half_ in range(2):
                    pt_, pb_ = pb()
                    for j in range(4):
                        hq = half_ * 4 + j
                        P.op(PE, lambda h: h.transpose(pt_[0:64, j * 128:(j + 1) * 128], QR[:, 1024 + hq * 64:1024 + (hq + 1) * 64], identF[:]),
                             [QR_b, identF_b], [pb_])
                    act(IQT[0:64, half_ * 4:half_ * 4 + 4, t * 128:(t + 1) * 128], pt_[0:64, :].rearrange("p (k n) -> p k n", k=4),
                        AF.Copy, [pb_], [IQT_b])

        def index_scores(st, t, keysets, SC, SC_b, row_lo=0, row_hi=128):
            IQT, IQT_b = st["IQT"]
            IW, IW_b = st["IW"]
            R = st["R"]
            ri = 0
            for (rhsf, kb_, ncols, sc0) in keysets:
                k0 = 0
                while k0 < ncols:
                    n = min(512, ncols - k0)
                    for h in range(8):
                        pt_, pb_ = pb()
                        mm(pt_[:, 0:n], IQT[0:64, h, t * 128:(t + 1) * 128], rhsf(k0, n), True, True, [IQT_b, kb_], [pb_])
                        r_, r_b = R[ri % 4]
                        ri += 1
                        act(r_[row_lo:row_hi, 0:n], pt_[row_lo:row_hi, 0:n], AF.Relu, [pb_], [r_b])
                        dst = SC[row_lo:row_hi, sc0 + k0:sc0 + k0 + n]
                        if h == 0:
                            tsc(dst, r_[row_lo:row_hi, 0:n], IW[row_lo:row_hi, t, 0:1], None, ALU.mult, ALU.bypass, [r_b, IW_b], [SC_b])
                        else:
                            stt(dst, r_[row_lo:row_hi, 0:n], IW[row_lo:row_hi, t, h:h + 1], dst, ALU.mult, ALU.add, [r_b, IW_b], [SC_b])
                    k0 += n

        def select_mask(st, L, SC, SC_b, MK, MK_b, cm, cm_b, cmcol, nsel):
            bs, bs_b = st["bs"]
            P.op(DVE, lambda h: h.tensor_reduce(out=bs[:, 0:1], in_=SC[:, 0:L], axis=AX.X, op=ALU.min), [SC_b], [bs_b])
            tt(SC[:, cmcol:cmcol + 128], SC[:, cmcol:cmcol + 128], cm[:], ALU.add, [SC_b, cm_b], [SC_b])
            P.op(DVE, lambda h: h.tensor_reduce(out=bs[:, 1:2], in_=SC[:, 0:L], axis=AX.X, op=ALU.max), [SC_b], [bs_b])
            tt(bs[:, 1:2], bs[:, 1:2], bs[:, 0:1], ALU.subtract, [bs_b], [bs_b])
            for i in range(NBIS):
                stt(bs[:, 2:3], bs[:, 1:2], float(2.0 ** -(i + 1)), bs[:, 0:1], ALU.mult, ALU.add, [bs_b], [bs_b])
                tsc(MK[:, 0:L], SC[:, 0:L], bs[:, 2:3], None, ALU.is_ge, ALU.add, [SC_b, bs_b], [MK_b, bs_b], accum=bs[:, 3:4])
                tsc(bs[:, 4:5], bs[:, 3:4], float(nsel) - 0.5, float(2.0 ** -(i + 1)), ALU.is_ge, ALU.mult, [bs_b], [bs_b])
                stt(bs[:, 0:1], bs[:, 1:2], bs[:, 4:5], bs[:, 0:1], ALU.mult, ALU.add, [bs_b], [bs_b])
            tsc(MK[:, 0:L], SC[:, 0:L], bs[:, 0:1], None, ALU.is_ge, ALU.bypass, [SC_b, bs_b], [MK_b])

        def attention(st, qcols, nq, sel, sel_b, keytiles, MK, MK_b, mixcols, page_loader=None):
            QT, QT_b = st["QT"]
            E = st["E"]
            PP = st["PP"]
            RD, RD_b = st["RD"]
            n3 = 3 * nq
            accO = [PS[0], PS[1]]
            accD = [PS[2], PS[3]]
            nk = len(keytiles)
            ei = 0
            pend = []

            def flush(pl):
                for (g, vf_g, vb_, p_, p_b, kidx) in pl:
                    mm(accO[g][0][:, 0:n3], vf_g, p_[:, 0:n3], kidx == 0, kidx == nk - 1, [vb_, p_b], [accO[g][1]])
                    mm(accD[g][0][:, 0:n3], onesB[:], p_[:, 0:n3], kidx == 0, kidx == nk - 1, [onesB_b, p_b], [accD[g][1]])
            for ki, ent in enumerate(keytiles):
                if ent[0] == "page":
                    mc0 = ent[2]
                    ktf, kb_, vf, vb_ = page_loader(ki, ent[1])
                else:
                    ktf, kb_, vf, vb_, mc0 = ent
                pm, pm_b = PS[6 + (ki % 2)]
                mm(pm[:, 0:n3], MK[:, mc0:mc0 + 128], sel, True, True, [MK_b, sel_b], [pm_b])
                pms, pms_b = st["PMS"][ki % 2]
                act(pms[:, 0:n3], pm[:, 0:n3], AF.Copy, [pm_b], [pms_b])
                cur = []
                for g in range(2):
                    ps_, ps_b = PS[4 + g]
                    mm(ps_[:, 0:n3].rearrange("p (h n) -> p h n", h=3), ktf(g), QT[:, 3 * g:3 * g + 3, qcols], True, True, [kb_, QT_b], [ps_b])
                    e_, e_b = E[ei % 4]
                    p_, p_b = PP[ei % 4]
                    ei += 1
                    act(e_[:, 0:n3], ps_[:, 0:n3], AF.Exp, [ps_b], [e_b], scale=float(128 ** -0.5))
                    tt(p_[:, 0:n3], e_[:, 0:n3], pms[:, 0:n3], ALU.mult, [e_b, pms_b], [p_b], eng=POOL)
                    cur.append((g, vf(g), vb_, p_, p_b, ki))
                flush(pend)
                pend = cur
            flush(pend)
            for g in range(2):
                act(RD[:, 0:n3], accD[g][0][:, 0:n3], AF.Ln, [accD[g][1]], [RD_b])
                act(RD[:, 0:n3], RD[:, 0:n3], AF.Exp, [RD_b], [RD_b], scale=-1.0)
                oe, oe_b = st["OE"]
                act(oe[:, 0:n3], accO[g][0][:, 0:n3], AF.Copy, [accO[g][1]], [oe_b])
                tt(MIX[:, 3 * g:3 * g + 3, mixcols], oe[:, 0:n3].rearrange("p (h n) -> p h n", h=3),
                   RD[:, 0:n3].rearrange("p (h n) -> p h n", h=3), ALU.mult, [oe_b, RD_b], [MIX_b], eng=POOL)

        def l1_common(ls, ntile):
            ntok = ntile * 128
            st = {}
            st["MQ"] = sb(ls, "MQ1", [128, 2, ntok])
            st["mq_sq"] = sb(ls, "mq_sq1", [128, ntok], BF16)
            st["mq_rs"] = sb(ls, "mq_rs1", [128, ntok])
            st["mq_bf"] = sb(ls, "mq_bf1", [128, 2, ntok], BF16)
            st["mexp"] = [sb(ls, "mexp1%d" % i, [128, ntok], BF16) for i in range(3)]
            st["mrd"] = sb(ls, "mrd1", [128, ntok])
            st["QT"] = sb(ls, "QT", [128, 6, ntok], BF16)
            st["IQT"] = sb(ls, "IQT", [64, 8, ntok], BF16)
            st["IW"] = sb(ls, "IW", [128, ntile, 8])
            return st

        def l1_inproj(ls, ntile, st):
            st["Zs"] = [sb(ls, "Z%d" % i, [128, 1864]) for i in range(ntile)]
            st["Z"] = st["Zs"][0]
            st["QN"] = sb(ls, "QN", [128, 1600])
            st["QR"] = sb(ls, "QR", [128, 1600])
            st["TB"] = sb(ls, "TB", [128, 1280])
            st["T1"] = sb(ls, "T1", [128, 384])
            st["T2"] = sb(ls, "T2", [128, 384])
            st["ss16"] = sb(ls, "ss16", [128, 16])
            st["sqj"] = st["T1"]

        def l1_attn(ls, st, sccols):
            st["R"] = [sb(ls, "R%d" % i, [128, 512], BF16) for i in range(4)]
            st["bs"] = sb(ls, "bs", [128, 8])
            st["E"] = [sb(ls, "E%d" % i, [128, 384], BF16) for i in range(4)]
            st["PP"] = [sb(ls, "PP%d" % i, [128, 384], BF16) for i in range(4)]
            st["RD"] = sb(ls, "RD", [128, 384])
            st["SC"] = sb(ls, "SC", [128, sccols])
            st["MK"] = sb(ls, "MK", [128, sccols], BF16)
            st["PMS"] = [sb(ls, "PMS%d" % i, [128, 384], BF16) for i in range(2)]
            st["OE"] = sb(ls, "OE", [128, 384])

        try:
            with ExitStack() as ms0:
                memf = [sb(ms0, "memf%d" % i, [128, D]) for i in range(2)]
                for mt in range(2):
                    P.dma(POOL, memf[mt][0][:], mem_d[mt * 128:(mt + 1) * 128, :], memf[mt][1], True)
                    for half in range(2):
                        pt_, pb_ = pb()
                        for j in range(4):
                            kc = half * 4 + j
                            P.op(PE, lambda h: h.transpose(pt_[:, j * 128:(j + 1) * 128], memf[mt][0][:, kc * 128:(kc + 1) * 128], identF[:]),
                                 [memf[mt][1], identF_b], [pb_])
                        act(MEMT[:, half * 4:half * 4 + 4, mt * 128:(mt + 1) * 128], pt_[:].rearrange("p (k n) -> p k n", k=4), AF.Copy, [pb_], [MEMT_b])
            P.barrier()
            MKTp = [sb(es, "MKTp%d" % l, [128, 2, 256], BF16) for l in range(2)]
            MVp = [sb(es, "MVp%d" % l, [128, 2, 256], BF16) for l in range(2)]
            for l in range(2):
                memkv_prompt(l, MKTp[l][0], MKTp[l][1], MVp[l][0], MVp[l][1])
            _stage(1)

            def sample_group():
                P.dma(POOL, X[:, 0, :], xs_d[:, :], X_b, True)
                P.op(DVE, lambda h: h.memset(UH[:], 0.0), [], [UH_b])
                for l in range(2):
                    with ExitStack() as ss_:
                        MKs = [sb(ss_, "MKs%d" % b, [128, 2, 256], BF16) for b in range(4)]
                        MVs = [sb(ss_, "MVs%d" % b, [128, 2, 256], BF16) for b in range(4)]
                        slots = []
                        for b in range(4):
                            P.dma(POOL, MKs[b][0][:], mkT_s[l, b], MKs[b][1], True)
                            P.dma(POOL, MVs[b][0][:], mv_s[l, b].rearrange("(t p) d -> p t d", p=128), MVs[b][1], True)
                            slots.append((32 * b, 32, MKs[b][0], MKs[b][1], MVs[b][0], MVs[b][1]))
                        if l == 0:
                            layer0(1, True, slots, True, True)
                            _stage(2)
                        else:
                            with ExitStack() as ls:
                                st = l1_common(ls, 1)
                                KTs, KTs_b = sb(ls, "KTs", [128, 2, 128], BF16)
                                VBs, VBs_b = sb(ls, "VBs", [128, 1, 256], BF16)
                                IKTs, IKTs_b = sb(ls, "IKTs", [64, 128], BF16)
                                with ExitStack() as zs:
                                    l1_inproj(zs, 1, st)
                                    dsa_inproj(1, 0, st, True, KTs, [KTs_b], VBs, [VBs_b], IKTs, [IKTs_b])
                                    _stage(5)
                                P.barrier()
                                with ExitStack() as as_:
                                    l1_attn(as_, st, PAST + 128)
                                    SC, SC_b = st["SC"]
                                    MK, MK_b = st["MK"]
                                    IKP, IKP_b = sb(as_, "IKP", [64, PAST], BF16)
                                    PTs, PTs_b = sb(as_, "PTs", [128, 4 * NPAGE], I32)
                                    IDX = sb(as_, "IDX", [128, 4, 4 * NPAGE], I32)
                                    sel3 = sb(as_, "sel3", [128, 4, 3, 32], BF16)
                                    for b in range(4):
                                        for j in range(3):
                                            cp(sel3[0][:, b, j, :], identF[:, 32 * b:32 * b + 32], [identF_b], [sel3[1]])
                                    KPr = [sb(as_, "KPr%d" % i, [128, 2, 128], BF16) for i in range(4)]
                                    VPr = [sb(as_, "VPr%d" % i, [128, 256], BF16) for i in range(4)]
                                    P.dma(POOL, PTs[:], pt_d.partition_broadcast(128), PTs_b, True)
                                    tsc(IDX[0][:, 0, :], PTs[:], 64.0, pidx[:, 0:1], ALU.mult, ALU.add, [PTs_b, pidx_b], [IDX[1]])
                                    tsc(IDX[0][:, 1, :], PTs[:], 256.0, pidx[:, 0:1], ALU.mult, ALU.add, [PTs_b, pidx_b], [IDX[1]])
                                    tsc(IDX[0][:, 2, :], IDX[0][:, 1, :], 128.0, None, ALU.add, ALU.bypass, [IDX[1]], [IDX[1]])
                                    tsc(IDX[0][:, 3, :], PTs[:], 128.0, pidx[:, 0:1], ALU.mult, ALU.add, [PTs_b, pidx_b], [IDX[1]])
                                    for b in (3, 2, 1, 0):
                                        rows = (64, 128) if b == 3 else (32 * b, 32 * b + 32)
                                        for pg in range(NPAGE):
                                            pi = b * NPAGE + pg
                                            P.idma(IKP[0:64, pg * 128:(pg + 1) * 128], pool_ik[:, :], IDX[0][0:64, 0, pi:pi + 1], IKP_b, [IDX[1]])
                                        keysets = [(lambda k0, n: IKP[0:64, k0:k0 + n], IKP_b, PAST, 0),
                                                   (lambda k0, n: IKTs[0:64, k0:k0 + n], IKTs_b, 128, PAST)]
                                        index_scores(st, 0, keysets, SC, SC_b, rows[0], rows[1])
                                    select_mask(st, PAST + 128, SC, SC_b, MK, MK_b, cms, cms_b, PAST, NSEL_S)
                                    _stage(6)
                                    for b in range(4):
                                        kts = []
                                        for pg in range(NPAGE):
                                            kts.append(("page", b * NPAGE + pg, pg * 128))
                                        kts.append(((lambda g: KTs[:, g, :]), KTs_b, (lambda g: VBs[:, 0, g * 128:(g + 1) * 128]), VBs_b, PAST))

                                        def page_loader(i, pi, KPr=KPr, VPr=VPr):
                                            kp, kp_b = KPr[i % 4]
                                            vp, vp_b = VPr[i % 4]
                                            P.idma(kp[:].rearrange("p g r -> p (g r)"), pool_k[:, :], IDX[0][:, 3, pi:pi + 1], kp_b, [IDX[1]])
                                            P.idma(vp[:], pool_v[:, :], IDX[0][:, 3, pi:pi + 1], vp_b, [IDX[1]])
                                            return ((lambda g: kp[:, g, :]), kp_b, (lambda g: vp[:, g * 128:(g + 1) * 128]), vp_b)
                                        attention(st, slice(32 * b, 32 * b + 32), 32, sel3[0][:, b, :, :].rearrange("p j n -> p (j n)"), sel3[1],
                                                  kts, MK, MK_b, slice(32 * b, 32 * b + 32), page_loader)
                                    mem_attention(st, 128, slots, 1)
                            P.barrier()
                    out_proj(l, 1)
                    _stage(3 + 4 * l)
                    ffn(l, 1, True, cvs_d[l])
                    _stage(4 + 4 * l)
                P.dma(POOL, ys_d[:, :], X[:, 0, :], X_b, False, is_out=True)

            sample_group()
            P.barrier()
            _stage(9)

            KT, _ = sb(es, "KT", [128, 2, T], BF16)
            VB, _ = sb(es, "VB", [128, NT, 256], BF16)
            IKT, _ = sb(es, "IKT", [64, T], BF16)
            KT_b = [Buf() for _ in range(NT)]
            VB_b = [Buf() for _ in range(NT)]
            IKT_b = [Buf() for _ in range(NT)]
            UHs = [sb(es, "UHs%d" % l, [128, NFC, 2]) for l in range(2)]
            for l in range(2):
                P.op(DVE, lambda h: h.memset(UHs[l][0][:], 0.0), [], [UHs[l][1]])

            for g in range(NG):
                first, last = (g == 0), (g == NG - 1)
                for t in range(G):
                    P.dma(POOL, X[:, t, :], x_d[(g * G + t) * 128:(g * G + t + 1) * 128, :], X_b, True)
                for l in range(2):
                    slots = [(0, G * 128, MKTp[l][0], MKTp[l][1], MVp[l][0], MVp[l][1])]
                    if l == 0:
                        layer0(G, False, slots, first, last)
                    else:
                        with ExitStack() as ls:
                            st = l1_common(ls, G)
                            with ExitStack() as zs:
                                l1_inproj(zs, G, st)
                                dsa_inproj(G, g * G, st, False, KT, KT_b, VB, VB_b, IKT, IKT_b)
                            P.barrier()
                            with ExitStack() as as_:
                                l1_attn(as_, st, T)
                                SC, SC_b = st["SC"]
                                MKs = [st["MK"], sb(as_, "MK2", [128, T], BF16)]

                                def emit_attention(t):
                                    gt = g * G + t
                                    MK, MK_b = MKs[t % 2]
                                    kts = []
                                    for kt in range(gt + 1):
                                        kts.append(((lambda gg, kt=kt: KT[:, gg, kt * 128:(kt + 1) * 128]), KT_b[kt],
                                                    (lambda gg, kt=kt: VB[:, kt, gg * 128:(gg + 1) * 128]), VB_b[kt], kt * 128))
                                    attention(st, slice(t * 128, (t + 1) * 128), 128, ident3[:].rearrange("p j n -> p (j n)"), ident3_b,
                                              kts, MK, MK_b, slice(t * 128, (t + 1) * 128))
                                for t in range(G):
                                    gt = g * G + t
                                    L = (gt + 1) * 128
                                    MK, MK_b = MKs[t % 2]
                                    merged = []
                                    kt = 0
                                    while kt <= gt:
                                        nkt = min(4, gt + 1 - kt)
                                        merged.append(((lambda k0, n, kt=kt: IKT[0:64, kt * 128 + k0:kt * 128 + k0 + n]), IKT_b[kt + nkt - 1], nkt * 128, kt * 128))
                                        for q in range(kt, kt + nkt - 1):
                                            P._deps(PE, [IKT_b[q]], [])
                                        kt += nkt
                                    index_scores(st, t, merged, SC, SC_b)
                                    if t > 0:
                                        emit_attention(t - 1)
                                    select_mask(st, L, SC, SC_b, MK, MK_b, cmdiag, cmdiag_b, gt * 128, NSEL)
                                emit_attention(G - 1)
                                mem_attention(st, G * 128, slots, 1)
                        P.barrier()
                    if DEBUG and g == 0 and l == 0:
                        with ExitStack() as ds:
                            dm, dm_b = sb(ds, "dbgm", [128, 8, G * 128])
                            cp(dm[:], MIX[:], [MIX_b], [dm_b])
                            P.dma(POOL, dbg_mix[:, :, :], dm[:], dm_b, False, is_out=True)
                            P.barrier()
                    out_proj(l, G)
                    if DEBUG and g == 0 and l == 0:
                        P.dma(POOL, dbg_x1[:, :, :], X[:], X_b, False, is_out=True)
                    cp(UH[:], UHs[l][0][:], [UHs[l][1]], [UH_b])
                    ffn(l, G, False, cvp_d[l] if last else None)
                    cp(UHs[l][0][:], UH[:], [UH_b], [UHs[l][1]])
                    if DEBUG and g == 0 and l == 0:
                        P.dma(POOL, dbg_x2[:, :, :], X[:], X_b, False, is_out=True)
                for t in range(G):
                    P.dma(POOL, y_d[(g * G + t) * 128:(g * G + t + 1) * 128, :], X[:, t, :], X_b, False, is_out=True)
        except _Stop:
            P.barrier()
        P.finish()
    return nc


def _rope_tab(pos, half):
    inv = 10000.0 ** (-np.arange(half, dtype=np.float32) / half)
    ang = pos.astype(np.float32)[:, None] * inv[None, :]
    return np.cos(ang).astype(np.float32), np.sin(ang).astype(np.float32)


def _consts(NT, PASTLEN):
    c = {}
    c["ident"] = np.eye(128, dtype=np.float32)
    s = np.arange(128)[:, None]
    t = np.arange(128)[None, :]
    m = ((s // 32 == t // 32) & (s <= t)).astype(np.float32)
    c["mhg"] = np.tile(m, (1, 3))
    r = np.ones((128, 512), np.float32)
    r[:, ::32] = 0.0
    c["resetm"] = r
    v = np.zeros((128, 128), np.float32)
    for b in range(4):
        v[:, 32 * b + 2:32 * b + 6] = 1.0
    c["validm"] = v
    c["cmdiag"] = np.where(t <= s, 0.0, NEG).astype(np.float32)
    cm = np.full((128, 128), NEG, np.float32)
    for b in range(4):
        for tq in range(4):
            cm[32 * b + 2 + tq, 32 * b + 2:32 * b + 3 + tq] = 0.0
    c["cms"] = cm
    bo = np.zeros((128, 128), np.float32)
    bo[:64, :64] = 1.0
    bo[64:, 64:] = 1.0
    c["bones"] = bo
    m9 = np.zeros((128, 128), np.float32)
    m9[:, 96:] = 1.0
    c["m96"] = m9
    c["pidx"] = np.arange(128, dtype=np.float32).reshape(128, 1)
    T = NT * 128
    pos = np.zeros(T + 128, np.float32)
    pos[:T] = np.arange(T)
    for b in range(4):
        for tq in range(4):
            pos[T + 32 * b + 2 + tq] = PASTLEN + tq
    cq, sq = _rope_tab(pos, 64)
    ci, si = _rope_tab(pos, 32)
    c["tab"] = np.concatenate([np.tile(cq, (1, 6)), np.tile(sq, (1, 6)), np.tile(ci, (1, 8)), np.tile(si, (1, 8))], axis=1).astype(np.float32)
    return c


def _fm(v, nch):
    return np.ascontiguousarray(v.reshape(nch, 128).T)


def make_in_maps(inp, NT, NPAGE, NPOOL, ncores=8):
    f = lambda a: np.ascontiguousarray(np.asarray(a, dtype=np.float32))
    PASTLEN = NPAGE * 128
    cst = _consts(NT, PASTLEN)
    T = NT * 128
    shared = {}
    shared["w_in0"] = f(inp["w_in_hgrn"][0])
    shared["w_in1"] = f(inp["w_in_dsa"][0])
    shared["w_kv"] = f(inp["w_mem_kv"])
    shared["w_o"] = f(inp["w_out"])
    shared["w_g"] = f(inp["w_ffn_gate"])
    shared["w_u"] = f(inp["w_ffn_up"])
    shared["w_d"] = f(inp["w_ffn_down"])
    bc = lambda a: f(np.broadcast_to(np.asarray(a)[:, None, :], (a.shape[0], 128, a.shape[1])))
    shared["bc_nmix"] = bc(np.asarray(inp["norm_mix"]))
    shared["bc_nffn"] = bc(np.asarray(inp["norm_ffn"]))
    lb = np.asarray(inp["hgrn_lb_logits"], np.float32)
    shared["lbl"] = f(np.stack([_fm(lb[0], 6), _fm(lb[1], 6)], axis=1))
    shared["onorm"] = f(np.asarray(inp["hgrn_out_norm"])[0].reshape(128, 1))
    shared["bc_qn"] = f(np.broadcast_to(np.asarray(inp["dsa_q_norm"])[0][None, :], (128, 128)))
    shared["bc_kn"] = f(np.broadcast_to(np.asarray(inp["dsa_k_norm"])[0][None, :], (128, 128)))
    shared["bc_ikn"] = f(np.broadcast_to(np.asarray(inp["idx_k_norm"])[0][None, :], (128, 64)))
    mq = np.asarray(inp["mem_q_norm"], np.float32)
    shared["mqn"] = f(np.concatenate([mq, mq], axis=1).T)
    mk = np.asarray(inp["mem_k_norm"], np.float32)
    shared["bc_mkn"] = f(np.broadcast_to(np.tile(mk, (1, 4))[None, :, :], (128, 2, 256)))
    cwv = np.asarray(inp["ffn_conv_w"], np.float32)
    shared["cw"] = f(cwv.reshape(2, 3, NFC, 128).transpose(3, 0, 2, 1))
    cbv = np.asarray(inp["ffn_conv_b"], np.float32)
    shared["cb"] = f(cbv.reshape(2, NFC, 128).transpose(2, 0, 1))
    ck = np.asarray(inp["cache_k"], np.float32)[0]
    shared["pool_k"] = f(ck.transpose(0, 3, 2, 1)).reshape(NPOOL * 128, 256)
    shared["pool_v"] = f(np.asarray(inp["cache_v"], np.float32)[0].reshape(NPOOL * 128, 256))
    shared["pool_ik"] = f(np.asarray(inp["cache_idx_k"], np.float32)[0].transpose(0, 2, 1)).reshape(NPOOL * 64, 128)
    for k in ("ident", "mhg", "resetm", "validm", "cmdiag", "cms", "bones", "tab", "m96", "pidx"):
        shared[k] = cst[k]
    xs_all = np.asarray(inp["x_sample"], np.float32)
    xp = np.asarray(inp["x_prompt"], np.float32)
    memp = np.asarray(inp["mem_prompt"], np.float32)
    cmk = np.asarray(inp["cache_mem_k"], np.float32)
    cmv = np.asarray(inp["cache_mem_v"], np.float32)
    sth = np.asarray(inp["state_hgrn"], np.float32)[0]
    sfc = np.asarray(inp["state_ffn_conv"], np.float32)
    ptab = np.asarray(inp["page_table"]).astype(np.int32)
    nseq = xp.shape[0]
    maps = []
    for c in range(ncores):
        m = dict(shared)
        sq_ = c % nseq
        m["x"] = f(xp[sq_, :T])
        m["mem"] = f(memp[sq_])
        xs = np.zeros((128, D), np.float32)
        cstt = np.zeros((2, 128, NFC, 128), np.float32)
        for b in range(4):
            gb = 4 * c + b
            xs[32 * b + 2:32 * b + 6] = xs_all[gb]
            for l in range(2):
                stt_ = sfc[l, gb].reshape(2, NFC, 128)
                cstt[l, :, :, 32 * b:32 * b + 2] = stt_.transpose(2, 1, 0)
        m["xs"] = xs
        m["cst"] = cstt
        kk = cmk[:, 4 * c:4 * c + 4]
        kk = kk.reshape(2, 4, 256, 2, 2, 64)
        m["mkT_s"] = f(kk.transpose(0, 1, 4, 5, 3, 2).reshape(2, 4, 128, 2, 256))
        m["mv_s"] = f(cmv[:, 4 * c:4 * c + 4].reshape(2, 4, 256, 256))
        m["st_h"] = f(sth[4 * c:4 * c + 4])
        m["pt"] = np.ascontiguousarray(ptab[4 * c:4 * c + 4].reshape(1, -1))
        maps.append(m)
    return maps


_CACHE = {}


def run(inp, NT, NPAGE, NPOOL, NSEL, NBIS=10):
    key = (NT, NPAGE, NPOOL, NSEL, NBIS)
    if key not in _CACHE:
        _CACHE[key] = build(NT, NPAGE, NPOOL, NSEL, NBIS)
    nc = _CACHE[key]
    maps = make_in_maps(inp, NT, NPAGE, NPOOL)
    res = run_bass_kernel_spmd(nc, maps, core_ids=list(range(8)))
    return res.results


def assemble(R, NT, nseq=4):
    T = NT * 128
    y = np.stack([R[c]["y"] for c in range(nseq)])
    rows = np.array([32 * b + 2 + t for b in range(4) for t in range(4)])
    ys = np.concatenate([R[c]["ys"][rows].reshape(4, 4, D) for c in range(8)])
    hp = np.stack([R[c]["hstp"] for c in range(nseq)])[None]
    hs = np.concatenate([R[c]["hsts"] for c in range(8)])[None]
    kp = np.stack([R[c]["ko"].reshape(T, 2, 128) for c in range(nseq)])[None]
    vp = np.stack([R[c]["vo"].reshape(T, 2, 128) for c in range(nseq)])[None]
    ikp = np.stack([R[c]["iko"] for c in range(nseq)])[None]
    ks = np.concatenate([R[c]["kso"][rows].reshape(4, 4, 2, 128) for c in range(8)])[None]
    vs = np.concatenate([R[c]["vso"][rows].reshape(4, 4, 2, 128) for c in range(8)])[None]
    iks = np.concatenate([R[c]["ikso"][rows].reshape(4, 4, 64) for c in range(8)])[None]
    mk = np.stack([R[c]["mko"].reshape(2, 256, 4, 64) for c in range(nseq)], axis=1)
    mv = np.stack([R[c]["mvo"].reshape(2, 256, 4, 64) for c in range(nseq)], axis=1)
    cvp = np.stack([R[c]["cvp"][:, 126:128, :] for c in range(nseq)], axis=1)
    rows2 = np.array([32 * b + 4 + j for b in range(4) for j in range(2)])
    cvs = np.concatenate([R[c]["cvs"][:, rows2, :].reshape(2, 4, 2, DFF) for c in range(8)], axis=1)
    outs = (y, ys, hp, hs, kp, vp, ikp, ks, vs, iks, mk, mv, cvp, cvs)
    return tuple(np.ascontiguousarray(o.astype(np.float32)) for o in outs)


def kernel(**inputs):
    NT = 32
    NPAGE = 64
    NPOOL = int(np.asarray(inputs["cache_k"]).shape[1])
    R = run(inputs, NT, NPAGE, NPOOL, 256)
    return assemble(R, NT)
```

```python
import numpy as np
from contextlib import ExitStack
import concourse.bass as bass
import concourse.mybir as mybir
from concourse.bass_utils import run_bass_kernel_spmd

F32 = mybir.dt.float32
BF16 = mybir.dt.bfloat16
I32 = mybir.dt.int32
AF = mybir.ActivationFunctionType
ALU = mybir.AluOpType
AX = mybir.AxisListType

D = 1024
DFF = 2816
NFC = 22
HG_COLS = 3328
DSA_COLS = 2120
EPS = 1e-6
NEG = -1.0e30
DEBUG = False
STAGE = 99


class _Stop(Exception):
    pass


_PROG = [None]


def _stage(n):
    if STAGE <= n and not _PROG[0].stopped:
        _PROG[0].barrier()
        _PROG[0].stopped = True


class Buf:
    __slots__ = ("w", "r", "grp")

    def __init__(self):
        self.w = None
        self.r = {}
        self.grp = None


class DGroup:
    def __init__(self, sem):
        self.sem = sem
        self.cnt = 0


class Eng:
    def __init__(self, h, sem):
        self.h = h
        self.sem = sem
        self.cnt = 0
        self.known = {}


class Prog:
    def __init__(self, nc, es):
        self.nc = nc
        self.es = es
        self.nsem = 0
        self.PE = Eng(nc.tensor, self.newsem())
        self.ACT = Eng(nc.scalar, self.newsem())
        self.DVE = Eng(nc.vector, self.newsem())
        self.POOL = Eng(nc.gpsimd, self.newsem())
        self.SP = Eng(nc.sync, self.newsem())
        self.engs = [self.PE, self.ACT, self.DVE, self.POOL, self.SP]
        self.groups = []
        self.out_deps = []
        self.stopped = False
        self.free_groups = {}
        _PROG[0] = self

    def newsem(self):
        self.nsem += 1
        return self.es.enter_context(self.nc.semaphore("s%d" % self.nsem))

    def newgroup(self, kind="sp"):
        fl = self.free_groups.setdefault(kind, [])
        if fl:
            g = fl.pop()
            g.recycled = True
            return g
        g = DGroup(self.newsem())
        g.recycled = False
        g.kind = kind
        self.groups.append(g)
        return g

    def release(self, buf):
        if buf.grp is not None:
            self.free_groups[buf.grp.kind].append(buf.grp)
            buf.grp = None

    def _wait(self, eng, dep):
        sem, val = dep
        if isinstance(val, DGroup):
            val = val.cnt
        if sem is eng.sem and eng is self.PE:
            return
        k = id(sem)
        if eng.known.get(k, 0) >= val:
            return
        eng.h.wait_ge(sem, val)
        eng.known[k] = val

    def _deps(self, eng, r, w, skip_grp=None):
        if self.stopped:
            return
        for b in r:
            if b.w is not None:
                self._wait(eng, b.w)
        for b in w:
            if b.w is not None:
                if not (skip_grp is not None and b.w[0] is skip_grp.sem):
                    self._wait(eng, b.w)
            for sem, val in list(b.r.values()):
                self._wait(eng, (sem, val))

    def op(self, eng, fn, r=(), w=()):
        if self.stopped:
            return None
        self._deps(eng, r, w)
        ins = fn(eng.h)
        eng.cnt += 1
        ins.then_inc(eng.sem, 1)
        me = (eng.sem, eng.cnt)
        for b in w:
            b.w = me
            b.r = {}
        for b in r:
            if b not in w:
                b.r[id(eng.sem)] = me
        return ins

    def dma(self, q, out, in_, buf, load, grp=None, is_out=False, other=()):
        if self.stopped:
            return None
        if grp is None:
            if buf.grp is None:
                buf.grp = self.newgroup("pool" if q is self.POOL else "sp")
            grp = buf.grp
            assert (grp.kind == "pool") == (q is self.POOL), "mixed DMA queue kinds on one buffer"
        if getattr(grp, "recycled", False):
            self._wait(q, (grp.sem, grp.cnt))
            grp.recycled = False
        if load:
            self._deps(q, other, [buf], skip_grp=grp)
        else:
            self._deps(q, [buf] + list(other), [])
        ins = q.h.dma_start(out=out, in_=in_)
        grp.cnt += 16
        ins.then_inc(grp.sem, 16)
        if load:
            buf.w = (grp.sem, grp)
            buf.r = {}
        else:
            buf.r[id(grp.sem)] = (grp.sem, grp.cnt)
            if is_out:
                self.out_deps.append(grp)
        return ins

    def idma(self, out, in_, idx_ap, buf, other=()):
        q = self.POOL
        if self.stopped:
            return None
        if buf.grp is None:
            buf.grp = self.newgroup("pool")
        grp = buf.grp
        assert grp.kind == "pool"
        if getattr(grp, "recycled", False):
            self._wait(q, (grp.sem, grp.cnt))
            grp.recycled = False
        self._deps(q, other, [buf], skip_grp=grp)
        ins = q.h.indirect_dma_start(out=out, out_offset=None, in_=in_,
                                     in_offset=bass.IndirectOffsetOnAxis(ap=idx_ap, axis=0))
        grp.cnt += 16
        ins.then_inc(grp.sem, 16)
        buf.w = (grp.sem, grp)
        buf.r = {}
        return ins

    def barrier(self, force=False):
        if self.stopped and not force:
            return
        for e in self.engs:
            if (e is self.SP or e is self.PE) and not force:
                continue
            for o in self.engs:
                if o is not e and o.cnt > 0:
                    self._wait(e, (o.sem, o.cnt))
            for g in self.groups:
                if g.cnt > 0:
                    self._wait(e, (g.sem, g.cnt))

    def finish(self):
        self.barrier(force=True)


def build(NT, NPAGE, NPOOL, NSEL, NBIS, G=4, NSEL_S=256):
    T = NT * 128
    NG = NT // G
    PAST = NPAGE * 128
    nc = bass.Bass("TRN2", target_bir_lowering=False)

    def din(name, shape, dt=F32):
        return nc.dram_tensor(name, list(shape), dt, kind="ExternalInput").ap()

    def dout(name, shape, dt=F32):
        return nc.dram_tensor(name, list(shape), dt, kind="ExternalOutput").ap()

    x_d = din("x", [T, D])
    xs_d = din("xs", [128, D])
    w_in0 = din("w_in0", [D, HG_COLS])
    w_in1 = din("w_in1", [D, DSA_COLS])
    w_kv = din("w_kv", [2, D, 512])
    w_o = din("w_o", [2, D, D])
    w_g = din("w_g", [2, D, DFF])
    w_u = din("w_u", [2, D, DFF])
    w_d = din("w_d", [2, DFF, D])
    bc_nmix = din("bc_nmix", [2, 128, D])
    bc_nffn = din("bc_nffn", [2, 128, D])
    lbl_d = din("lbl", [128, 2, 6])
    onorm_d = din("onorm", [128, 1])
    bc_qn = din("bc_qn", [128, 128])
    bc_kn = din("bc_kn", [128, 128])
    bc_ikn = din("bc_ikn", [128, 64])
    mqn_d = din("mqn", [128, 2])
    bc_mkn = din("bc_mkn", [128, 2, 256])
    cw_d = din("cw", [128, 2, NFC, 3])
    cb_d = din("cb", [128, 2, NFC])
    mem_d = din("mem", [256, D])
    mkT_s = din("mkT_s", [2, 4, 128, 2, 256])
    mv_s = din("mv_s", [2, 4, 256, 256])
    st_h = din("st_h", [4, 6, 128, 128])
    cst_d = din("cst", [2, 128, NFC, 128])
    pt_d = din("pt", [1, 4 * NPAGE], I32)
    pool_ik = din("pool_ik", [NPOOL * 64, 128])
    pool_k = din("pool_k", [NPOOL * 128, 256])
    pool_v = din("pool_v", [NPOOL * 128, 256])
    pidx_d = din("pidx", [128, 1])
    tab_d = din("tab", [(NT + 1) * 128, 1280])
    ident_d = din("ident", [128, 128])
    mhg_d = din("mhg", [128, 384])
    resetm_d = din("resetm", [128, 512])
    validm_d = din("validm", [128, 128])
    cmdiag_d = din("cmdiag", [128, 128])
    cms_d = din("cms", [128, 128])
    bones_d = din("bones", [128, 128])
    m96_d = din("m96", [128, 128])

    y_d = dout("y", [T, D])
    ys_d = dout("ys", [128, D])
    hstp_d = dout("hstp", [6, 128, 128])
    hsts_d = dout("hsts", [4, 6, 128, 128])
    ko_d = dout("ko", [T, 256])
    vo_d = dout("vo", [T, 256])
    iko_d = dout("iko", [T, 64])
    kso_d = dout("kso", [128, 256])
    vso_d = dout("vso", [128, 256])
    ikso_d = dout("ikso", [128, 64])
    mko_d = dout("mko", [2, 256, 256])
    mvo_d = dout("mvo", [2, 256, 256])
    cvp_d = dout("cvp", [2, 128, DFF])
    cvs_d = dout("cvs", [2, 128, DFF])

    if DEBUG:
        dbg_mix = dout("dbg_mix", [128, 8, G * 128])
        dbg_x1 = dout("dbg_x1", [128, G, D])
        dbg_x2 = dout("dbg_x2", [128, G, D])
    es = ExitStack()
    with es:
        P = Prog(nc, es)
        PE, ACT, DVE, POOL, SP = P.PE, P.ACT, P.DVE, P.POOL, P.SP
        cgrp = P.newgroup("pool")

        uid = [0]

        def sb(stack, name, shape, dt=F32):
            uid[0] += 1
            t = stack.enter_context(nc.sbuf_tensor("%s_%d" % (name, uid[0]), list(shape), dt))
            b = Buf()
            if stack is not es:
                stack.callback(P.release, b)
            return t, b

        PS = []
        for i in range(8):
            t = es.enter_context(nc.psum_tensor("ps%d" % i, [128, 512], F32))
            PS.append((t, Buf()))
        psrr = [0]

        def pb():
            i = psrr[0]
            psrr[0] = (i + 1) % 8
            return PS[i]

        def act(out, in_, func, r, w, scale=1.0, bias=0.0, accum=None):
            kw = {}
            if accum is not None:
                kw["accum_out"] = accum
            return P.op(ACT, lambda h: h.activation(out=out, in_=in_, func=func, bias=bias, scale=scale, **kw), r, w)

        def tt(out, in0, in1, op, r, w, eng=None):
            return P.op(eng or DVE, lambda h: h.tensor_tensor(out=out, in0=in0, in1=in1, op=op), r, w)

        def tsc(out, in0, s1, s2, op0, op1, r, w, accum=None, eng=None):
            kw = {}
            if accum is not None:
                kw["accum_out"] = accum
            return P.op(eng or DVE, lambda h: h.tensor_scalar(out=out, in0=in0, scalar1=s1, scalar2=s2, op0=op0, op1=op1, **kw), r, w)

        def stt(out, in0, scalar, in1, op0, op1, r, w):
            return P.op(DVE, lambda h: h.scalar_tensor_tensor(out=out, in0=in0, scalar=scalar, in1=in1, op0=op0, op1=op1), r, w)

        def mm(out, lhsT, rhs, start, stop, r, w):
            return P.op(PE, lambda h: h.matmul(out, lhsT=lhsT, rhs=rhs, start=start, stop=stop), r, w)

        def cp(out, in_, r, w, eng=None):
            e = eng or DVE
            if e is ACT:
                return P.op(ACT, lambda h: h.activation(out=out, in_=in_, func=AF.Copy), r, w)
            return P.op(e, lambda h: h.tensor_copy(out=out, in_=in_), r, w)

        def const(name, src, shape, dt=F32, q=None):
            t, b = sb(es, name, shape, F32)
            P.dma(q or POOL, t[:], src, b, True, grp=cgrp)
            if dt is F32:
                return t, b
            t2, b2 = sb(es, name + "_bf", shape, dt)
            deferred.append(lambda: cp(t2[:], t[:], [b], [b2]))
            return t2, b2
        deferred = []

        identF, identF_b = const("identF", ident_d[:, :], [128, 128])
        mhg, mhg_b = const("mhg", mhg_d[:, 0:128], [128, 128])
        resetm, resetm_b = const("resetm", resetm_d[:, :], [128, 512])
        validm, validm_b = const("validm", validm_d[:, :], [128, 128])
        cmdiag, cmdiag_b = const("cmdiag", cmdiag_d[:, :], [128, 128])
        cms, cms_b = const("cms", cms_d[:, :], [128, 128])
        bones, bones_b = const("bones", bones_d[:, :], [128, 128], BF16)
        m96, m96_b = const("m96", m96_d[:, :], [128, 128])
        pidx, pidx_b = const("pidx", pidx_d[:, :], [128, 1])
        lbl, lbl_b = const("lblc", lbl_d[:, :, :], [128, 2, 6])
        onorm, onorm_b = const("onormc", onorm_d[:, :], [128, 1])
        gq, gq_b = const("gq", bc_qn[:, :], [128, 128])
        gk, gk_b = const("gk", bc_kn[:, :], [128, 128])
        gik, gik_b = const("gik", bc_ikn[:, :], [128, 64])
        mqn, mqn_b = const("mqnc", mqn_d[:, :], [128, 2])
        gmk, gmk_b = const("gmk", bc_mkn[:, :, :], [128, 2, 256])
        cw, cw_b = const("cwc", cw_d[:, :, :, :], [128, 2, NFC, 3])
        cbv, cbv_b = const("cbc", cb_d[:, :, :], [128, 2, NFC])
        for fn_ in deferred:
            fn_()
        identB, identB_b = sb(es, "identB", [128, 128], BF16)
        cp(identB[:], identF[:], [identF_b], [identB_b])
        onesB, onesB_b = sb(es, "onesB", [128, 128], BF16)
        P.op(DVE, lambda h: h.memset(onesB[:], 1.0), [], [onesB_b])
        GN, GN_b = sb(es, "GN", [128, D])
        gmix, gmix_b = bc_nmix, None
        gffn, gffn_b = bc_nffn, None
        ident3, ident3_b = sb(es, "ident3", [128, 3, 128], BF16)
        for j in range(3):
            cp(ident3[:, j, :], identF[:], [identF_b], [ident3_b])
        oml, oml_b = sb(es, "oml", [128, 6])
        tt(oml[:], lbl[:, 1, :], lbl[:, 0, :], ALU.subtract, [lbl_b], [oml_b])
        act(oml[:], oml[:], AF.Sigmoid, [oml_b], [oml_b])
        epsc, epsc_b = sb(es, "epsc", [128, 1])
        P.op(DVE, lambda h: h.memset(epsc[:], EPS), [], [epsc_b])

        NSLOT = 3
        ring = [sb(es, "wr%d" % i, [128, 8, 512], BF16) for i in range(NSLOT)]
        rr = [0]

        wconv = {}

        def convert(name, wap, K_, N_):
            wb_ = nc.dram_tensor(name + "_bf", [K_, N_], BF16, kind="Internal").ap()
            cb_ = Buf()
            cb_.grp = P.newgroup("pool")
            for r0 in range(0, K_, 128):
                if P.stopped:
                    break
                ins = POOL.h.dma_start(out=wb_[r0:r0 + 128, :], in_=wap[r0:r0 + 128, :], max_dma_last_dim=4096)
                cb_.grp.cnt += 16
                ins.then_inc(cb_.grp.sem, 16)
            cb_.w = (cb_.grp.sem, cb_.grp)
            wconv[name] = (wb_, cb_)

        def wblock(wname, r0, kc, c0, ncols):
            wap, cb_ = wconv[wname]
            t, b = ring[rr[0]]
            rr[0] = (rr[0] + 1) % NSLOT
            src = wap[r0:r0 + kc * 128, c0:c0 + ncols].rearrange("(k p) n -> p k n", p=128)
            P.dma(SP, t[:, 0:kc, 0:ncols], src, b, True, other=[cb_])
            return t, b

        convert("w_in0", w_in0, D, HG_COLS)
        for l in range(2):
            convert("w_kv%d" % l, w_kv[l], D, 512)
        convert("w_o0", w_o[0], D, D)
        convert("w_g0", w_g[0], D, DFF)
        convert("w_u0", w_u[0], D, DFF)
        convert("w_d0", w_d[0], DFF, D)
        convert("w_in1", w_in1, D, DSA_COLS)
        convert("w_o1", w_o[1], D, D)
        convert("w_g1", w_g[1], D, DFF)
        convert("w_u1", w_u[1], D, DFF)
        convert("w_d1", w_d[1], DFF, D)

        X, X_b = sb(es, "X", [128, G, D])
        HT, HT_b = sb(es, "HT", [128, 8, G * 128], BF16)
        MIX, MIX_b = sb(es, "MIX", [128, 8, G * 128], BF16)
        HN = [sb(es, "HN%d" % i, [128, D]) for i in range(1)]
        SS, SS_b = sb(es, "SS", [128, 16])
        Sf, Sf_b = [], []
        for h in range(6):
            t, b = sb(es, "Sf%d" % h, [128, 128])
            Sf.append(t); Sf_b.append(b)
        Sb = [[sb(es, "Sb%d_%d" % (h, i), [128, 128], BF16) for i in range(2)] for h in range(6)]
        UH, UH_b = sb(es, "UH", [128, NFC, 2])
        MEMT, MEMT_b = sb(es, "MEMT", [128, 8, 256], BF16)
        hn_i = [0]

        def rmsnorm_T(gt, ntile, gain, gain_b, lyr):
            P.dma(POOL, GN[:], gain[lyr], GN_b, True)
            gain_b = GN_b
            for t in range(ntile):
                hn, hn_b = HN[0]
                act(hn[:], X[:, t, :], AF.Square, [X_b], [hn_b, SS_b], accum=SS[:, 0:1])
                act(SS[:, 1:2], SS[:, 0:1], AF.Ln, [SS_b], [SS_b], scale=1.0 / D, bias=epsc[:, 0:1])
                act(SS[:, 2:3], SS[:, 1:2], AF.Exp, [SS_b], [SS_b], scale=-0.5)
                stt(hn[:], X[:, t, :], SS[:, 2:3], GN[:], ALU.mult, ALU.mult, [X_b, SS_b, gain_b], [hn_b])
                for half in range(2):
                    pt_, pb_ = pb()
                    for j in range(4):
                        kc = half * 4 + j
                        P.op(PE, lambda h: h.transpose(pt_[:, j * 128:(j + 1) * 128], hn[:, kc * 128:(kc + 1) * 128], identF[:]),
                             [hn_b, identF_b], [pb_])
                    act(HT[:, half * 4:half * 4 + 4, t * 128:(t + 1) * 128],
                        pt_[:].rearrange("p (k n) -> p k n", k=4), AF.Copy, [pb_], [HT_b])

        def proj_fm(wap, c0, nch, ntok, consume):
            ci = 0
            while ci < nch:
                nb = min(4, nch - ci)
                wt, wb = wblock(wap, 0, 8, c0 + ci * 128, nb * 128)
                for j in range(nb):
                    pt_, pb_ = pb()
                    for kc in range(8):
                        mm(pt_[:, 0:ntok], wt[:, kc, j * 128:(j + 1) * 128], HT[:, kc, 0:ntok], kc == 0, kc == 7, [wb, HT_b], [pb_])
                    consume(ci + j, pt_, pb_)
                ci += nb

        def proj_tm(wap, c0, ncols, ntile, consume, src=None, src_b=None, kchunks=8, r0=0):
            src = HT if src is None else src
            src_b = HT_b if src_b is None else src_b
            cb_ = 0
            c = 0
            while c < ncols:
                n = min(512, ncols - c)
                wt, wb = wblock(wap, r0, kchunks, c0 + c, n)
                for t in range(ntile):
                    pt_, pb_ = pb()
                    for kc in range(kchunks):
                        mm(pt_[:, 0:n], src[:, kc, t * 128:(t + 1) * 128], wt[:, kc, 0:n], kc == 0, kc == kchunks - 1, [wb, src_b], [pb_])
                    consume(cb_, c, n, t, pt_, pb_)
                c += n
                cb_ += 1

        def mem_attention(st, ntok, slots, l):
            MQ, MQ_b = st["MQ"]
            sq, sq_b = st["mq_sq"]
            rs, rs_b = st["mq_rs"]
            mqb, mqb_b = st["mq_bf"]
            for c in range(2):
                act(sq[:, 0:ntok], MQ[:, c, 0:ntok], AF.Square, [MQ_b], [sq_b])
                pt_, pb_ = pb()
                mm(pt_[:, 0:ntok], bones[:], sq[:, 0:ntok], True, True, [bones_b, sq_b], [pb_])
                act(rs[:, 0:ntok], pt_[:, 0:ntok], AF.Ln, [pb_], [rs_b], scale=1.0 / 64, bias=epsc[:, 0:1])
                act(rs[:, 0:ntok], rs[:, 0:ntok], AF.Exp, [rs_b], [rs_b], scale=-0.5)
                stt(mqb[:, c, 0:ntok], MQ[:, c, 0:ntok], mqn[:, l:l + 1], rs[:, 0:ntok], ALU.mult, ALU.mult, [MQ_b, mqn_b, rs_b], [mqb_b])
            mexps = st["mexp"]
            mei = [0]
            rd, rd_b = st["mrd"]
            for (c0, ncol, MKT, MKT_b, MV, MV_b) in slots:
                for c in range(2):
                    pn, pn_b = pb()
                    pd, pd_b = pb()
                    for j in range(2):
                        hh = 2 * c + j
                        lo, hi = 64 * j, 64 * j + 64
                        for mt in range(2):
                            ps_, ps_b = pb()
                            pe_t, pe_b = mexps[mei[0] % len(mexps)]
                            mei[0] += 1
                            mm(ps_[:, 0:ncol], MKT[lo:hi, c, mt * 128:(mt + 1) * 128], mqb[lo:hi, c, c0:c0 + ncol], True, True,
                               [MKT_b, mqb_b], [ps_b])
                            act(pe_t[:, 0:ncol], ps_[:, 0:ncol], AF.Exp, [ps_b], [pe_b], scale=0.125)
                            mm(pn[lo:hi, 0:ncol], MV[:, mt, hh * 64:(hh + 1) * 64], pe_t[:, 0:ncol], mt == 0, mt == 1, [MV_b, pe_b], [pn_b])
                            mm(pd[lo:hi, 0:ncol], onesB[:, 0:64], pe_t[:, 0:ncol], mt == 0, mt == 1, [onesB_b, pe_b], [pd_b])
                    act(rd[:, 0:ncol], pd[:, 0:ncol], AF.Ln, [pd_b], [rd_b])
                    act(rd[:, 0:ncol], rd[:, 0:ncol], AF.Exp, [rd_b], [rd_b], scale=-1.0)
                    tt(MIX[:, 6 + c, c0:c0 + ncol], pn[:, 0:ncol], rd[:, 0:ncol], ALU.mult, [pn_b, rd_b], [MIX_b])

        def out_proj(l, ntile):
            def consume(cb_, c, n, t, pt_, pb_):
                tt(X[:, t, c:c + n], X[:, t, c:c + n], pt_[:, 0:n], ALU.add, [X_b, pb_], [X_b])
            proj_tm("w_o%d" % l, 0, D, ntile, consume, src=MIX, src_b=MIX_b)

        def ffn(l, ntile, sample, cv_out):
            ntok = ntile * 128
            with ExitStack() as fs:
                AT, AT_b = sb(fs, "AT", [128, NFC, ntok], BF16)
                Uc = [sb(fs, "Uc%d" % i, [128, 2 + ntok]) for i in range(2)]
                Cc = [sb(fs, "Cc%d" % i, [128, ntok]) for i in range(2)]
                Sg = [sb(fs, "Sg%d" % i, [128, ntok]) for i in range(4)]
                UTOK, UTOK_b = sb(fs, "UTOK", [128, DFF])
                if sample:
                    CST, CST_b = sb(fs, "CST", [128, NFC, 128])
                    P.dma(POOL, CST[:], cst_d[l], CST_b, True)
                rmsnorm_T(None, ntile, gffn, gffn_b, l)
                gps = {}

                def cons_g(ci, pt_, pb_):
                    u, u_b = Uc[ci % 2]
                    cp(u[:, 0:2], UH[:, ci, :], [UH_b], [u_b])
                    if sample:
                        tt(u[:, 2:2 + ntok], pt_[:, 0:ntok], validm[:, 0:ntok], ALU.mult, [pb_, validm_b], [u_b])
                        tt(u[:, 2:2 + ntok], u[:, 2:2 + ntok], CST[:, ci, :], ALU.add, [u_b, CST_b], [u_b])
                    else:
                        act(u[:, 2:2 + ntok], pt_[:, 0:ntok], AF.Copy, [pb_], [u_b])
                    cp(UH[:, ci, :], u[:, ntok:ntok + 2], [u_b], [UH_b])
                    c_, c_b = Cc[ci % 2]
                    if sample:
                        tsc(c_[:], u[:, 2:2 + ntok], cw[:, l, ci, 2:3], cbv[:, l, ci:ci + 1], ALU.mult, ALU.add, [u_b, cw_b, cbv_b], [c_b])
                    else:
                        act(c_[:], pt_[:, 0:ntok], AF.Identity, [pb_, cw_b, cbv_b], [c_b], scale=cw[:, l, ci, 2:3], bias=cbv[:, l, ci:ci + 1])
                    stt(c_[:], u[:, 1:1 + ntok], cw[:, l, ci, 1:2], c_[:], ALU.mult, ALU.add, [u_b, cw_b], [c_b])
                    stt(c_[:], u[:, 0:ntok], cw[:, l, ci, 0:1], c_[:], ALU.mult, ALU.add, [u_b, cw_b], [c_b])
                    s_, s_b = Sg[ci % 4]
                    act(s_[:], c_[:], AF.Silu, [c_b], [s_b])
                    if cv_out is not None:
                        pq, pq_b = pb()
                        P.op(PE, lambda h: h.transpose(pq[:, 0:128], u[:, 2 + ntok - 128:2 + ntok], identF[:]), [u_b, identF_b], [pq_b])
                        act(UTOK[:, ci * 128:(ci + 1) * 128], pq[:, 0:128], AF.Copy, [pq_b], [UTOK_b])
                    gps[ci] = (s_, s_b)

                ci = 0
                while ci < NFC:
                    nb = min(4, NFC - ci)
                    wt, wb = wblock("w_g%d" % l, 0, 8, ci * 128, nb * 128)
                    for j in range(nb):
                        pt_, pb_ = pb()
                        for kc in range(8):
                            mm(pt_[:, 0:ntok], wt[:, kc, j * 128:(j + 1) * 128], HT[:, kc, 0:ntok], kc == 0, kc == 7, [wb, HT_b], [pb_])
                        cons_g(ci + j, pt_, pb_)
                    wt, wb = wblock("w_u%d" % l, 0, 8, ci * 128, nb * 128)
                    for j in range(nb):
                        pt_, pb_ = pb()
                        for kc in range(8):
                            mm(pt_[:, 0:ntok], wt[:, kc, j * 128:(j + 1) * 128], HT[:, kc, 0:ntok], kc == 0, kc == 7, [wb, HT_b], [pb_])
                        s_, s_b = gps[ci + j]
                        tt(AT[:, ci + j, :], s_[:], pt_[:, 0:ntok], ALU.mult, [s_b, pb_], [AT_b])
                    ci += nb
                if cv_out is not None:
                    P.dma(POOL, cv_out, UTOK[:], UTOK_b, False, is_out=True)
                for cbk in range(2):
                    accs = [pb() for _ in range(ntile)]
                    for kh, (k0, kn) in enumerate(((0, 8), (8, 8), (16, 6))):
                        wt, wb = wblock("w_d%d" % l, k0 * 128, kn, cbk * 512, 512)
                        for t in range(ntile):
                            pt_, pb_ = accs[t]
                            for k in range(kn):
                                fc = k0 + k
                                mm(pt_[:, :], AT[:, fc, t * 128:(t + 1) * 128], wt[:, k, :], fc == 0, fc == NFC - 1, [wb, AT_b], [pb_])
                    for t in range(ntile):
                        pt_, pb_ = accs[t]
                        tt(X[:, t, cbk * 512:(cbk + 1) * 512], X[:, t, cbk * 512:(cbk + 1) * 512], pt_[:, :], ALU.add, [X_b, pb_], [X_b])
            P.barrier()

        def layer0(ntile, sample, mslots, first, last):
            ntok = ntile * 128
            nch = ntok // 32
            with ExitStack() as ls:
                st = {}
                st["MQ"] = sb(ls, "MQ", [128, 2, ntok])
                st["mq_sq"] = sb(ls, "mq_sq", [128, ntok], BF16)
                st["mq_rs"] = sb(ls, "mq_rs", [128, ntok])
                st["mq_bf"] = sb(ls, "mq_bf", [128, 2, ntok], BF16)
                st["mexp"] = [sb(ls, "mexp%d" % i, [128, ntok], BF16) for i in range(2)]
                st["mrd"] = sb(ls, "mrd", [128, ntok])
                QS, QS_b = sb(ls, "QS", [128, 6, ntok], BF16)
                GS, GS_b = sb(ls, "GS", [128, 6, ntok], BF16)
                VBh, VBh_b = sb(ls, "VBh", [128, ntile, 768], BF16)
                QD, QD_b = sb(ls, "QD", [128, 6, ntok], BF16)
                KD, KD_b = sb(ls, "KD", [128, 6, ntok], BF16)
                EBL, EBL_b = sb(ls, "EBL", [128, 6, nch])
                tmp = [sb(ls, "l0t%d" % i, [128, ntok]) for i in range(4)]
                K2, K2_b = sb(ls, "K2", [128, ntok])
                K2m, K2m_b = sb(ls, "K2m", [128, 128])
                K2T, K2T_b = sb(ls, "K2T", [128, ntile, 768], BF16)
                K2T3, K2T3_b = sb(ls, "K2T3", [128, ntile, 768], BF16)
                SMs = [sb(ls, "SM%d" % i, [128, 128], BF16) for i in range(3)]
                Ohs = [sb(ls, "Oh%d" % i, [128, ntok]) for i in range(3)]
                Osq, Osq_b = sb(ls, "Osq", [128, ntok], BF16)
                Ors, Ors_b = sb(ls, "Ors", [128, ntok])
                MQ, MQ_b = st["MQ"]

                rmsnorm_T(None, ntile, gmix, gmix_b, 0)
                _stage(1.20)

                def cons_q(ci, pt_, pb_):
                    act(QS[:, ci, :], pt_[:, 0:ntok], AF.Silu, [pb_], [QS_b])
                proj_fm("w_in0", 0, 6, ntok, cons_q)
                _stage(1.21)

                def cons_f(h, pt_, pb_):
                    a, a_b = tmp[0]
                    kk, kk_b = tmp[1]
                    lf, lf_b = tmp[2]
                    bb, bb_b = tmp[0]
                    eb, eb_b = tmp[3]
                    en, en_b = tmp[2]
                    act(a[:], pt_[:, 0:ntok], AF.Sigmoid, [pb_], [a_b], scale=-1.0)
                    tsc(kk[:], a[:], oml[:, h:h + 1], None, ALU.mult, ALU.bypass, [a_b, oml_b], [kk_b])
                    _stage(1.211)
                    act(lf[:], kk[:], AF.Ln, [kk_b], [lf_b], scale=-1.0, bias=1.0)
                    if sample:
                        tt(lf[:], lf[:], validm[:, 0:ntok], ALU.mult, [lf_b, validm_b], [lf_b])
                    _stage(1.212)
                    P.op(DVE, lambda hd: hd.tensor_tensor_scan(out=bb[:], data0=resetm[:, 0:ntok], data1=lf[:], initial=0.0,
                                                               op0=ALU.mult, op1=ALU.add), [resetm_b, lf_b], [bb_b])
                    _stage(1.213)
                    act(eb[:], bb[:], AF.Exp, [bb_b], [eb_b])
                    act(en[:], bb[:], AF.Exp, [bb_b], [en_b], scale=-1.0)
                    tt(QD[:, h, :], QS[:, h, :], eb[:], ALU.mult, [QS_b, eb_b], [QD_b])
                    tt(KD[:, h, :], kk[:], en[:], ALU.mult, [kk_b, en_b], [KD_b])
                    _stage(1.214)
                    cp(EBL[:, h, :], eb[:].rearrange("p (c j) -> p c j", j=32)[:, :, 31], [eb_b], [EBL_b])
                    _stage(1.215)
                    tt(K2[:].rearrange("p (c j) -> p c j", j=32), KD[:, h, :].rearrange("p (c j) -> p c j", j=32),
                       EBL[:, h, :].unsqueeze(2).to_broadcast([128, nch, 32]), ALU.mult, [KD_b, EBL_b], [K2_b])
                    _stage(1.216)
                    for t in range(ntile):
                        pq, pq_b = pb()
                        P.op(PE, lambda hd: hd.transpose(pq[:, 0:128], K2[:, t * 128:(t + 1) * 128], identF[:]), [K2_b, identF_b], [pq_b])
                        act(K2T[:, t, h * 128:(h + 1) * 128], pq[:, 0:128], AF.Copy, [pq_b], [K2T_b])
                        _stage(1.217)
                        tt(K2m[:], K2[:, t * 128:(t + 1) * 128], m96[:], ALU.mult, [K2_b, m96_b], [K2m_b])
                        pq2, pq2_b = pb()
                        P.op(PE, lambda hd: hd.transpose(pq2[:, 0:128], K2m[:], identF[:]), [K2m_b, identF_b], [pq2_b])
                        act(K2T3[:, t, h * 128:(h + 1) * 128], pq2[:, 0:128], AF.Copy, [pq2_b], [K2T3_b])
                    _stage(1.2171 + 0.0001 * h)
                proj_fm("w_in0", 768, 6, ntok, cons_f)
                _stage(1.22)

                def cons_g(ci, pt_, pb_):
                    act(GS[:, ci, :], pt_[:, 0:ntok], AF.Silu, [pb_], [GS_b])
                proj_fm("w_in0", 2304, 6, ntok, cons_g)

                def cons_m(ci, pt_, pb_):
                    act(MQ[:, ci, :], pt_[:, 0:ntok], AF.Copy, [pb_], [MQ_b])
                proj_fm("w_in0", 3072, 2, ntok, cons_m)

                def cons_i(cb_, c, n, t, pt_, pb_):
                    act(VBh[:, t, c:c + n], pt_[:, 0:n], AF.Copy, [pb_], [VBh_b])
                proj_tm("w_in0", 1536, 768, ntile, cons_i)
                _stage(1.23)

                for hh in range(2):
                    heads = [3 * hh + i for i in range(3)]
                    par = {h: 0 for h in heads}
                    if first and not sample:
                        for h in heads:
                            P.op(DVE, lambda hd: hd.memset(Sf[h][:], 0.0), [], [Sf_b[h]])
                            P.op(DVE, lambda hd: hd.memset(Sb[h][0][0][:], 0.0), [], [Sb[h][0][1]])
                    k_ps = 0
                    k_pu = 0
                    for t in range(ntile):
                        cols = slice(t * 128, (t + 1) * 128)
                        pos = {}
                        for i, h in enumerate(heads):
                            ps_, ps_b = PS[3 + (k_ps % 2)]
                            k_ps += 1
                            sm, sm_b = SMs[i]
                            mm(ps_[:, 0:128], KD[:, h, cols], QD[:, h, cols], True, True, [KD_b, QD_b], [ps_b])
                            tt(sm[:], ps_[:, 0:128], mhg[:], ALU.mult, [ps_b, mhg_b], [sm_b])
                            po, po_b = PS[i]
                            pos[h] = (po, po_b)
                            mm(po[:, 0:128], VBh[:, t, h * 128:(h + 1) * 128], sm[:], True, False, [VBh_b, sm_b], [po_b])
                        for c in range(4):
                            ccols = slice(t * 128 + c * 32, t * 128 + c * 32 + 32)
                            for i, h in enumerate(heads):
                                po, po_b = pos[h]
                                if sample:
                                    P.dma(POOL, Sf[h][:], st_h[c, h], Sf_b[h], True)
                                    par[h] = 0
                                    cp(Sb[h][0][0][:], Sf[h][:], [Sf_b[h]], [Sb[h][0][1]], eng=ACT)
                                sbt, sbb = Sb[h][par[h]]
                                mm(po[:, c * 32:(c + 1) * 32], sbt[:], QD[:, h, ccols], False, c == 3, [sbb, QD_b], [po_b])
                                pu, pu_b = PS[5 + (k_pu % 3)]
                                k_pu += 1
                                if c < 3:
                                    mm(pu[:, 0:128], K2T[c * 32:(c + 1) * 32, t, h * 128:(h + 1) * 128],
                                       VBh[c * 32:(c + 1) * 32, t, h * 128:(h + 1) * 128], True, True, [K2T_b, VBh_b], [pu_b])
                                else:
                                    mm(pu[:, 0:128], K2T3[64:128, t, h * 128:(h + 1) * 128],
                                       VBh[64:128, t, h * 128:(h + 1) * 128], True, True, [K2T3_b, VBh_b], [pu_b])
                                stt(Sf[h][:], Sf[h][:], EBL[:, h, t * 4 + c:t * 4 + c + 1], pu[:, 0:128], ALU.mult, ALU.add,
                                    [Sf_b[h], EBL_b, pu_b], [Sf_b[h]])
                                par[h] ^= 1
                                cp(Sb[h][par[h]][0][:], Sf[h][:], [Sf_b[h]], [Sb[h][par[h]][1]], eng=ACT)
                                if sample:
                                    P.dma(POOL, hsts_d[c, h], Sf[h][:], Sf_b[h], False, is_out=True)
                        for i, h in enumerate(heads):
                            po, po_b = pos[h]
                            act(Ohs[i][0][:, cols], po[:, 0:128], AF.Copy, [po_b], [Ohs[i][1]])
                    for i, h in enumerate(heads):
                        Oh, Oh_b = Ohs[i]
                        if par[h] == 1:
                            cp(Sb[h][0][0][:], Sf[h][:], [Sf_b[h]], [Sb[h][0][1]], eng=ACT)
                        if last and not sample:
                            P.dma(POOL, hstp_d[h], Sf[h][:], Sf_b[h], False, is_out=True)
                        act(Osq[:], Oh[:], AF.Square, [Oh_b], [Osq_b])
                        pn_, pn_b = pb()
                        mm(pn_[:, 0:ntok], onesB[:], Osq[:], True, True, [onesB_b, Osq_b], [pn_b])
                        act(Ors[:], pn_[:, 0:ntok], AF.Ln, [pn_b], [Ors_b], scale=1.0 / 128, bias=epsc[:, 0:1])
                        act(Ors[:], Ors[:], AF.Exp, [Ors_b], [Ors_b], scale=-0.5)
                        tt(Oh[:], Oh[:], Ors[:], ALU.mult, [Oh_b, Ors_b], [Oh_b])
                        stt(MIX[:, h, 0:ntok], Oh[:], onorm[:, 0:1], GS[:, h, :], ALU.mult, ALU.mult, [Oh_b, onorm_b, GS_b], [MIX_b])

                _stage(1.24)
                mem_attention(st, ntok, mslots, 0)
            P.barrier()

        def memkv_prompt(l, MKT, MKT_b, MV, MV_b):
            with ExitStack() as ms:
                KVt, KVt_b = sb(ms, "KVt", [128, 512])
                KN, KN_b = sb(ms, "KN", [128, 256])
                sq, sq_b = sb(ms, "kvsq", [128, 64])
                ss4, ss4_b = sb(ms, "ss4", [128, 8])

                def consume(cb_, c, n, t, pt_, pb_):
                    act(KVt[:], pt_[:, :], AF.Copy, [pb_], [KVt_b])
                    for hh in range(4):
                        act(sq[:], KVt[:, hh * 64:(hh + 1) * 64], AF.Square, [KVt_b], [sq_b, ss4_b], accum=ss4[:, hh:hh + 1])
                    act(ss4[:, 4:8], ss4[:, 0:4], AF.Ln, [ss4_b], [ss4_b], scale=1.0 / 64, bias=epsc[:, 0:1])
                    act(ss4[:, 4:8], ss4[:, 4:8], AF.Exp, [ss4_b], [ss4_b], scale=-0.5)
                    for hh in range(4):
                        stt(KN[:, hh * 64:(hh + 1) * 64], KVt[:, hh * 64:(hh + 1) * 64], ss4[:, 4 + hh:5 + hh], gmk[:, l, hh * 64:(hh + 1) * 64],
                            ALU.mult, ALU.mult, [KVt_b, ss4_b, gmk_b], [KN_b])
                    P.dma(POOL, mko_d[l, t * 128:(t + 1) * 128, :], KN[:], KN_b, False, is_out=True)
                    P.dma(POOL, mvo_d[l, t * 128:(t + 1) * 128, :], KVt[:, 256:512], KVt_b, False, is_out=True)
                    cp(MV[:, t, :], KVt[:, 256:512], [KVt_b], [MV_b])
                    for c2 in range(2):
                        pq, pq_b = pb()
                        P.op(PE, lambda h: h.transpose(pq[:, 0:128], KN[:, c2 * 128:(c2 + 1) * 128], identF[:]), [KN_b, identF_b], [pq_b])
                        act(MKT[:, c2, t * 128:(t + 1) * 128], pq[:, 0:128], AF.Copy, [pq_b], [MKT_b])
                proj_tm("w_kv%d" % l, 0, 512, 2, consume, src=MEMT, src_b=MEMT_b)
            P.barrier()

        def dsa_inproj(ntile, tile0, st, sample, KT, KT_b, VB, VB_b, IKT, IKT_b):
            ntok = ntile * 128
            Z, Z_b = st["Z"]
            QN, QN_b = st["QN"]
            QR, QR_b = st["QR"]
            TB, TB_b = st["TB"]
            T1, T1_b = st["T1"]
            T2, T2_b = st["T2"]
            QT, QT_b = st["QT"]
            IQT, IQT_b = st["IQT"]
            IW, IW_b = st["IW"]
            MQ, MQ_b = st["MQ"]
            ss, ss_b = st["ss16"]
            sq, sq_b = st["sqj"]
            rmsnorm_T(None, ntile, gmix, gmix_b, 1)

            def cons_m(ci, pt_, pb_):
                act(MQ[:, ci, 0:ntok], pt_[:, 0:ntok], AF.Copy, [pb_], [MQ_b])
            proj_fm("w_in1", 1864, 2, ntok, cons_m)

            blocks = [(0, 512), (512, 512), (1024, 512), (1536, 328)]
            wts = []
            Zs = st["Zs"]

            def consume(cb_, c, n, t, pt_, pb_):
                z, z_b = Zs[t]
                act(z[:, c:c + n], pt_[:, 0:n], AF.Copy, [pb_], [z_b])
            proj_tm("w_in1", 0, 1864, ntile, consume)

            for t in range(ntile):
                z, z_b = Zs[t]
                gt = tile0 + t
                trow = (NT * 128) if sample else gt * 128
                P.dma(POOL, TB[:], tab_d[trow:trow + 128, :], TB_b, True)
                for j in range(8):
                    act(sq[:, 0:128], z[:, j * 128:(j + 1) * 128], AF.Square, [z_b], [sq_b, ss_b], accum=ss[:, j:j + 1])
                act(sq[:, 0:64], z[:, 1792:1856], AF.Square, [z_b], [sq_b, ss_b], accum=ss[:, 8:9])
                act(ss[:, 0:8], ss[:, 0:8], AF.Ln, [ss_b], [ss_b], scale=1.0 / 128, bias=epsc[:, 0:1])
                act(ss[:, 8:9], ss[:, 8:9], AF.Ln, [ss_b], [ss_b], scale=1.0 / 64, bias=epsc[:, 0:1])
                act(ss[:, 0:9], ss[:, 0:9], AF.Exp, [ss_b], [ss_b], scale=-0.5)
                for j in range(6):
                    stt(QN[:, j * 128:(j + 1) * 128], z[:, j * 128:(j + 1) * 128], ss[:, j:j + 1], gq[:], ALU.mult, ALU.mult, [z_b, ss_b, gq_b], [QN_b])
                for j in range(2):
                    stt(QN[:, 768 + j * 128:768 + (j + 1) * 128], z[:, 768 + j * 128:768 + (j + 1) * 128], ss[:, 6 + j:7 + j], gk[:],
                        ALU.mult, ALU.mult, [z_b, ss_b, gk_b], [QN_b])
                cp(QN[:, 1024:1536], z[:, 1280:1792], [z_b], [QN_b], eng=POOL)
                stt(QN[:, 1536:1600], z[:, 1792:1856], ss[:, 8:9], gik[:], ALU.mult, ALU.mult, [z_b, ss_b, gik_b], [QN_b])
                cp(IW[:, t, :], z[:, 1856:1864], [z_b], [IW_b], eng=POOL)

                def rope(c0, nh, half, cc, sc):
                    w_ = nh * half
                    xv = QN[:, c0:c0 + 2 * w_].rearrange("p (h two d) -> p h two d", h=nh, two=2)
                    ov = QR[:, c0:c0 + 2 * w_].rearrange("p (h two d) -> p h two d", h=nh, two=2)
                    cosv = TB[:, cc:cc + w_].rearrange("p (h d) -> p h d", h=nh)
                    sinv = TB[:, sc:sc + w_].rearrange("p (h d) -> p h d", h=nh)
                    t1 = T1[:, 0:w_].rearrange("p (h d) -> p h d", h=nh)
                    t2 = T2[:, 0:w_].rearrange("p (h d) -> p h d", h=nh)
                    x1, x2 = xv[:, :, 0, :], xv[:, :, 1, :]
                    tt(t1, x1, cosv, ALU.mult, [QN_b, TB_b], [T1_b])
                    tt(t2, x2, sinv, ALU.mult, [QN_b, TB_b], [T2_b], eng=POOL)
                    tt(ov[:, :, 0, :], t1, t2, ALU.subtract, [T1_b, T2_b], [QR_b])
                    tt(t1, x2, cosv, ALU.mult, [QN_b, TB_b], [T1_b])
                    tt(t2, x1, sinv, ALU.mult, [QN_b, TB_b], [T2_b], eng=POOL)
                    tt(ov[:, :, 1, :], t1, t2, ALU.add, [T1_b, T2_b], [QR_b])
                rope(0, 6, 64, 0, 384)
                rope(768, 2, 64, 0, 384)
                rope(1024, 8, 32, 768, 1024)
                rope(1536, 1, 32, 768, 1024)
                if sample:
                    P.dma(POOL, kso_d[:, :], QR[:, 768:1024], QR_b, False, is_out=True)
                    P.dma(POOL, vso_d[:, :], z[:, 1024:1280], z_b, False, is_out=True)
                    P.dma(POOL, ikso_d[:, :], QR[:, 1536:1600], QR_b, False, is_out=True)
                else:
                    P.dma(POOL, ko_d[gt * 128:(gt + 1) * 128, :], QR[:, 768:1024], QR_b, False, is_out=True)
                    P.dma(POOL, vo_d[gt * 128:(gt + 1) * 128, :], z[:, 1024:1280], z_b, False, is_out=True)
                    P.dma(POOL, iko_d[gt * 128:(gt + 1) * 128, :], QR[:, 1536:1600], QR_b, False, is_out=True)
                kcol = 0 if sample else gt * 128
                cp(VB[:, 0 if sample else gt, :], z[:, 1024:1280], [z_b], [VB_b[0 if sample else gt]], eng=ACT)
                for half_ in range(2):
                    pt_, pb_ = pb()
                    nh = 4 if half_ == 0 else 2
                    for j in range(nh):
                        hq = half_ * 4 + j
                        P.op(PE, lambda h: h.transpose(pt_[:, j * 128:(j + 1) * 128], QR[:, hq * 128:(hq + 1) * 128], identF[:]), [QR_b, identF_b], [pb_])
                    act(QT[:, half_ * 4:half_ * 4 + nh, t * 128:(t + 1) * 128], pt_[:, 0:nh * 128].rearrange("p (k n) -> p k n", k=nh),
                        AF.Copy, [pb_], [QT_b])
                pt_, pb_ = pb()
                for j in range(2):
                    P.op(PE, lambda h: h.transpose(pt_[:, j * 128:(j + 1) * 128], QR[:, 768 + j * 128:768 + (j + 1) * 128], identF[:]), [QR_b, identF_b], [pb_])
                P.op(PE, lambda h: h.transpose(pt_[0:64, 256:384], QR[:, 1536:1600], identF[:]), [QR_b, identF_b], [pb_])
                act(KT[:, :, kcol:kcol + 128], pt_[:, 0:256].rearrange("p (k n) -> p k n", k=2), AF.Copy, [pb_], [KT_b[0 if sample else gt]])
                act(IKT[0:64, kcol:kcol + 128], pt_[0:64, 256:384], AF.Copy, [pb_], [IKT_b[0 if sample else gt]])
                for half_ in range(2):
                    pt_, pb_ = pb()
                    for j in range(4):
                        hq = half_ * 4 + j
                        P.op(PE, lambda h: h.transpose(pt_[0:64, j * 128:(j + 1) * 128], QR[:, 1024 + hq * 64:1024 + (hq + 1) * 64], identF[:]),
                             [QR_b, identF_b], [pb_])
                    act(IQT[0:64, half_ * 4:half_ * 4 + 4, t * 128:(t + 1) * 128], pt_[0:64, :].rearrange("p (k n) -> p k n", k=4),
                        AF.Copy, [pb_], [IQT_b])

        def index_scores(st, t, keysets, SC, SC_b, row_lo=0, row_hi=128):
            IQT, IQT_b = st["IQT"]
            IW, IW_b = st["IW"]
            R = st["R"]
            ri = 0
            for (rhsf, kb_, ncols, sc0) in keysets:
                k0 = 0
                while k0 < ncols:
                    n = min(512, ncols - k0)
                    for h in range(8):
                        pt_, pb_ = pb()
                        mm(pt_[:, 0:n], IQT[0:64, h, t * 128:(t + 1) * 128], rhsf(k0, n), True, True, [IQT_b, kb_], [pb_])
                        r_, r_b = R[ri % 4]
                        ri += 1
                        act(r_[row_lo:row_hi, 0:n], pt_[row_lo:row_hi, 0:n], AF.Relu, [pb_], [r_b])
                        dst = SC[row_lo:row_hi, sc0 + k0:sc0 + k0 + n]
                        if h == 0:
                            tsc(dst, r_[row_lo:row_hi, 0:n], IW[row_lo:row_hi, t, 0:1], None, ALU.mult, ALU.bypass, [r_b, IW_b], [SC_b])
                        else:
                            stt(dst, r_[row_lo:row_hi, 0:n], IW[row_lo:row_hi, t, h:h + 1], dst, ALU.mult, ALU.add, [r_b, IW_b], [SC_b])
                    k0 += n

        def select_mask(st, L, SC, SC_b, MK, MK_b, cm, cm_b, cmcol, nsel):
            bs, bs_b = st["bs"]
            P.op(DVE, lambda h: h.tensor_reduce(out=bs[:, 0:1], in_=SC[:, 0:L], axis=AX.X, op=ALU.min), [SC_b], [bs_b])
            tt(SC[:, cmcol:cmcol + 128], SC[:, cmcol:cmcol + 128], cm[:], ALU.add, [SC_b, cm_b], [SC_b])
            P.op(DVE, lambda h: h.tensor_reduce(out=bs[:, 1:2], in_=SC[:, 0:L], axis=AX.X, op=ALU.max), [SC_b], [bs_b])
            tt(bs[:, 1:2], bs[:, 1:2], bs[:, 0:1], ALU.subtract, [bs_b], [bs_b])
            for i in range(NBIS):
                stt(bs[:, 2:3], bs[:, 1:2], float(2.0 ** -(i + 1)), bs[:, 0:1], ALU.mult, ALU.add, [bs_b], [bs_b])
                tsc(MK[:, 0:L], SC[:, 0:L], bs[:, 2:3], None, ALU.is_ge, ALU.add, [SC_b, bs_b], [MK_b, bs_b], accum=bs[:, 3:4])
                tsc(bs[:, 4:5], bs[:, 3:4], float(nsel) - 0.5, float(2.0 ** -(i + 1)), ALU.is_ge, ALU.mult, [bs_b], [bs_b])
                stt(bs[:, 0:1], bs[:, 1:2], bs[:, 4:5], bs[:, 0:1], ALU.mult, ALU.add, [bs_b], [bs_b])
            tsc(MK[:, 0:L], SC[:, 0:L], bs[:, 0:1], None, ALU.is_ge, ALU.bypass, [SC_b, bs_b], [MK_b])

        def attention(st, qcols, nq, sel, sel_b, keytiles, MK, MK_b, mixcols, page_loader=None):
            QT, QT_b = st["QT"]
            E = st["E"]
            PP = st["PP"]
            RD, RD_b = st["RD"]
            n3 = 3 * nq
            accO = [PS[0], PS[1]]
            accD = [PS[2], PS[3]]
            nk = len(keytiles)
            ei = 0
            pend = []

            def flush(pl):
                for (g, vf_g, vb_, p_, p_b, kidx) in pl:
                    mm(accO[g][0][:, 0:n3], vf_g, p_[:, 0:n3], kidx == 0, kidx == nk - 1, [vb_, p_b], [accO[g][1]])
                    mm(accD[g][0][:, 0:n3], onesB[:], p_[:, 0:n3], kidx == 0, kidx == nk - 1, [onesB_b, p_b], [accD[g][1]])
            for ki, ent in enumerate(keytiles):
                if ent[0] == "page":
                    mc0 = ent[2]
                    ktf, kb_, vf, vb_ = page_loader(ki, ent[1])
                else:
                    ktf, kb_, vf, vb_, mc0 = ent
                pm, pm_b = PS[6 + (ki % 2)]
                mm(pm[:, 0:n3], MK[:, mc0:mc0 + 128], sel, True, True, [MK_b, sel_b], [pm_b])
                cur = []
                for g in range(2):
                    ps_, ps_b = PS[4 + g]
                    mm(ps_[:, 0:n3].rearrange("p (h n) -> p h n", h=3), ktf(g), QT[:, 3 * g:3 * g + 3, qcols], True, True, [kb_, QT_b], [ps_b])
                    e_, e_b = E[ei % 4]
                    p_, p_b = PP[ei % 4]
                    ei += 1
                    act(e_[:, 0:n3], ps_[:, 0:n3], AF.Exp, [ps_b], [e_b], scale=float(128 ** -0.5))
                    tt(p_[:, 0:n3], e_[:, 0:n3], pm[:, 0:n3], ALU.mult, [e_b, pm_b], [p_b])
                    cur.append((g, vf(g), vb_, p_, p_b, ki))
                flush(pend)
                pend = cur
            flush(pend)
            for g in range(2):
                act(RD[:, 0:n3], accD[g][0][:, 0:n3], AF.Ln, [accD[g][1]], [RD_b])
                act(RD[:, 0:n3], RD[:, 0:n3], AF.Exp, [RD_b], [RD_b], scale=-1.0)
                tt(MIX[:, 3 * g:3 * g + 3, mixcols], accO[g][0][:, 0:n3].rearrange("p (h n) -> p h n", h=3),
                   RD[:, 0:n3].rearrange("p (h n) -> p h n", h=3), ALU.mult, [accO[g][1], RD_b], [MIX_b])

        def l1_common(ls, ntile):
            ntok = ntile * 128
            st = {}
            st["MQ"] = sb(ls, "MQ1", [128, 2, ntok])
            st["mq_sq"] = sb(ls, "mq_sq1", [128, ntok], BF16)
            st["mq_rs"] = sb(ls, "mq_rs1", [128, ntok])
            st["mq_bf"] = sb(ls, "mq_bf1", [128, 2, ntok], BF16)
            st["mexp"] = [sb(ls, "mexp1%d" % i, [128, ntok], BF16) for i in range(3)]
            st["mrd"] = sb(ls, "mrd1", [128, ntok])
            st["QT"] = sb(ls, "QT", [128, 6, ntok], BF16)
            st["IQT"] = sb(ls, "IQT", [64, 8, ntok], BF16)
            st["IW"] = sb(ls, "IW", [128, ntile, 8])
            return st

        def l1_inproj(ls, ntile, st):
            st["Zs"] = [sb(ls, "Z%d" % i, [128, 1864]) for i in range(ntile)]
            st["Z"] = st["Zs"][0]
            st["QN"] = sb(ls, "QN", [128, 1600])
            st["QR"] = sb(ls, "QR", [128, 1600])
            st["TB"] = sb(ls, "TB", [128, 1280])
            st["T1"] = sb(ls, "T1", [128, 384])
            st["T2"] = sb(ls, "T2", [128, 384])
            st["ss16"] = sb(ls, "ss16", [128, 16])
            st["sqj"] = st["T1"]

        def l1_attn(ls, st, sccols):
            st["R"] = [sb(ls, "R%d" % i, [128, 512], BF16) for i in range(4)]
            st["bs"] = sb(ls, "bs", [128, 8])
            st["E"] = [sb(ls, "E%d" % i, [128, 384], BF16) for i in range(4)]
            st["PP"] = [sb(ls, "PP%d" % i, [128, 384], BF16) for i in range(4)]
            st["RD"] = sb(ls, "RD", [128, 384])
            st["SC"] = sb(ls, "SC", [128, sccols])
            st["MK"] = sb(ls, "MK", [128, sccols], BF16)

        try:
            with ExitStack() as ms0:
                memf = [sb(ms0, "memf%d" % i, [128, D]) for i in range(2)]
                for mt in range(2):
                    P.dma(POOL, memf[mt][0][:], mem_d[mt * 128:(mt + 1) * 128, :], memf[mt][1], True)
                    for half in range(2):
                        pt_, pb_ = pb()
                        for j in range(4):
                            kc = half * 4 + j
                            P.op(PE, lambda h: h.transpose(pt_[:, j * 128:(j + 1) * 128], memf[mt][0][:, kc * 128:(kc + 1) * 128], identF[:]),
                                 [memf[mt][1], identF_b], [pb_])
                        act(MEMT[:, half * 4:half * 4 + 4, mt * 128:(mt + 1) * 128], pt_[:].rearrange("p (k n) -> p k n", k=4), AF.Copy, [pb_], [MEMT_b])
            P.barrier()
            MKTp = [sb(es, "MKTp%d" % l, [128, 2, 256], BF16) for l in range(2)]
            MVp = [sb(es, "MVp%d" % l, [128, 2, 256], BF16) for l in range(2)]
            for l in range(2):
                memkv_prompt(l, MKTp[l][0], MKTp[l][1], MVp[l][0], MVp[l][1])
            _stage(1)

            def sample_group():
                P.dma(POOL, X[:, 0, :], xs_d[:, :], X_b, True)
                P.op(DVE, lambda h: h.memset(UH[:], 0.0), [], [UH_b])
                for l in range(2):
                    with ExitStack() as ss_:
                        MKs = [sb(ss_, "MKs%d" % b, [128, 2, 256], BF16) for b in range(4)]
                        MVs = [sb(ss_, "MVs%d" % b, [128, 2, 256], BF16) for b in range(4)]
                        slots = []
                        for b in range(4):
                            P.dma(POOL, MKs[b][0][:], mkT_s[l, b], MKs[b][1], True)
                            P.dma(POOL, MVs[b][0][:], mv_s[l, b].rearrange("(t p) d -> p t d", p=128), MVs[b][1], True)
                            slots.append((32 * b, 32, MKs[b][0], MKs[b][1], MVs[b][0], MVs[b][1]))
                        if l == 0:
                            layer0(1, True, slots, True, True)
                            _stage(2)
                        else:
                            with ExitStack() as ls:
                                st = l1_common(ls, 1)
                                KTs, KTs_b = sb(ls, "KTs", [128, 2, 128], BF16)
                                VBs, VBs_b = sb(ls, "VBs", [128, 1, 256], BF16)
                                IKTs, IKTs_b = sb(ls, "IKTs", [64, 128], BF16)
                                with ExitStack() as zs:
                                    l1_inproj(zs, 1, st)
                                    dsa_inproj(1, 0, st, True, KTs, [KTs_b], VBs, [VBs_b], IKTs, [IKTs_b])
                                    _stage(5)
                                P.barrier()
                                with ExitStack() as as_:
                                    l1_attn(as_, st, PAST + 128)
                                    SC, SC_b = st["SC"]
                                    MK, MK_b = st["MK"]
                                    IKP, IKP_b = sb(as_, "IKP", [64, PAST], BF16)
                                    PTs, PTs_b = sb(as_, "PTs", [128, 4 * NPAGE], I32)
                                    IDX = sb(as_, "IDX", [128, 4, 4 * NPAGE], I32)
                                    sel3 = sb(as_, "sel3", [128, 4, 3, 32], BF16)
                                    for b in range(4):
                                        for j in range(3):
                                            cp(sel3[0][:, b, j, :], identF[:, 32 * b:32 * b + 32], [identF_b], [sel3[1]])
                                    KPr = [sb(as_, "KPr%d" % i, [128, 2, 128], BF16) for i in range(4)]
                                    VPr = [sb(as_, "VPr%d" % i, [128, 256], BF16) for i in range(4)]
                                    P.dma(POOL, PTs[:], pt_d.partition_broadcast(128), PTs_b, True)
                                    tsc(IDX[0][:, 0, :], PTs[:], 64.0, pidx[:, 0:1], ALU.mult, ALU.add, [PTs_b, pidx_b], [IDX[1]])
                                    tsc(IDX[0][:, 1, :], PTs[:], 256.0, pidx[:, 0:1], ALU.mult, ALU.add, [PTs_b, pidx_b], [IDX[1]])
                                    tsc(IDX[0][:, 2, :], IDX[0][:, 1, :], 128.0, None, ALU.add, ALU.bypass, [IDX[1]], [IDX[1]])
                                    tsc(IDX[0][:, 3, :], PTs[:], 128.0, pidx[:, 0:1], ALU.mult, ALU.add, [PTs_b, pidx_b], [IDX[1]])
                                    for b in (3, 2, 1, 0):
                                        rows = (64, 128) if b == 3 else (32 * b, 32 * b + 32)
                                        for pg in range(NPAGE):
                                            pi = b * NPAGE + pg
                                            P.idma(IKP[0:64, pg * 128:(pg + 1) * 128], pool_ik[:, :], IDX[0][0:64, 0, pi:pi + 1], IKP_b, [IDX[1]])
                                        keysets = [(lambda k0, n: IKP[0:64, k0:k0 + n], IKP_b, PAST, 0),
                                                   (lambda k0, n: IKTs[0:64, k0:k0 + n], IKTs_b, 128, PAST)]
                                        index_scores(st, 0, keysets, SC, SC_b, rows[0], rows[1])
                                    select_mask(st, PAST + 128, SC, SC_b, MK, MK_b, cms, cms_b, PAST, NSEL_S)
                                    _stage(6)
                                    for b in range(4):
                                        kts = []
                                        for pg in range(NPAGE):
                                            kts.append(("page", b * NPAGE + pg, pg * 128))
                                        kts.append(((lambda g: KTs[:, g, :]), KTs_b, (lambda g: VBs[:, 0, g * 128:(g + 1) * 128]), VBs_b, PAST))

                                        def page_loader(i, pi, KPr=KPr, VPr=VPr):
                                            kp, kp_b = KPr[i % 4]
                                            vp, vp_b = VPr[i % 4]
                                            P.idma(kp[:].rearrange("p g r -> p (g r)"), pool_k[:, :], IDX[0][:, 3, pi:pi + 1], kp_b, [IDX[1]])
                                            P.idma(vp[:], pool_v[:, :], IDX[0][:, 3, pi:pi + 1], vp_b, [IDX[1]])
                                            return ((lambda g: kp[:, g, :]), kp_b, (lambda g: vp[:, g * 128:(g + 1) * 128]), vp_b)
                                        attention(st, slice(32 * b, 32 * b + 32), 32, sel3[0][:, b, :, :].rearrange("p j n -> p (j n)"), sel3[1],
                                                  kts, MK, MK_b, slice(32 * b, 32 * b + 32), page_loader)
                                    mem_attention(st, 128, slots, 1)
                            P.barrier()
                    out_proj(l, 1)
                    _stage(3 + 4 * l)
                    ffn(l, 1, True, cvs_d[l])
                    _stage(4 + 4 * l)
                P.dma(POOL, ys_d[:, :], X[:, 0, :], X_b, False, is_out=True)

            sample_group()
            P.barrier()
            _stage(9)

            KT, _ = sb(es, "KT", [128, 2, T], BF16)
            VB, _ = sb(es, "VB", [128, NT, 256], BF16)
            IKT, _ = sb(es, "IKT", [64, T], BF16)
            KT_b = [Buf() for _ in range(NT)]
            VB_b = [Buf() for _ in range(NT)]
            IKT_b = [Buf() for _ in range(NT)]
            UHs = [sb(es, "UHs%d" % l, [128, NFC, 2]) for l in range(2)]
            for l in range(2):
                P.op(DVE, lambda h: h.memset(UHs[l][0][:], 0.0), [], [UHs[l][1]])

            for g in range(NG):
                first, last = (g == 0), (g == NG - 1)
                for t in range(G):
                    P.dma(POOL, X[:, t, :], x_d[(g * G + t) * 128:(g * G + t + 1) * 128, :], X_b, True)
                for l in range(2):
                    slots = [(0, G * 128, MKTp[l][0], MKTp[l][1], MVp[l][0], MVp[l][1])]
                    if l == 0:
                        layer0(G, False, slots, first, last)
                    else:
                        with ExitStack() as ls:
                            st = l1_common(ls, G)
                            with ExitStack() as zs:
                                l1_inproj(zs, G, st)
                                dsa_inproj(G, g * G, st, False, KT, KT_b, VB, VB_b, IKT, IKT_b)
                            P.barrier()
                            with ExitStack() as as_:
                                l1_attn(as_, st, T)
                                SC, SC_b = st["SC"]
                                MK, MK_b = st["MK"]
                                for t in range(G):
                                    gt = g * G + t
                                    L = (gt + 1) * 128
                                    merged = []
                                    kt = 0
                                    while kt <= gt:
                                        nkt = min(4, gt + 1 - kt)
                                        merged.append(((lambda k0, n, kt=kt: IKT[0:64, kt * 128 + k0:kt * 128 + k0 + n]), IKT_b[kt + nkt - 1], nkt * 128, kt * 128))
                                        for q in range(kt, kt + nkt - 1):
                                            P._deps(PE, [IKT_b[q]], [])
                                        kt += nkt
                                    index_scores(st, t, merged, SC, SC_b)
                                    select_mask(st, L, SC, SC_b, MK, MK_b, cmdiag, cmdiag_b, gt * 128, NSEL)
                                    kts = []
                                    for kt in range(gt + 1):
                                        kts.append(((lambda gg, kt=kt: KT[:, gg, kt * 128:(kt + 1) * 128]), KT_b[kt],
                                                    (lambda gg, kt=kt: VB[:, kt, gg * 128:(gg + 1) * 128]), VB_b[kt], kt * 128))
                                    attention(st, slice(t * 128, (t + 1) * 128), 128, ident3[:].rearrange("p j n -> p (j n)"), ident3_b,
                                              kts, MK, MK_b, slice(t * 128, (t + 1) * 128))
                                mem_attention(st, G * 128, slots, 1)
                        P.barrier()
                    if DEBUG and g == 0 and l == 0:
                        with ExitStack() as ds:
                            dm, dm_b = sb(ds, "dbgm", [128, 8, G * 128])
                            cp(dm[:], MIX[:], [MIX_b], [dm_b])
                            P.dma(POOL, dbg_mix[:, :, :], dm[:], dm_b, False, is_out=True)
                            P.barrier()
                    out_proj(l, G)
                    if DEBUG and g == 0 and l == 0:
                        P.dma(POOL, dbg_x1[:, :, :], X[:], X_b, False, is_out=True)
                    cp(UH[:], UHs[l][0][:], [UHs[l][1]], [UH_b])
                    ffn(l, G, False, cvp_d[l] if last else None)
                    cp(UHs[l][0][:], UH[:], [UH_b], [UHs[l][1]])
                    if DEBUG and g == 0 and l == 0:
                        P.dma(POOL, dbg_x2[:, :, :], X[:], X_b, False, is_out=True)
                for t in range(G):
                    P.dma(POOL, y_d[(g * G + t) * 128:(g * G + t + 1) * 128, :], X[:, t, :], X_b, False, is_out=True)
        except _Stop:
            P.barrier()
        P.finish()
    return nc


def _rope_tab(pos, half):
    inv = 10000.0 ** (-np.arange(half, dtype=np.float32) / half)
    ang = pos.astype(np.float32)[:, None] * inv[None, :]
    return np.cos(ang).astype(np.float32), np.sin(ang).astype(np.float32)


def _consts(NT, PASTLEN):
    c = {}
    c["ident"] = np.eye(128, dtype=np.float32)
    s = np.arange(128)[:, None]
    t = np.arange(128)[None, :]
    m = ((s // 32 == t // 32) & (s <= t)).astype(np.float32)
    c["mhg"] = np.tile(m, (1, 3))
    r = np.ones((128, 512), np.float32)
    r[:, ::32] = 0.0
    c["resetm"] = r
    v = np.zeros((128, 128), np.float32)
    for b in range(4):
        v[:, 32 * b + 2:32 * b + 6] = 1.0
    c["validm"] = v
    c["cmdiag"] = np.where(t <= s, 0.0, NEG).astype(np.float32)
    cm = np.full((128, 128), NEG, np.float32)
    for b in range(4):
        for tq in range(4):
            cm[32 * b + 2 + tq, 32 * b + 2:32 * b + 3 + tq] = 0.0
    c["cms"] = cm
    bo = np.zeros((128, 128), np.float32)
    bo[:64, :64] = 1.0
    bo[64:, 64:] = 1.0
    c["bones"] = bo
    m9 = np.zeros((128, 128), np.float32)
    m9[:, 96:] = 1.0
    c["m96"] = m9
    c["pidx"] = np.arange(128, dtype=np.float32).reshape(128, 1)
    T = NT * 128
    pos = np.zeros(T + 128, np.float32)
    pos[:T] = np.arange(T)
    for b in range(4):
        for tq in range(4):
            pos[T + 32 * b + 2 + tq] = PASTLEN + tq
    cq, sq = _rope_tab(pos, 64)
    ci, si = _rope_tab(pos, 32)
    c["tab"] = np.concatenate([np.tile(cq, (1, 6)), np.tile(sq, (1, 6)), np.tile(ci, (1, 8)), np.tile(si, (1, 8))], axis=1).astype(np.float32)
    return c


def _fm(v, nch):
    return np.ascontiguousarray(v.reshape(nch, 128).T)


def make_in_maps(inp, NT, NPAGE, NPOOL, ncores=8):
    f = lambda a: np.ascontiguousarray(np.asarray(a, dtype=np.float32))
    PASTLEN = NPAGE * 128
    cst = _consts(NT, PASTLEN)
    T = NT * 128
    shared = {}
    shared["w_in0"] = f(inp["w_in_hgrn"][0])
    shared["w_in1"] = f(inp["w_in_dsa"][0])
    shared["w_kv"] = f(inp["w_mem_kv"])
    shared["w_o"] = f(inp["w_out"])
    shared["w_g"] = f(inp["w_ffn_gate"])
    shared["w_u"] = f(inp["w_ffn_up"])
    shared["w_d"] = f(inp["w_ffn_down"])
    bc = lambda a: f(np.broadcast_to(np.asarray(a)[:, None, :], (a.shape[0], 128, a.shape[1])))
    shared["bc_nmix"] = bc(np.asarray(inp["norm_mix"]))
    shared["bc_nffn"] = bc(np.asarray(inp["norm_ffn"]))
    lb = np.asarray(inp["hgrn_lb_logits"], np.float32)
    shared["lbl"] = f(np.stack([_fm(lb[0], 6), _fm(lb[1], 6)], axis=1))
    shared["onorm"] = f(np.asarray(inp["hgrn_out_norm"])[0].reshape(128, 1))
    shared["bc_qn"] = f(np.broadcast_to(np.asarray(inp["dsa_q_norm"])[0][None, :], (128, 128)))
    shared["bc_kn"] = f(np.broadcast_to(np.asarray(inp["dsa_k_norm"])[0][None, :], (128, 128)))
    shared["bc_ikn"] = f(np.broadcast_to(np.asarray(inp["idx_k_norm"])[0][None, :], (128, 64)))
    mq = np.asarray(inp["mem_q_norm"], np.float32)
    shared["mqn"] = f(np.concatenate([mq, mq], axis=1).T)
    mk = np.asarray(inp["mem_k_norm"], np.float32)
    shared["bc_mkn"] = f(np.broadcast_to(np.tile(mk, (1, 4))[None, :, :], (128, 2, 256)))
    cwv = np.asarray(inp["ffn_conv_w"], np.float32)
    shared["cw"] = f(cwv.reshape(2, 3, NFC, 128).transpose(3, 0, 2, 1))
    cbv = np.asarray(inp["ffn_conv_b"], np.float32)
    shared["cb"] = f(cbv.reshape(2, NFC, 128).transpose(2, 0, 1))
    ck = np.asarray(inp["cache_k"], np.float32)[0]
    shared["pool_k"] = f(ck.transpose(0, 3, 2, 1)).reshape(NPOOL * 128, 256)
    shared["pool_v"] = f(np.asarray(inp["cache_v"], np.float32)[0].reshape(NPOOL * 128, 256))
    shared["pool_ik"] = f(np.asarray(inp["cache_idx_k"], np.float32)[0].transpose(0, 2, 1)).reshape(NPOOL * 64, 128)
    for k in ("ident", "mhg", "resetm", "validm", "cmdiag", "cms", "bones", "tab", "m96", "pidx"):
        shared[k] = cst[k]
    xs_all = np.asarray(inp["x_sample"], np.float32)
    xp = np.asarray(inp["x_prompt"], np.float32)
    memp = np.asarray(inp["mem_prompt"], np.float32)
    cmk = np.asarray(inp["cache_mem_k"], np.float32)
    cmv = np.asarray(inp["cache_mem_v"], np.float32)
    sth = np.asarray(inp["state_hgrn"], np.float32)[0]
    sfc = np.asarray(inp["state_ffn_conv"], np.float32)
    ptab = np.asarray(inp["page_table"]).astype(np.int32)
    nseq = xp.shape[0]
    maps = []
    for c in range(ncores):
        m = dict(shared)
        sq_ = c % nseq
        m["x"] = f(xp[sq_, :T])
        m["mem"] = f(memp[sq_])
        xs = np.zeros((128, D), np.float32)
        cstt = np.zeros((2, 128, NFC, 128), np.float32)
        for b in range(4):
            gb = 4 * c + b
            xs[32 * b + 2:32 * b + 6] = xs_all[gb]
            for l in range(2):
                stt_ = sfc[l, gb].reshape(2, NFC, 128)
                cstt[l, :, :, 32 * b:32 * b + 2] = stt_.transpose(2, 1, 0)
        m["xs"] = xs
        m["cst"] = cstt
        kk = cmk[:, 4 * c:4 * c + 4]
        kk = kk.reshape(2, 4, 256, 2, 2, 64)
        m["mkT_s"] = f(kk.transpose(0, 1, 4, 5, 3, 2).reshape(2, 4, 128, 2, 256))
        m["mv_s"] = f(cmv[:, 4 * c:4 * c + 4].reshape(2, 4, 256, 256))
        m["st_h"] = f(sth[4 * c:4 * c + 4])
        m["pt"] = np.ascontiguousarray(ptab[4 * c:4 * c + 4].reshape(1, -1))
        maps.append(m)
    return maps


_CACHE = {}


def run(inp, NT, NPAGE, NPOOL, NSEL, NBIS=10):
    key = (NT, NPAGE, NPOOL, NSEL, NBIS)
    if key not in _CACHE:
        _CACHE[key] = build(NT, NPAGE, NPOOL, NSEL, NBIS)
    nc = _CACHE[key]
    maps = make_in_maps(inp, NT, NPAGE, NPOOL)
    res = run_bass_kernel_spmd(nc, maps, core_ids=list(range(8)))
    return res.results


def assemble(R, NT, nseq=4):
    T = NT * 128
    y = np.stack([R[c]["y"] for c in range(nseq)])
    rows = np.array([32 * b + 2 + t for b in range(4) for t in range(4)])
    ys = np.concatenate([R[c]["ys"][rows].reshape(4, 4, D) for c in range(8)])
    hp = np.stack([R[c]["hstp"] for c in range(nseq)])[None]
    hs = np.concatenate([R[c]["hsts"] for c in range(8)])[None]
    kp = np.stack([R[c]["ko"].reshape(T, 2, 128) for c in range(nseq)])[None]
    vp = np.stack([R[c]["vo"].reshape(T, 2, 128) for c in range(nseq)])[None]
    ikp = np.stack([R[c]["iko"] for c in range(nseq)])[None]
    ks = np.concatenate([R[c]["kso"][rows].reshape(4, 4, 2, 128) for c in range(8)])[None]
    vs = np.concatenate([R[c]["vso"][rows].reshape(4, 4, 2, 128) for c in range(8)])[None]
    iks = np.concatenate([R[c]["ikso"][rows].reshape(4, 4, 64) for c in range(8)])[None]
    mk = np.stack([R[c]["mko"].reshape(2, 256, 4, 64) for c in range(nseq)], axis=1)
    mv = np.stack([R[c]["mvo"].reshape(2, 256, 4, 64) for c in range(nseq)], axis=1)
    cvp = np.stack([R[c]["cvp"][:, 126:128, :] for c in range(nseq)], axis=1)
    rows2 = np.array([32 * b + 4 + j for b in range(4) for j in range(2)])
    cvs = np.concatenate([R[c]["cvs"][:, rows2, :].reshape(2, 4, 2, DFF) for c in range(8)], axis=1)
    outs = (y, ys, hp, hs, kp, vp, ikp, ks, vs, iks, mk, mv, cvp, cvs)
    return tuple(np.ascontiguousarray(o.astype(np.float32)) for o in outs)


def kernel(**inputs):
    NT = 32
    NPAGE = 64
    NPOOL = int(np.asarray(inputs["cache_k"]).shape[1])
    R = run(inputs, NT, NPAGE, NPOOL, 256)
    return assemble(R, NT)
```

```python
import numpy as np
from contextlib import ExitStack
import concourse.bass as bass
import concourse.mybir as mybir
from concourse.bass_utils import run_bass_kernel_spmd

F32 = mybir.dt.float32
BF16 = mybir.dt.bfloat16
I32 = mybir.dt.int32
AF = mybir.ActivationFunctionType
ALU = mybir.AluOpType
AX = mybir.AxisListType

D = 1024
DFF = 2816
NFC = 22
HG_COLS = 3328
DSA_COLS = 2120
EPS = 1e-6
NEG = -1.0e30
DEBUG = False
STAGE = 99


class _Stop(Exception):
    pass


_PROG = [None]


def _stage(n):
    if STAGE <= n and not _PROG[0].stopped:
        _PROG[0].barrier()
        _PROG[0].stopped = True


class Buf:
    __slots__ = ("w", "r", "grp")

    def __init__(self):
        self.w = None
        self.r = {}
        self.grp = None


class DGroup:
    def __init__(self, sem):
        self.sem = sem
        self.cnt = 0


class Eng:
    def __init__(self, h, sem):
        self.h = h
        self.sem = sem
        self.cnt = 0
        self.known = {}


class Prog:
    def __init__(self, nc, es):
        self.nc = nc
        self.es = es
        self.nsem = 0
        self.PE = Eng(nc.tensor, self.newsem())
        self.ACT = Eng(nc.scalar, self.newsem())
        self.DVE = Eng(nc.vector, self.newsem())
        self.POOL = Eng(nc.gpsimd, self.newsem())
        self.SP = Eng(nc.sync, self.newsem())
        self.engs = [self.PE, self.ACT, self.DVE, self.POOL, self.SP]
        self.groups = []
        self.out_deps = []
        self.stopped = False
        self.free_groups = {}
        _PROG[0] = self

    def newsem(self):
        self.nsem += 1
        return self.es.enter_context(self.nc.semaphore("s%d" % self.nsem))

    def newgroup(self, kind="sp"):
        fl = self.free_groups.setdefault(kind, [])
        if fl:
            g = fl.pop()
            g.recycled = True
            return g
        g = DGroup(self.newsem())
        g.recycled = False
        g.kind = kind
        self.groups.append(g)
        return g

    def release(self, buf):
        if buf.grp is not None:
            self.free_groups[buf.grp.kind].append(buf.grp)
            buf.grp = None

    def _wait(self, eng, dep):
        sem, val = dep
        if isinstance(val, DGroup):
            val = val.cnt
        if sem is eng.sem and eng is self.PE:
            return
        k = id(sem)
        if eng.known.get(k, 0) >= val:
            return
        eng.h.wait_ge(sem, val)
        eng.known[k] = val

    def _deps(self, eng, r, w, skip_grp=None):
        if self.stopped:
            return
        for b in r:
            if b.w is not None:
                self._wait(eng, b.w)
        for b in w:
            if b.w is not None:
                if not (skip_grp is not None and b.w[0] is skip_grp.sem):
                    self._wait(eng, b.w)
            for sem, val in list(b.r.values()):
                self._wait(eng, (sem, val))

    def op(self, eng, fn, r=(), w=()):
        if self.stopped:
            return None
        self._deps(eng, r, w)
        ins = fn(eng.h)
        eng.cnt += 1
        ins.then_inc(eng.sem, 1)
        me = (eng.sem, eng.cnt)
        for b in w:
            b.w = me
            b.r = {}
        for b in r:
            if b not in w:
                b.r[id(eng.sem)] = me
        return ins

    def dma(self, q, out, in_, buf, load, grp=None, is_out=False, other=()):
        if self.stopped:
            return None
        if grp is None:
            if buf.grp is None:
                buf.grp = self.newgroup("pool" if q is self.POOL else "sp")
            grp = buf.grp
            assert (grp.kind == "pool") == (q is self.POOL), "mixed DMA queue kinds on one buffer"
        if getattr(grp, "recycled", False):
            self._wait(q, (grp.sem, grp.cnt))
            grp.recycled = False
        if load:
            self._deps(q, other, [buf], skip_grp=grp)
        else:
            self._deps(q, [buf] + list(other), [])
        ins = q.h.dma_start(out=out, in_=in_)
        grp.cnt += 16
        ins.then_inc(grp.sem, 16)
        if load:
            buf.w = (grp.sem, grp)
            buf.r = {}
        else:
            buf.r[id(grp.sem)] = (grp.sem, grp.cnt)
            if is_out:
                self.out_deps.append(grp)
        return ins

    def idma(self, out, in_, idx_ap, buf, other=()):
        q = self.POOL
        if self.stopped:
            return None
        if buf.grp is None:
            buf.grp = self.newgroup("pool")
        grp = buf.grp
        assert grp.kind == "pool"
        if getattr(grp, "recycled", False):
            self._wait(q, (grp.sem, grp.cnt))
            grp.recycled = False
        self._deps(q, other, [buf], skip_grp=grp)
        ins = q.h.indirect_dma_start(out=out, out_offset=None, in_=in_,
                                     in_offset=bass.IndirectOffsetOnAxis(ap=idx_ap, axis=0))
        grp.cnt += 16
        ins.then_inc(grp.sem, 16)
        buf.w = (grp.sem, grp)
        buf.r = {}
        return ins

    def barrier(self, force=False):
        if self.stopped and not force:
            return
        for e in self.engs:
            if (e is self.SP or e is self.PE) and not force:
                continue
            for o in self.engs:
                if o is not e and o.cnt > 0:
                    self._wait(e, (o.sem, o.cnt))
            for g in self.groups:
                if g.cnt > 0:
                    self._wait(e, (g.sem, g.cnt))

    def finish(self):
        self.barrier(force=True)


def build(NT, NPAGE, NPOOL, NSEL, NBIS, G=4, NSEL_S=256):
    T = NT * 128
    NG = NT // G
    PAST = NPAGE * 128
    nc = bass.Bass("TRN2", target_bir_lowering=False)

    def din(name, shape, dt=F32):
        return nc.dram_tensor(name, list(shape), dt, kind="ExternalInput").ap()

    def dout(name, shape, dt=F32):
        return nc.dram_tensor(name, list(shape), dt, kind="ExternalOutput").ap()

    x_d = din("x", [T, D])
    xs_d = din("xs", [128, D])
    w_in0 = din("w_in0", [D, HG_COLS])
    w_in1 = din("w_in1", [D, DSA_COLS])
    w_kv = din("w_kv", [2, D, 512])
    w_o = din("w_o", [2, D, D])
    w_g = din("w_g", [2, D, DFF])
    w_u = din("w_u", [2, D, DFF])
    w_d = din("w_d", [2, DFF, D])
    bc_nmix = din("bc_nmix", [2, 128, D])
    bc_nffn = din("bc_nffn", [2, 128, D])
    lbl_d = din("lbl", [128, 2, 6])
    onorm_d = din("onorm", [128, 1])
    bc_qn = din("bc_qn", [128, 128])
    bc_kn = din("bc_kn", [128, 128])
    bc_ikn = din("bc_ikn", [128, 64])
    mqn_d = din("mqn", [128, 2])
    bc_mkn = din("bc_mkn", [128, 2, 256])
    cw_d = din("cw", [128, 2, NFC, 3])
    cb_d = din("cb", [128, 2, NFC])
    mem_d = din("mem", [256, D])
    mkT_s = din("mkT_s", [2, 4, 128, 2, 256])
    mv_s = din("mv_s", [2, 4, 256, 256])
    st_h = din("st_h", [4, 6, 128, 128])
    cst_d = din("cst", [2, 128, NFC, 128])
    pt_d = din("pt", [1, 4 * NPAGE], I32)
    pool_ik = din("pool_ik", [NPOOL * 64, 128])
    pool_k = din("pool_k", [NPOOL * 128, 256])
    pool_v = din("pool_v", [NPOOL * 128, 256])
    pidx_d = din("pidx", [128, 1])
    tab_d = din("tab", [(NT + 1) * 128, 1280])
    ident_d = din("ident", [128, 128])
    mhg_d = din("mhg", [128, 384])
    resetm_d = din("resetm", [128, 512])
    validm_d = din("validm", [128, 128])
    cmdiag_d = din("cmdiag", [128, 128])
    cms_d = din("cms", [128, 128])
    bones_d = din("bones", [128, 128])
    m96_d = din("m96", [128, 128])

    y_d = dout("y", [T, D])
    ys_d = dout("ys", [128, D])
    hstp_d = dout("hstp", [6, 128, 128])
    hsts_d = dout("hsts", [4, 6, 128, 128])
    ko_d = dout("ko", [T, 256])
    vo_d = dout("vo", [T, 256])
    iko_d = dout("iko", [T, 64])
    kso_d = dout("kso", [128, 256])
    vso_d = dout("vso", [128, 256])
    ikso_d = dout("ikso", [128, 64])
    mko_d = dout("mko", [2, 256, 256])
    mvo_d = dout("mvo", [2, 256, 256])
    cvp_d = dout("cvp", [2, 128, DFF])
    cvs_d = dout("cvs", [2, 128, DFF])

    if DEBUG:
        dbg_mix = dout("dbg_mix", [128, 8, G * 128])
        dbg_x1 = dout("dbg_x1", [128, G, D])
        dbg_x2 = dout("dbg_x2", [128, G, D])
    es = ExitStack()
    with es:
        P = Prog(nc, es)
        PE, ACT, DVE, POOL, SP = P.PE, P.ACT, P.DVE, P.POOL, P.SP
        cgrp = P.newgroup("pool")

        uid = [0]

        def sb(stack, name, shape, dt=F32):
            uid[0] += 1
            t = stack.enter_context(nc.sbuf_tensor("%s_%d" % (name, uid[0]), list(shape), dt))
            b = Buf()
            if stack is not es:
                stack.callback(P.release, b)
            return t, b

        PS = []
        for i in range(8):
            t = es.enter_context(nc.psum_tensor("ps%d" % i, [128, 512], F32))
            PS.append((t, Buf()))
        psrr = [0]

        def pb():
            i = psrr[0]
            psrr[0] = (i + 1) % 8
            return PS[i]

        def act(out, in_, func, r, w, scale=1.0, bias=0.0, accum=None):
            kw = {}
            if accum is not None:
                kw["accum_out"] = accum
            return P.op(ACT, lambda h: h.activation(out=out, in_=in_, func=func, bias=bias, scale=scale, **kw), r, w)

        def tt(out, in0, in1, op, r, w, eng=None):
            return P.op(eng or DVE, lambda h: h.tensor_tensor(out=out, in0=in0, in1=in1, op=op), r, w)

        def tsc(out, in0, s1, s2, op0, op1, r, w, accum=None, eng=None):
            kw = {}
            if accum is not None:
                kw["accum_out"] = accum
            return P.op(eng or DVE, lambda h: h.tensor_scalar(out=out, in0=in0, scalar1=s1, scalar2=s2, op0=op0, op1=op1, **kw), r, w)

        def stt(out, in0, scalar, in1, op0, op1, r, w):
            return P.op(DVE, lambda h: h.scalar_tensor_tensor(out=out, in0=in0, scalar=scalar, in1=in1, op0=op0, op1=op1), r, w)

        def mm(out, lhsT, rhs, start, stop, r, w):
            return P.op(PE, lambda h: h.matmul(out, lhsT=lhsT, rhs=rhs, start=start, stop=stop), r, w)

        def cp(out, in_, r, w, eng=None):
            e = eng or DVE
            if e is ACT:
                return P.op(ACT, lambda h: h.activation(out=out, in_=in_, func=AF.Copy), r, w)
            return P.op(e, lambda h: h.tensor_copy(out=out, in_=in_), r, w)

        def const(name, src, shape, dt=F32, q=None):
            t, b = sb(es, name, shape, F32)
            P.dma(q or POOL, t[:], src, b, True, grp=cgrp)
            if dt is F32:
                return t, b
            t2, b2 = sb(es, name + "_bf", shape, dt)
            deferred.append(lambda: cp(t2[:], t[:], [b], [b2]))
            return t2, b2
        deferred = []

        identF, identF_b = const("identF", ident_d[:, :], [128, 128])
        mhg, mhg_b = const("mhg", mhg_d[:, 0:128], [128, 128])
        resetm, resetm_b = const("resetm", resetm_d[:, :], [128, 512])
        validm, validm_b = const("validm", validm_d[:, :], [128, 128])
        cmdiag, cmdiag_b = const("cmdiag", cmdiag_d[:, :], [128, 128])
        cms, cms_b = const("cms", cms_d[:, :], [128, 128])
        bones, bones_b = const("bones", bones_d[:, :], [128, 128], BF16)
        m96, m96_b = const("m96", m96_d[:, :], [128, 128])
        pidx, pidx_b = const("pidx", pidx_d[:, :], [128, 1])
        lbl, lbl_b = const("lblc", lbl_d[:, :, :], [128, 2, 6])
        onorm, onorm_b = const("onormc", onorm_d[:, :], [128, 1])
        gq, gq_b = const("gq", bc_qn[:, :], [128, 128])
        gk, gk_b = const("gk", bc_kn[:, :], [128, 128])
        gik, gik_b = const("gik", bc_ikn[:, :], [128, 64])
        mqn, mqn_b = const("mqnc", mqn_d[:, :], [128, 2])
        gmk, gmk_b = const("gmk", bc_mkn[:, :, :], [128, 2, 256])
        cw, cw_b = const("cwc", cw_d[:, :, :, :], [128, 2, NFC, 3])
        cbv, cbv_b = const("cbc", cb_d[:, :, :], [128, 2, NFC])
        for fn_ in deferred:
            fn_()
        identB, identB_b = sb(es, "identB", [128, 128], BF16)
        cp(identB[:], identF[:], [identF_b], [identB_b])
        onesB, onesB_b = sb(es, "onesB", [128, 128], BF16)
        P.op(DVE, lambda h: h.memset(onesB[:], 1.0), [], [onesB_b])
        GN, GN_b = sb(es, "GN", [128, D])
        gmix, gmix_b = bc_nmix, None
        gffn, gffn_b = bc_nffn, None
        ident3, ident3_b = sb(es, "ident3", [128, 3, 128], BF16)
        for j in range(3):
            cp(ident3[:, j, :], identF[:], [identF_b], [ident3_b])
        oml, oml_b = sb(es, "oml", [128, 6])
        tt(oml[:], lbl[:, 1, :], lbl[:, 0, :], ALU.subtract, [lbl_b], [oml_b])
        act(oml[:], oml[:], AF.Sigmoid, [oml_b], [oml_b])
        epsc, epsc_b = sb(es, "epsc", [128, 1])
        P.op(DVE, lambda h: h.memset(epsc[:], EPS), [], [epsc_b])

        NSLOT = 3
        ring = [sb(es, "wr%d" % i, [128, 8, 512], BF16) for i in range(NSLOT)]
        rr = [0]

        wconv = {}

        def convert(name, wap, K_, N_):
            wb_ = nc.dram_tensor(name + "_bf", [K_, N_], BF16, kind="Internal").ap()
            cb_ = Buf()
            cb_.grp = P.newgroup("pool")
            for r0 in range(0, K_, 128):
                if P.stopped:
                    break
                ins = POOL.h.dma_start(out=wb_[r0:r0 + 128, :], in_=wap[r0:r0 + 128, :], max_dma_last_dim=4096)
                cb_.grp.cnt += 16
                ins.then_inc(cb_.grp.sem, 16)
            cb_.w = (cb_.grp.sem, cb_.grp)
            wconv[name] = (wb_, cb_)

        def wblock(wname, r0, kc, c0, ncols):
            wap, cb_ = wconv[wname]
            t, b = ring[rr[0]]
            rr[0] = (rr[0] + 1) % NSLOT
            src = wap[r0:r0 + kc * 128, c0:c0 + ncols].rearrange("(k p) n -> p k n", p=128)
            P.dma(SP, t[:, 0:kc, 0:ncols], src, b, True, other=[cb_])
            return t, b

        convert("w_in0", w_in0, D, HG_COLS)
        for l in range(2):
            convert("w_kv%d" % l, w_kv[l], D, 512)
        convert("w_o0", w_o[0], D, D)
        convert("w_g0", w_g[0], D, DFF)
        convert("w_u0", w_u[0], D, DFF)
        convert("w_d0", w_d[0], DFF, D)
        convert("w_in1", w_in1, D, DSA_COLS)
        convert("w_o1", w_o[1], D, D)
        convert("w_g1", w_g[1], D, DFF)
        convert("w_u1", w_u[1], D, DFF)
        convert("w_d1", w_d[1], DFF, D)

        X, X_b = sb(es, "X", [128, G, D])
        HT, HT_b = sb(es, "HT", [128, 8, G * 128], BF16)
        MIX, MIX_b = sb(es, "MIX", [128, 8, G * 128], BF16)
        HN = [sb(es, "HN%d" % i, [128, D]) for i in range(1)]
        SS, SS_b = sb(es, "SS", [128, 16])
        Sf, Sf_b = [], []
        for h in range(6):
            t, b = sb(es, "Sf%d" % h, [128, 128])
            Sf.append(t); Sf_b.append(b)
        Sb = [[sb(es, "Sb%d_%d" % (h, i), [128, 128], BF16) for i in range(2)] for h in range(6)]
        UH, UH_b = sb(es, "UH", [128, NFC, 2])
        MEMT, MEMT_b = sb(es, "MEMT", [128, 8, 256], BF16)
        hn_i = [0]

        def rmsnorm_T(gt, ntile, gain, gain_b, lyr):
            P.dma(POOL, GN[:], gain[lyr], GN_b, True)
            gain_b = GN_b
            for t in range(ntile):
                hn, hn_b = HN[0]
                act(hn[:], X[:, t, :], AF.Square, [X_b], [hn_b, SS_b], accum=SS[:, 0:1])
                act(SS[:, 1:2], SS[:, 0:1], AF.Ln, [SS_b], [SS_b], scale=1.0 / D, bias=epsc[:, 0:1])
                act(SS[:, 2:3], SS[:, 1:2], AF.Exp, [SS_b], [SS_b], scale=-0.5)
                stt(hn[:], X[:, t, :], SS[:, 2:3], GN[:], ALU.mult, ALU.mult, [X_b, SS_b, gain_b], [hn_b])
                for half in range(2):
                    pt_, pb_ = pb()
                    for j in range(4):
                        kc = half * 4 + j
                        P.op(PE, lambda h: h.transpose(pt_[:, j * 128:(j + 1) * 128], hn[:, kc * 128:(kc + 1) * 128], identF[:]),
                             [hn_b, identF_b], [pb_])
                    act(HT[:, half * 4:half * 4 + 4, t * 128:(t + 1) * 128],
                        pt_[:].rearrange("p (k n) -> p k n", k=4), AF.Copy, [pb_], [HT_b])

        def proj_fm(wap, c0, nch, ntok, consume):
            ci = 0
            while ci < nch:
                nb = min(4, nch - ci)
                wt, wb = wblock(wap, 0, 8, c0 + ci * 128, nb * 128)
                for j in range(nb):
                    pt_, pb_ = pb()
                    for kc in range(8):
                        mm(pt_[:, 0:ntok], wt[:, kc, j * 128:(j + 1) * 128], HT[:, kc, 0:ntok], kc == 0, kc == 7, [wb, HT_b], [pb_])
                    consume(ci + j, pt_, pb_)
                ci += nb

        def proj_tm(wap, c0, ncols, ntile, consume, src=None, src_b=None, kchunks=8, r0=0):
            src = HT if src is None else src
            src_b = HT_b if src_b is None else src_b
            cb_ = 0
            c = 0
            while c < ncols:
                n = min(512, ncols - c)
                wt, wb = wblock(wap, r0, kchunks, c0 + c, n)
                for t in range(ntile):
                    pt_, pb_ = pb()
                    for kc in range(kchunks):
                        mm(pt_[:, 0:n], src[:, kc, t * 128:(t + 1) * 128], wt[:, kc, 0:n], kc == 0, kc == kchunks - 1, [wb, src_b], [pb_])
                    consume(cb_, c, n, t, pt_, pb_)
                c += n
                cb_ += 1

        def mem_attention(st, ntok, slots, l):
            MQ, MQ_b = st["MQ"]
            sq, sq_b = st["mq_sq"]
            rs, rs_b = st["mq_rs"]
            mqb, mqb_b = st["mq_bf"]
            for c in range(2):
                act(sq[:, 0:ntok], MQ[:, c, 0:ntok], AF.Square, [MQ_b], [sq_b])
                pt_, pb_ = pb()
                mm(pt_[:, 0:ntok], bones[:], sq[:, 0:ntok], True, True, [bones_b, sq_b], [pb_])
                act(rs[:, 0:ntok], pt_[:, 0:ntok], AF.Ln, [pb_], [rs_b], scale=1.0 / 64, bias=epsc[:, 0:1])
                act(rs[:, 0:ntok], rs[:, 0:ntok], AF.Exp, [rs_b], [rs_b], scale=-0.5)
                stt(mqb[:, c, 0:ntok], MQ[:, c, 0:ntok], mqn[:, l:l + 1], rs[:, 0:ntok], ALU.mult, ALU.mult, [MQ_b, mqn_b, rs_b], [mqb_b])
            mexps = st["mexp"]
            mei = [0]
            rd, rd_b = st["mrd"]
            for (c0, ncol, MKT, MKT_b, MV, MV_b) in slots:
                for c in range(2):
                    pn, pn_b = pb()
                    pd, pd_b = pb()
                    for j in range(2):
                        hh = 2 * c + j
                        lo, hi = 64 * j, 64 * j + 64
                        for mt in range(2):
                            ps_, ps_b = pb()
                            pe_t, pe_b = mexps[mei[0] % len(mexps)]
                            mei[0] += 1
                            mm(ps_[:, 0:ncol], MKT[lo:hi, c, mt * 128:(mt + 1) * 128], mqb[lo:hi, c, c0:c0 + ncol], True, True,
                               [MKT_b, mqb_b], [ps_b])
                            act(pe_t[:, 0:ncol], ps_[:, 0:ncol], AF.Exp, [ps_b], [pe_b], scale=0.125)
                            mm(pn[lo:hi, 0:ncol], MV[:, mt, hh * 64:(hh + 1) * 64], pe_t[:, 0:ncol], mt == 0, mt == 1, [MV_b, pe_b], [pn_b])
                            mm(pd[lo:hi, 0:ncol], onesB[:, 0:64], pe_t[:, 0:ncol], mt == 0, mt == 1, [onesB_b, pe_b], [pd_b])
                    act(rd[:, 0:ncol], pd[:, 0:ncol], AF.Ln, [pd_b], [rd_b])
                    act(rd[:, 0:ncol], rd[:, 0:ncol], AF.Exp, [rd_b], [rd_b], scale=-1.0)
                    tt(MIX[:, 6 + c, c0:c0 + ncol], pn[:, 0:ncol], rd[:, 0:ncol], ALU.mult, [pn_b, rd_b], [MIX_b])

        def out_proj(l, ntile):
            def consume(cb_, c, n, t, pt_, pb_):
                tt(X[:, t, c:c + n], X[:, t, c:c + n], pt_[:, 0:n], ALU.add, [X_b, pb_], [X_b])
            proj_tm("w_o%d" % l, 0, D, ntile, consume, src=MIX, src_b=MIX_b)

        def ffn(l, ntile, sample, cv_out):
            ntok = ntile * 128
            with ExitStack() as fs:
                AT, AT_b0 = sb(fs, "AT", [128, NFC, ntok], BF16)
                AT_bs = [Buf() for _ in range(NFC)]
                Uc = [sb(fs, "Uc%d" % i, [128, 2 + ntok]) for i in range(2)]
                Cc = [sb(fs, "Cc%d" % i, [128, ntok]) for i in range(2)]
                Sg = [sb(fs, "Sg%d" % i, [128, ntok]) for i in range(4)]
                UTOK, UTOK_b = sb(fs, "UTOK", [128, DFF])
                if sample:
                    CST, CST_b = sb(fs, "CST", [128, NFC, 128])
                    P.dma(POOL, CST[:], cst_d[l], CST_b, True)
                rmsnorm_T(None, ntile, gffn, gffn_b, l)
                gps = {}

                def cons_g(ci, pt_, pb_):
                    u, u_b = Uc[ci % 2]
                    cp(u[:, 0:2], UH[:, ci, :], [UH_b], [u_b])
                    if sample:
                        tt(u[:, 2:2 + ntok], pt_[:, 0:ntok], validm[:, 0:ntok], ALU.mult, [pb_, validm_b], [u_b])
                        tt(u[:, 2:2 + ntok], u[:, 2:2 + ntok], CST[:, ci, :], ALU.add, [u_b, CST_b], [u_b])
                    else:
                        act(u[:, 2:2 + ntok], pt_[:, 0:ntok], AF.Copy, [pb_], [u_b])
                    cp(UH[:, ci, :], u[:, ntok:ntok + 2], [u_b], [UH_b])
                    c_, c_b = Cc[ci % 2]
                    if sample:
                        tsc(c_[:], u[:, 2:2 + ntok], cw[:, l, ci, 2:3], cbv[:, l, ci:ci + 1], ALU.mult, ALU.add, [u_b, cw_b, cbv_b], [c_b])
                    else:
                        act(c_[:], pt_[:, 0:ntok], AF.Identity, [pb_, cw_b, cbv_b], [c_b], scale=cw[:, l, ci, 2:3], bias=cbv[:, l, ci:ci + 1])
                    stt(c_[:], u[:, 1:1 + ntok], cw[:, l, ci, 1:2], c_[:], ALU.mult, ALU.add, [u_b, cw_b], [c_b])
                    stt(c_[:], u[:, 0:ntok], cw[:, l, ci, 0:1], c_[:], ALU.mult, ALU.add, [u_b, cw_b], [c_b])
                    s_, s_b = Sg[ci % 4]
                    act(s_[:], c_[:], AF.Silu, [c_b], [s_b])
                    if cv_out is not None:
                        pq, pq_b = pb()
                        P.op(PE, lambda h: h.transpose(pq[:, 0:128], u[:, 2 + ntok - 128:2 + ntok], identF[:]), [u_b, identF_b], [pq_b])
                        act(UTOK[:, ci * 128:(ci + 1) * 128], pq[:, 0:128], AF.Copy, [pq_b], [UTOK_b])
                    gps[ci] = (s_, s_b)

                ci = 0
                while ci < NFC:
                    nb = min(4, NFC - ci)
                    wt, wb = wblock("w_g%d" % l, 0, 8, ci * 128, nb * 128)
                    for j in range(nb):
                        pt_, pb_ = pb()
                        for kc in range(8):
                            mm(pt_[:, 0:ntok], wt[:, kc, j * 128:(j + 1) * 128], HT[:, kc, 0:ntok], kc == 0, kc == 7, [wb, HT_b], [pb_])
                        cons_g(ci + j, pt_, pb_)
                    wt, wb = wblock("w_u%d" % l, 0, 8, ci * 128, nb * 128)
                    for j in range(nb):
                        pt_, pb_ = pb()
                        for kc in range(8):
                            mm(pt_[:, 0:ntok], wt[:, kc, j * 128:(j + 1) * 128], HT[:, kc, 0:ntok], kc == 0, kc == 7, [wb, HT_b], [pb_])
                        s_, s_b = gps[ci + j]
                        tt(AT[:, ci + j, :], s_[:], pt_[:, 0:ntok], ALU.mult, [s_b, pb_], [AT_bs[ci + j]])
                    ci += nb
                if cv_out is not None:
                    P.dma(POOL, cv_out, UTOK[:], UTOK_b, False, is_out=True)
                for cbk in range(2):
                    accs = [pb() for _ in range(ntile)]
                    for kh, (k0, kn) in enumerate(((0, 8), (8, 8), (16, 6))):
                        wt, wb = wblock("w_d%d" % l, k0 * 128, kn, cbk * 512, 512)
                        for t in range(ntile):
                            pt_, pb_ = accs[t]
                            for k in range(kn):
                                fc = k0 + k
                                mm(pt_[:, :], AT[:, fc, t * 128:(t + 1) * 128], wt[:, k, :], fc == 0, fc == NFC - 1, [wb, AT_bs[fc]], [pb_])
                    for t in range(ntile):
                        pt_, pb_ = accs[t]
                        tt(X[:, t, cbk * 512:(cbk + 1) * 512], X[:, t, cbk * 512:(cbk + 1) * 512], pt_[:, :], ALU.add, [X_b, pb_], [X_b])
            P.barrier()

        def layer0(ntile, sample, mslots, first, last):
            ntok = ntile * 128
            nch = ntok // 32
            with ExitStack() as ls:
                st = {}
                st["MQ"] = sb(ls, "MQ", [128, 2, ntok])
                st["mq_sq"] = sb(ls, "mq_sq", [128, ntok], BF16)
                st["mq_rs"] = sb(ls, "mq_rs", [128, ntok])
                st["mq_bf"] = sb(ls, "mq_bf", [128, 2, ntok], BF16)
                st["mexp"] = [sb(ls, "mexp%d" % i, [128, ntok], BF16) for i in range(2)]
                st["mrd"] = sb(ls, "mrd", [128, ntok])
                QS, QS_b = sb(ls, "QS", [128, 6, ntok], BF16)
                GS, GS_b = sb(ls, "GS", [128, 6, ntok], BF16)
                VBh, VBh_b = sb(ls, "VBh", [128, ntile, 768], BF16)
                QD, QD_b = sb(ls, "QD", [128, 6, ntok], BF16)
                KD, KD_b = sb(ls, "KD", [128, 6, ntok], BF16)
                EBL, EBL_b = sb(ls, "EBL", [128, 6, nch])
                tmp = [sb(ls, "l0t%d" % i, [128, ntok]) for i in range(4)]
                K2, K2_b = sb(ls, "K2", [128, ntok])
                K2m, K2m_b = sb(ls, "K2m", [128, 128])
                K2T, K2T_b = sb(ls, "K2T", [128, ntile, 768], BF16)
                K2T3, K2T3_b = sb(ls, "K2T3", [128, ntile, 768], BF16)
                SMs = [sb(ls, "SM%d" % i, [128, 128], BF16) for i in range(3)]
                Ohs = [sb(ls, "Oh%d" % i, [128, ntok]) for i in range(3)]
                Osq, Osq_b = sb(ls, "Osq", [128, ntok], BF16)
                Ors, Ors_b = sb(ls, "Ors", [128, ntok])
                MQ, MQ_b = st["MQ"]

                rmsnorm_T(None, ntile, gmix, gmix_b, 0)
                _stage(1.20)

                def cons_q(ci, pt_, pb_):
                    act(QS[:, ci, :], pt_[:, 0:ntok], AF.Silu, [pb_], [QS_b])
                proj_fm("w_in0", 0, 6, ntok, cons_q)
                _stage(1.21)

                def cons_f(h, pt_, pb_):
                    a, a_b = tmp[0]
                    kk, kk_b = tmp[1]
                    lf, lf_b = tmp[2]
                    bb, bb_b = tmp[0]
                    eb, eb_b = tmp[3]
                    en, en_b = tmp[2]
                    act(a[:], pt_[:, 0:ntok], AF.Sigmoid, [pb_], [a_b], scale=-1.0)
                    tsc(kk[:], a[:], oml[:, h:h + 1], None, ALU.mult, ALU.bypass, [a_b, oml_b], [kk_b])
                    _stage(1.211)
                    act(lf[:], kk[:], AF.Ln, [kk_b], [lf_b], scale=-1.0, bias=1.0)
                    if sample:
                        tt(lf[:], lf[:], validm[:, 0:ntok], ALU.mult, [lf_b, validm_b], [lf_b])
                    _stage(1.212)
                    P.op(DVE, lambda hd: hd.tensor_tensor_scan(out=bb[:], data0=resetm[:, 0:ntok], data1=lf[:], initial=0.0,
                                                               op0=ALU.mult, op1=ALU.add), [resetm_b, lf_b], [bb_b])
                    _stage(1.213)
                    act(eb[:], bb[:], AF.Exp, [bb_b], [eb_b])
                    act(en[:], bb[:], AF.Exp, [bb_b], [en_b], scale=-1.0)
                    tt(QD[:, h, :], QS[:, h, :], eb[:], ALU.mult, [QS_b, eb_b], [QD_b])
                    tt(KD[:, h, :], kk[:], en[:], ALU.mult, [kk_b, en_b], [KD_b])
                    _stage(1.214)
                    cp(EBL[:, h, :], eb[:].rearrange("p (c j) -> p c j", j=32)[:, :, 31], [eb_b], [EBL_b])
                    _stage(1.215)
                    tt(K2[:].rearrange("p (c j) -> p c j", j=32), KD[:, h, :].rearrange("p (c j) -> p c j", j=32),
                       EBL[:, h, :].unsqueeze(2).to_broadcast([128, nch, 32]), ALU.mult, [KD_b, EBL_b], [K2_b])
                    _stage(1.216)
                    for t in range(ntile):
                        pq, pq_b = pb()
                        P.op(PE, lambda hd: hd.transpose(pq[:, 0:128], K2[:, t * 128:(t + 1) * 128], identF[:]), [K2_b, identF_b], [pq_b])
                        act(K2T[:, t, h * 128:(h + 1) * 128], pq[:, 0:128], AF.Copy, [pq_b], [K2T_b])
                        _stage(1.217)
                        tt(K2m[:], K2[:, t * 128:(t + 1) * 128], m96[:], ALU.mult, [K2_b, m96_b], [K2m_b])
                        pq2, pq2_b = pb()
                        P.op(PE, lambda hd: hd.transpose(pq2[:, 0:128], K2m[:], identF[:]), [K2m_b, identF_b], [pq2_b])
                        act(K2T3[:, t, h * 128:(h + 1) * 128], pq2[:, 0:128], AF.Copy, [pq2_b], [K2T3_b])
                    _stage(1.2171 + 0.0001 * h)
                proj_fm("w_in0", 768, 6, ntok, cons_f)
                _stage(1.22)

                def cons_g(ci, pt_, pb_):
                    act(GS[:, ci, :], pt_[:, 0:ntok], AF.Silu, [pb_], [GS_b])
                proj_fm("w_in0", 2304, 6, ntok, cons_g)

                def cons_m(ci, pt_, pb_):
                    act(MQ[:, ci, :], pt_[:, 0:ntok], AF.Copy, [pb_], [MQ_b])
                proj_fm("w_in0", 3072, 2, ntok, cons_m)

                def cons_i(cb_, c, n, t, pt_, pb_):
                    act(VBh[:, t, c:c + n], pt_[:, 0:n], AF.Copy, [pb_], [VBh_b])
                proj_tm("w_in0", 1536, 768, ntile, cons_i)
                _stage(1.23)

                for hh in range(2):
                    heads = [3 * hh + i for i in range(3)]
                    par = {h: 0 for h in heads}
                    if first and not sample:
                        for h in heads:
                            P.op(DVE, lambda hd: hd.memset(Sf[h][:], 0.0), [], [Sf_b[h]])
                            P.op(DVE, lambda hd: hd.memset(Sb[h][0][0][:], 0.0), [], [Sb[h][0][1]])
                    k_ps = 0
                    k_pu = 0
                    for t in range(ntile):
                        cols = slice(t * 128, (t + 1) * 128)
                        pos = {}
                        for i, h in enumerate(heads):
                            ps_, ps_b = PS[3 + (k_ps % 2)]
                            k_ps += 1
                            sm, sm_b = SMs[i]
                            mm(ps_[:, 0:128], KD[:, h, cols], QD[:, h, cols], True, True, [KD_b, QD_b], [ps_b])
                            tt(sm[:], ps_[:, 0:128], mhg[:], ALU.mult, [ps_b, mhg_b], [sm_b])
                            po, po_b = PS[i]
                            pos[h] = (po, po_b)
                            mm(po[:, 0:128], VBh[:, t, h * 128:(h + 1) * 128], sm[:], True, False, [VBh_b, sm_b], [po_b])
                        for c in range(4):
                            ccols = slice(t * 128 + c * 32, t * 128 + c * 32 + 32)
                            for i, h in enumerate(heads):
                                po, po_b = pos[h]
                                if sample:
                                    P.dma(POOL, Sf[h][:], st_h[c, h], Sf_b[h], True)
                                    par[h] = 0
                                    cp(Sb[h][0][0][:], Sf[h][:], [Sf_b[h]], [Sb[h][0][1]], eng=ACT)
                                sbt, sbb = Sb[h][par[h]]
                                mm(po[:, c * 32:(c + 1) * 32], sbt[:], QD[:, h, ccols], False, c == 3, [sbb, QD_b], [po_b])
                                pu, pu_b = PS[5 + (k_pu % 3)]
                                k_pu += 1
                                if c < 3:
                                    mm(pu[:, 0:128], K2T[c * 32:(c + 1) * 32, t, h * 128:(h + 1) * 128],
                                       VBh[c * 32:(c + 1) * 32, t, h * 128:(h + 1) * 128], True, True, [K2T_b, VBh_b], [pu_b])
                                else:
                                    mm(pu[:, 0:128], K2T3[64:128, t, h * 128:(h + 1) * 128],
                                       VBh[64:128, t, h * 128:(h + 1) * 128], True, True, [K2T3_b, VBh_b], [pu_b])
                                stt(Sf[h][:], Sf[h][:], EBL[:, h, t * 4 + c:t * 4 + c + 1], pu[:, 0:128], ALU.mult, ALU.add,
                                    [Sf_b[h], EBL_b, pu_b], [Sf_b[h]])
                                par[h] ^= 1
                                cp(Sb[h][par[h]][0][:], Sf[h][:], [Sf_b[h]], [Sb[h][par[h]][1]], eng=ACT)
                                if sample:
                                    P.dma(POOL, hsts_d[c, h], Sf[h][:], Sf_b[h], False, is_out=True)
                        for i, h in enumerate(heads):
                            po, po_b = pos[h]
                            act(Ohs[i][0][:, cols], po[:, 0:128], AF.Copy, [po_b], [Ohs[i][1]])
                    for i, h in enumerate(heads):
                        Oh, Oh_b = Ohs[i]
                        if par[h] == 1:
                            cp(Sb[h][0][0][:], Sf[h][:], [Sf_b[h]], [Sb[h][0][1]], eng=ACT)
                        if last and not sample:
                            P.dma(POOL, hstp_d[h], Sf[h][:], Sf_b[h], False, is_out=True)
                        act(Osq[:], Oh[:], AF.Square, [Oh_b], [Osq_b])
                        pn_, pn_b = pb()
                        mm(pn_[:, 0:ntok], onesB[:], Osq[:], True, True, [onesB_b, Osq_b], [pn_b])
                        act(Ors[:], pn_[:, 0:ntok], AF.Ln, [pn_b], [Ors_b], scale=1.0 / 128, bias=epsc[:, 0:1])
                        act(Ors[:], Ors[:], AF.Exp, [Ors_b], [Ors_b], scale=-0.5)
                        tt(Oh[:], Oh[:], Ors[:], ALU.mult, [Oh_b, Ors_b], [Oh_b])
                        stt(MIX[:, h, 0:ntok], Oh[:], onorm[:, 0:1], GS[:, h, :], ALU.mult, ALU.mult, [Oh_b, onorm_b, GS_b], [MIX_b])

                _stage(1.24)
                mem_attention(st, ntok, mslots, 0)
            P.barrier()

        def memkv_prompt(l, MKT, MKT_b, MV, MV_b):
            with ExitStack() as ms:
                KVt, KVt_b = sb(ms, "KVt", [128, 512])
                KN, KN_b = sb(ms, "KN", [128, 256])
                sq, sq_b = sb(ms, "kvsq", [128, 64])
                ss4, ss4_b = sb(ms, "ss4", [128, 8])

                def consume(cb_, c, n, t, pt_, pb_):
                    act(KVt[:], pt_[:, :], AF.Copy, [pb_], [KVt_b])
                    for hh in range(4):
                        act(sq[:], KVt[:, hh * 64:(hh + 1) * 64], AF.Square, [KVt_b], [sq_b, ss4_b], accum=ss4[:, hh:hh + 1])
                    act(ss4[:, 4:8], ss4[:, 0:4], AF.Ln, [ss4_b], [ss4_b], scale=1.0 / 64, bias=epsc[:, 0:1])
                    act(ss4[:, 4:8], ss4[:, 4:8], AF.Exp, [ss4_b], [ss4_b], scale=-0.5)
                    for hh in range(4):
                        stt(KN[:, hh * 64:(hh + 1) * 64], KVt[:, hh * 64:(hh + 1) * 64], ss4[:, 4 + hh:5 + hh], gmk[:, l, hh * 64:(hh + 1) * 64],
                            ALU.mult, ALU.mult, [KVt_b, ss4_b, gmk_b], [KN_b])
                    P.dma(POOL, mko_d[l, t * 128:(t + 1) * 128, :], KN[:], KN_b, False, is_out=True)
                    P.dma(POOL, mvo_d[l, t * 128:(t + 1) * 128, :], KVt[:, 256:512], KVt_b, False, is_out=True)
                    cp(MV[:, t, :], KVt[:, 256:512], [KVt_b], [MV_b])
                    for c2 in range(2):
                        pq, pq_b = pb()
                        P.op(PE, lambda h: h.transpose(pq[:, 0:128], KN[:, c2 * 128:(c2 + 1) * 128], identF[:]), [KN_b, identF_b], [pq_b])
                        act(MKT[:, c2, t * 128:(t + 1) * 128], pq[:, 0:128], AF.Copy, [pq_b], [MKT_b])
                proj_tm("w_kv%d" % l, 0, 512, 2, consume, src=MEMT, src_b=MEMT_b)
            P.barrier()

        def dsa_inproj(ntile, tile0, st, sample, KT, KT_b, VB, VB_b, IKT, IKT_b):
            ntok = ntile * 128
            Z, Z_b = st["Z"]
            QN, QN_b = st["QN"]
            QR, QR_b = st["QR"]
            TB, TB_b = st["TB"]
            T1, T1_b = st["T1"]
            T2, T2_b = st["T2"]
            QT, QT_b = st["QT"]
            IQT, IQT_b = st["IQT"]
            IW, IW_b = st["IW"]
            MQ, MQ_b = st["MQ"]
            ss, ss_b = st["ss16"]
            sq, sq_b = st["sqj"]
            rmsnorm_T(None, ntile, gmix, gmix_b, 1)

            def cons_m(ci, pt_, pb_):
                act(MQ[:, ci, 0:ntok], pt_[:, 0:ntok], AF.Copy, [pb_], [MQ_b])
            proj_fm("w_in1", 1864, 2, ntok, cons_m)

            blocks = [(0, 512), (512, 512), (1024, 512), (1536, 328)]
            wts = []
            Zs = st["Zs"]

            def consume(cb_, c, n, t, pt_, pb_):
                z, z_b = Zs[t]
                act(z[:, c:c + n], pt_[:, 0:n], AF.Copy, [pb_], [z_b])
            proj_tm("w_in1", 0, 1864, ntile, consume)

            for t in range(ntile):
                z, z_b = Zs[t]
                gt = tile0 + t
                trow = (NT * 128) if sample else gt * 128
                P.dma(POOL, TB[:], tab_d[trow:trow + 128, :], TB_b, True)
                for j in range(8):
                    act(sq[:, 0:128], z[:, j * 128:(j + 1) * 128], AF.Square, [z_b], [sq_b, ss_b], accum=ss[:, j:j + 1])
                act(sq[:, 0:64], z[:, 1792:1856], AF.Square, [z_b], [sq_b, ss_b], accum=ss[:, 8:9])
                act(ss[:, 0:8], ss[:, 0:8], AF.Ln, [ss_b], [ss_b], scale=1.0 / 128, bias=epsc[:, 0:1])
                act(ss[:, 8:9], ss[:, 8:9], AF.Ln, [ss_b], [ss_b], scale=1.0 / 64, bias=epsc[:, 0:1])
                act(ss[:, 0:9], ss[:, 0:9], AF.Exp, [ss_b], [ss_b], scale=-0.5)
                for j in range(6):
                    stt(QN[:, j * 128:(j + 1) * 128], z[:, j * 128:(j + 1) * 128], ss[:, j:j + 1], gq[:], ALU.mult, ALU.mult, [z_b, ss_b, gq_b], [QN_b])
                for j in range(2):
                    stt(QN[:, 768 + j * 128:768 + (j + 1) * 128], z[:, 768 + j * 128:768 + (j + 1) * 128], ss[:, 6 + j:7 + j], gk[:],
                        ALU.mult, ALU.mult, [z_b, ss_b, gk_b], [QN_b])
                cp(QN[:, 1024:1536], z[:, 1280:1792], [z_b], [QN_b], eng=POOL)
                stt(QN[:, 1536:1600], z[:, 1792:1856], ss[:, 8:9], gik[:], ALU.mult, ALU.mult, [z_b, ss_b, gik_b], [QN_b])
                cp(IW[:, t, :], z[:, 1856:1864], [z_b], [IW_b], eng=POOL)

                def rope(c0, nh, half, cc, sc):
                    w_ = nh * half
                    xv = QN[:, c0:c0 + 2 * w_].rearrange("p (h two d) -> p h two d", h=nh, two=2)
                    ov = QR[:, c0:c0 + 2 * w_].rearrange("p (h two d) -> p h two d", h=nh, two=2)
                    cosv = TB[:, cc:cc + w_].rearrange("p (h d) -> p h d", h=nh)
                    sinv = TB[:, sc:sc + w_].rearrange("p (h d) -> p h d", h=nh)
                    t1 = T1[:, 0:w_].rearrange("p (h d) -> p h d", h=nh)
                    t2 = T2[:, 0:w_].rearrange("p (h d) -> p h d", h=nh)
                    x1, x2 = xv[:, :, 0, :], xv[:, :, 1, :]
                    tt(t1, x1, cosv, ALU.mult, [QN_b, TB_b], [T1_b])
                    tt(t2, x2, sinv, ALU.mult, [QN_b, TB_b], [T2_b], eng=POOL)
                    tt(ov[:, :, 0, :], t1, t2, ALU.subtract, [T1_b, T2_b], [QR_b])
                    tt(t1, x2, cosv, ALU.mult, [QN_b, TB_b], [T1_b])
                    tt(t2, x1, sinv, ALU.mult, [QN_b, TB_b], [T2_b], eng=POOL)
                    tt(ov[:, :, 1, :], t1, t2, ALU.add, [T1_b, T2_b], [QR_b])
                rope(0, 6, 64, 0, 384)
                rope(768, 2, 64, 0, 384)
                rope(1024, 8, 32, 768, 1024)
                rope(1536, 1, 32, 768, 1024)
                if sample:
                    P.dma(POOL, kso_d[:, :], QR[:, 768:1024], QR_b, False, is_out=True)
                    P.dma(POOL, vso_d[:, :], z[:, 1024:1280], z_b, False, is_out=True)
                    P.dma(POOL, ikso_d[:, :], QR[:, 1536:1600], QR_b, False, is_out=True)
                else:
                    P.dma(POOL, ko_d[gt * 128:(gt + 1) * 128, :], QR[:, 768:1024], QR_b, False, is_out=True)
                    P.dma(POOL, vo_d[gt * 128:(gt + 1) * 128, :], z[:, 1024:1280], z_b, False, is_out=True)
                    P.dma(POOL, iko_d[gt * 128:(gt + 1) * 128, :], QR[:, 1536:1600], QR_b, False, is_out=True)
                kcol = 0 if sample else gt * 128
                cp(VB[:, 0 if sample else gt, :], z[:, 1024:1280], [z_b], [VB_b[0 if sample else gt]], eng=ACT)
                for half_ in range(2):
                    pt_, pb_ = pb()
                    nh = 4 if half_ == 0 else 2
                    for j in range(nh):
                        hq = half_ * 4 + j
                        P.op(PE, lambda h: h.transpose(pt_[:, j * 128:(j + 1) * 128], QR[:, hq * 128:(hq + 1) * 128], identF[:]), [QR_b, identF_b], [pb_])
                    act(QT[:, half_ * 4:half_ * 4 + nh, t * 128:(t + 1) * 128], pt_[:, 0:nh * 128].rearrange("p (k n) -> p k n", k=nh),
                        AF.Copy, [pb_], [QT_b])
                pt_, pb_ = pb()
                for j in range(2):
                    P.op(PE, lambda h: h.transpose(pt_[:, j * 128:(j + 1) * 128], QR[:, 768 + j * 128:768 + (j + 1) * 128], identF[:]), [QR_b, identF_b], [pb_])
                P.op(PE, lambda h: h.transpose(pt_[0:64, 256:384], QR[:, 1536:1600], identF[:]), [QR_b, identF_b], [pb_])
                act(KT[:, :, kcol:kcol + 128], pt_[:, 0:256].rearrange("p (k n) -> p k n", k=2), AF.Copy, [pb_], [KT_b[0 if sample else gt]])
                act(IKT[0:64, kcol:kcol + 128], pt_[0:64, 256:384], AF.Copy, [pb_], [IKT_b[0 if sample else gt]])
                for half_ in range(2):
                    pt_, pb_ = pb()
                    for j in range(4):
                        hq = half_ * 4 + j
                        P.op(PE, lambda h: h.transpose(pt_[0:64, j * 128:(j + 1) * 128], QR[:, 1024 + hq * 64:1024 + (hq + 1) * 64], identF[:]),
                             [QR_b, identF_b], [pb_])
                    act(IQT[0:64, half_ * 4:half_ * 4 + 4, t * 128:(t + 1) * 128], pt_[0:64, :].rearrange("p (k n) -> p k n", k=4),
                        AF.Copy, [pb_], [IQT_b])

        def index_scores(st, t, keysets, SC, SC_b, row_lo=0, row_hi=128):
            IQT, IQT_b = st["IQT"]
            IW, IW_b = st["IW"]
            R = st["R"]
            ri = 0
            for (rhsf, kb_, ncols, sc0) in keysets:
                k0 = 0
                while k0 < ncols:
                    n = min(512, ncols - k0)
                    for h in range(8):
                        pt_, pb_ = pb()
                        mm(pt_[:, 0:n], IQT[0:64, h, t * 128:(t + 1) * 128], rhsf(k0, n), True, True, [IQT_b, kb_], [pb_])
                        r_, r_b = R[ri % 4]
                        ri += 1
                        act(r_[row_lo:row_hi, 0:n], pt_[row_lo:row_hi, 0:n], AF.Relu, [pb_], [r_b])
                        dst = SC[row_lo:row_hi, sc0 + k0:sc0 + k0 + n]
                        if h == 0:
                            tsc(dst, r_[row_lo:row_hi, 0:n], IW[row_lo:row_hi, t, 0:1], None, ALU.mult, ALU.bypass, [r_b, IW_b], [SC_b])
                        else:
                            stt(dst, r_[row_lo:row_hi, 0:n], IW[row_lo:row_hi, t, h:h + 1], dst, ALU.mult, ALU.add, [r_b, IW_b], [SC_b])
                    k0 += n

        def select_mask(st, L, SC, SC_b, MK, MK_b, cm, cm_b, cmcol, nsel):
            bs, bs_b = st["bs"]
            P.op(DVE, lambda h: h.tensor_reduce(out=bs[:, 0:1], in_=SC[:, 0:L], axis=AX.X, op=ALU.min), [SC_b], [bs_b])
            tt(SC[:, cmcol:cmcol + 128], SC[:, cmcol:cmcol + 128], cm[:], ALU.add, [SC_b, cm_b], [SC_b])
            P.op(DVE, lambda h: h.tensor_reduce(out=bs[:, 1:2], in_=SC[:, 0:L], axis=AX.X, op=ALU.max), [SC_b], [bs_b])
            tt(bs[:, 1:2], bs[:, 1:2], bs[:, 0:1], ALU.subtract, [bs_b], [bs_b])
            for i in range(NBIS if L > nsel else 0):
                stt(bs[:, 2:3], bs[:, 1:2], float(2.0 ** -(i + 1)), bs[:, 0:1], ALU.mult, ALU.add, [bs_b], [bs_b])
                tsc(MK[:, 0:L], SC[:, 0:L], bs[:, 2:3], None, ALU.is_ge, ALU.add, [SC_b, bs_b], [MK_b, bs_b], accum=bs[:, 3:4])
                tsc(bs[:, 4:5], bs[:, 3:4], float(nsel) - 0.5, float(2.0 ** -(i + 1)), ALU.is_ge, ALU.mult, [bs_b], [bs_b])
                stt(bs[:, 0:1], bs[:, 1:2], bs[:, 4:5], bs[:, 0:1], ALU.mult, ALU.add, [bs_b], [bs_b])
            tsc(MK[:, 0:L], SC[:, 0:L], bs[:, 0:1], None, ALU.is_ge, ALU.bypass, [SC_b, bs_b], [MK_b])

        def attention(st, qcols, nq, sel, sel_b, keytiles, MK, MK_b, mixcols, page_loader=None):
            QT, QT_b = st["QT"]
            E = st["E"]
            PP = st["PP"]
            RD, RD_b = st["RD"]
            n3 = 3 * nq
            accO = [PS[0], PS[1]]
            accD = [PS[2], PS[3]]
            nk = len(keytiles)
            ei = 0
            pend = []

            def flush(pl):
                for (g, vf_g, vb_, p_, p_b, kidx) in pl:
                    mm(accO[g][0][:, 0:n3], vf_g, p_[:, 0:n3], kidx == 0, kidx == nk - 1, [vb_, p_b], [accO[g][1]])
                    mm(accD[g][0][:, 0:n3], onesB[:], p_[:, 0:n3], kidx == 0, kidx == nk - 1, [onesB_b, p_b], [accD[g][1]])
            for ki, ent in enumerate(keytiles):
                if ent[0] == "page":
                    mc0 = ent[2]
                    ktf, kb_, vf, vb_ = page_loader(ki, ent[1])
                else:
                    ktf, kb_, vf, vb_, mc0 = ent
                pm, pm_b = PS[6 + (ki % 2)]
                mm(pm[:, 0:n3], MK[:, mc0:mc0 + 128], sel, True, True, [MK_b, sel_b], [pm_b])
                cur = []
                for g in range(2):
                    ps_, ps_b = PS[4 + g]
                    mm(ps_[:, 0:n3].rearrange("p (h n) -> p h n", h=3), ktf(g), QT[:, 3 * g:3 * g + 3, qcols], True, True, [kb_, QT_b], [ps_b])
                    e_, e_b = E[ei % 4]
                    p_, p_b = PP[ei % 4]
                    ei += 1
                    act(e_[:, 0:n3], ps_[:, 0:n3], AF.Exp, [ps_b], [e_b], scale=float(128 ** -0.5))
                    tt(p_[:, 0:n3], e_[:, 0:n3], pm[:, 0:n3], ALU.mult, [e_b, pm_b], [p_b])
                    cur.append((g, vf(g), vb_, p_, p_b, ki))
                flush(pend)
                pend = cur
            flush(pend)
            for g in range(2):
                act(RD[:, 0:n3], accD[g][0][:, 0:n3], AF.Ln, [accD[g][1]], [RD_b])
                act(RD[:, 0:n3], RD[:, 0:n3], AF.Exp, [RD_b], [RD_b], scale=-1.0)
                tt(MIX[:, 3 * g:3 * g + 3, mixcols], accO[g][0][:, 0:n3].rearrange("p (h n) -> p h n", h=3),
                   RD[:, 0:n3].rearrange("p (h n) -> p h n", h=3), ALU.mult, [accO[g][1], RD_b], [MIX_b])

        def l1_common(ls, ntile):
            ntok = ntile * 128
            st = {}
            st["MQ"] = sb(ls, "MQ1", [128, 2, ntok])
            st["mq_sq"] = sb(ls, "mq_sq1", [128, ntok], BF16)
            st["mq_rs"] = sb(ls, "mq_rs1", [128, ntok])
            st["mq_bf"] = sb(ls, "mq_bf1", [128, 2, ntok], BF16)
            st["mexp"] = [sb(ls, "mexp1%d" % i, [128, ntok], BF16) for i in range(3)]
            st["mrd"] = sb(ls, "mrd1", [128, ntok])
            st["QT"] = sb(ls, "QT", [128, 6, ntok], BF16)
            st["IQT"] = sb(ls, "IQT", [64, 8, ntok], BF16)
            st["IW"] = sb(ls, "IW", [128, ntile, 8])
            return st

        def l1_inproj(ls, ntile, st):
            st["Zs"] = [sb(ls, "Z%d" % i, [128, 1864]) for i in range(ntile)]
            st["Z"] = st["Zs"][0]
            st["QN"] = sb(ls, "QN", [128, 1600])
            st["QR"] = sb(ls, "QR", [128, 1600])
            st["TB"] = sb(ls, "TB", [128, 1280])
            st["T1"] = sb(ls, "T1", [128, 384])
            st["T2"] = sb(ls, "T2", [128, 384])
            st["ss16"] = sb(ls, "ss16", [128, 16])
            st["sqj"] = st["T1"]

        def l1_attn(ls, st, sccols):
            st["R"] = [sb(ls, "R%d" % i, [128, 512], BF16) for i in range(4)]
            st["bs"] = sb(ls, "bs", [128, 8])
            st["E"] = [sb(ls, "E%d" % i, [128, 384], BF16) for i in range(4)]
            st["PP"] = [sb(ls, "PP%d" % i, [128, 384], BF16) for i in range(4)]
            st["RD"] = sb(ls, "RD", [128, 384])
            st["SC"] = sb(ls, "SC", [128, sccols])
            st["MK"] = sb(ls, "MK", [128, sccols], BF16)

        try:
            with ExitStack() as ms0:
                memf = [sb(ms0, "memf%d" % i, [128, D]) for i in range(2)]
                for mt in range(2):
                    P.dma(POOL, memf[mt][0][:], mem_d[mt * 128:(mt + 1) * 128, :], memf[mt][1], True)
                    for half in range(2):
                        pt_, pb_ = pb()
                        for j in range(4):
                            kc = half * 4 + j
                            P.op(PE, lambda h: h.transpose(pt_[:, j * 128:(j + 1) * 128], memf[mt][0][:, kc * 128:(kc + 1) * 128], identF[:]),
                                 [memf[mt][1], identF_b], [pb_])
                        act(MEMT[:, half * 4:half * 4 + 4, mt * 128:(mt + 1) * 128], pt_[:].rearrange("p (k n) -> p k n", k=4), AF.Copy, [pb_], [MEMT_b])
            P.barrier()
            MKTp = [sb(es, "MKTp%d" % l, [128, 2, 256], BF16) for l in range(2)]
            MVp = [sb(es, "MVp%d" % l, [128, 2, 256], BF16) for l in range(2)]
            for l in range(2):
                memkv_prompt(l, MKTp[l][0], MKTp[l][1], MVp[l][0], MVp[l][1])
            _stage(1)

            def sample_group():
                P.dma(POOL, X[:, 0, :], xs_d[:, :], X_b, True)
                P.op(DVE, lambda h: h.memset(UH[:], 0.0), [], [UH_b])
                for l in range(2):
                    with ExitStack() as ss_:
                        MKs = [sb(ss_, "MKs%d" % b, [128, 2, 256], BF16) for b in range(4)]
                        MVs = [sb(ss_, "MVs%d" % b, [128, 2, 256], BF16) for b in range(4)]
                        slots = []
                        for b in range(4):
                            P.dma(POOL, MKs[b][0][:], mkT_s[l, b], MKs[b][1], True)
                            P.dma(POOL, MVs[b][0][:], mv_s[l, b].rearrange("(t p) d -> p t d", p=128), MVs[b][1], True)
                            slots.append((32 * b, 32, MKs[b][0], MKs[b][1], MVs[b][0], MVs[b][1]))
                        if l == 0:
                            layer0(1, True, slots, True, True)
                            _stage(2)
                        else:
                            with ExitStack() as ls:
                                st = l1_common(ls, 1)
                                KTs, KTs_b = sb(ls, "KTs", [128, 2, 128], BF16)
                                VBs, VBs_b = sb(ls, "VBs", [128, 1, 256], BF16)
                                IKTs, IKTs_b = sb(ls, "IKTs", [64, 128], BF16)
                                with ExitStack() as zs:
                                    l1_inproj(zs, 1, st)
                                    dsa_inproj(1, 0, st, True, KTs, [KTs_b], VBs, [VBs_b], IKTs, [IKTs_b])
                                    _stage(5)
                                P.barrier()
                                with ExitStack() as as_:
                                    l1_attn(as_, st, PAST + 128)
                                    SC, SC_b = st["SC"]
                                    MK, MK_b = st["MK"]
                                    IKP, IKP_b = sb(as_, "IKP", [64, PAST], BF16)
                                    PTs, PTs_b = sb(as_, "PTs", [128, 4 * NPAGE], I32)
                                    IDX = sb(as_, "IDX", [128, 4, 4 * NPAGE], I32)
                                    sel3 = sb(as_, "sel3", [128, 4, 3, 32], BF16)
                                    for b in range(4):
                                        for j in range(3):
                                            cp(sel3[0][:, b, j, :], identF[:, 32 * b:32 * b + 32], [identF_b], [sel3[1]])
                                    KPr = [sb(as_, "KPr%d" % i, [128, 2, 128], BF16) for i in range(4)]
                                    VPr = [sb(as_, "VPr%d" % i, [128, 256], BF16) for i in range(4)]
                                    P.dma(POOL, PTs[:], pt_d.partition_broadcast(128), PTs_b, True)
                                    tsc(IDX[0][:, 0, :], PTs[:], 64.0, pidx[:, 0:1], ALU.mult, ALU.add, [PTs_b, pidx_b], [IDX[1]])
                                    tsc(IDX[0][:, 1, :], PTs[:], 256.0, pidx[:, 0:1], ALU.mult, ALU.add, [PTs_b, pidx_b], [IDX[1]])
                                    tsc(IDX[0][:, 2, :], IDX[0][:, 1, :], 128.0, None, ALU.add, ALU.bypass, [IDX[1]], [IDX[1]])
                                    tsc(IDX[0][:, 3, :], PTs[:], 128.0, pidx[:, 0:1], ALU.mult, ALU.add, [PTs_b, pidx_b], [IDX[1]])
                                    for b in (3, 2, 1, 0):
                                        rows = (64, 128) if b == 3 else (32 * b, 32 * b + 32)
                                        for pg in range(NPAGE):
                                            pi = b * NPAGE + pg
                                            P.idma(IKP[0:64, pg * 128:(pg + 1) * 128], pool_ik[:, :], IDX[0][0:64, 0, pi:pi + 1], IKP_b, [IDX[1]])
                                        keysets = [(lambda k0, n: IKP[0:64, k0:k0 + n], IKP_b, PAST, 0),
                                                   (lambda k0, n: IKTs[0:64, k0:k0 + n], IKTs_b, 128, PAST)]
                                        index_scores(st, 0, keysets, SC, SC_b, rows[0], rows[1])
                                    select_mask(st, PAST + 128, SC, SC_b, MK, MK_b, cms, cms_b, PAST, NSEL_S)
                                    _stage(6)
                                    for b in range(4):
                                        kts = []
                                        for pg in range(NPAGE):
                                            kts.append(("page", b * NPAGE + pg, pg * 128))
                                        kts.append(((lambda g: KTs[:, g, :]), KTs_b, (lambda g: VBs[:, 0, g * 128:(g + 1) * 128]), VBs_b, PAST))

                                        def page_loader(i, pi, KPr=KPr, VPr=VPr):
                                            kp, kp_b = KPr[i % 4]
                                            vp, vp_b = VPr[i % 4]
                                            P.idma(kp[:].rearrange("p g r -> p (g r)"), pool_k[:, :], IDX[0][:, 3, pi:pi + 1], kp_b, [IDX[1]])
                                            P.idma(vp[:], pool_v[:, :], IDX[0][:, 3, pi:pi + 1], vp_b, [IDX[1]])
                                            return ((lambda g: kp[:, g, :]), kp_b, (lambda g: vp[:, g * 128:(g + 1) * 128]), vp_b)
                                        attention(st, slice(32 * b, 32 * b + 32), 32, sel3[0][:, b, :, :].rearrange("p j n -> p (j n)"), sel3[1],
                                                  kts, MK, MK_b, slice(32 * b, 32 * b + 32), page_loader)
                                    mem_attention(st, 128, slots, 1)
                            P.barrier()
                    out_proj(l, 1)
                    _stage(3 + 4 * l)
                    ffn(l, 1, True, cvs_d[l])
                    _stage(4 + 4 * l)
                P.dma(POOL, ys_d[:, :], X[:, 0, :], X_b, False, is_out=True)

            sample_group()
            P.barrier()
            _stage(9)

            KT, _ = sb(es, "KT", [128, 2, T], BF16)
            VB, _ = sb(es, "VB", [128, NT, 256], BF16)
            IKT, _ = sb(es, "IKT", [64, T], BF16)
            KT_b = [Buf() for _ in range(NT)]
            VB_b = [Buf() for _ in range(NT)]
            IKT_b = [Buf() for _ in range(NT)]
            UHs = [sb(es, "UHs%d" % l, [128, NFC, 2]) for l in range(2)]
            for l in range(2):
                P.op(DVE, lambda h: h.memset(UHs[l][0][:], 0.0), [], [UHs[l][1]])

            for g in range(NG):
                first, last = (g == 0), (g == NG - 1)
                for t in range(G):
                    P.dma(POOL, X[:, t, :], x_d[(g * G + t) * 128:(g * G + t + 1) * 128, :], X_b, True)
                for l in range(2):
                    slots = [(0, G * 128, MKTp[l][0], MKTp[l][1], MVp[l][0], MVp[l][1])]
                    if l == 0:
                        layer0(G, False, slots, first, last)
                    else:
                        with ExitStack() as ls:
                            st = l1_common(ls, G)
                            with ExitStack() as zs:
                                l1_inproj(zs, G, st)
                                dsa_inproj(G, g * G, st, False, KT, KT_b, VB, VB_b, IKT, IKT_b)
                            P.barrier()
                            with ExitStack() as as_:
                                l1_attn(as_, st, T)
                                SC, SC_b = st["SC"]
                                MK, MK_b = st["MK"]
                                for t in range(G):
                                    gt = g * G + t
                                    L = (gt + 1) * 128
                                    merged = []
                                    kt = 0
                                    while kt <= gt:
                                        nkt = min(4, gt + 1 - kt)
                                        merged.append(((lambda k0, n, kt=kt: IKT[0:64, kt * 128 + k0:kt * 128 + k0 + n]), IKT_b[kt + nkt - 1], nkt * 128, kt * 128))
                                        for q in range(kt, kt + nkt - 1):
                                            P._deps(PE, [IKT_b[q]], [])
                                        kt += nkt
                                    index_scores(st, t, merged, SC, SC_b)
                                    select_mask(st, L, SC, SC_b, MK, MK_b, cmdiag, cmdiag_b, gt * 128, NSEL)
                                    kts = []
                                    for kt in range(gt + 1):
                                        kts.append(((lambda gg, kt=kt: KT[:, gg, kt * 128:(kt + 1) * 128]), KT_b[kt],
                                                    (lambda gg, kt=kt: VB[:, kt, gg * 128:(gg + 1) * 128]), VB_b[kt], kt * 128))
                                    attention(st, slice(t * 128, (t + 1) * 128), 128, ident3[:].rearrange("p j n -> p (j n)"), ident3_b,
                                              kts, MK, MK_b, slice(t * 128, (t + 1) * 128))
                                mem_attention(st, G * 128, slots, 1)
                        P.barrier()
                    if DEBUG and g == 0 and l == 0:
                        with ExitStack() as ds:
                            dm, dm_b = sb(ds, "dbgm", [128, 8, G * 128])
                            cp(dm[:], MIX[:], [MIX_b], [dm_b])
                            P.dma(POOL, dbg_mix[:, :, :], dm[:], dm_b, False, is_out=True)
                            P.barrier()
                    out_proj(l, G)
                    if DEBUG and g == 0 and l == 0:
                        P.dma(POOL, dbg_x1[:, :, :], X[:], X_b, False, is_out=True)
                    cp(UH[:], UHs[l][0][:], [UHs[l][1]], [UH_b])
                    ffn(l, G, False, cvp_d[l] if last else None)
                    cp(UHs[l][0][:], UH[:], [UH_b], [UHs[l][1]])
                    if DEBUG and g == 0 and l == 0:
                        P.dma(POOL, dbg_x2[:, :, :], X[:], X_b, False, is_out=True)
                for t in range(G):
                    P.dma(POOL, y_d[(g * G + t) * 128:(g * G + t + 1) * 128, :], X[:, t, :], X_b, False, is_out=True)
        except _Stop:
            P.barrier()
        P.finish()
    return nc


def _rope_tab(pos, half):
    inv = 10000.0 ** (-np.arange(half, dtype=np.float32) / half)
    ang = pos.astype(np.float32)[:, None] * inv[None, :]
    return np.cos(ang).astype(np.float32), np.sin(ang).astype(np.float32)


def _consts(NT, PASTLEN):
    c = {}
    c["ident"] = np.eye(128, dtype=np.float32)
    s = np.arange(128)[:, None]
    t = np.arange(128)[None, :]
    m = ((s // 32 == t // 32) & (s <= t)).astype(np.float32)
    c["mhg"] = np.tile(m, (1, 3))
    r = np.ones((128, 512), np.float32)
    r[:, ::32] = 0.0
    c["resetm"] = r
    v = np.zeros((128, 128), np.float32)
    for b in range(4):
        v[:, 32 * b + 2:32 * b + 6] = 1.0
    c["validm"] = v
    c["cmdiag"] = np.where(t <= s, 0.0, NEG).astype(np.float32)
    cm = np.full((128, 128), NEG, np.float32)
    for b in range(4):
        for tq in range(4):
            cm[32 * b + 2 + tq, 32 * b + 2:32 * b + 3 + tq] = 0.0
    c["cms"] = cm
    bo = np.zeros((128, 128), np.float32)
    bo[:64, :64] = 1.0
    bo[64:, 64:] = 1.0
    c["bones"] = bo
    m9 = np.zeros((128, 128), np.float32)
    m9[:, 96:] = 1.0
    c["m96"] = m9
    c["pidx"] = np.arange(128, dtype=np.float32).reshape(128, 1)
    T = NT * 128
    pos = np.zeros(T + 128, np.float32)
    pos[:T] = np.arange(T)
    for b in range(4):
        for tq in range(4):
            pos[T + 32 * b + 2 + tq] = PASTLEN + tq
    cq, sq = _rope_tab(pos, 64)
    ci, si = _rope_tab(pos, 32)
    c["tab"] = np.concatenate([np.tile(cq, (1, 6)), np.tile(sq, (1, 6)), np.tile(ci, (1, 8)), np.tile(si, (1, 8))], axis=1).astype(np.float32)
    return c


def _fm(v, nch):
    return np.ascontiguousarray(v.reshape(nch, 128).T)


def make_in_maps(inp, NT, NPAGE, NPOOL, ncores=8):
    f = lambda a: np.ascontiguousarray(np.asarray(a, dtype=np.float32))
    PASTLEN = NPAGE * 128
    cst = _consts(NT, PASTLEN)
    T = NT * 128
    shared = {}
    shared["w_in0"] = f(inp["w_in_hgrn"][0])
    shared["w_in1"] = f(inp["w_in_dsa"][0])
    shared["w_kv"] = f(inp["w_mem_kv"])
    shared["w_o"] = f(inp["w_out"])
    shared["w_g"] = f(inp["w_ffn_gate"])
    shared["w_u"] = f(inp["w_ffn_up"])
    shared["w_d"] = f(inp["w_ffn_down"])
    bc = lambda a: f(np.broadcast_to(np.asarray(a)[:, None, :], (a.shape[0], 128, a.shape[1])))
    shared["bc_nmix"] = bc(np.asarray(inp["norm_mix"]))
    shared["bc_nffn"] = bc(np.asarray(inp["norm_ffn"]))
    lb = np.asarray(inp["hgrn_lb_logits"], np.float32)
    shared["lbl"] = f(np.stack([_fm(lb[0], 6), _fm(lb[1], 6)], axis=1))
    shared["onorm"] = f(np.asarray(inp["hgrn_out_norm"])[0].reshape(128, 1))
    shared["bc_qn"] = f(np.broadcast_to(np.asarray(inp["dsa_q_norm"])[0][None, :], (128, 128)))
    shared["bc_kn"] = f(np.broadcast_to(np.asarray(inp["dsa_k_norm"])[0][None, :], (128, 128)))
    shared["bc_ikn"] = f(np.broadcast_to(np.asarray(inp["idx_k_norm"])[0][None, :], (128, 64)))
    mq = np.asarray(inp["mem_q_norm"], np.float32)
    shared["mqn"] = f(np.concatenate([mq, mq], axis=1).T)
    mk = np.asarray(inp["mem_k_norm"], np.float32)
    shared["bc_mkn"] = f(np.broadcast_to(np.tile(mk, (1, 4))[None, :, :], (128, 2, 256)))
    cwv = np.asarray(inp["ffn_conv_w"], np.float32)
    shared["cw"] = f(cwv.reshape(2, 3, NFC, 128).transpose(3, 0, 2, 1))
    cbv = np.asarray(inp["ffn_conv_b"], np.float32)
    shared["cb"] = f(cbv.reshape(2, NFC, 128).transpose(2, 0, 1))
    ck = np.asarray(inp["cache_k"], np.float32)[0]
    shared["pool_k"] = f(ck.transpose(0, 3, 2, 1)).reshape(NPOOL * 128, 256)
    shared["pool_v"] = f(np.asarray(inp["cache_v"], np.float32)[0].reshape(NPOOL * 128, 256))
    shared["pool_ik"] = f(np.asarray(inp["cache_idx_k"], np.float32)[0].transpose(0, 2, 1)).reshape(NPOOL * 64, 128)
    for k in ("ident", "mhg", "resetm", "validm", "cmdiag", "cms", "bones", "tab", "m96", "pidx"):
        shared[k] = cst[k]
    xs_all = np.asarray(inp["x_sample"], np.float32)
    xp = np.asarray(inp["x_prompt"], np.float32)
    memp = np.asarray(inp["mem_prompt"], np.float32)
    cmk = np.asarray(inp["cache_mem_k"], np.float32)
    cmv = np.asarray(inp["cache_mem_v"], np.float32)
    sth = np.asarray(inp["state_hgrn"], np.float32)[0]
    sfc = np.asarray(inp["state_ffn_conv"], np.float32)
    ptab = np.asarray(inp["page_table"]).astype(np.int32)
    nseq = xp.shape[0]
    maps = []
    for c in range(ncores):
        m = dict(shared)
        sq_ = c % nseq
        m["x"] = f(xp[sq_, :T])
        m["mem"] = f(memp[sq_])
        xs = np.zeros((128, D), np.float32)
        cstt = np.zeros((2, 128, NFC, 128), np.float32)
        for b in range(4):
            gb = 4 * c + b
            xs[32 * b + 2:32 * b + 6] = xs_all[gb]
            for l in range(2):
                stt_ = sfc[l, gb].reshape(2, NFC, 128)
                cstt[l, :, :, 32 * b:32 * b + 2] = stt_.transpose(2, 1, 0)
        m["xs"] = xs
        m["cst"] = cstt
        kk = cmk[:, 4 * c:4 * c + 4]
        kk = kk.reshape(2, 4, 256, 2, 2, 64)
        m["mkT_s"] = f(kk.transpose(0, 1, 4, 5, 3, 2).reshape(2, 4, 128, 2, 256))
        m["mv_s"] = f(cmv[:, 4 * c:4 * c + 4].reshape(2, 4, 256, 256))
        m["st_h"] = f(sth[4 * c:4 * c + 4])
        m["pt"] = np.ascontiguousarray(ptab[4 * c:4 * c + 4].reshape(1, -1))
        maps.append(m)
    return maps


_CACHE = {}


def run(inp, NT, NPAGE, NPOOL, NSEL, NBIS=10):
    key = (NT, NPAGE, NPOOL, NSEL, NBIS)
    if key not in _CACHE:
        _CACHE[key] = build(NT, NPAGE, NPOOL, NSEL, NBIS)
    nc = _CACHE[key]
    maps = make_in_maps(inp, NT, NPAGE, NPOOL)
    res = run_bass_kernel_spmd(nc, maps, core_ids=list(range(8)))
    return res.results


def assemble(R, NT, nseq=4):
    T = NT * 128
    y = np.stack([R[c]["y"] for c in range(nseq)])
    rows = np.array([32 * b + 2 + t for b in range(4) for t in range(4)])
    ys = np.concatenate([R[c]["ys"][rows].reshape(4, 4, D) for c in range(8)])
    hp = np.stack([R[c]["hstp"] for c in range(nseq)])[None]
    hs = np.concatenate([R[c]["hsts"] for c in range(8)])[None]
    kp = np.stack([R[c]["ko"].reshape(T, 2, 128) for c in range(nseq)])[None]
    vp = np.stack([R[c]["vo"].reshape(T, 2, 128) for c in range(nseq)])[None]
    ikp = np.stack([R[c]["iko"] for c in range(nseq)])[None]
    ks = np.concatenate([R[c]["kso"][rows].reshape(4, 4, 2, 128) for c in range(8)])[None]
    vs = np.concatenate([R[c]["vso"][rows].reshape(4, 4, 2, 128) for c in range(8)])[None]
    iks = np.concatenate([R[c]["ikso"][rows].reshape(4, 4, 64) for c in range(8)])[None]
    mk = np.stack([R[c]["mko"].reshape(2, 256, 4, 64) for c in range(nseq)], axis=1)
    mv = np.stack([R[c]["mvo"].reshape(2, 256, 4, 64) for c in range(nseq)], axis=1)
    cvp = np.stack([R[c]["cvp"][:, 126:128, :] for c in range(nseq)], axis=1)
    rows2 = np.array([32 * b + 4 + j for b in range(4) for j in range(2)])
    cvs = np.concatenate([R[c]["cvs"][:, rows2, :].reshape(2, 4, 2, DFF) for c in range(8)], axis=1)
    outs = (y, ys, hp, hs, kp, vp, ikp, ks, vs, iks, mk, mv, cvp, cvs)
    return tuple(np.ascontiguousarray(o.astype(np.float32)) for o in outs)


def kernel(**inputs):
    NT = 32
    NPAGE = 64
    NPOOL = int(np.asarray(inputs["cache_k"]).shape[1])
    R = run(inputs, NT, NPAGE, NPOOL, 256)
    return assemble(R, NT)
```
